# Optimizing a Trainium2 kernel written in Bass

```python
import jax, jax.numpy as jnp
from jax import lax
import numpy as np

D_MODEL = 1024
BATCH = 8
SEQ = 4096
DEPTH = 2

NORM_EPS = 1e-6
ROPE_THETA = 500000.0

A_HEADS = 4
A_HEAD_DIM = 64
A_WIDTH = A_HEADS * A_HEAD_DIM
A_DECAY_LORA = 64
A_ICLR_LORA = 64
A_GATE_LORA = 160
A_GN_EPS = 64e-5
A_SIZES = (A_WIDTH, A_WIDTH, A_WIDTH, A_DECAY_LORA, A_ICLR_LORA, A_GATE_LORA)
A_PROJ = sum(A_SIZES)

B_HEADS = 4
B_HEAD_DIM = 64
B_WIDTH = B_HEADS * B_HEAD_DIM
B_ROT_DIM = B_HEAD_DIM // 4
B_PATTERNS = ((128, 1), (512, 4), (2048, 16))
B_BLOCK = 128
B_PROJ = 3 * B_WIDTH

C_HEADS = 4
C_NOPE_DIM = 128
C_ROPE_DIM = 64
C_V_DIM = 128
C_Q_LORA = 256
C_KV_LORA = 128
C_WIDTH = C_HEADS * C_V_DIM
C_Q_BLOCK = 128
C_PROJ = C_Q_LORA + C_KV_LORA + C_ROPE_DIM

MIX_WIDTH = A_WIDTH + B_WIDTH + C_WIDTH
P_TOTAL = A_PROJ + B_PROJ + C_PROJ
D_FF = ((8 * D_MODEL + 767) // 768) * 256

kernel_name = "hybrid_rwkv7_dilated_mla_trunk"


def split_sizes(t, sizes):
    idx = [int(i) for i in np.cumsum(sizes)[:-1]]
    return jnp.split(t, idx, axis=-1)


def rms_norm(x, g):
    xf = x.astype(jnp.float32)
    y = xf * lax.rsqrt(jnp.mean(xf * xf, axis=-1, keepdims=True) + NORM_EPS)
    return (y * g.astype(jnp.float32)).astype(x.dtype)


def rope_tables(positions, dim):
    inv_freq = 1.0 / (ROPE_THETA ** (jnp.arange(0, dim, 2, dtype=jnp.float32) / dim))
    ang = positions.astype(jnp.float32)[..., None] * inv_freq
    return jnp.cos(ang), jnp.sin(ang)


def apply_rope(t, cos, sin):
    t1, t2 = jnp.split(t, 2, axis=-1)
    c = cos[:, :, None, :].astype(t.dtype)
    s = sin[:, :, None, :].astype(t.dtype)
    return jnp.concatenate([t1 * c - t2 * s, t1 * s + t2 * c], axis=-1)


def partial_rope(t, cos, sin):
    rot, rest = jnp.split(t, [B_ROT_DIM], axis=-1)
    return jnp.concatenate([apply_rope(rot, cos, sin), rest], axis=-1)


def token_shift(t):
    return jnp.pad(t, ((0, 0), (1, 0), (0, 0)))[:, :-1]


def rwkv7_mixer(pa, mu, w0, decay_up, a0, iclr_up, gate_up, k_k, k_a, r_k, ln_w, ln_b):
    bsz, seq, _ = pa.shape
    pf = pa.astype(jnp.float32)
    pf = pf + (token_shift(pf) - pf) * mu
    r, k, v, xw, xa, xg = split_sizes(pf, A_SIZES)
    w_log = -jax.nn.softplus(-(w0 + jnp.tanh(xw) @ decay_up)) - 0.5
    decay = jnp.exp(-jnp.exp(w_log))
    a = jax.nn.sigmoid(a0 + xa @ iclr_up)
    g = jax.nn.sigmoid(xg) @ gate_up

    def heads(t):
        return t.reshape(bsz, seq, A_HEADS, A_HEAD_DIM)

    kk = heads(k * k_k)
    kk = kk / jnp.maximum(jnp.linalg.norm(kk, axis=-1, keepdims=True), 1e-12)
    k = k * (1.0 + (a - 1.0) * k_a)
    r_h, k_h, v_h, w_h, a_h = heads(r), heads(k), heads(v), heads(decay), heads(a)
    b_h = kk * a_h

    def step(state, inp):
        r_t, w_t, k_t, v_t, kk_t, b_t = inp
        sa = jnp.einsum('bhvk,bhk->bhv', state, kk_t)
        state = (state * w_t[:, :, None, :] - sa[..., None] * b_t[:, :, None, :]
                 + v_t[..., None] * k_t[:, :, None, :])
        y = jnp.einsum('bhvk,bhk->bhv', state, r_t)
        return state, y

    seq_first = lambda t: jnp.moveaxis(t, 1, 0)
    state0 = jnp.zeros((bsz, A_HEADS, A_HEAD_DIM, A_HEAD_DIM), jnp.float32)
    _, y = lax.scan(step, state0, tuple(seq_first(t) for t in (r_h, w_h, k_h, v_h, kk, b_h)))
    y = jnp.moveaxis(y, 0, 1)
    mean = jnp.mean(y, axis=-1, keepdims=True)
    var = jnp.mean(jnp.square(y - mean), axis=-1, keepdims=True)
    y = ((y - mean) * lax.rsqrt(var + A_GN_EPS)).reshape(bsz, seq, A_WIDTH) * ln_w + ln_b
    bonus = jnp.sum(r_h * k_h * r_k, axis=-1, keepdims=True) * v_h
    y = y + bonus.reshape(bsz, seq, A_WIDTH)
    return (y * g).astype(pa.dtype)


def dilated_branch(q, k, v, dil, band):
    bsz, seq, nh, hd = q.shape
    span = dil * B_BLOCK
    seq_pad = -(-seq // span) * span
    pad = ((0, 0), (0, seq_pad - seq), (0, 0), (0, 0))
    q, k, v = (jnp.pad(t, pad) for t in (q, k, v))
    sub = seq_pad // dil
    nb = sub // B_BLOCK

    def to_blocks(t):
        return t.reshape(bsz, sub, dil, nh, hd).transpose(0, 2, 1, 3, 4).reshape(bsz, dil, nb, B_BLOCK, nh, hd)

    def with_prev(t):
        prev = jnp.pad(t, ((0, 0), (0, 0), (1, 0), (0, 0), (0, 0), (0, 0)))[:, :, :-1]
        return jnp.concatenate([prev, t], axis=3)

    qb = to_blocks(q)
    kw, vw = with_prev(to_blocks(k)), with_prev(to_blocks(v))
    s = jnp.einsum('brnqhd,brnkhd->brnhqk', qb, kw).astype(jnp.float32) * (hd ** -0.5)
    qi = jnp.arange(B_BLOCK)[:, None] + B_BLOCK
    ki = jnp.arange(2 * B_BLOCK)[None, :]
    dist = qi - ki
    in_band = (dist >= 0) & (dist <= band)
    has_prev = (jnp.arange(nb) > 0)[:, None, None] | (ki >= B_BLOCK)[None]
    mask = in_band[None] & has_prev
    s = jnp.where(mask[None, None, :, None], s, -jnp.inf)
    lse = jax.nn.logsumexp(s, axis=-1)
    p = jnp.exp(s - lse[..., None])
    o = jnp.einsum('brnhqk,brnkhd->brnqhd', p.astype(v.dtype), vw)
    o = o.reshape(bsz, dil, sub, nh, hd).transpose(0, 2, 1, 3, 4).reshape(bsz, seq_pad, nh, hd)[:, :seq]
    lse = lse.transpose(0, 1, 2, 4, 3).reshape(bsz, dil, sub, nh).transpose(0, 2, 1, 3).reshape(bsz, seq_pad, nh)[:, :seq]
    return o, lse


def dilated_mixer(pb, cos, sin):
    bsz, seq, _ = pb.shape
    q, k, v = (t.reshape(bsz, seq, B_HEADS, B_HEAD_DIM) for t in jnp.split(pb, 3, axis=-1))
    q = partial_rope(q, cos, sin)
    k = partial_rope(k, cos, sin)
    outs, lses = zip(*[dilated_branch(q, k, v, d, w // d) for (w, d) in B_PATTERNS])
    wts = jax.nn.softmax(jnp.stack(lses), axis=0)
    o = jnp.sum(wts[..., None] * jnp.stack(outs).astype(jnp.float32), axis=0)
    return o.reshape(bsz, seq, B_WIDTH).astype(pb.dtype)


def mla_mixer(pc, cos, sin, q_norm_g, kv_norm_g, w_uq, w_ukv):
    bsz, seq, _ = pc.shape
    cq, ckv, k_rope = split_sizes(pc, (C_Q_LORA, C_KV_LORA, C_ROPE_DIM))
    q = (rms_norm(cq, q_norm_g) @ w_uq).reshape(bsz, seq, C_HEADS, C_NOPE_DIM + C_ROPE_DIM)
    kv = (rms_norm(ckv, kv_norm_g) @ w_ukv).reshape(bsz, seq, C_HEADS, C_NOPE_DIM + C_V_DIM)
    q_nope, q_rope = jnp.split(q, [C_NOPE_DIM], axis=-1)
    k_nope, v = jnp.split(kv, [C_NOPE_DIM], axis=-1)
    q_rope = apply_rope(q_rope, cos, sin)
    k_rope = apply_rope(k_rope[:, :, None, :], cos, sin)[:, :, 0]
    nq = seq // C_Q_BLOCK
    scale = (C_NOPE_DIM + C_ROPE_DIM) ** -0.5
    key_pos = jnp.arange(seq)

    def blocks(t):
        return jnp.moveaxis(t.reshape(bsz, nq, C_Q_BLOCK, *t.shape[2:]), 1, 0)

    def attend(args):
        qn, qr, start = args
        s = (jnp.einsum('bqhd,bkhd->bhqk', qn, k_nope)
             + jnp.einsum('bqhd,bkd->bhqk', qr, k_rope)).astype(jnp.float32) * scale
        q_pos = start + jnp.arange(C_Q_BLOCK)
        s = jnp.where(key_pos[None, :] <= q_pos[:, None], s, -jnp.inf)
        p = jax.nn.softmax(s, axis=-1)
        return jnp.einsum('bhqk,bkhd->bqhd', p.astype(v.dtype), v)

    o = lax.map(attend, (blocks(q_nope), blocks(q_rope), jnp.arange(nq) * C_Q_BLOCK))
    return jnp.moveaxis(o, 0, 1).reshape(bsz, seq, C_WIDTH)


def setup_inputs(seed: int = 0) -> dict:
    key = jax.random.key(seed)
    ks = iter(jax.random.split(key, 32))
    L = DEPTH

    def nrm(shape, scale):
        return jax.random.normal(next(ks), shape, jnp.float32) * scale

    def uni(shape, lo, hi):
        return jax.random.uniform(next(ks), shape, jnp.float32, lo, hi)

    return {
        'x': nrm((BATCH, SEQ, D_MODEL), 1.0),
        'positions': jnp.tile(jnp.arange(SEQ, dtype=jnp.int32)[None, :], (BATCH, 1)),
        'attn_norm_g': 1.0 + nrm((L, D_MODEL), 0.02),
        'w_in': nrm((L, D_MODEL, P_TOTAL), D_MODEL ** -0.5),
        'a_mu': uni((L, A_PROJ), 0.0, 1.0),
        'a_w0': uni((L, A_WIDTH), -6.0, -1.0),
        'a_decay_up': nrm((L, A_DECAY_LORA, A_WIDTH), 0.5 * A_DECAY_LORA ** -0.5),
        'a_a0': nrm((L, A_WIDTH), 0.1),
        'a_iclr_up': nrm((L, A_ICLR_LORA, A_WIDTH), A_ICLR_LORA ** -0.5),
        'a_gate_up': nrm((L, A_GATE_LORA, A_WIDTH), A_GATE_LORA ** -0.5),
        'a_k_k': 0.85 + nrm((L, A_WIDTH), 0.02),
        'a_k_a': 1.0 + nrm((L, A_WIDTH), 0.02),
        'a_r_k': nrm((L, A_HEADS, A_HEAD_DIM), 0.1),
        'a_ln_w': 1.0 + nrm((L, A_WIDTH), 0.02),
        'a_ln_b': nrm((L, A_WIDTH), 0.02),
        'c_q_norm_g': 1.0 + nrm((L, C_Q_LORA), 0.02),
        'c_kv_norm_g': 1.0 + nrm((L, C_KV_LORA), 0.02),
        'c_w_uq': nrm((L, C_Q_LORA, C_HEADS * (C_NOPE_DIM + C_ROPE_DIM)), C_Q_LORA ** -0.5),
        'c_w_ukv': nrm((L, C_KV_LORA, C_HEADS * (C_NOPE_DIM + C_V_DIM)), C_KV_LORA ** -0.5),
        'w_out': nrm((L, MIX_WIDTH, D_MODEL), MIX_WIDTH ** -0.5),
        'ffn_norm_g': 1.0 + nrm((L, D_MODEL), 0.02),
        'ffn_w_gate': nrm((L, D_MODEL, D_FF), D_MODEL ** -0.5),
        'ffn_w_up': nrm((L, D_MODEL, D_FF), D_MODEL ** -0.5),
        'ffn_w_down': nrm((L, D_FF, D_MODEL), D_FF ** -0.5),
        'final_norm_g': 1.0 + nrm((D_MODEL,), 0.02),
    }


def reference(x, positions, attn_norm_g, w_in, a_mu, a_w0, a_decay_up, a_a0, a_iclr_up,
              a_gate_up, a_k_k, a_k_a, a_r_k, a_ln_w, a_ln_b, c_q_norm_g, c_kv_norm_g,
              c_w_uq, c_w_ukv, w_out, ffn_norm_g, ffn_w_gate, ffn_w_up, ffn_w_down,
              final_norm_g):
    cos_b, sin_b = rope_tables(positions, B_ROT_DIM)
    cos_c, sin_c = rope_tables(positions, C_ROPE_DIM)
    for l in range(DEPTH):
        h = rms_norm(x, attn_norm_g[l])
        pa, pb, pc = split_sizes(h @ w_in[l], (A_PROJ, B_PROJ, C_PROJ))
        ya = rwkv7_mixer(pa, a_mu[l], a_w0[l], a_decay_up[l], a_a0[l], a_iclr_up[l],
                         a_gate_up[l], a_k_k[l], a_k_a[l], a_r_k[l], a_ln_w[l], a_ln_b[l])
        yb = dilated_mixer(pb, cos_b, sin_b)
        yc = mla_mixer(pc, cos_c, sin_c, c_q_norm_g[l], c_kv_norm_g[l], c_w_uq[l], c_w_ukv[l])
        x = x + jnp.concatenate([ya, yb, yc], axis=-1) @ w_out[l]
        h = rms_norm(x, ffn_norm_g[l])
        x = x + (jax.nn.silu(h @ ffn_w_gate[l]) * (h @ ffn_w_up[l])) @ ffn_w_down[l]
    return rms_norm(x, final_norm_g)
```

```python
import contextlib
import math
import numpy as np
import concourse.bass as bass
import concourse.mybir as mybir
from concourse.bass_utils import run_bass_kernel_spmd

F32 = mybir.dt.float32
BF16 = mybir.dt.bfloat16
I32 = mybir.dt.int32
AF = mybir.ActivationFunctionType
ALU = mybir.AluOpType

S_LEN = 4096
D = 1024
DFF = 2816
L = 2
NG = 8
EPS = 1e-6
THETA = 500000.0
MAGIC = 12582912.0

ENGS = ("tensor", "vector", "scalar", "gpsimd", "sync")
NSLOT = 12


class Buf:
    __slots__ = ("w", "r")

    def __init__(self):
        self.w = {}
        self.r = []


class Tl:
    __slots__ = ("t", "b")

    def __init__(self, t):
        self.t = t
        self.b = Buf()

    def __getitem__(self, idx):
        return self.t[idx]


class Sched:
    def __init__(self, nc, sems):
        self.nc = nc
        self.lists = {k: [] for k in ENGS}
        self.cnt = {k: 0 for k in ENGS}
        self.seen = {k: {} for k in ENGS}
        self.semh = sems
        self.dq = {q: {"n": 0, "slots": [f"dma_{q}_{i}" for i in range(NSLOT)]} for q in ("sync", "gpsimd")}

    def _waits(self, eng, reads, writes, merge=False):
        deps = {}

        def need(ev):
            if ev is None:
                return
            k, v = ev
            if k == "tensor" and eng == "tensor":
                return
            if deps.get(k, 0) < v:
                deps[k] = v
        for b in reads:
            for ev in b.w.items():
                need(ev)
        for b in writes:
            if not merge:
                for ev in b.w.items():
                    need(ev)
            for ev in b.r:
                need(ev)
        out = []
        seen = self.seen[eng]
        for k, v in deps.items():
            if seen.get(k, 0) < v:
                seen[k] = v
                out.append((k, v))
        return out

    def _mark(self, ev, reads, writes, merge=False):
        for b in reads:
            b.r.append(ev)
        for b in writes:
            if merge:
                b.w[ev[0]] = max(b.w.get(ev[0], 0), ev[1])
            else:
                b.w = {ev[0]: ev[1]}
                b.r = []

    def op(self, eng, emit, reads=(), writes=()):
        reads = [x.b if isinstance(x, Tl) else x for x in reads]
        writes = [x.b if isinstance(x, Tl) else x for x in writes]
        waits = self._waits(eng, reads, writes)
        self.cnt[eng] += 1
        ev = (eng, self.cnt[eng])
        semh = self.semh
        mysem = semh[eng]

        def run(e):
            for k, v in waits:
                e.wait_ge(semh[k], v)
            emit(e).then_inc(mysem, 1)
        self.lists[eng].append(run)
        self._mark(ev, reads, writes)

    def dma(self, q, out, in_, reads=(), writes=(), merge=False):
        reads = [x.b if isinstance(x, Tl) else x for x in reads]
        writes = [x.b if isinstance(x, Tl) else x for x in writes]
        st = self.dq[q]
        n = st["n"]
        st["n"] += 1
        slots = st["slots"]
        key = slots[n % len(slots)]
        use = n // len(slots)
        waits = self._waits(q, reads, writes, merge)
        seen = self.seen[q]
        if use > 0 and seen.get(key, 0) < 16 * use:
            seen[key] = 16 * use
            waits.append((key, 16 * use))
        ev = (key, 16 * (use + 1))
        semh = self.semh

        def run(e):
            for k, v in waits:
                e.wait_ge(semh[k], v)
            e.dma_start(out=out, in_=in_).then_inc(semh[key], 16)
        self.lists[q].append(run)
        self._mark(ev, reads, writes, merge)

    def barrier(self):
        targets = {k: self.cnt[k] for k in ENGS if self.cnt[k] > 0}
        for q, st in self.dq.items():
            n = st["n"]
            ns = len(st["slots"])
            for i, key in enumerate(st["slots"]):
                uses = (n - i + ns - 1) // ns if n > i else 0
                if uses > 0:
                    targets[key] = 16 * uses
        semh = self.semh
        for eng in ENGS:
            seen = self.seen[eng]
            waits = []
            for k, v in targets.items():
                if k == eng:
                    continue
                if seen.get(k, 0) < v:
                    seen[k] = v
                    waits.append((k, v))

            def run(e, waits=waits):
                for k, v in waits:
                    e.wait_ge(semh[k], v)
            self.lists[eng].append(run)

    def emit(self):
        with self.nc.Block() as block:
            for name in ENGS:
                lst = self.lists[name]

                def body(e, lst=lst):
                    for f in lst:
                        f(e)
                getattr(block, name)(body)


class K:
    def __init__(self, nc, S):
        self.nc = nc
        self.S = S
        self.uid = 0
        self.psum = []
        self.pi = 0
        self.rot = None

    def name(self, p="t"):
        self.uid += 1
        return f"{p}{self.uid}"

    @contextlib.contextmanager
    def scope(self):
        es = contextlib.ExitStack()
        sc = Scope(self, es)
        try:
            yield sc
            self.S.barrier()
        finally:
            es.close()

    def ps(self):
        rot = self.rot if self.rot else list(range(7))
        t = self.psum[rot[self.pi % len(rot)]]
        self.pi += 1
        return t

    def mm(self, out, lhsT, rhs, start=True, stop=True, reads=(), writes=(), skip=False):
        if skip:
            self.S.op("tensor", lambda e: e.matmul(out, lhsT=lhsT, rhs=rhs, start=start, stop=stop, skip_group_check=True), reads, writes)
        else:
            self.S.op("tensor", lambda e: e.matmul(out, lhsT=lhsT, rhs=rhs, start=start, stop=stop), reads, writes)

    def tr(self, out, in_, ident, reads=(), writes=()):
        self.S.op("tensor", lambda e: e.transpose(out=out, in_=in_, identity=ident), reads, writes)

    def act(self, out, in_, func, bias=0.0, scale=1.0, accum_out=None, reads=(), writes=()):
        if accum_out is None:
            self.S.op("scalar", lambda e: e.activation(out=out, in_=in_, func=func, bias=bias, scale=scale), reads, writes)
        else:
            self.S.op("scalar", lambda e: e.activation(out=out, in_=in_, func=func, bias=bias, scale=scale,
                                                       accum_out=accum_out), reads, writes)

    def tt(self, eng, out, in0, in1, op, reads=(), writes=()):
        self.S.op(eng, lambda e: e.tensor_tensor(out=out, in0=in0, in1=in1, op=op), reads, writes)

    def ts(self, eng, out, in0, s1, s2=None, op0=ALU.mult, op1=None, reads=(), writes=()):
        if op1 is None:
            self.S.op(eng, lambda e: e.tensor_scalar(out=out, in0=in0, scalar1=s1, scalar2=None, op0=op0), reads, writes)
        else:
            self.S.op(eng, lambda e: e.tensor_scalar(out=out, in0=in0, scalar1=s1, scalar2=s2, op0=op0, op1=op1), reads, writes)

    def stt(self, eng, out, in0, scalar, in1, op0, op1, reads=(), writes=()):
        self.S.op(eng, lambda e: e.scalar_tensor_tensor(out=out, in0=in0, scalar=scalar, in1=in1, op0=op0, op1=op1), reads, writes)

    def copy(self, eng, out, in_, reads=(), writes=()):
        if eng == "scalar":
            self.S.op(eng, lambda e: e.activation(out=out, in_=in_, func=AF.Copy), reads, writes)
        else:
            self.S.op(eng, lambda e: e.tensor_copy(out=out, in_=in_), reads, writes)

    def recip(self, out, in_, reads=(), writes=()):
        self.S.op("vector", lambda e: e.reciprocal(out=out, in_=in_), reads, writes)

    def memset(self, eng, out, val, writes=()):
        self.S.op(eng, lambda e: e.memset(out, val), (), writes)

    def dma(self, q, out, in_, reads=(), writes=(), merge=False):
        self.S.dma(q, out, in_, reads, writes, merge)


class Scope:
    def __init__(self, k, es):
        self.k = k
        self.es = es

    def tile(self, shape, dtype, name="t"):
        t = self.es.enter_context(self.k.nc.sbuf_tensor(self.k.name(name), list(shape), dtype))
        return Tl(t)

    def pool(self, n, shape, dtype, name="p"):
        return Pool([self.tile(shape, dtype, name) for _ in range(n)])


class Pool:
    def __init__(self, tiles):
        self.tiles = tiles
        self.i = 0

    def get(self):
        t = self.tiles[self.i % len(self.tiles)]
        self.i += 1
        return t


C_ID, C_TRI, C_PB, C_PC, C_FC, C_BO, C_ONE, C_NB = 0, 128, 640, 768, 896, 898, 1026, 1154
NCONST = 1154 + 256
NEG = -30000.0


def make_consts():
    c = np.zeros((128, NCONST), np.float32)
    c[:, C_ID:C_ID + 128] = np.eye(128)
    r = np.arange(128)[:, None]
    q = np.arange(128)[None, :]
    c[:, C_TRI + 0:C_TRI + 128] = (q >= r)
    c[:, C_TRI + 128:C_TRI + 256] = (q <= r)
    c[:, C_TRI + 256:C_TRI + 384] = (q > r)
    c[:, C_TRI + 384:C_TRI + 512] = (q < r)
    pb = np.zeros((128, 128), np.float32)
    for rr in range(128):
        d = rr % 64
        if d < 8:
            pb[rr + 8, rr] = -1.0
        elif d < 16:
            pb[rr - 8, rr] = 1.0
    c[:, C_PB:C_PB + 128] = pb
    pc = np.zeros((128, 128), np.float32)
    for rr in range(64):
        if rr < 32:
            pc[rr + 32, rr] = -1.0
        else:
            pc[rr - 32, rr] = 1.0
    c[:, C_PC:C_PC + 128] = pc
    invb = 1.0 / (THETA ** (np.arange(0, 16, 2, dtype=np.float32) / 16))
    invc = 1.0 / (THETA ** (np.arange(0, 64, 2, dtype=np.float32) / 64))
    for rr in range(128):
        d = rr % 64
        c[rr, C_FC] = invb[d % 8] / (2 * math.pi) if d < 16 else 0.0
        c[rr, C_FC + 1] = invc[rr % 32] / (2 * math.pi)
    bo = np.zeros((128, 128), np.float32)
    bo[:64, :64] = 1
    bo[64:, 64:] = 1
    c[:, C_BO:C_BO + 128] = bo
    c[:, C_ONE:C_ONE + 128] = 1.0
    c[:, C_NB:C_NB + 128] = np.where(q >= r, 0.0, NEG)
    c[:, C_NB + 128:C_NB + 256] = np.where(q <= r, 0.0, NEG)
    return c


V_GATT, V_GFFN, V_GFIN, V_CQG, V_CKVG, V_MU, V_W0, V_A0, V_KK, V_KA, V_RK = 0, 8, 16, 24, 26, 27, 36, 38, 40, 42, 44
NV = 46


def col(v, n):
    return np.ascontiguousarray(v.reshape(n, 128).T)


def make_vecs(inp, l):
    v = np.zeros((128, NV), np.float32)
    v[:, V_GATT:V_GATT + 8] = col(inp["attn_norm_g"][l], 8)
    v[:, V_GFFN:V_GFFN + 8] = col(inp["ffn_norm_g"][l], 8)
    v[:, V_GFIN:V_GFIN + 8] = col(inp["final_norm_g"], 8)
    v[:, V_CQG:V_CQG + 2] = col(inp["c_q_norm_g"][l], 2)
    v[:, V_CKVG:V_CKVG + 1] = col(inp["c_kv_norm_g"][l], 1)
    mu = np.zeros(9 * 128, np.float32)
    mu[:1056] = inp["a_mu"][l]
    v[:, V_MU:V_MU + 9] = col(mu, 9)
    v[:, V_W0:V_W0 + 2] = col(inp["a_w0"][l], 2)
    v[:, V_A0:V_A0 + 2] = col(inp["a_a0"][l], 2)
    v[:, V_KK:V_KK + 2] = col(inp["a_k_k"][l], 2)
    v[:, V_KA:V_KA + 2] = col(inp["a_k_a"][l], 2)
    v[:, V_RK:V_RK + 2] = col(inp["a_r_k"][l].reshape(-1), 2)
    return v


def ktile(w):
    kk = w.shape[0] // 128
    return np.ascontiguousarray(w.reshape(kk, 128, w.shape[1]).transpose(1, 0, 2))


class Prog:
    pass


def build(debug=False, stop=None, nlayers=L, skip=()):
    nc = bass.Bass("TRN2", target_bir_lowering=False)
    P = Prog()
    din = {}

    def inp(name, shape, dt=F32):
        din[name] = nc.dram_tensor(name, list(shape), dt, kind="ExternalInput").ap()
        return din[name]

    skind = "ExternalOutput" if debug else "Internal"
    dscr = {}

    def scr(name, shape, dt):
        dscr[name] = nc.dram_tensor(name, list(shape), dt, kind=skind).ap()
        return dscr[name]

    x_in = inp("x", [S_LEN, D])
    pos = inp("pos", [1, S_LEN], I32)
    consts = inp("consts", [128, NCONST])
    vecs = inp("vecs", [L, 128, NV])
    lnwb = inp("lnwb", [L, 128, 512])
    gfinb = inp("gfinb", [128, D])
    w_in = inp("w_in", [L, 128, 8, 2272])
    w_uq = inp("w_uq", [L, 128, 2, 768])
    w_ukv = inp("w_ukv", [L, 128, 1, 1024])
    w_out = inp("w_out", [L, 128, 8, 1024])
    w_gate = inp("w_gate", [L, 128, 8, DFF])
    w_up = inp("w_up", [L, 128, 8, DFF])
    w_down = inp("w_down", [L, 128, 22, 1024])
    a_dec = inp("a_dec", [L, 64, 256])
    a_icl = inp("a_icl", [L, 64, 256])
    a_gate = inp("a_gate", [L, 160, 256])
    out = nc.dram_tensor("out", [S_LEN, D], F32, kind="ExternalOutput").ap()

    pA = scr("pA", [1056, S_LEN], F32)
    qB = [scr(f"qB{i}", [256, S_LEN], BF16) for i in range(3)]
    kB = [scr(f"kB{i}", [256, S_LEN], BF16) for i in range(3)]
    vB = scr("vB", [S_LEN, 256], BF16)
    qn = scr("qn", [512, S_LEN], BF16)
    qr = scr("qr", [256, S_LEN], BF16)
    kn = scr("kn", [512, S_LEN], BF16)
    kr = scr("kr", [64, S_LEN], BF16)
    vC = scr("vC", [S_LEN, 512], BF16)
    cat = scr("cat", [1024, S_LEN], BF16)
    x1 = scr("x1", [S_LEN, D], F32)
    xr = scr("xr", [S_LEN, D], F32)
    aT = scr("aT", [DFF, S_LEN], BF16)
    tabB = scr("tabB", [2, 128, S_LEN], F32)
    tabC = scr("tabC", [2, 64, S_LEN], F32)

    with contextlib.ExitStack() as es:
        E = es.enter_context
        names = list(ENGS) + [f"dma_{q}_{i}" for q in ("sync", "gpsimd") for i in range(NSLOT)]
        assert len(names) <= 40
        sems = {n: E(nc.semaphore(n)) for n in names}
        S = Sched(nc, sems)
        k = K(nc, S)
        k.psum = [Tl(E(nc.psum_tensor(f"ps{i}", [128, 512], F32))) for i in range(8)]
        pTt = Tl(k.psum[7][:].bitcast(BF16).rearrange("p (a b) -> p a b", a=8))
        pTt.b = k.psum[7].b
        cst = Tl(E(nc.sbuf_tensor("cst", [128, NCONST], F32)))
        cstb = Tl(E(nc.sbuf_tensor("cstb", [128, NCONST], BF16)))
        k.dma("sync", cst[:], consts[:, :], writes=[cst])
        k.copy("vector", cstb[:], cst[:], reads=[cst], writes=[cstb])
        P.__dict__.update(locals())

        phase_tables(P)
        done = (stop == "tables")
        for l in range(nlayers):
            if done:
                break
            xsrc = x_in if l == 0 else xr
            phase_p1(P, l, xsrc)
            if stop == f"p1_{l}":
                break
            if "mla" not in skip:
                phase_mla(P, l)
            if stop == f"mla_{l}":
                break
            if "dil" not in skip:
                phase_dil(P, l)
            if stop == f"dil_{l}":
                break
            phase_rwkv(P, l)
            if stop == f"rwkv_{l}":
                break
            phase_tail(P, l, xsrc, last=(l == nlayers - 1))
        S.barrier()
        S.emit()
    return nc


def phase_tables(P):
    k, cst = P.k, P.cst
    with k.scope() as sc:
        posi = sc.tile([128, S_LEN], I32)
        posf = sc.tile([128, S_LEN], F32)
        y = sc.tile([128, S_LEN], F32)
        t = sc.tile([128, S_LEN], F32)
        r = sc.tile([128, S_LEN], F32)
        o = sc.tile([128, S_LEN], F32)
        src = bass.AP(P.pos.tensor, 0, [[0, 128], [1, S_LEN]])
        k.dma("sync", posi[:], src, writes=[posi])
        k.copy("vector", posf[:], posi[:], reads=[posi], writes=[posf])
        for ti, (fc, rows, dst) in enumerate(((C_FC, 128, P.tabB), (C_FC + 1, 64, P.tabC))):
            for cs in range(2):
                add = 0.25 if cs == 0 else 0.0
                k.ts("vector", y[:rows], posf[:rows], cst[:rows, fc:fc + 1], add, ALU.mult, ALU.add,
                     reads=[posf, cst], writes=[y])
                k.ts("vector", t[:rows], y[:rows], MAGIC, None, ALU.add, reads=[y], writes=[t])
                k.ts("vector", t[:rows], t[:rows], -MAGIC, None, ALU.add, reads=[t], writes=[t])
                k.tt("vector", r[:rows], y[:rows], t[:rows], ALU.subtract, reads=[y, t], writes=[r])
                k.act(o[:rows], r[:rows], AF.Sin, scale=6.28318, reads=[r], writes=[o])
                k.dma("sync", dst[cs, :, :], o[:rows], reads=[o])


def load_w(P, sc, dst, src, kc, n, stage, eng="gpsimd"):
    k = P.k
    for i in range(kc):
        st = stage.get()
        k.dma("sync", st[:, 0:n], src[:, i, :], writes=[st])
        k.copy(eng, dst[:, i, :], st[:, 0:n], reads=[st], writes=[dst])


def load_w2(P, dst, src, kc, nsplit):
    step = (kc + nsplit - 1) // nsplit
    for i in range(0, kc, step):
        j = min(kc, i + step)
        P.k.dma("gpsimd", dst[:, i:j, :], src[:, i:j, :], writes=[dst], merge=True)


def rmsnorm_T(P, xt, junk, ss, rstd, xn, pT, gb, hT_ap, hT):
    k = P.k
    k.act(junk[:], xt[:], AF.Square, accum_out=ss[:], reads=[xt], writes=[junk, ss])
    k.act(rstd[:], ss[:], AF.Sqrt, bias=EPS, scale=1.0 / D, reads=[ss], writes=[rstd])
    k.recip(rstd[:], rstd[:], reads=[rstd], writes=[rstd])
    k.ts("vector", xn[:], xt[:], rstd[:, 0:1], None, ALU.mult, reads=[xt, rstd], writes=[xn])
    for kk in range(8):
        k.tr(pT[:, kk, :], xn[:, kk * 128:(kk + 1) * 128], P.cstb[:, C_ID:C_ID + 128], reads=[xn, P.cstb], writes=[pT])
    k.tt("vector", hT_ap, pT[:], gb[:], ALU.mult, reads=[pT, gb], writes=[hT])


def make_gb(P, sc, vec, c0):
    k = P.k
    gb = sc.tile([128, 8, 128], BF16)
    for kk in range(8):
        k.ts("gpsimd", gb[:, kk, :], P.cst[:, C_ONE:C_ONE + 128], vec[:, c0 + kk:c0 + kk + 1], None, ALU.mult,
             reads=[P.cst, vec], writes=[gb])
    return gb


def rope(P, ps, rows, permT, cos, sin, scale, tmp, outs, srcs, after=None):
    k = P.k
    qs = tmp.get()
    k.act(qs[:rows], ps[:rows], AF.Copy, scale=scale, reads=[ps], writes=[qs])

    def second():
        rope_b(P, qs, rows, permT, cos, sin, tmp, outs, srcs)
        if after is not None:
            after()
    return second


def rope_b(P, qs, rows, permT, cos, sin, tmp, outs, srcs):
    k = P.k
    pp = k.ps()
    k.mm(pp[:rows], permT, qs[:rows], reads=[qs, P.cst], writes=[pp])
    t1 = tmp.get()
    k.tt("vector", t1[:rows], qs[:rows], cos, ALU.mult, reads=[qs] + srcs, writes=[t1])
    t2 = tmp.get()
    k.tt("vector", t2[:rows], pp[:rows], sin, ALU.mult, reads=[pp] + srcs, writes=[t2])
    for eng, out_ap, view, tl in outs:
        k.tt(eng, out_ap, view(t1[:rows]), view(t2[:rows]), ALU.add, reads=[t1, t2], writes=[tl])


def phase_p1(P, l, xsrc):
    k, cst, cstb = P.k, P.cst, P.cstb
    with k.scope() as sc:
        winb = sc.tile([128, 8, 2272], BF16)
        wuqb = sc.tile([128, 2, 768], BF16)
        wukvb = sc.tile([128, 1, 1024], BF16)
        vec = sc.tile([128, NV], F32)
        k.dma("sync", vec[:], P.vecs[l], writes=[vec])
        load_w2(P, winb, P.w_in[l], 8, 4)
        load_w2(P, wuqb, P.w_uq[l], 2, 1)
        load_w2(P, wukvb, P.w_ukv[l], 1, 1)
        gb = make_gb(P, sc, vec, V_GATT)
        xts = sc.pool(3, [128, D], F32)
        junk = sc.tile([128, D], BF16)
        ss = sc.tile([128, 1], F32)
        rstd = sc.tile([128, 1], F32)
        xn = sc.tile([128, D], BF16)
        hTs = sc.pool(2, [128, 8, 512], BF16)
        pTt = P.pTt
        stA = sc.pool(3, [128, 512], F32)
        tmp = sc.pool(6, [128, 512], F32)
        ob = sc.pool(8, [128, 512], BF16)
        o16 = [[sc.tile([128, 16, 128], BF16) for _ in range(4)] for _ in range(2)]
        tabs = sc.pool(2, [128, 4, 512], F32)
        vbt = sc.pool(2, [128, 4, 256], BF16)
        vct = sc.pool(2, [128, 4, 512], BF16)
        csb = sc.tile([128, 3, 512], F32)
        sqb = sc.tile([128, 3, 512], F32)
        rs = sc.pool(2, [128, 512], F32)
        cqn = sc.tile([128, 3, 512], BF16)
        ones = cst[:, C_ONE:C_ONE + 128]
        A_CH = [(i * 128, 128) for i in range(8)] + [(1024, 32)]

        def norm_gen(g_, hT_):
            for tt in range(4):
                xt = xts.get()
                k.dma("sync", xt[:], xsrc[g_ * 512 + tt * 128: g_ * 512 + (tt + 1) * 128, :], writes=[xt])
                rmsnorm_T(P, xt, junk, ss, rstd, xn, pTt, gb, hT_[:, :, tt * 128:(tt + 1) * 128], hT_)
                yield
        hT_next = hTs.get()
        for _ in norm_gen(0, hT_next):
            pass
        for g in range(NG):
            cols = slice(g * 512, (g + 1) * 512)
            hT = hT_next
            gen = None
            if g + 1 < NG:
                hT_next = hTs.get()
                gen = norm_gen(g + 1, hT_next)

            def step(gen=gen):
                if gen is not None:
                    next(gen, None)
            tb = tabs.get()
            k.dma("sync", tb[:, 0:2, :], P.tabB[:, :, cols].rearrange("c p t -> p c t"), writes=[tb])
            k.dma("sync", tb[0:64, 2:4, :], P.tabC[:, :, cols].rearrange("c p t -> p c t"), writes=[tb])

            def proj(c0, m):
                ps = k.ps()
                for kk in range(8):
                    k.mm(ps[:m], winb[:, kk, c0:c0 + m], hT[:, kk, :], start=(kk == 0), stop=(kk == 7),
                         reads=[winb, hT], writes=[ps])
                return ps
            pend = []

            def flush():
                while pend:
                    pend.pop(0)()

            def a_chunk(ci):
                c0, m = A_CH[ci]
                ps = proj(c0, m)
                st = stA.get()
                k.copy("scalar", st[:m], ps[:m], reads=[ps], writes=[st])
                k.dma("gpsimd", P.pA[c0:c0 + m, cols], st[:m], reads=[st])
                if ci in (2, 5, 8):
                    step()
            for j, (c0, m) in enumerate(((1824, 128), (1952, 128), (2080, 128))):
                ps = proj(c0, m)
                k.copy("scalar", csb[:, j, :], ps[:], reads=[ps], writes=[csb])
                k.tt("gpsimd", sqb[:, j, :], csb[:, j, :], csb[:, j, :], ALU.mult, reads=[csb], writes=[sqb])
            for ci in range(3):
                a_chunk(ci)
            for which in range(2):
                js = (0, 1) if which == 0 else (2,)
                nfeat = 256.0 if which == 0 else 128.0
                ps = k.ps()
                for i, j in enumerate(js):
                    k.mm(ps[:], ones, sqb[:, j, :], start=(i == 0), stop=(i == len(js) - 1), reads=[cst, sqb], writes=[ps])
                r_ = rs.get()
                k.act(r_[:], ps[:], AF.Sqrt, bias=EPS, scale=1.0 / nfeat, reads=[ps], writes=[r_])
                k.recip(r_[:], r_[:], reads=[r_], writes=[r_])
                for j in js:
                    gcol = vec[:, V_CQG + j:V_CQG + j + 1]
                    k.stt("vector", cqn[:, j, :], csb[:, j, :], gcol, r_[:], ALU.mult, ALU.mult,
                          reads=[csb, vec, r_], writes=[cqn])
            for ci in range(3, 9):
                a_chunk(ci)
            for qk in range(2):
                for ch in range(2):
                    c0 = 1056 + qk * 256 + ch * 128
                    ps = proj(c0, 128)
                    flush()
                    o1 = ob.get()
                    o4 = ob.get()
                    o16t = o16[qk][ch * 2 + 0]
                    off = (g % 4) * 32
                    outs = [
                        ("vector", o1[:], (lambda v: v), o1),
                        ("gpsimd", o4[:].rearrange("p (r l) -> p r l", r=4), (lambda v: v.rearrange("p (l r) -> p r l", r=4)), o4),
                        ("gpsimd", o16t[:, :, off:off + 32], (lambda v: v.rearrange("p (l r) -> p r l", r=16)), o16t),
                    ]
                    dst = P.qB if qk == 0 else P.kB
                    rows = slice(ch * 128, (ch + 1) * 128)

                    def after(dst=dst, rows=rows, o1=o1, o4=o4, o16t=o16t):
                        k.dma("gpsimd", dst[0][rows, cols], o1[:], reads=[o1])
                        k.dma("gpsimd", dst[1][rows, cols], o4[:], reads=[o4])
                        if g % 4 == 3:
                            sp = g // 4
                            k.dma("gpsimd", dst[2][rows, sp * 2048:(sp + 1) * 2048], o16t[:].rearrange("p r l -> p (r l)"), reads=[o16t])
                    pend.append(rope(P, ps, 128, cst[:, C_PB:C_PB + 128], tb[:, 0, :], tb[:, 1, :], (0.125 if qk == 0 else 1.0),
                                     tmp, outs, [tb], after))
            step()
            vb = vbt.get()
            for tt in range(4):
                ps = k.ps()
                for kk in range(8):
                    k.mm(ps[:, 0:256], hT[:, kk, tt * 128:(tt + 1) * 128], winb[:, kk, 1568:1824], start=(kk == 0), stop=(kk == 7),
                         reads=[winb, hT], writes=[ps])
                if tt == 0:
                    flush()
                k.copy("scalar", vb[:, tt, :], ps[:, 0:256], reads=[ps], writes=[vb])
            k.dma("gpsimd", P.vB[cols, :].rearrange("(t p) c -> p t c", p=128), vb[:], reads=[vb])
            qscale = 192.0 ** -0.5
            for h in range(4):
                ps = k.ps()
                for kc in range(2):
                    k.mm(ps[:], wuqb[:, kc, h * 192:h * 192 + 128], cqn[:, kc, :], start=(kc == 0), stop=(kc == 1),
                         reads=[wuqb, cqn], writes=[ps])
                o = ob.get()
                k.act(o[:], ps[:], AF.Copy, scale=qscale, reads=[ps], writes=[o])
                k.dma("gpsimd", P.qn[h * 128:(h + 1) * 128, cols], o[:], reads=[o])
                ps = k.ps()
                for kc in range(2):
                    k.mm(ps[0:64], wuqb[:, kc, h * 192 + 128:h * 192 + 192], cqn[:, kc, :], start=(kc == 0), stop=(kc == 1),
                         reads=[wuqb, cqn], writes=[ps])
                flush()
                o = ob.get()

                def after(o=o, h=h):
                    k.dma("gpsimd", P.qr[h * 64:(h + 1) * 64, cols], o[0:64], reads=[o])
                pend.append(rope(P, ps, 64, cst[0:64, C_PC:C_PC + 64], tb[0:64, 2, :], tb[0:64, 3, :], qscale, tmp,
                                 [("vector", o[0:64], (lambda v: v), o)], [tb], after))
            for h in range(4):
                ps = k.ps()
                k.mm(ps[:], wukvb[:, 0, h * 128:(h + 1) * 128], cqn[:, 2, :], reads=[wukvb, cqn], writes=[ps])
                if h == 0:
                    flush()
                o = ob.get()
                k.copy("scalar", o[:], ps[:], reads=[ps], writes=[o])
                k.dma("gpsimd", P.kn[h * 128:(h + 1) * 128, cols], o[:], reads=[o])
            ps = proj(2208, 64)
            o = ob.get()

            def after(o=o):
                k.dma("gpsimd", P.kr[:, cols], o[0:64], reads=[o])
            pend.append(rope(P, ps, 64, cst[0:64, C_PC:C_PC + 64], tb[0:64, 2, :], tb[0:64, 3, :], 1.0, tmp,
                             [("vector", o[0:64], (lambda v: v), o)], [tb], after))
            vc = vct.get()
            for tt in range(4):
                ps = k.ps()
                k.mm(ps[:], cqn[:, 2, tt * 128:(tt + 1) * 128], wukvb[:, 0, 512:1024], reads=[wukvb, cqn], writes=[ps])
                if tt == 0:
                    flush()
                k.copy("scalar", vc[:, tt, :], ps[:], reads=[ps], writes=[vc])
            k.dma("gpsimd", P.vC[cols, :].rearrange("(t p) c -> p t c", p=128), vc[:], reads=[vc])
            flush()


def phase_mla(P, l):
    k, cst, cstb = P.k, P.cst, P.cstb
    with k.scope() as sc:
        knp = sc.pool(2, [128, S_LEN], BF16)
        qnp = sc.pool(2, [128, S_LEN], BF16)
        qrp = sc.pool(2, [128, S_LEN], BF16)
        vp = sc.pool(2, [128, 32, 128], BF16)
        krt = sc.tile([128, S_LEN], BF16)
        for t_ in qrp.tiles + [krt]:
            k.memset("gpsimd", t_[64:128, :], 0.0, writes=[t_])
        pacc1p = sc.pool(3, [128, 512], F32)
        pTp = sc.pool(8, [128, 512], BF16)
        ob = sc.pool(2, [128, 512], BF16)
        rd = sc.pool(2, [128, 512], F32)
        paccp = sc.pool(3, [128, 512], F32)
        k.dma("sync", krt[0:64, :], P.kr[:, :], writes=[krt])
        ones = cst[:, C_ONE:C_ONE + 128]
        identb = cstb[:, C_ID:C_ID + 128]
        nbias = cstb[:, C_NB:C_NB + 128]
        onesb = cstb[:, C_ONE:C_ONE + 128]
        pending = []
        for h in range(4):
            knh, qnh, qrh, vh = knp.get(), qnp.get(), qrp.get(), vp.get()
            k.dma("sync", knh[:], P.kn[h * 128:(h + 1) * 128, :], writes=[knh])
            k.dma("sync", qnh[:], P.qn[h * 128:(h + 1) * 128, :], writes=[qnh])
            k.dma("sync", qrh[0:64, :], P.qr[h * 64:(h + 1) * 64, :], writes=[qrh])
            k.dma("sync", vh[:], P.vC[:, h * 128:(h + 1) * 128].rearrange("(t p) c -> p t c", p=128), writes=[vh])
            for g in range(NG):
                k.rot = [0, 1, 2, 3]
                po = k.psum[4 + (g % 2)]
                pden = k.psum[6 + (g % 2)]
                pe_den = [kt for kt in range(4 * g + 4) if kt % 3 == 1] if g > 0 else []
                pacc = paccp.get()
                pacc1 = pacc1p.get()
                k.memset("gpsimd", pacc1[:], 0.0, writes=[pacc1])
                nkt = 4 * g + 4

                def s_stage(kt, g=g, pacc=pacc, pacc1=pacc1, pe_den=pe_den):
                    o = kt - 4 * g
                    c0 = max(o, 0) * 128
                    ks = slice(kt * 128, (kt + 1) * 128)
                    qs = slice(g * 512 + c0, (g + 1) * 512)
                    pS = k.ps()
                    k.mm(pS[:, c0:512], knh[:, ks], qnh[:, qs], start=True, stop=False, reads=[knh, qnh], writes=[pS])
                    k.mm(pS[:, c0:512], krt[:, ks], qrh[:, qs], start=False, stop=(o < 0), reads=[krt, qrh], writes=[pS])
                    if o >= 0:
                        k.mm(pS[:, c0:c0 + 128], identb, nbias, start=False, stop=True, reads=[cstb], writes=[pS])
                    pT = pTp.get()
                    k.act(pT[:, c0:512], pS[:, c0:512], AF.Exp, reads=[pS], writes=[pT])
                    if kt == 0:
                        k.copy("vector", pacc[:], pT[:], reads=[pT], writes=[pacc])
                    elif kt in pe_den:
                        pass
                    elif kt % 3 == 2:
                        k.tt("gpsimd", pacc1[:, c0:512], pacc1[:, c0:512], pT[:, c0:512], ALU.add, reads=[pT, pacc1], writes=[pacc1])
                    else:
                        k.tt("vector", pacc[:, c0:512], pacc[:, c0:512], pT[:, c0:512], ALU.add, reads=[pT, pacc], writes=[pacc])
                    return kt, c0, pT

                def pv_stage(st, po=po, nkt=nkt, pden=pden, pe_den=pe_den):
                    kt, c0, pT = st
                    k.mm(po[:, c0:512], vh[:, kt, :], pT[:, c0:512], start=(kt == 0), stop=(kt == nkt - 1), reads=[vh, pT], writes=[po])
                    if kt in pe_den:
                        k.mm(pden[:, c0:512], onesb, pT[:, c0:512], start=(kt == pe_den[0]), stop=False, reads=[cstb, pT], writes=[pden],
                             skip=True)

                def finalize(g=g, h=h, po=po, pacc=pacc, pacc1=pacc1, pd=pden, pe_den=pe_den):
                    k.mm(pd[:], ones, pacc[:], start=(not pe_den), stop=False, reads=[cst, pacc], writes=[pd], skip=True)
                    k.mm(pd[:], ones, pacc1[:], start=False, stop=True, reads=[cst, pacc1], writes=[pd], skip=True)
                    r = rd.get()
                    k.recip(r[:], pd[:], reads=[pd], writes=[r])
                    oo = ob.get()
                    k.tt("vector", oo[:], po[:], r[:], ALU.mult, reads=[po, r], writes=[oo])
                    k.dma("gpsimd", P.cat[512 + h * 128:512 + (h + 1) * 128, g * 512:(g + 1) * 512], oo[:], reads=[oo])
                DEPTH = 2
                q_ = []
                for kt in range(nkt):
                    q_.append(s_stage(kt))
                    if kt == 2 and pending:
                        pending.pop()()
                    if len(q_) > DEPTH:
                        pv_stage(q_.pop(0))
                while q_:
                    pv_stage(q_.pop(0))
                pending.append(finalize)
        while pending:
            pending.pop()()
        k.rot = None


def phase_dil(P, l):
    k, cst, cstb = P.k, P.cst, P.cstb
    with k.scope() as sc:
        vts = [sc.tile([128, 32, 256], BF16) for _ in range(3)]
        vap = sc.pool(2, [128, 32, 128], BF16)
        qp = [sc.pool(2, [128, S_LEN], BF16) for _ in range(2)]
        kp = sc.pool(2, [128, S_LEN], BF16)
        acc = sc.tile([128, S_LEN], F32)
        pTp = sc.pool(8, [128, 512], BF16)
        ob = sc.pool(2, [64, S_LEN], BF16)
        rdp = sc.pool(2, [64, 512], F32)
        for par in range(2):
            for t_ in qp[par].tiles:
                k.memset("gpsimd", t_[(1 - par) * 64:(2 - par) * 64, :], 0.0, writes=[t_])
        for t_ in vap.tiles:
            k.memset("gpsimd", t_[:, :, 64:128], 1.0, writes=[t_])
        DS = (1, 4, 16)
        for pi, d in enumerate(DS):
            if d == 1:
                src = P.vB[:, :].rearrange("(s l) c -> l s c", l=128)
                k.dma("sync", vts[pi][:], src, writes=[vts[pi]])
            else:
                for r in range(d):
                    src = P.vB[:, :].rearrange("(s l r) c -> r l s c", l=128, r=d)[r]
                    dstv = vts[pi][:].rearrange("p (s r) c -> p r s c", r=d)[:, r]
                    k.dma("sync", dstv, src, writes=[vts[pi]], merge=True)
        identb = cstb[:, C_ID:C_ID + 128]
        nb_cur = cstb[:, C_NB:C_NB + 128]
        nb_prev = cstb[:, C_NB + 128:C_NB + 256]
        shiftT = cst[:, C_ID + 64:C_ID + 128]
        k.rot = [0, 1, 2, 3, 4, 5, 6]
        for h in range(4):
            hc = slice(h * 64, (h + 1) * 64)
            par = h % 2
            pr = slice((h // 2) * 128, (h // 2 + 1) * 128)
            for pi, d in enumerate(DS):
                qh, kh = qp[par].get(), kp.get()
                k.dma("sync", qh[par * 64:(par + 1) * 64, :], P.qB[pi][hc, :], writes=[qh])
                k.dma("sync", kh[:], P.kB[pi][pr, :], writes=[kh])
                vt = vap.get()
                k.copy("gpsimd", vt[:, :, 0:64], vts[pi][:, :, hc], reads=[vts[pi]], writes=[vt])

                def s_stage(bt, d=d, qh=qh, kh=kh):
                    b0 = bt * 4
                    pc, pp = k.ps(), k.ps()
                    hasp = [(b0 + j - d) >= 0 for j in range(4)]
                    for j in range(4):
                        cs = slice((b0 + j) * 128, (b0 + j + 1) * 128)
                        js = slice(j * 128, (j + 1) * 128)
                        k.mm(pc[:, js], kh[:, cs], qh[:, cs], start=True, stop=False, reads=[kh, qh], writes=[pc])
                        k.mm(pc[:, js], identb, nb_cur, start=False, stop=True, reads=[cstb], writes=[pc])
                        if hasp[j]:
                            ps_ = slice((b0 + j - d) * 128, (b0 + j - d + 1) * 128)
                            k.mm(pp[:, js], kh[:, ps_], qh[:, cs], start=True, stop=False, reads=[kh, qh], writes=[pp])
                            k.mm(pp[:, js], identb, nb_prev, start=False, stop=True, reads=[cstb], writes=[pp])
                    tc_ = pTp.get()
                    k.act(tc_[:], pc[:], AF.Exp, reads=[pc], writes=[tc_])
                    tp_ = None
                    if any(hasp):
                        j0 = hasp.index(True)
                        tp_ = pTp.get()
                        k.act(tp_[:, j0 * 128:512], pp[:, j0 * 128:512], AF.Exp, reads=[pp], writes=[tp_])
                    return bt, hasp, tc_, tp_

                def pv_stage(st, d=d, pi=pi, vt=vt):
                    bt, hasp, tc_, tp_ = st
                    b0 = bt * 4
                    pn = k.ps()
                    for j in range(4):
                        js = slice(j * 128, (j + 1) * 128)
                        if hasp[j]:
                            k.mm(pn[:, js], vt[:, b0 + j - d, :], tp_[:, js], start=True, stop=False, reads=[vt, tp_], writes=[pn])
                        k.mm(pn[:, js], vt[:, b0 + j, :], tc_[:, js], start=(not hasp[j]), stop=True, reads=[vt, tc_], writes=[pn])
                    if d == 1:
                        va = acc[:, b0 * 128:b0 * 128 + 512]
                        vpz = pn[:, :]
                    elif d == 4:
                        va = acc[:, bt * 512:(bt + 1) * 512].rearrange("p (l r) -> p r l", r=4)
                        vpz = pn[:, :].rearrange("p (r l) -> p r l", r=4)
                    else:
                        sp, r0 = b0 // 16, b0 % 16
                        va = acc[:, sp * 2048:(sp + 1) * 2048].rearrange("p (l r) -> p r l", r=16)[:, r0:r0 + 4, :]
                        vpz = pn[:, :].rearrange("p (r l) -> p r l", r=4)
                    if pi == 0:
                        k.copy("vector", va, vpz, reads=[pn], writes=[acc])
                    else:
                        k.tt("vector", va, va, vpz, ALU.add, reads=[pn, acc], writes=[acc])
                DEPTH = 2
                q_ = []
                for bt in range(8):
                    q_.append(s_stage(bt))
                    if len(q_) > DEPTH:
                        pv_stage(q_.pop(0))
                while q_:
                    pv_stage(q_.pop(0))
            oo = ob.get()
            for c8 in range(8):
                cs8 = slice(c8 * 512, (c8 + 1) * 512)
                pd = k.ps()
                k.mm(pd[0:64, :], shiftT, acc[:, cs8], reads=[cst, acc], writes=[pd])
                r_ = rdp.get()
                k.recip(r_[:], pd[0:64, :], reads=[pd], writes=[r_])
                k.tt("gpsimd", oo[:, cs8], acc[0:64, cs8], r_[:], ALU.mult, reads=[acc, r_], writes=[oo])
            k.dma("gpsimd", P.cat[256 + h * 64:256 + (h + 1) * 64, :], oo[:], reads=[oo])
        k.rot = None


C0 = 0.6065306597126334
GN_EPS = 64e-5


def bc(ap, n):
    return bass.AP(ap.tensor, ap.offset, [list(x) for x in ap.ap] + [[0, n]])


def phase_rwkv(P, l):
    k, cst, cstb = P.k, P.cst, P.cstb
    AX = mybir.AxisListType
    with k.scope() as sc:
        identf = cst[:, C_ID:C_ID + 128]
        bones = cst[:, C_BO:C_BO + 128]
        ones = cst[:, C_ONE:C_ONE + 128]
        vec = sc.tile([128, NV], F32)
        lnwb = sc.tile([128, 512], F32)
        dec = sc.tile([128, 256], F32)
        icl = sc.tile([128, 256], F32)
        gateA = sc.tile([128, 256], F32)
        gateB = sc.tile([128, 256], F32)
        for t_ in (dec, icl, gateB):
            k.memset("gpsimd", t_[:], 0.0, writes=[t_])
        k.dma("sync", vec[:], P.vecs[l], writes=[vec])
        k.dma("sync", lnwb[:], P.lnwb[l], writes=[lnwb])
        k.dma("sync", dec[0:64, :], P.a_dec[l], writes=[dec])
        k.dma("sync", icl[64:128, :], P.a_icl[l], writes=[icl])
        k.dma("sync", gateA[:], P.a_gate[l, 0:128, :], writes=[gateA])
        k.dma("sync", gateB[0:32, :], P.a_gate[l, 128:160, :], writes=[gateB])
        msk4 = sc.tile([128, 512], F32)
        miu4 = sc.tile([128, 512], F32)
        msu2 = sc.tile([128, 256], F32)
        for j in range(4):
            src = C_TRI + (384 if j % 2 == 0 else 256)
            k.copy("vector", msk4[:, j * 128:(j + 1) * 128], cst[:, src:src + 128], reads=[cst], writes=[msk4])
            k.copy("vector", miu4[:, j * 128:(j + 1) * 128], cst[:, C_TRI:C_TRI + 128], reads=[cst], writes=[miu4])
        for j in range(2):
            k.copy("vector", msu2[:, j * 128:(j + 1) * 128], cst[:, C_TRI + 256:C_TRI + 384], reads=[cst], writes=[msu2])
        rawp = sc.pool(2, [128, 513], F32)
        dtmp = sc.pool(2, [128, 512], F32)
        PFk = [sc.tile([128, 512], F32) for _ in range(2)]
        PF6 = sc.tile([128, 512], F32)
        PF7 = sc.tile([128, 512], F32)
        PF8 = sc.tile([128, 512], F32)
        tmpf = sc.pool(3, [128, 512], F32)
        th = sc.tile([128, 512], F32)
        SG = sc.tile([128, 512], F32)
        Aic = sc.tile([128, 512], F32)

        class HOc:
            pass
        HO = []
        for i in range(2):
            ho = HOc()
            for nm_ in ("R", "V", "CS", "CSX", "KN", "KF", "Bv", "RKR"):
                setattr(ho, nm_, [sc.tile([128, 512], F32) for _ in range(2)])
            ho.SX7 = sc.tile([128, 512], F32)
            ho.SX8 = sc.tile([128, 512], F32)
            k.memset("gpsimd", ho.SX8[:], 0.0, writes=[ho.SX8])
            HO.append(ho)

        class Slot:
            pass
        slots = []
        for s_ in range(4):
            sl = Slot()
            sl.ep = sc.pool(5, [128, 128], F32)
            sl.dp = sc.pool(4, [128, 128], F32)
            sl.APz = [sc.tile([128, 128], F32) for _ in range(2)]
            sl.RPz = [sc.tile([128, 128], F32) for _ in range(2)]
            for t_ in sl.APz + sl.RPz:
                k.memset("gpsimd", t_[:], 0.0, writes=[t_])
            sl.smp = sc.pool(8, [128, 2], F32)
            sl.tmp = sc.pool(1, [128, 3, 128], F32)
            sl.MNp = sc.pool(2, [128, 2, 2, 128], BF16)
            sl.mkp = sc.pool(1, [128, 2, 128], F32)
            sl.wrp = sc.pool(1, [128, 2, 2, 128], F32)
            sl.Xbp = sc.pool(2, [128, 2, 2, 64], BF16)
            sl.Xfp = sc.pool(1, [128, 2, 2, 64], F32)
            sl.sq = sc.pool(6, [128, 128], F32)
            sl.Hpp = sc.pool(1, [128, 128], F32)
            sl.px = k.psum[4 + s_]
            slots.append(sl)
        k.rot = [0, 1, 2, 3]
        STp = [sc.pool(2, [128, 128], F32) for _ in range(2)]
        catA = sc.pool(3, [128, 2, 512], BF16)
        ST = []
        for cp in range(2):
            t = STp[cp].get()
            k.memset("gpsimd", t[:], 0.0, writes=[t])
            ST.append(t)
        f2 = lambda t: t[:].rearrange("p a b c -> p (a b c)")

        def a1gen(g, ho):
            lo = g * 512 - 1

            def mix(ch, dst_ap, dst_tl):
                rows = 128 if ch < 8 else 32
                rw = rawp.get()
                rsl = slice(ch * 128, ch * 128 + rows)
                if g == 0:
                    k.memset("gpsimd", rw[:, 0:1], 0.0, writes=[rw])
                    k.dma("sync", rw[:rows, 1:513], P.pA[rsl, 0:512], writes=[rw])
                else:
                    k.dma("sync", rw[:rows, :], P.pA[rsl, lo:lo + 513], writes=[rw])
                d_ = dtmp.get()
                k.tt("gpsimd", d_[:rows], rw[:rows, 0:512], rw[:rows, 1:513], ALU.subtract, reads=[rw], writes=[d_])
                k.stt("vector", dst_ap, d_[:rows], vec[:rows, V_MU + ch:V_MU + ch + 1], rw[:rows, 1:513],
                      ALU.mult, ALU.add, reads=[d_, vec, rw], writes=[dst_tl])
            dsts = [ho.R[0], ho.R[1], PFk[0], PFk[1], ho.V[0], ho.V[1], PF6, PF7, PF8]
            for ch in range(9):
                d = dsts[ch]
                mix(ch, d[:] if ch < 8 else d[0:32, :], d)
                if ch % 2 == 1:
                    yield
            k.act(th[:], PF6[:], AF.Tanh, reads=[PF6], writes=[th])
            k.act(ho.SX7[:], PF7[:], AF.Sigmoid, reads=[PF7], writes=[ho.SX7])
            k.act(ho.SX8[0:32, :], PF8[0:32, :], AF.Sigmoid, reads=[PF8], writes=[ho.SX8])
            yield
            for cp in range(2):
                pcs = slice(cp * 128, (cp + 1) * 128)
                CS, CSX, KN, KF, Bv, RKR = ho.CS[cp], ho.CSX[cp], ho.KN[cp], ho.KF[cp], ho.Bv[cp], ho.RKR[cp]
                pz = k.ps()
                k.mm(pz[:], dec[:, pcs], th[:, :], reads=[dec, th], writes=[pz])
                k.act(SG[:], pz[:], AF.Sigmoid, bias=vec[:, V_W0 + cp:V_W0 + cp + 1], reads=[pz, vec], writes=[SG])
                pa_ = k.ps()
                k.mm(pa_[:], icl[:, pcs], PF6[:, :], reads=[icl, PF6], writes=[pa_])
                k.act(Aic[:], pa_[:], AF.Sigmoid, bias=vec[:, V_A0 + cp:V_A0 + cp + 1], reads=[pa_, vec], writes=[Aic])
                yield
                for c4 in range(4):
                    cc = slice(c4 * 128, (c4 + 1) * 128)
                    k.S.op("vector", (lambda e, o=CS[:, cc], d1=SG[:, cc]: e.tensor_tensor_scan(
                        out=o, data0=ones, data1=d1, initial=0.0, op0=ALU.mult, op1=ALU.add)),
                        reads=[cst, SG], writes=[CS])
                k.tt("gpsimd", CSX[:], CS[:], SG[:], ALU.subtract, reads=[CS, SG], writes=[CSX])
                yield
                kraw = PFk[cp]
                kkr = tmpf.get()
                k.ts("vector", kkr[:], kraw[:], vec[:, V_KK + cp:V_KK + cp + 1], None, ALU.mult, reads=[kraw, vec], writes=[kkr])
                sq = tmpf.get()
                k.tt("gpsimd", sq[:], kkr[:], kkr[:], ALU.mult, reads=[kkr], writes=[sq])
                pn_ = k.ps()
                k.mm(pn_[:], bones, sq[:], reads=[cst, sq], writes=[pn_])
                nrm = tmpf.get()
                k.act(nrm[:], pn_[:], AF.Sqrt, reads=[pn_], writes=[nrm])
                yield
                k.ts("vector", nrm[:], nrm[:], 1e-12, None, ALU.max, reads=[nrm], writes=[nrm])
                k.recip(nrm[:], nrm[:], reads=[nrm], writes=[nrm])
                k.tt("vector", KN[:], kkr[:], nrm[:], ALU.mult, reads=[kkr, nrm], writes=[KN])
                yield
                t1 = tmpf.get()
                k.ts("vector", t1[:], Aic[:], -1.0, vec[:, V_KA + cp:V_KA + cp + 1], ALU.add, ALU.mult, reads=[Aic, vec], writes=[t1])
                k.stt("vector", KF[:], t1[:], 1.0, kraw[:], ALU.add, ALU.mult, reads=[t1, kraw], writes=[KF])
                k.tt("gpsimd", Bv[:], KN[:], Aic[:], ALU.mult, reads=[KN, Aic], writes=[Bv])
                k.stt("vector", RKR[:], ho.R[cp][:], vec[:, V_RK + cp:V_RK + cp + 1], KF[:], ALU.mult, ALU.mult,
                      reads=[ho.R[cp], vec, KF], writes=[RKR])
                yield

        def unit(sl, ho, cp, c4, cA):
            cc = slice(c4 * 128, (c4 + 1) * 128)
            Rt_, Vt_ = ho.R[cp], ho.V[cp]
            CS, CSX, KN, KF, Bv, RKR = ho.CS[cp], ho.CSX[cp], ho.KN[cp], ho.KF[cp], ho.Bv[cp], ho.RKR[cp]
            rfe = Rt_[:, cc]
            vfe = Vt_[:, cc]
            cs_ = CS[:, cc]
            csx = CSX[:, cc]
            cend = CS[:, c4 * 128 + 127:c4 * 128 + 128]
            sm = sl.smp.get()
            k.ts("vector", sm[:, 0:1], cend, C0, None, ALU.mult, reads=[CS], writes=[sm])
            k.ts("vector", sm[:, 1:2], cend, -C0, None, ALU.mult, reads=[CS], writes=[sm])
            pce, nce = sm[:, 0:1], sm[:, 1:2]
            e1, e2, e3, e4, e5 = (sl.ep.get() for _ in range(5))
            k.act(e1[:], csx, AF.Exp, scale=-C0, reads=[CSX], writes=[e1])
            k.act(e2[:], csx, AF.Exp, scale=-C0, bias=pce, reads=[CSX, sm], writes=[e2])
            k.act(e3[:], cs_, AF.Exp, scale=C0, bias=nce, reads=[CS, sm], writes=[e3])
            k.act(e4[:], cs_, AF.Exp, scale=-C0, reads=[CS], writes=[e4])
            k.act(e5[:], cs_, AF.Exp, scale=-C0, bias=pce, reads=[CS, sm], writes=[e5])
            AT, BT, KT, RT = (sl.dp.get() for _ in range(4))
            APz, RPz = sl.APz, sl.RPz
            k.stt("vector", AT[:], KN[:, cc], -1.0, e1[:], ALU.mult, ALU.mult, reads=[KN, e1], writes=[AT])
            for h in range(2):
                hr = slice(h * 64, (h + 1) * 64)
                k.stt("vector", APz[h][hr, :], KN[hr, cc], -1.0, e2[hr, :], ALU.mult, ALU.mult, reads=[KN, e2], writes=[APz[h]])
                k.tt("gpsimd", RPz[h][hr, :], Rt_[hr, cc], e5[hr, :], ALU.mult, reads=[Rt_, e5], writes=[RPz[h]])
            k.tt("gpsimd", BT[:], Bv[:, cc], e3[:], ALU.mult, reads=[Bv, e3], writes=[BT])
            k.tt("gpsimd", KT[:], KF[:, cc], e3[:], ALU.mult, reads=[KF, e3], writes=[KT])
            k.tt("vector", RT[:], rfe, e4[:], ALU.mult, reads=[Rt_, e4], writes=[RT])
            yield
            pt = k.ps()
            for q, (srcT, stl) in enumerate(((BT[:], BT), (KT[:], KT), (vfe, Vt_))):
                k.tr(pt[:, q * 128:(q + 1) * 128], srcT, identf, reads=[stl, cst], writes=[pt])
            tm = sl.tmp.get()
            k.copy("scalar", tm[:].rearrange("p a b -> p (a b)"), pt[:, 0:384], reads=[pt], writes=[tm])
            yield
            pmA, pmB, pw = k.ps(), k.ps(), k.ps()
            for h in range(2):
                k.mm(pmA[:, h * 256:h * 256 + 128], APz[h][:], BT[:], reads=[APz[h], BT], writes=[pmA])
                k.mm(pmA[:, h * 256 + 128:h * 256 + 256], BT[:], APz[h][:], reads=[APz[h], BT], writes=[pmA])
                k.mm(pmB[:, h * 128:(h + 1) * 128], KT[:], APz[h][:], reads=[APz[h], KT], writes=[pmB])
                k.mm(pw[:, h * 256:h * 256 + 128], BT[:], RPz[h][:], reads=[RPz[h], BT], writes=[pw])
                k.mm(pw[:, h * 256 + 128:h * 256 + 256], KT[:], RPz[h][:], reads=[RPz[h], KT], writes=[pw])
            MN = sl.MNp.get()
            k.tt("vector", f2(MN), pmA[:], msk4[:], ALU.mult, reads=[pmA, msk4], writes=[MN])
            mk = sl.mkp.get()
            k.tt("vector", mk[:].rearrange("p a b -> p (a b)"), pmB[:, 0:256], msu2[:], ALU.mult, reads=[pmB, msu2], writes=[mk])
            wr = sl.wrp.get()
            k.tt("vector", f2(wr), pw[:], miu4[:], ALU.mult, reads=[pw, miu4], writes=[wr])
            yield
            px = sl.px
            pxv = px[:, 0:256].rearrange("p (a h v) -> p a h v", a=2, h=2)
            for h in range(2):
                k.mm(pxv[:, 1, h, :], mk[:, h, :], tm[:, 2, h * 64:(h + 1) * 64], start=(h == 0), stop=False,
                     reads=[mk, tm], writes=[px], skip=True)
            k.mm(px[:, 0:128], AT[:], identf, start=False, stop=False, reads=[AT, cst], writes=[px], skip=True)
            Xb = sl.Xbp.get()
            k.copy("vector", f2(Xb), px[:, 0:256], reads=[px], writes=[Xb])
            yield
            Xf = None
            for i in range(7):
                for h in range(2):
                    k.mm(pxv[:, :, h, :], MN[:, h, 1, :], Xb[:, :, h, :], start=False, stop=(i == 6 and h == 1),
                         reads=[MN, Xb], writes=[px], skip=True)
                if i < 6:
                    Xb = sl.Xbp.get()
                    k.copy("vector", f2(Xb), px[:, 0:256], reads=[px], writes=[Xb])
                    pn = k.ps()
                    pnv = pn[:].rearrange("p (h m t) -> p h m t", h=2, m=2)
                    for h in range(2):
                        k.mm(pnv[:, h, 1, :], MN[:, h, 0, :], MN[:, h, 1, :], reads=[MN], writes=[pn])
                        k.mm(pnv[:, h, 0, :], MN[:, h, 1, :], MN[:, h, 0, :], reads=[MN], writes=[pn])
                    MNn = sl.MNp.get()
                    k.copy("scalar", f2(MNn), pn[:], reads=[pn], writes=[MNn])
                    MN = MNn
                else:
                    Xf = sl.Xfp.get()
                    k.copy("vector", f2(Xf), px[:, 0:256], reads=[px], writes=[Xf])
                yield
            X = Xf
            Ahat = X[:, 0].rearrange("p h v -> p (h v)")
            U0 = X[:, 1].rearrange("p h v -> p (h v)")
            pg = k.ps()
            k.mm(pg[:, 0:128], Ahat, tm[:, 0, :], reads=[X, tm], writes=[pg])
            gt1 = sl.sq.get()
            k.tt("vector", gt1[:], pg[:, 0:128], bones, ALU.mult, reads=[pg, cst], writes=[gt1])
            GT = sl.sq.get()
            k.stt("vector", GT[:], identf, e4[:, 127:128], gt1[:], ALU.mult, ALU.add, reads=[cst, e4, gt1], writes=[GT])
            ph = k.ps()
            k.mm(ph[:, 0:128], tm[:, 0, :], U0, start=True, stop=False, reads=[X, tm], writes=[ph])
            k.mm(ph[:, 0:128], tm[:, 1, :], tm[:, 2, :], start=False, stop=True, reads=[tm], writes=[ph])
            Hp = sl.Hpp.get()
            k.tt("vector", Hp[:], ph[:, 0:128], bones, ALU.mult, reads=[ph, cst], writes=[Hp])
            QT = sl.sq.get()
            for h in range(2):
                hr = slice(h * 64, (h + 1) * 64)
                pq = k.ps()
                k.mm(pq[:, 0:128], Ahat, wr[:, h, 0, :], reads=[X, wr], writes=[pq])
                k.tt("vector", QT[hr, :], pq[hr, 0:128], RT[hr, :], ALU.add, reads=[pq, RT], writes=[QT])
            yield
            py = k.ps()
            So = ST[cp]
            k.mm(py[:, 0:128], QT[:], So[:], start=True, stop=False, reads=[QT, So], writes=[py])
            for h in range(2):
                hr = slice(h * 64, (h + 1) * 64)
                ys = py[:, h * 64:(h + 1) * 64]
                k.mm(ys, wr[:, h, 0, :], X[:, 1, h, :], start=False, stop=False, reads=[wr, X], writes=[py])
                k.mm(ys, wr[:, h, 1, :], tm[:, 2, hr], start=False, stop=(h == 1), reads=[wr, tm], writes=[py])
            pS = k.ps()
            k.mm(pS[:, 0:128], GT[:], So[:], reads=[GT, So], writes=[pS])
            Sn = STp[cp].get()
            k.tt("vector", Sn[:], pS[:, 0:128], Hp[:], ALU.add, reads=[pS, Hp], writes=[Sn])
            ST[cp] = Sn
            ysb = sl.sq.get()
            k.copy("scalar", ysb[:], py[:, 0:128], reads=[py], writes=[ysb])
            pb_ = k.ps()
            k.mm(pb_[:, 0:2], RKR[:, cc], cst[:, C_BO:C_BO + 128:64], reads=[RKR, cst], writes=[pb_])
            bon = sl.smp.get()
            k.copy("scalar", bon[:], pb_[:, 0:2], reads=[pb_], writes=[bon])
            yield
            yc = sl.sq.get()
            yn = sl.sq.get()
            v3 = lambda t: t[:].rearrange("p (h v) -> p h v", h=2)
            st_ = sl.smp.get()
            k.S.op("vector", (lambda e, o=st_[:, 0:2], i_=v3(ysb): e.tensor_reduce(out=o, in_=i_, axis=AX.X, op=ALU.add)),
                   reads=[ysb], writes=[st_])
            nm = sl.smp.get()
            k.ts("vector", nm[:, 0:2], st_[:, 0:2], -1.0 / 64, None, ALU.mult, reads=[st_], writes=[nm])
            k.tt("vector", v3(yc), v3(ysb), bc(nm[:, 0:2], 64), ALU.add, reads=[ysb, nm], writes=[yc])
            s2 = sl.smp.get()
            jk = sl.ep.get()
            for h in range(2):
                hs = slice(h * 64, (h + 1) * 64)
                k.act(jk[:, hs], yc[:, hs], AF.Square, accum_out=s2[:, h:h + 1], reads=[yc], writes=[jk, s2])
            rs2 = sl.smp.get()
            k.act(rs2[:, 0:2], s2[:, 0:2], AF.Sqrt, bias=GN_EPS, scale=1.0 / 64, reads=[s2], writes=[rs2])
            k.recip(rs2[:, 0:2], rs2[:, 0:2], reads=[rs2], writes=[rs2])
            hg0 = cp * 2
            k.tt("vector", v3(yn), v3(yc), bc(rs2[:, 0:2], 64), ALU.mult, reads=[yc, rs2], writes=[yn])
            k.tt("gpsimd", yn[:], yn[:], lnwb[:, hg0 * 64:(hg0 + 2) * 64], ALU.mult, reads=[yn, lnwb], writes=[yn])
            k.tt("gpsimd", yn[:], yn[:], lnwb[:, 256 + hg0 * 64:256 + (hg0 + 2) * 64], ALU.add, reads=[yn, lnwb], writes=[yn])
            bv = sl.sq.get()
            k.tt("gpsimd", v3(bv), tm[:, 2, :].rearrange("p (h v) -> p h v", h=2), bc(bon[:, 0:2], 64), ALU.mult,
                 reads=[tm, bon], writes=[bv])
            k.tt("vector", yn[:], yn[:], bv[:], ALU.add, reads=[yn, bv], writes=[yn])
            pgt = k.ps()
            pcs = slice(cp * 128, (cp + 1) * 128)
            k.mm(pgt[:, 0:128], ho.SX7[:, cc], gateA[:, pcs], start=True, stop=False, reads=[ho.SX7, gateA], writes=[pgt])
            k.mm(pgt[:, 0:128], ho.SX8[:, cc], gateB[:, pcs], start=False, stop=True, reads=[ho.SX8, gateB], writes=[pgt])
            ya = sl.sq.get()
            k.tt("vector", ya[:], yn[:], pgt[:, 0:128], ALU.mult, reads=[yn, pgt], writes=[ya])
            pT2 = k.ps()
            k.tr(pT2[:, 0:128], ya[:], identf, reads=[ya, cst], writes=[pT2])
            k.copy("scalar", cA[:, cp, cc], pT2[:, 0:128], reads=[pT2], writes=[cA])

        tasks = [(g, c4) for g in range(NG) for c4 in range(4)]
        a1 = {}
        a1_finished = set()

        def finish_a1(g):
            if g not in a1_finished:
                for _ in a1[g]:
                    pass
                a1_finished.add(g)
        a1[0] = a1gen(0, HO[0])
        finish_a1(0)
        if NG > 1:
            a1[1] = a1gen(1, HO[1])
        bg = [1] if NG > 1 else []
        cAs = {}
        active = []
        steps = {}
        remaining = {}
        done = set()
        nxt = 0
        while True:
            while nxt < len(tasks):
                g, c4 = tasks[nxt]
                ok_slots = nxt < 2 or (nxt - 2) in done
                ok_stag = nxt == 0 or (nxt - 1) in done or steps.get(nxt - 1, 0) >= 8
                if not (ok_slots and ok_stag):
                    break
                if c4 == 0:
                    finish_a1(g)
                    if g in bg:
                        bg.remove(g)
                    cAs[g] = catA.get()
                lane = nxt % 2
                for cp in range(2):
                    active.append((nxt, unit(slots[lane * 2 + cp], HO[g % 2], cp, c4, cAs[g])))
                steps[nxt] = 0
                remaining[nxt] = 2
                nxt += 1
            if not active:
                break
            for item in list(active):
                idx, gen = item
                try:
                    next(gen)
                except StopIteration:
                    active.remove(item)
                    remaining[idx] -= 1
                    if remaining[idx] == 0:
                        done.add(idx)
                        g, c4 = tasks[idx]
                        if c4 == 3:
                            k.dma("gpsimd", P.cat[0:256, g * 512:(g + 1) * 512].rearrange("(c p) t -> p c t", p=128),
                                  cAs[g][:], reads=[cAs[g]])
                            if g + 2 < NG:
                                a1[g + 2] = a1gen(g + 2, HO[g % 2])
                                bg.append(g + 2)
            for idx in set(i for i, _ in active):
                steps[idx] += 1
            if bg:
                gb_ = bg[0]
                try:
                    next(a1[gb_])
                except StopIteration:
                    a1_finished.add(gb_)
                    bg.remove(gb_)
        k.rot = None


def phase_tail(P, l, xsrc, last):
    k, cst = P.k, P.cst
    with k.scope() as wsc:
        woutb = wsc.tile([128, 8, 1024], BF16)
        wgb = wsc.tile([128, 8, DFF], BF16)
        wub = wsc.tile([128, 8, DFF], BF16)
        wdb = wsc.tile([128, 22, 1024], BF16)
        load_w2(P, woutb, P.w_out[l], 8, 2)
        load_w2(P, wgb, P.w_gate[l], 8, 2)
        load_w2(P, wub, P.w_up[l], 8, 2)
        load_w2(P, wdb, P.w_down[l], 22, 2)
        with k.scope() as sc:
            vec = sc.tile([128, NV], F32)
            k.dma("sync", vec[:], P.vecs[l], writes=[vec])
            gb = make_gb(P, sc, vec, V_GFFN)
            xts = sc.pool(2, [128, D], F32)
            x1s = sc.pool(2, [128, D], F32)
            cts = sc.pool(1, [128, 8, 512], BF16)
            catv = P.cat.rearrange("(k p) t -> p k t", p=128)
            ss = sc.tile([128, 1], F32)
            rstd = sc.tile([128, 1], F32)
            xn = sc.tile([128, D], BF16)
            junk = xn
            hTs = sc.pool(2, [128, 8, 512], BF16)
            sg = sc.pool(1, [128, 512], F32)
            ab = sc.pool(3, [128, 512], BF16)

            def norm_gen(g_, hT_):
                ct = cts.get()
                k.dma("sync", ct[:], catv[:, :, g_ * 512:(g_ + 1) * 512], writes=[ct])
                for tt in range(4):
                    rows = slice(g_ * 512 + tt * 128, g_ * 512 + (tt + 1) * 128)
                    xt = xts.get()
                    k.dma("sync", xt[:], xsrc[rows, :], writes=[xt])
                    x1t = x1s.get()
                    for nb in range(2):
                        ps = k.ps()
                        for kk in range(8):
                            k.mm(ps[:], ct[:, kk, tt * 128:(tt + 1) * 128], woutb[:, kk, nb * 512:(nb + 1) * 512],
                                 start=(kk == 0), stop=(kk == 7), reads=[ct, woutb], writes=[ps])
                        k.tt("vector", x1t[:, nb * 512:(nb + 1) * 512], ps[:], xt[:, nb * 512:(nb + 1) * 512], ALU.add,
                             reads=[ps, xt], writes=[x1t])
                    k.dma("gpsimd", P.x1[rows, :], x1t[:], reads=[x1t])
                    rmsnorm_T(P, x1t, junk, ss, rstd, xn, P.pTt, gb, hT_[:, :, tt * 128:(tt + 1) * 128], hT_)
                    yield
            hT_next = hTs.get()
            for _ in norm_gen(0, hT_next):
                pass
            for g in range(NG):
                cols = slice(g * 512, (g + 1) * 512)
                hT = hT_next
                gen = None
                if g + 1 < NG:
                    hT_next = hTs.get()
                    gen = norm_gen(g + 1, hT_next)
                for c in range(22):
                    if gen is not None and c in (4, 9, 14, 19):
                        next(gen, None)
                    cs = slice(c * 128, (c + 1) * 128)
                    pg, pu = k.ps(), k.ps()
                    for kk in range(8):
                        k.mm(pg[:], wgb[:, kk, cs], hT[:, kk, :], start=(kk == 0), stop=(kk == 7), reads=[wgb, hT], writes=[pg])
                    for kk in range(8):
                        k.mm(pu[:], wub[:, kk, cs], hT[:, kk, :], start=(kk == 0), stop=(kk == 7), reads=[wub, hT], writes=[pu])
                    s_ = sg.get()
                    k.act(s_[:], pg[:], AF.Silu, reads=[pg], writes=[s_])
                    a_ = ab.get()
                    k.tt("vector", a_[:], s_[:], pu[:], ALU.mult, reads=[s_, pu], writes=[a_])
                    k.dma("gpsimd", P.aT[cs, cols], a_[:], reads=[a_])
        with k.scope() as sc:
            gfb = sc.tile([128, D], F32)
            k.dma("sync", gfb[:], P.gfinb[:, :], writes=[gfb])
            ats = sc.pool(2, [128, 22, 256], BF16)
            xts = sc.pool(2, [128, D], F32)
            x2s = sc.pool(2, [128, D], F32)
            junk = sc.tile([128, D], BF16)
            ss = sc.pool(2, [128, 1], F32)
            aTv = P.aT.rearrange("(k p) t -> p k t", p=128)
            for hg in range(2 * NG):
                at = ats.get()
                k.dma("sync", at[:], aTv[:, :, hg * 256:(hg + 1) * 256], writes=[at])
                for tt in range(2):
                    rows = slice(hg * 256 + tt * 128, hg * 256 + (tt + 1) * 128)
                    xt = xts.get()
                    k.dma("sync", xt[:], P.x1[rows, :], writes=[xt])
                    x2 = x2s.get()
                    for nb in range(2):
                        ps = k.ps()
                        for c in range(22):
                            k.mm(ps[:], at[:, c, tt * 128:(tt + 1) * 128], wdb[:, c, nb * 512:(nb + 1) * 512],
                                 start=(c == 0), stop=(c == 21), reads=[at, wdb], writes=[ps])
                        k.tt("vector", x2[:, nb * 512:(nb + 1) * 512], ps[:], xt[:, nb * 512:(nb + 1) * 512], ALU.add,
                             reads=[ps, xt], writes=[x2])
                    if not last:
                        k.dma("gpsimd", P.xr[rows, :], x2[:], reads=[x2])
                    else:
                        s1 = ss.get()
                        k.act(junk[:], x2[:], AF.Square, accum_out=s1[:], reads=[x2], writes=[junk, s1])
                        k.act(s1[:], s1[:], AF.Sqrt, bias=EPS, scale=1.0 / D, reads=[s1], writes=[s1])
                        k.recip(s1[:], s1[:], reads=[s1], writes=[s1])
                        k.stt("vector", x2[:], x2[:], s1[:, 0:1], gfb[:], ALU.mult, ALU.mult, reads=[x2, s1, gfb], writes=[x2])
                        k.dma("gpsimd", P.out[rows, :], x2[:], reads=[x2])


def make_in_maps(inputs):
    f = lambda a: np.ascontiguousarray(np.asarray(a))
    inp = {kk: f(v) for kk, v in inputs.items()}
    shared = {
        "consts": make_consts(),
        "vecs": np.stack([make_vecs(inp, l) for l in range(L)]),
        "lnwb": np.stack([np.broadcast_to(np.concatenate([inp["a_ln_w"][l], inp["a_ln_b"][l]])[None, :], (128, 512)).copy()
                          for l in range(L)]),
        "gfinb": np.broadcast_to(inp["final_norm_g"][None, :], (128, D)).copy(),
        "w_in": np.stack([ktile(inp["w_in"][l]) for l in range(L)]),
        "w_uq": np.stack([ktile(inp["c_w_uq"][l]) for l in range(L)]),
        "w_ukv": np.stack([ktile(np.concatenate(
            [inp["c_w_ukv"][l].reshape(128, 4, 256)[:, :, :128].reshape(128, 512),
             inp["c_w_ukv"][l].reshape(128, 4, 256)[:, :, 128:].reshape(128, 512)], axis=1)) for l in range(L)]),
        "w_out": np.stack([ktile(inp["w_out"][l]) for l in range(L)]),
        "w_gate": np.stack([ktile(inp["ffn_w_gate"][l]) for l in range(L)]),
        "w_up": np.stack([ktile(inp["ffn_w_up"][l]) for l in range(L)]),
        "w_down": np.stack([ktile(inp["ffn_w_down"][l]) for l in range(L)]),
        "a_dec": inp["a_decay_up"], "a_icl": inp["a_iclr_up"], "a_gate": inp["a_gate_up"],
    }
    maps = []
    for b in range(8):
        m = dict(shared)
        m["x"] = inp["x"][b]
        m["pos"] = inp["positions"][b].reshape(1, S_LEN).astype(np.int32)
        maps.append(m)
    return maps


def kernel(**inputs):
    nc = build()
    maps = make_in_maps(inputs)
    res = run_bass_kernel_spmd(nc, maps, core_ids=list(range(8)))
    return np.stack([np.asarray(r["out"]) for r in res.results]).astype(np.float32)
```

```python
import contextlib
import math
import numpy as np
import concourse.bass as bass
import concourse.mybir as mybir
from concourse.bass_utils import run_bass_kernel_spmd

F32 = mybir.dt.float32
BF16 = mybir.dt.bfloat16
I32 = mybir.dt.int32
AF = mybir.ActivationFunctionType
ALU = mybir.AluOpType

S_LEN = 4096
D = 1024
DFF = 2816
L = 2
NG = 8
EPS = 1e-6
THETA = 500000.0
MAGIC = 12582912.0

ENGS = ("tensor", "vector", "scalar", "gpsimd", "sync")
NSLOTS = {"sync": 12, "gpsimd": 48}


class Buf:
    __slots__ = ("w", "r")

    def __init__(self):
        self.w = {}
        self.r = []


class Tl:
    __slots__ = ("t", "b")

    def __init__(self, t):
        self.t = t
        self.b = Buf()

    def __getitem__(self, idx):
        return self.t[idx]


class Sched:
    def __init__(self, nc, sems):
        self.nc = nc
        self.lists = {k: [] for k in ENGS}
        self.cnt = {k: 0 for k in ENGS}
        self.seen = {k: {} for k in ENGS}
        self.semh = sems
        self.dq = {q: {"n": 0, "slots": [f"dma_{q}_{i}" for i in range(NSLOTS[q])]} for q in ("sync", "gpsimd")}

    def _waits(self, eng, reads, writes, merge=False):
        deps = {}

        def need(ev):
            if ev is None:
                return
            k, v = ev
            if k == "tensor" and eng == "tensor":
                return
            if deps.get(k, 0) < v:
                deps[k] = v
        for b in reads:
            for ev in b.w.items():
                need(ev)
        for b in writes:
            if not merge:
                for ev in b.w.items():
                    need(ev)
            for ev in b.r:
                need(ev)
        out = []
        seen = self.seen[eng]
        for k, v in deps.items():
            if seen.get(k, 0) < v:
                seen[k] = v
                out.append((k, v))
        return out

    def _mark(self, ev, reads, writes, merge=False):
        for b in reads:
            b.r.append(ev)
        for b in writes:
            if merge:
                b.w[ev[0]] = max(b.w.get(ev[0], 0), ev[1])
            else:
                b.w = {ev[0]: ev[1]}
                b.r = []

    def op(self, eng, emit, reads=(), writes=()):
        reads = [x.b if isinstance(x, Tl) else x for x in reads]
        writes = [x.b if isinstance(x, Tl) else x for x in writes]
        waits = self._waits(eng, reads, writes)
        self.cnt[eng] += 1
        ev = (eng, self.cnt[eng])
        semh = self.semh
        mysem = semh[eng]

        def run(e):
            for k, v in waits:
                e.wait_ge(semh[k], v)
            emit(e).then_inc(mysem, 1)
        self.lists[eng].append(run)
        self._mark(ev, reads, writes)

    def dma(self, q, out, in_, reads=(), writes=(), merge=False):
        reads = [x.b if isinstance(x, Tl) else x for x in reads]
        writes = [x.b if isinstance(x, Tl) else x for x in writes]
        st = self.dq[q]
        n = st["n"]
        st["n"] += 1
        slots = st["slots"]
        key = slots[n % len(slots)]
        use = n // len(slots)
        waits = self._waits(q, reads, writes, merge)
        seen = self.seen[q]
        if use > 0 and seen.get(key, 0) < 16 * use:
            seen[key] = 16 * use
            waits.append((key, 16 * use))
        ev = (key, 16 * (use + 1))
        semh = self.semh

        def run(e):
            for k, v in waits:
                e.wait_ge(semh[k], v)
            e.dma_start(out=out, in_=in_).then_inc(semh[key], 16)
        self.lists[q].append(run)
        self._mark(ev, reads, writes, merge)

    def barrier(self):
        targets = {k: self.cnt[k] for k in ENGS if self.cnt[k] > 0}
        for q, st in self.dq.items():
            n = st["n"]
            ns = len(st["slots"])
            for i, key in enumerate(st["slots"]):
                uses = (n - i + ns - 1) // ns if n > i else 0
                if uses > 0:
                    targets[key] = 16 * uses
        semh = self.semh
        for eng in ENGS:
            seen = self.seen[eng]
            waits = []
            for k, v in targets.items():
                if k == eng:
                    continue
                if seen.get(k, 0) < v:
                    seen[k] = v
                    waits.append((k, v))

            def run(e, waits=waits):
                for k, v in waits:
                    e.wait_ge(semh[k], v)
            self.lists[eng].append(run)

    def emit(self):
        with self.nc.Block() as block:
            for name in ENGS:
                lst = self.lists[name]

                def body(e, lst=lst):
                    for f in lst:
                        f(e)
                getattr(block, name)(body)


class K:
    def __init__(self, nc, S):
        self.nc = nc
        self.S = S
        self.uid = 0
        self.psum = []
        self.pi = 0
        self.rot = None

    def name(self, p="t"):
        self.uid += 1
        return f"{p}{self.uid}"

    @contextlib.contextmanager
    def scope(self):
        es = contextlib.ExitStack()
        sc = Scope(self, es)
        try:
            yield sc
            self.S.barrier()
        finally:
            es.close()

    def ps(self):
        rot = self.rot if self.rot else list(range(7))
        t = self.psum[rot[self.pi % len(rot)]]
        self.pi += 1
        return t

    def mm(self, out, lhsT, rhs, start=True, stop=True, reads=(), writes=(), skip=False):
        if skip:
            self.S.op("tensor", lambda e: e.matmul(out, lhsT=lhsT, rhs=rhs, start=start, stop=stop, skip_group_check=True), reads, writes)
        else:
            self.S.op("tensor", lambda e: e.matmul(out, lhsT=lhsT, rhs=rhs, start=start, stop=stop), reads, writes)

    def tr(self, out, in_, ident, reads=(), writes=()):
        self.S.op("tensor", lambda e: e.transpose(out=out, in_=in_, identity=ident), reads, writes)

    def act(self, out, in_, func, bias=0.0, scale=1.0, accum_out=None, reads=(), writes=()):
        if accum_out is None:
            self.S.op("scalar", lambda e: e.activation(out=out, in_=in_, func=func, bias=bias, scale=scale), reads, writes)
        else:
            self.S.op("scalar", lambda e: e.activation(out=out, in_=in_, func=func, bias=bias, scale=scale,
                                                       accum_out=accum_out), reads, writes)

    def tt(self, eng, out, in0, in1, op, reads=(), writes=()):
        self.S.op(eng, lambda e: e.tensor_tensor(out=out, in0=in0, in1=in1, op=op), reads, writes)

    def ts(self, eng, out, in0, s1, s2=None, op0=ALU.mult, op1=None, reads=(), writes=()):
        if op1 is None:
            self.S.op(eng, lambda e: e.tensor_scalar(out=out, in0=in0, scalar1=s1, scalar2=None, op0=op0), reads, writes)
        else:
            self.S.op(eng, lambda e: e.tensor_scalar(out=out, in0=in0, scalar1=s1, scalar2=s2, op0=op0, op1=op1), reads, writes)

    def stt(self, eng, out, in0, scalar, in1, op0, op1, reads=(), writes=()):
        self.S.op(eng, lambda e: e.scalar_tensor_tensor(out=out, in0=in0, scalar=scalar, in1=in1, op0=op0, op1=op1), reads, writes)

    def copy(self, eng, out, in_, reads=(), writes=()):
        if eng == "scalar":
            self.S.op(eng, lambda e: e.activation(out=out, in_=in_, func=AF.Copy), reads, writes)
        else:
            self.S.op(eng, lambda e: e.tensor_copy(out=out, in_=in_), reads, writes)

    def recip(self, out, in_, reads=(), writes=()):
        self.S.op("vector", lambda e: e.reciprocal(out=out, in_=in_), reads, writes)

    def memset(self, eng, out, val, writes=()):
        self.S.op(eng, lambda e: e.memset(out, val), (), writes)

    def dma(self, q, out, in_, reads=(), writes=(), merge=False):
        self.S.dma(q, out, in_, reads, writes, merge)


class Scope:
    def __init__(self, k, es):
        self.k = k
        self.es = es

    def tile(self, shape, dtype, name="t"):
        t = self.es.enter_context(self.k.nc.sbuf_tensor(self.k.name(name), list(shape), dtype))
        return Tl(t)

    def pool(self, n, shape, dtype, name="p"):
        return Pool([self.tile(shape, dtype, name) for _ in range(n)])


class Pool:
    def __init__(self, tiles):
        self.tiles = tiles
        self.i = 0

    def get(self):
        t = self.tiles[self.i % len(self.tiles)]
        self.i += 1
        return t


C_ID, C_TRI, C_PB, C_PC, C_FC, C_BO, C_ONE, C_NB = 0, 128, 640, 768, 896, 898, 1026, 1154
NCONST = 1154 + 256
NEG = -30000.0


def make_consts():
    c = np.zeros((128, NCONST), np.float32)
    c[:, C_ID:C_ID + 128] = np.eye(128)
    r = np.arange(128)[:, None]
    q = np.arange(128)[None, :]
    c[:, C_TRI + 0:C_TRI + 128] = (q >= r)
    c[:, C_TRI + 128:C_TRI + 256] = (q <= r)
    c[:, C_TRI + 256:C_TRI + 384] = (q > r)
    c[:, C_TRI + 384:C_TRI + 512] = (q < r)
    pb = np.zeros((128, 128), np.float32)
    for rr in range(128):
        d = rr % 64
        if d < 8:
            pb[rr + 8, rr] = -1.0
        elif d < 16:
            pb[rr - 8, rr] = 1.0
    c[:, C_PB:C_PB + 128] = pb
    pc = np.zeros((128, 128), np.float32)
    for rr in range(64):
        if rr < 32:
            pc[rr + 32, rr] = -1.0
        else:
            pc[rr - 32, rr] = 1.0
    c[:, C_PC:C_PC + 128] = pc
    invb = 1.0 / (THETA ** (np.arange(0, 16, 2, dtype=np.float32) / 16))
    invc = 1.0 / (THETA ** (np.arange(0, 64, 2, dtype=np.float32) / 64))
    for rr in range(128):
        d = rr % 64
        c[rr, C_FC] = invb[d % 8] / (2 * math.pi) if d < 16 else 0.0
        c[rr, C_FC + 1] = invc[rr % 32] / (2 * math.pi)
    bo = np.zeros((128, 128), np.float32)
    bo[:64, :64] = 1
    bo[64:, 64:] = 1
    c[:, C_BO:C_BO + 128] = bo
    c[:, C_ONE:C_ONE + 128] = 1.0
    c[:, C_NB:C_NB + 128] = np.where(q >= r, 0.0, NEG)
    c[:, C_NB + 128:C_NB + 256] = np.where(q <= r, 0.0, NEG)
    return c


V_GATT, V_GFFN, V_GFIN, V_CQG, V_CKVG, V_MU, V_W0, V_A0, V_KK, V_KA, V_RK = 0, 8, 16, 24, 26, 27, 36, 38, 40, 42, 44
NV = 46


def col(v, n):
    return np.ascontiguousarray(v.reshape(n, 128).T)


def make_vecs(inp, l):
    v = np.zeros((128, NV), np.float32)
    v[:, V_GATT:V_GATT + 8] = col(inp["attn_norm_g"][l], 8)
    v[:, V_GFFN:V_GFFN + 8] = col(inp["ffn_norm_g"][l], 8)
    v[:, V_GFIN:V_GFIN + 8] = col(inp["final_norm_g"], 8)
    v[:, V_CQG:V_CQG + 2] = col(inp["c_q_norm_g"][l], 2)
    v[:, V_CKVG:V_CKVG + 1] = col(inp["c_kv_norm_g"][l], 1)
    mu = np.zeros(9 * 128, np.float32)
    mu[:1056] = inp["a_mu"][l]
    v[:, V_MU:V_MU + 9] = col(mu, 9)
    v[:, V_W0:V_W0 + 2] = col(inp["a_w0"][l], 2)
    v[:, V_A0:V_A0 + 2] = col(inp["a_a0"][l], 2)
    v[:, V_KK:V_KK + 2] = col(inp["a_k_k"][l], 2)
    v[:, V_KA:V_KA + 2] = col(inp["a_k_a"][l], 2)
    v[:, V_RK:V_RK + 2] = col(inp["a_r_k"][l].reshape(-1), 2)
    return v


def ktile(w):
    kk = w.shape[0] // 128
    return np.ascontiguousarray(w.reshape(kk, 128, w.shape[1]).transpose(1, 0, 2))


class Prog:
    pass


def build(debug=False, stop=None, nlayers=L, skip=()):
    nc = bass.Bass("TRN2", target_bir_lowering=False)
    P = Prog()
    din = {}

    def inp(name, shape, dt=F32):
        din[name] = nc.dram_tensor(name, list(shape), dt, kind="ExternalInput").ap()
        return din[name]

    skind = "ExternalOutput" if debug else "Internal"
    dscr = {}

    def scr(name, shape, dt):
        dscr[name] = nc.dram_tensor(name, list(shape), dt, kind=skind).ap()
        return dscr[name]

    x_in = inp("x", [S_LEN, D])
    pos = inp("pos", [1, S_LEN], I32)
    consts = inp("consts", [128, NCONST])
    vecs = inp("vecs", [L, 128, NV])
    lnwb = inp("lnwb", [L, 128, 512])
    gfinb = inp("gfinb", [128, D])
    w_in = inp("w_in", [L, 128, 8, 2272])
    w_uq = inp("w_uq", [L, 128, 2, 768])
    w_ukv = inp("w_ukv", [L, 128, 1, 1024])
    w_out = inp("w_out", [L, 128, 8, 1024])
    w_gate = inp("w_gate", [L, 128, 8, DFF])
    w_up = inp("w_up", [L, 128, 8, DFF])
    w_down = inp("w_down", [L, 128, 22, 1024])
    a_dec = inp("a_dec", [L, 64, 256])
    a_icl = inp("a_icl", [L, 64, 256])
    a_gate = inp("a_gate", [L, 160, 256])
    out = nc.dram_tensor("out", [S_LEN, D], F32, kind="ExternalOutput").ap()

    pA = scr("pA", [1056, S_LEN], F32)
    qB = [scr(f"qB{i}", [256, S_LEN], BF16) for i in range(3)]
    kB = [scr(f"kB{i}", [256, S_LEN], BF16) for i in range(3)]
    vB = scr("vB", [S_LEN, 256], BF16)
    qn = scr("qn", [512, S_LEN], BF16)
    qr = scr("qr", [256, S_LEN], BF16)
    kn = scr("kn", [512, S_LEN], BF16)
    kr = scr("kr", [64, S_LEN], BF16)
    vC = scr("vC", [S_LEN, 512], BF16)
    cat = scr("cat", [1024, S_LEN], BF16)
    x1 = scr("x1", [S_LEN, D], F32)
    xr = scr("xr", [S_LEN, D], F32)
    aT = scr("aT", [DFF, S_LEN], BF16)
    tabB = scr("tabB", [2, 128, S_LEN], F32)
    tabC = scr("tabC", [2, 64, S_LEN], F32)

    with contextlib.ExitStack() as es:
        E = es.enter_context
        names = list(ENGS) + [f"dma_{q}_{i}" for q in ("sync", "gpsimd") for i in range(NSLOTS[q])]
        sems = {n: E(nc.semaphore(n)) for n in names}
        S = Sched(nc, sems)
        k = K(nc, S)
        k.psum = [Tl(E(nc.psum_tensor(f"ps{i}", [128, 512], F32))) for i in range(8)]
        pTt = Tl(k.psum[7][:].bitcast(BF16).rearrange("p (a b) -> p a b", a=8))
        pTt.b = k.psum[7].b
        cst = Tl(E(nc.sbuf_tensor("cst", [128, NCONST], F32)))
        cstb = Tl(E(nc.sbuf_tensor("cstb", [128, NCONST], BF16)))
        k.dma("sync", cst[:], consts[:, :], writes=[cst])
        k.copy("vector", cstb[:], cst[:], reads=[cst], writes=[cstb])
        P.__dict__.update(locals())

        phase_tables(P)
        done = (stop == "tables")
        for l in range(nlayers):
            if done:
                break
            xsrc = x_in if l == 0 else xr
            phase_p1(P, l, xsrc)
            if stop == f"p1_{l}":
                break
            if "mla" not in skip:
                phase_mla(P, l)
            if stop == f"mla_{l}":
                break
            if "dil" not in skip:
                phase_dil(P, l)
            if stop == f"dil_{l}":
                break
            phase_rwkv(P, l)
            if stop == f"rwkv_{l}":
                break
            phase_tail(P, l, xsrc, last=(l == nlayers - 1))
        S.barrier()
        S.emit()
    return nc


def phase_tables(P):
    k, cst = P.k, P.cst
    with k.scope() as sc:
        posi = sc.tile([128, S_LEN], I32)
        posf = sc.tile([128, S_LEN], F32)
        y = sc.tile([128, S_LEN], F32)
        t = sc.tile([128, S_LEN], F32)
        r = sc.tile([128, S_LEN], F32)
        o = sc.tile([128, S_LEN], F32)
        src = bass.AP(P.pos.tensor, 0, [[0, 128], [1, S_LEN]])
        k.dma("sync", posi[:], src, writes=[posi])
        k.copy("vector", posf[:], posi[:], reads=[posi], writes=[posf])
        for ti, (fc, rows, dst) in enumerate(((C_FC, 128, P.tabB), (C_FC + 1, 64, P.tabC))):
            for cs in range(2):
                add = 0.25 if cs == 0 else 0.0
                k.ts("vector", y[:rows], posf[:rows], cst[:rows, fc:fc + 1], add, ALU.mult, ALU.add,
                     reads=[posf, cst], writes=[y])
                k.ts("vector", t[:rows], y[:rows], MAGIC, None, ALU.add, reads=[y], writes=[t])
                k.ts("vector", t[:rows], t[:rows], -MAGIC, None, ALU.add, reads=[t], writes=[t])
                k.tt("vector", r[:rows], y[:rows], t[:rows], ALU.subtract, reads=[y, t], writes=[r])
                k.act(o[:rows], r[:rows], AF.Sin, scale=6.28318, reads=[r], writes=[o])
                k.dma("sync", dst[cs, :, :], o[:rows], reads=[o])


def load_w(P, sc, dst, src, kc, n, stage, eng="gpsimd"):
    k = P.k
    for i in range(kc):
        st = stage.get()
        k.dma("sync", st[:, 0:n], src[:, i, :], writes=[st])
        k.copy(eng, dst[:, i, :], st[:, 0:n], reads=[st], writes=[dst])


def load_w2(P, dst, src, kc, nsplit):
    step = (kc + nsplit - 1) // nsplit
    for i in range(0, kc, step):
        j = min(kc, i + step)
        P.k.dma("gpsimd", dst[:, i:j, :], src[:, i:j, :], writes=[dst], merge=True)


def rmsnorm_T(P, xt, junk, ss, rstd, xn, pT, gb, hT_ap, hT):
    k = P.k
    k.act(junk[:], xt[:], AF.Square, accum_out=ss[:], reads=[xt], writes=[junk, ss])
    k.act(rstd[:], ss[:], AF.Sqrt, bias=EPS, scale=1.0 / D, reads=[ss], writes=[rstd])
    k.recip(rstd[:], rstd[:], reads=[rstd], writes=[rstd])
    k.ts("vector", xn[:], xt[:], rstd[:, 0:1], None, ALU.mult, reads=[xt, rstd], writes=[xn])
    for kk in range(8):
        k.tr(pT[:, kk, :], xn[:, kk * 128:(kk + 1) * 128], P.cstb[:, C_ID:C_ID + 128], reads=[xn, P.cstb], writes=[pT])
    k.tt("vector", hT_ap, pT[:], gb[:], ALU.mult, reads=[pT, gb], writes=[hT])


def make_gb(P, sc, vec, c0):
    k = P.k
    gb = sc.tile([128, 8, 128], BF16)
    for kk in range(8):
        k.ts("gpsimd", gb[:, kk, :], P.cst[:, C_ONE:C_ONE + 128], vec[:, c0 + kk:c0 + kk + 1], None, ALU.mult,
             reads=[P.cst, vec], writes=[gb])
    return gb


def rope(P, ps, rows, permT, cos, sin, scale, tmp, outs, srcs, after=None):
    k = P.k
    qs = tmp.get()
    k.act(qs[:rows], ps[:rows], AF.Copy, scale=scale, reads=[ps], writes=[qs])

    def second():
        rope_b(P, qs, rows, permT, cos, sin, tmp, outs, srcs)
        if after is not None:
            after()
    return second


def rope_b(P, qs, rows, permT, cos, sin, tmp, outs, srcs):
    k = P.k
    pp = k.ps()
    k.mm(pp[:rows], permT, qs[:rows], reads=[qs, P.cst], writes=[pp])
    t1 = tmp.get()
    k.tt("vector", t1[:rows], qs[:rows], cos, ALU.mult, reads=[qs] + srcs, writes=[t1])
    t2 = tmp.get()
    k.tt("vector", t2[:rows], pp[:rows], sin, ALU.mult, reads=[pp] + srcs, writes=[t2])
    for eng, out_ap, view, tl in outs:
        k.tt(eng, out_ap, view(t1[:rows]), view(t2[:rows]), ALU.add, reads=[t1, t2], writes=[tl])


def phase_p1(P, l, xsrc):
    k, cst, cstb = P.k, P.cst, P.cstb
    with k.scope() as sc:
        winb = sc.tile([128, 8, 2272], BF16)
        wuqb = sc.tile([128, 2, 768], BF16)
        wukvb = sc.tile([128, 1, 1024], BF16)
        vec = sc.tile([128, NV], F32)
        k.dma("sync", vec[:], P.vecs[l], writes=[vec])
        load_w2(P, winb, P.w_in[l], 8, 4)
        load_w2(P, wuqb, P.w_uq[l], 2, 1)
        load_w2(P, wukvb, P.w_ukv[l], 1, 1)
        gb = make_gb(P, sc, vec, V_GATT)
        xts = sc.pool(3, [128, D], F32)
        junk = sc.tile([128, D], BF16)
        ss = sc.tile([128, 1], F32)
        rstd = sc.tile([128, 1], F32)
        xn = sc.tile([128, D], BF16)
        hTs = sc.pool(2, [128, 8, 512], BF16)
        pTt = P.pTt
        stA = sc.pool(3, [128, 512], F32)
        tmp = sc.pool(6, [128, 512], F32)
        ob = sc.pool(8, [128, 512], BF16)
        o16 = [[sc.tile([128, 16, 128], BF16) for _ in range(4)] for _ in range(2)]
        tabs = sc.pool(2, [128, 4, 512], F32)
        vbt = sc.pool(2, [128, 4, 256], BF16)
        vct = sc.pool(2, [128, 4, 512], BF16)
        csb = sc.tile([128, 3, 512], F32)
        sqb = sc.tile([128, 3, 512], F32)
        rs = sc.pool(2, [128, 512], F32)
        cqn = sc.tile([128, 3, 512], BF16)
        ones = cst[:, C_ONE:C_ONE + 128]
        A_CH = [(i * 128, 128) for i in range(8)] + [(1024, 32)]

        def norm_gen(g_, hT_):
            for tt in range(4):
                xt = xts.get()
                k.dma("sync", xt[:], xsrc[g_ * 512 + tt * 128: g_ * 512 + (tt + 1) * 128, :], writes=[xt])
                rmsnorm_T(P, xt, junk, ss, rstd, xn, pTt, gb, hT_[:, :, tt * 128:(tt + 1) * 128], hT_)
                yield
        hT_next = hTs.get()
        for _ in norm_gen(0, hT_next):
            pass
        for g in range(NG):
            cols = slice(g * 512, (g + 1) * 512)
            hT = hT_next
            gen = None
            if g + 1 < NG:
                hT_next = hTs.get()
                gen = norm_gen(g + 1, hT_next)

            def step(gen=gen):
                if gen is not None:
                    next(gen, None)
            tb = tabs.get()
            k.dma("sync", tb[:, 0:2, :], P.tabB[:, :, cols].rearrange("c p t -> p c t"), writes=[tb])
            k.dma("sync", tb[0:64, 2:4, :], P.tabC[:, :, cols].rearrange("c p t -> p c t"), writes=[tb])

            def proj(c0, m):
                ps = k.ps()
                for kk in range(8):
                    k.mm(ps[:m], winb[:, kk, c0:c0 + m], hT[:, kk, :], start=(kk == 0), stop=(kk == 7),
                         reads=[winb, hT], writes=[ps])
                return ps
            pend = []

            def flush():
                while pend:
                    pend.pop(0)()

            def a_chunk(ci):
                c0, m = A_CH[ci]
                ps = proj(c0, m)
                st = stA.get()
                k.copy("scalar", st[:m], ps[:m], reads=[ps], writes=[st])
                k.dma("gpsimd", P.pA[c0:c0 + m, cols], st[:m], reads=[st])
                if ci in (2, 5, 8):
                    step()
            for j, (c0, m) in enumerate(((1824, 128), (1952, 128), (2080, 128))):
                ps = proj(c0, m)
                k.copy("scalar", csb[:, j, :], ps[:], reads=[ps], writes=[csb])
                k.tt("gpsimd", sqb[:, j, :], csb[:, j, :], csb[:, j, :], ALU.mult, reads=[csb], writes=[sqb])
            for ci in range(3):
                a_chunk(ci)
            for which in range(2):
                js = (0, 1) if which == 0 else (2,)
                nfeat = 256.0 if which == 0 else 128.0
                ps = k.ps()
                for i, j in enumerate(js):
                    k.mm(ps[:], ones, sqb[:, j, :], start=(i == 0), stop=(i == len(js) - 1), reads=[cst, sqb], writes=[ps])
                r_ = rs.get()
                k.act(r_[:], ps[:], AF.Sqrt, bias=EPS, scale=1.0 / nfeat, reads=[ps], writes=[r_])
                k.recip(r_[:], r_[:], reads=[r_], writes=[r_])
                for j in js:
                    gcol = vec[:, V_CQG + j:V_CQG + j + 1]
                    k.stt("vector", cqn[:, j, :], csb[:, j, :], gcol, r_[:], ALU.mult, ALU.mult,
                          reads=[csb, vec, r_], writes=[cqn])
            for ci in range(3, 9):
                a_chunk(ci)
            for qk in range(2):
                for ch in range(2):
                    c0 = 1056 + qk * 256 + ch * 128
                    ps = proj(c0, 128)
                    flush()
                    o1 = ob.get()
                    o4 = ob.get()
                    o16t = o16[qk][ch * 2 + 0]
                    off = (g % 4) * 32
                    outs = [
                        ("vector", o1[:], (lambda v: v), o1),
                        ("gpsimd", o4[:].rearrange("p (r l) -> p r l", r=4), (lambda v: v.rearrange("p (l r) -> p r l", r=4)), o4),
                        ("gpsimd", o16t[:, :, off:off + 32], (lambda v: v.rearrange("p (l r) -> p r l", r=16)), o16t),
                    ]
                    dst = P.qB if qk == 0 else P.kB
                    rows = slice(ch * 128, (ch + 1) * 128)

                    def after(dst=dst, rows=rows, o1=o1, o4=o4, o16t=o16t):
                        k.dma("gpsimd", dst[0][rows, cols], o1[:], reads=[o1])
                        k.dma("gpsimd", dst[1][rows, cols], o4[:], reads=[o4])
                        if g % 4 == 3:
                            sp = g // 4
                            k.dma("gpsimd", dst[2][rows, sp * 2048:(sp + 1) * 2048], o16t[:].rearrange("p r l -> p (r l)"), reads=[o16t])
                    pend.append(rope(P, ps, 128, cst[:, C_PB:C_PB + 128], tb[:, 0, :], tb[:, 1, :], (0.125 if qk == 0 else 1.0),
                                     tmp, outs, [tb], after))
            step()
            vb = vbt.get()
            for tt in range(4):
                ps = k.ps()
                for kk in range(8):
                    k.mm(ps[:, 0:256], hT[:, kk, tt * 128:(tt + 1) * 128], winb[:, kk, 1568:1824], start=(kk == 0), stop=(kk == 7),
                         reads=[winb, hT], writes=[ps])
                if tt == 0:
                    flush()
                k.copy("scalar", vb[:, tt, :], ps[:, 0:256], reads=[ps], writes=[vb])
            k.dma("gpsimd", P.vB[cols, :].rearrange("(t p) c -> p t c", p=128), vb[:], reads=[vb])
            qscale = 192.0 ** -0.5
            for h in range(4):
                ps = k.ps()
                for kc in range(2):
                    k.mm(ps[:], wuqb[:, kc, h * 192:h * 192 + 128], cqn[:, kc, :], start=(kc == 0), stop=(kc == 1),
                         reads=[wuqb, cqn], writes=[ps])
                o = ob.get()
                k.act(o[:], ps[:], AF.Copy, scale=qscale, reads=[ps], writes=[o])
                k.dma("gpsimd", P.qn[h * 128:(h + 1) * 128, cols], o[:], reads=[o])
                ps = k.ps()
                for kc in range(2):
                    k.mm(ps[0:64], wuqb[:, kc, h * 192 + 128:h * 192 + 192], cqn[:, kc, :], start=(kc == 0), stop=(kc == 1),
                         reads=[wuqb, cqn], writes=[ps])
                flush()
                o = ob.get()

                def after(o=o, h=h):
                    k.dma("gpsimd", P.qr[h * 64:(h + 1) * 64, cols], o[0:64], reads=[o])
                pend.append(rope(P, ps, 64, cst[0:64, C_PC:C_PC + 64], tb[0:64, 2, :], tb[0:64, 3, :], qscale, tmp,
                                 [("vector", o[0:64], (lambda v: v), o)], [tb], after))
            for h in range(4):
                ps = k.ps()
                k.mm(ps[:], wukvb[:, 0, h * 128:(h + 1) * 128], cqn[:, 2, :], reads=[wukvb, cqn], writes=[ps])
                if h == 0:
                    flush()
                o = ob.get()
                k.copy("scalar", o[:], ps[:], reads=[ps], writes=[o])
                k.dma("gpsimd", P.kn[h * 128:(h + 1) * 128, cols], o[:], reads=[o])
            ps = proj(2208, 64)
            o = ob.get()

            def after(o=o):
                k.dma("gpsimd", P.kr[:, cols], o[0:64], reads=[o])
            pend.append(rope(P, ps, 64, cst[0:64, C_PC:C_PC + 64], tb[0:64, 2, :], tb[0:64, 3, :], 1.0, tmp,
                             [("vector", o[0:64], (lambda v: v), o)], [tb], after))
            vc = vct.get()
            for tt in range(4):
                ps = k.ps()
                k.mm(ps[:], cqn[:, 2, tt * 128:(tt + 1) * 128], wukvb[:, 0, 512:1024], reads=[wukvb, cqn], writes=[ps])
                if tt == 0:
                    flush()
                k.copy("scalar", vc[:, tt, :], ps[:], reads=[ps], writes=[vc])
            k.dma("gpsimd", P.vC[cols, :].rearrange("(t p) c -> p t c", p=128), vc[:], reads=[vc])
            flush()


def phase_mla(P, l):
    k, cst, cstb = P.k, P.cst, P.cstb
    with k.scope() as sc:
        knp = sc.pool(2, [128, S_LEN], BF16)
        qnp = sc.pool(2, [128, S_LEN], BF16)
        qrp = sc.pool(2, [128, S_LEN], BF16)
        vp = sc.pool(2, [128, 32, 128], BF16)
        krt = sc.tile([128, S_LEN], BF16)
        for t_ in qrp.tiles + [krt]:
            k.memset("gpsimd", t_[64:128, :], 0.0, writes=[t_])
        pacc1p = sc.pool(3, [128, 512], F32)
        pTp = sc.pool(8, [128, 512], BF16)
        ob = sc.pool(2, [128, 512], BF16)
        rd = sc.pool(2, [128, 512], F32)
        paccp = sc.pool(3, [128, 512], F32)
        k.dma("sync", krt[0:64, :], P.kr[:, :], writes=[krt])
        ones = cst[:, C_ONE:C_ONE + 128]
        identb = cstb[:, C_ID:C_ID + 128]
        nbias = cstb[:, C_NB:C_NB + 128]
        onesb = cstb[:, C_ONE:C_ONE + 128]
        pending = []
        for h in range(4):
            knh, qnh, qrh, vh = knp.get(), qnp.get(), qrp.get(), vp.get()
            k.dma("sync", knh[:], P.kn[h * 128:(h + 1) * 128, :], writes=[knh])
            k.dma("sync", qnh[:], P.qn[h * 128:(h + 1) * 128, :], writes=[qnh])
            k.dma("sync", qrh[0:64, :], P.qr[h * 64:(h + 1) * 64, :], writes=[qrh])
            k.dma("sync", vh[:], P.vC[:, h * 128:(h + 1) * 128].rearrange("(t p) c -> p t c", p=128), writes=[vh])
            for g in range(NG):
                k.rot = [0, 1, 2, 3]
                po = k.psum[4 + (g % 2)]
                pden = k.psum[6 + (g % 2)]
                pe_den = [kt for kt in range(4 * g + 4) if kt % 3 == 1] if g > 0 else []
                pacc = paccp.get()
                pacc1 = pacc1p.get()
                k.memset("gpsimd", pacc1[:], 0.0, writes=[pacc1])
                nkt = 4 * g + 4

                def s_stage(kt, g=g, pacc=pacc, pacc1=pacc1, pe_den=pe_den):
                    o = kt - 4 * g
                    c0 = max(o, 0) * 128
                    ks = slice(kt * 128, (kt + 1) * 128)
                    qs = slice(g * 512 + c0, (g + 1) * 512)
                    pS = k.ps()
                    k.mm(pS[:, c0:512], knh[:, ks], qnh[:, qs], start=True, stop=False, reads=[knh, qnh], writes=[pS])
                    k.mm(pS[:, c0:512], krt[:, ks], qrh[:, qs], start=False, stop=(o < 0), reads=[krt, qrh], writes=[pS])
                    if o >= 0:
                        k.mm(pS[:, c0:c0 + 128], identb, nbias, start=False, stop=True, reads=[cstb], writes=[pS])
                    pT = pTp.get()
                    k.act(pT[:, c0:512], pS[:, c0:512], AF.Exp, reads=[pS], writes=[pT])
                    if kt == 0:
                        k.copy("vector", pacc[:], pT[:], reads=[pT], writes=[pacc])
                    elif kt in pe_den:
                        pass
                    elif kt % 3 == 2:
                        k.tt("gpsimd", pacc1[:, c0:512], pacc1[:, c0:512], pT[:, c0:512], ALU.add, reads=[pT, pacc1], writes=[pacc1])
                    else:
                        k.tt("vector", pacc[:, c0:512], pacc[:, c0:512], pT[:, c0:512], ALU.add, reads=[pT, pacc], writes=[pacc])
                    return kt, c0, pT

                def pv_stage(st, po=po, nkt=nkt, pden=pden, pe_den=pe_den):
                    kt, c0, pT = st
                    k.mm(po[:, c0:512], vh[:, kt, :], pT[:, c0:512], start=(kt == 0), stop=(kt == nkt - 1), reads=[vh, pT], writes=[po])
                    if kt in pe_den:
                        k.mm(pden[:, c0:512], onesb, pT[:, c0:512], start=(kt == pe_den[0]), stop=False, reads=[cstb, pT], writes=[pden],
                             skip=True)

                def finalize(g=g, h=h, po=po, pacc=pacc, pacc1=pacc1, pd=pden, pe_den=pe_den):
                    k.mm(pd[:], ones, pacc[:], start=(not pe_den), stop=False, reads=[cst, pacc], writes=[pd], skip=True)
                    k.mm(pd[:], ones, pacc1[:], start=False, stop=True, reads=[cst, pacc1], writes=[pd], skip=True)
                    r = rd.get()
                    k.recip(r[:], pd[:], reads=[pd], writes=[r])
                    oo = ob.get()
                    k.tt("vector", oo[:], po[:], r[:], ALU.mult, reads=[po, r], writes=[oo])
                    k.dma("gpsimd", P.cat[512 + h * 128:512 + (h + 1) * 128, g * 512:(g + 1) * 512], oo[:], reads=[oo])
                DEPTH = 2
                q_ = []
                for kt in range(nkt):
                    q_.append(s_stage(kt))
                    if kt == 2 and pending:
                        pending.pop()()
                    if len(q_) > DEPTH:
                        pv_stage(q_.pop(0))
                while q_:
                    pv_stage(q_.pop(0))
                pending.append(finalize)
        while pending:
            pending.pop()()
        k.rot = None


def phase_dil(P, l):
    k, cst, cstb = P.k, P.cst, P.cstb
    with k.scope() as sc:
        vts = [sc.tile([128, 32, 256], BF16) for _ in range(3)]
        vap = sc.pool(2, [128, 32, 128], BF16)
        qp = [sc.pool(2, [128, S_LEN], BF16) for _ in range(2)]
        kp = sc.pool(2, [128, S_LEN], BF16)
        acc = sc.tile([128, S_LEN], F32)
        pTp = sc.pool(8, [128, 512], BF16)
        ob = sc.pool(2, [64, S_LEN], BF16)
        rdp = sc.pool(2, [64, 512], F32)
        for par in range(2):
            for t_ in qp[par].tiles:
                k.memset("gpsimd", t_[(1 - par) * 64:(2 - par) * 64, :], 0.0, writes=[t_])
        for t_ in vap.tiles:
            k.memset("gpsimd", t_[:, :, 64:128], 1.0, writes=[t_])
        DS = (1, 4, 16)
        for pi, d in enumerate(DS):
            if d == 1:
                src = P.vB[:, :].rearrange("(s l) c -> l s c", l=128)
                k.dma("sync", vts[pi][:], src, writes=[vts[pi]])
            else:
                for r in range(d):
                    src = P.vB[:, :].rearrange("(s l r) c -> r l s c", l=128, r=d)[r]
                    dstv = vts[pi][:].rearrange("p (s r) c -> p r s c", r=d)[:, r]
                    k.dma("sync", dstv, src, writes=[vts[pi]], merge=True)
        identb = cstb[:, C_ID:C_ID + 128]
        nb_cur = cstb[:, C_NB:C_NB + 128]
        nb_prev = cstb[:, C_NB + 128:C_NB + 256]
        shiftT = cst[:, C_ID + 64:C_ID + 128]
        k.rot = [0, 1, 2, 3, 4, 5, 6]
        for h in range(4):
            hc = slice(h * 64, (h + 1) * 64)
            par = h % 2
            pr = slice((h // 2) * 128, (h // 2 + 1) * 128)
            for pi, d in enumerate(DS):
                qh, kh = qp[par].get(), kp.get()
                k.dma("sync", qh[par * 64:(par + 1) * 64, :], P.qB[pi][hc, :], writes=[qh])
                k.dma("sync", kh[:], P.kB[pi][pr, :], writes=[kh])
                vt = vap.get()
                k.copy("gpsimd", vt[:, :, 0:64], vts[pi][:, :, hc], reads=[vts[pi]], writes=[vt])

                def s_stage(bt, d=d, qh=qh, kh=kh):
                    b0 = bt * 4
                    pc, pp = k.ps(), k.ps()
                    hasp = [(b0 + j - d) >= 0 for j in range(4)]
                    for j in range(4):
                        cs = slice((b0 + j) * 128, (b0 + j + 1) * 128)
                        js = slice(j * 128, (j + 1) * 128)
                        k.mm(pc[:, js], kh[:, cs], qh[:, cs], start=True, stop=False, reads=[kh, qh], writes=[pc])
                        k.mm(pc[:, js], identb, nb_cur, start=False, stop=True, reads=[cstb], writes=[pc])
                        if hasp[j]:
                            ps_ = slice((b0 + j - d) * 128, (b0 + j - d + 1) * 128)
                            k.mm(pp[:, js], kh[:, ps_], qh[:, cs], start=True, stop=False, reads=[kh, qh], writes=[pp])
                            k.mm(pp[:, js], identb, nb_prev, start=False, stop=True, reads=[cstb], writes=[pp])
                    tc_ = pTp.get()
                    k.act(tc_[:], pc[:], AF.Exp, reads=[pc], writes=[tc_])
                    tp_ = None
                    if any(hasp):
                        j0 = hasp.index(True)
                        tp_ = pTp.get()
                        k.act(tp_[:, j0 * 128:512], pp[:, j0 * 128:512], AF.Exp, reads=[pp], writes=[tp_])
                    return bt, hasp, tc_, tp_

                def pv_stage(st, d=d, pi=pi, vt=vt):
                    bt, hasp, tc_, tp_ = st
                    b0 = bt * 4
                    pn = k.ps()
                    for j in range(4):
                        js = slice(j * 128, (j + 1) * 128)
                        if hasp[j]:
                            k.mm(pn[:, js], vt[:, b0 + j - d, :], tp_[:, js], start=True, stop=False, reads=[vt, tp_], writes=[pn])
                        k.mm(pn[:, js], vt[:, b0 + j, :], tc_[:, js], start=(not hasp[j]), stop=True, reads=[vt, tc_], writes=[pn])
                    if d == 1:
                        va = acc[:, b0 * 128:b0 * 128 + 512]
                        vpz = pn[:, :]
                    elif d == 4:
                        va = acc[:, bt * 512:(bt + 1) * 512].rearrange("p (l r) -> p r l", r=4)
                        vpz = pn[:, :].rearrange("p (r l) -> p r l", r=4)
                    else:
                        sp, r0 = b0 // 16, b0 % 16
                        va = acc[:, sp * 2048:(sp + 1) * 2048].rearrange("p (l r) -> p r l", r=16)[:, r0:r0 + 4, :]
                        vpz = pn[:, :].rearrange("p (r l) -> p r l", r=4)
                    if pi == 0:
                        k.copy("vector", va, vpz, reads=[pn], writes=[acc])
                    else:
                        k.tt("vector", va, va, vpz, ALU.add, reads=[pn, acc], writes=[acc])
                DEPTH = 2
                q_ = []
                for bt in range(8):
                    q_.append(s_stage(bt))
                    if len(q_) > DEPTH:
                        pv_stage(q_.pop(0))
                while q_:
                    pv_stage(q_.pop(0))
            oo = ob.get()
            for c8 in range(8):
                cs8 = slice(c8 * 512, (c8 + 1) * 512)
                pd = k.ps()
                k.mm(pd[0:64, :], shiftT, acc[:, cs8], reads=[cst, acc], writes=[pd])
                r_ = rdp.get()
                k.recip(r_[:], pd[0:64, :], reads=[pd], writes=[r_])
                k.tt("gpsimd", oo[:, cs8], acc[0:64, cs8], r_[:], ALU.mult, reads=[acc, r_], writes=[oo])
            k.dma("gpsimd", P.cat[256 + h * 64:256 + (h + 1) * 64, :], oo[:], reads=[oo])
        k.rot = None


C0 = 0.6065306597126334
GN_EPS = 64e-5


def bc(ap, n):
    return bass.AP(ap.tensor, ap.offset, [list(x) for x in ap.ap] + [[0, n]])


def phase_rwkv(P, l):
    k, cst, cstb = P.k, P.cst, P.cstb
    AX = mybir.AxisListType
    with k.scope() as sc:
        identf = cst[:, C_ID:C_ID + 128]
        bones = cst[:, C_BO:C_BO + 128]
        ones = cst[:, C_ONE:C_ONE + 128]
        vec = sc.tile([128, NV], F32)
        lnwb = sc.tile([128, 512], F32)
        dec = sc.tile([128, 256], F32)
        icl = sc.tile([128, 256], F32)
        gateA = sc.tile([128, 256], F32)
        gateB = sc.tile([128, 256], F32)
        for t_ in (dec, icl, gateB):
            k.memset("gpsimd", t_[:], 0.0, writes=[t_])
        k.dma("sync", vec[:], P.vecs[l], writes=[vec])
        k.dma("sync", lnwb[:], P.lnwb[l], writes=[lnwb])
        k.dma("sync", dec[0:64, :], P.a_dec[l], writes=[dec])
        k.dma("sync", icl[64:128, :], P.a_icl[l], writes=[icl])
        k.dma("sync", gateA[:], P.a_gate[l, 0:128, :], writes=[gateA])
        k.dma("sync", gateB[0:32, :], P.a_gate[l, 128:160, :], writes=[gateB])
        msk4 = sc.tile([128, 512], F32)
        miu4 = sc.tile([128, 512], F32)
        msu2 = sc.tile([128, 256], F32)
        for j in range(4):
            src = C_TRI + (384 if j % 2 == 0 else 256)
            k.copy("vector", msk4[:, j * 128:(j + 1) * 128], cst[:, src:src + 128], reads=[cst], writes=[msk4])
            k.copy("vector", miu4[:, j * 128:(j + 1) * 128], cst[:, C_TRI:C_TRI + 128], reads=[cst], writes=[miu4])
        for j in range(2):
            k.copy("vector", msu2[:, j * 128:(j + 1) * 128], cst[:, C_TRI + 256:C_TRI + 384], reads=[cst], writes=[msu2])
        rawp = sc.pool(2, [128, 513], F32)
        dtmp = sc.pool(2, [128, 512], F32)
        PFk = [sc.tile([128, 512], F32) for _ in range(2)]
        PF6 = sc.tile([128, 512], F32)
        PF7 = sc.tile([128, 512], F32)
        PF8 = sc.tile([128, 512], F32)
        tmpf = sc.pool(3, [128, 512], F32)
        th = sc.tile([128, 512], F32)
        SG = sc.tile([128, 512], F32)
        Aic = sc.tile([128, 512], F32)

        class HOc:
            pass
        HO = []
        for i in range(2):
            ho = HOc()
            for nm_ in ("R", "V", "CS", "CSX", "KN", "KF", "Bv", "RKR"):
                setattr(ho, nm_, [sc.tile([128, 512], F32) for _ in range(2)])
            ho.SX7 = sc.tile([128, 512], F32)
            ho.SX8 = sc.tile([128, 512], F32)
            k.memset("gpsimd", ho.SX8[:], 0.0, writes=[ho.SX8])
            HO.append(ho)

        class Slot:
            pass
        slots = []
        for s_ in range(4):
            sl = Slot()
            sl.ep = sc.pool(5, [128, 128], F32)
            sl.dp = sc.pool(4, [128, 128], F32)
            sl.APz = [sc.tile([128, 128], BF16) for _ in range(2)]
            sl.bkb = sc.pool(2, [128, 2, 128], BF16)
            sl.RPz = [sc.tile([128, 128], F32) for _ in range(2)]
            for t_ in sl.APz + sl.RPz:
                k.memset("gpsimd", t_[:], 0.0, writes=[t_])
            sl.smp = sc.pool(8, [128, 2], F32)
            sl.tmp = sc.pool(1, [128, 3, 128], F32)
            sl.MNp = sc.pool(2, [128, 2, 2, 128], BF16)
            sl.mkp = sc.pool(1, [128, 2, 128], F32)
            sl.wrp = sc.pool(1, [128, 2, 2, 128], F32)
            sl.Xbp = sc.pool(2, [128, 2, 2, 64], BF16)
            sl.Xfp = sc.pool(1, [128, 2, 2, 64], F32)
            sl.sq = sc.pool(6, [128, 128], F32)
            sl.Hpp = sc.pool(1, [128, 128], F32)
            sl.px = k.psum[4 + s_]
            slots.append(sl)
        k.rot = [0, 1, 2, 3]
        STp = [sc.pool(2, [128, 128], F32) for _ in range(2)]
        catA = sc.pool(3, [128, 2, 512], BF16)
        ST = []
        for cp in range(2):
            t = STp[cp].get()
            k.memset("gpsimd", t[:], 0.0, writes=[t])
            ST.append(t)
        f2 = lambda t: t[:].rearrange("p a b c -> p (a b c)")

        def a1gen(g, ho):
            lo = g * 512 - 1

            def mix(ch, dst_ap, dst_tl):
                rows = 128 if ch < 8 else 32
                rw = rawp.get()
                rsl = slice(ch * 128, ch * 128 + rows)
                if g == 0:
                    k.memset("gpsimd", rw[:, 0:1], 0.0, writes=[rw])
                    k.dma("sync", rw[:rows, 1:513], P.pA[rsl, 0:512], writes=[rw])
                else:
                    k.dma("sync", rw[:rows, :], P.pA[rsl, lo:lo + 513], writes=[rw])
                d_ = dtmp.get()
                k.tt("gpsimd", d_[:rows], rw[:rows, 0:512], rw[:rows, 1:513], ALU.subtract, reads=[rw], writes=[d_])
                k.stt("vector", dst_ap, d_[:rows], vec[:rows, V_MU + ch:V_MU + ch + 1], rw[:rows, 1:513],
                      ALU.mult, ALU.add, reads=[d_, vec, rw], writes=[dst_tl])
            dsts = [ho.R[0], ho.R[1], PFk[0], PFk[1], ho.V[0], ho.V[1], PF6, PF7, PF8]
            for ch in range(9):
                d = dsts[ch]
                mix(ch, d[:] if ch < 8 else d[0:32, :], d)
                if ch % 2 == 1:
                    yield
            k.act(th[:], PF6[:], AF.Tanh, reads=[PF6], writes=[th])
            k.act(ho.SX7[:], PF7[:], AF.Sigmoid, reads=[PF7], writes=[ho.SX7])
            k.act(ho.SX8[0:32, :], PF8[0:32, :], AF.Sigmoid, reads=[PF8], writes=[ho.SX8])
            yield
            for cp in range(2):
                pcs = slice(cp * 128, (cp + 1) * 128)
                CS, CSX, KN, KF, Bv, RKR = ho.CS[cp], ho.CSX[cp], ho.KN[cp], ho.KF[cp], ho.Bv[cp], ho.RKR[cp]
                pz = k.ps()
                k.mm(pz[:], dec[:, pcs], th[:, :], reads=[dec, th], writes=[pz])
                k.act(SG[:], pz[:], AF.Sigmoid, bias=vec[:, V_W0 + cp:V_W0 + cp + 1], reads=[pz, vec], writes=[SG])
                pa_ = k.ps()
                k.mm(pa_[:], icl[:, pcs], PF6[:, :], reads=[icl, PF6], writes=[pa_])
                k.act(Aic[:], pa_[:], AF.Sigmoid, bias=vec[:, V_A0 + cp:V_A0 + cp + 1], reads=[pa_, vec], writes=[Aic])
                yield
                for c4 in range(4):
                    cc = slice(c4 * 128, (c4 + 1) * 128)
                    k.S.op("vector", (lambda e, o=CS[:, cc], d1=SG[:, cc]: e.tensor_tensor_scan(
                        out=o, data0=ones, data1=d1, initial=0.0, op0=ALU.mult, op1=ALU.add)),
                        reads=[cst, SG], writes=[CS])
                k.tt("gpsimd", CSX[:], CS[:], SG[:], ALU.subtract, reads=[CS, SG], writes=[CSX])
                yield
                kraw = PFk[cp]
                kkr = tmpf.get()
                k.ts("vector", kkr[:], kraw[:], vec[:, V_KK + cp:V_KK + cp + 1], None, ALU.mult, reads=[kraw, vec], writes=[kkr])
                sq = tmpf.get()
                k.tt("gpsimd", sq[:], kkr[:], kkr[:], ALU.mult, reads=[kkr], writes=[sq])
                pn_ = k.ps()
                k.mm(pn_[:], bones, sq[:], reads=[cst, sq], writes=[pn_])
                nrm = tmpf.get()
                k.act(nrm[:], pn_[:], AF.Sqrt, reads=[pn_], writes=[nrm])
                yield
                k.ts("vector", nrm[:], nrm[:], 1e-12, None, ALU.max, reads=[nrm], writes=[nrm])
                k.recip(nrm[:], nrm[:], reads=[nrm], writes=[nrm])
                k.tt("vector", KN[:], kkr[:], nrm[:], ALU.mult, reads=[kkr, nrm], writes=[KN])
                yield
                t1 = tmpf.get()
                k.ts("vector", t1[:], Aic[:], -1.0, vec[:, V_KA + cp:V_KA + cp + 1], ALU.add, ALU.mult, reads=[Aic, vec], writes=[t1])
                k.stt("vector", KF[:], t1[:], 1.0, kraw[:], ALU.add, ALU.mult, reads=[t1, kraw], writes=[KF])
                k.tt("gpsimd", Bv[:], KN[:], Aic[:], ALU.mult, reads=[KN, Aic], writes=[Bv])
                k.stt("vector", RKR[:], ho.R[cp][:], vec[:, V_RK + cp:V_RK + cp + 1], KF[:], ALU.mult, ALU.mult,
                      reads=[ho.R[cp], vec, KF], writes=[RKR])
                yield

        def unit(sl, ho, cp, c4, cA):
            cc = slice(c4 * 128, (c4 + 1) * 128)
            Rt_, Vt_ = ho.R[cp], ho.V[cp]
            CS, CSX, KN, KF, Bv, RKR = ho.CS[cp], ho.CSX[cp], ho.KN[cp], ho.KF[cp], ho.Bv[cp], ho.RKR[cp]
            rfe = Rt_[:, cc]
            vfe = Vt_[:, cc]
            cs_ = CS[:, cc]
            csx = CSX[:, cc]
            cend = CS[:, c4 * 128 + 127:c4 * 128 + 128]
            sm = sl.smp.get()
            k.ts("vector", sm[:, 0:1], cend, C0, None, ALU.mult, reads=[CS], writes=[sm])
            k.ts("vector", sm[:, 1:2], cend, -C0, None, ALU.mult, reads=[CS], writes=[sm])
            pce, nce = sm[:, 0:1], sm[:, 1:2]
            e1, e2, e3, e4, e5 = (sl.ep.get() for _ in range(5))
            k.act(e1[:], csx, AF.Exp, scale=-C0, reads=[CSX], writes=[e1])
            k.act(e2[:], csx, AF.Exp, scale=-C0, bias=pce, reads=[CSX, sm], writes=[e2])
            k.act(e3[:], cs_, AF.Exp, scale=C0, bias=nce, reads=[CS, sm], writes=[e3])
            k.act(e4[:], cs_, AF.Exp, scale=-C0, reads=[CS], writes=[e4])
            k.act(e5[:], cs_, AF.Exp, scale=-C0, bias=pce, reads=[CS, sm], writes=[e5])
            AT, BT, KT, RT = (sl.dp.get() for _ in range(4))
            APz, RPz = sl.APz, sl.RPz
            k.stt("vector", AT[:], KN[:, cc], -1.0, e1[:], ALU.mult, ALU.mult, reads=[KN, e1], writes=[AT])
            for h in range(2):
                hr = slice(h * 64, (h + 1) * 64)
                k.stt("vector", APz[h][hr, :], KN[hr, cc], -1.0, e2[hr, :], ALU.mult, ALU.mult, reads=[KN, e2], writes=[APz[h]])
                k.tt("gpsimd", RPz[h][hr, :], Rt_[hr, cc], e5[hr, :], ALU.mult, reads=[Rt_, e5], writes=[RPz[h]])
            k.tt("gpsimd", BT[:], Bv[:, cc], e3[:], ALU.mult, reads=[Bv, e3], writes=[BT])
            k.tt("gpsimd", KT[:], KF[:, cc], e3[:], ALU.mult, reads=[KF, e3], writes=[KT])
            k.tt("vector", RT[:], rfe, e4[:], ALU.mult, reads=[Rt_, e4], writes=[RT])
            bkb = sl.bkb.get()
            k.copy("gpsimd", bkb[:, 0, :], BT[:], reads=[BT], writes=[bkb])
            k.copy("gpsimd", bkb[:, 1, :], KT[:], reads=[KT], writes=[bkb])
            yield
            pt = k.ps()
            for q, (srcT, stl) in enumerate(((BT[:], BT), (KT[:], KT), (vfe, Vt_))):
                k.tr(pt[:, q * 128:(q + 1) * 128], srcT, identf, reads=[stl, cst], writes=[pt])
            tm = sl.tmp.get()
            k.copy("scalar", tm[:].rearrange("p a b -> p (a b)"), pt[:, 0:384], reads=[pt], writes=[tm])
            yield
            pmA, pmB, pw = k.ps(), k.ps(), k.ps()
            for h in range(2):
                k.mm(pmA[:, h * 256:h * 256 + 128], APz[h][:], bkb[:, 0, :], reads=[APz[h], bkb], writes=[pmA])
                k.mm(pmA[:, h * 256 + 128:h * 256 + 256], bkb[:, 0, :], APz[h][:], reads=[APz[h], bkb], writes=[pmA])
                k.mm(pmB[:, h * 128:(h + 1) * 128], bkb[:, 1, :], APz[h][:], reads=[APz[h], bkb], writes=[pmB])
                k.mm(pw[:, h * 256:h * 256 + 128], BT[:], RPz[h][:], reads=[RPz[h], BT], writes=[pw])
                k.mm(pw[:, h * 256 + 128:h * 256 + 256], KT[:], RPz[h][:], reads=[RPz[h], KT], writes=[pw])
            MN = sl.MNp.get()
            k.tt("vector", f2(MN), pmA[:], msk4[:], ALU.mult, reads=[pmA, msk4], writes=[MN])
            mk = sl.mkp.get()
            k.tt("vector", mk[:].rearrange("p a b -> p (a b)"), pmB[:, 0:256], msu2[:], ALU.mult, reads=[pmB, msu2], writes=[mk])
            wr = sl.wrp.get()
            k.tt("vector", f2(wr), pw[:], miu4[:], ALU.mult, reads=[pw, miu4], writes=[wr])
            yield
            px = sl.px
            pxv = px[:, 0:256].rearrange("p (a h v) -> p a h v", a=2, h=2)
            for h in range(2):
                k.mm(pxv[:, 1, h, :], mk[:, h, :], tm[:, 2, h * 64:(h + 1) * 64], start=(h == 0), stop=False,
                     reads=[mk, tm], writes=[px], skip=True)
            k.mm(px[:, 0:128], AT[:], identf, start=False, stop=False, reads=[AT, cst], writes=[px], skip=True)
            Xb = sl.Xbp.get()
            k.copy("vector", f2(Xb), px[:, 0:256], reads=[px], writes=[Xb])
            yield
            Xf = None
            for i in range(7):
                for h in range(2):
                    k.mm(pxv[:, :, h, :], MN[:, h, 1, :], Xb[:, :, h, :], start=False, stop=(i == 6 and h == 1),
                         reads=[MN, Xb], writes=[px], skip=True)
                if i < 6:
                    Xb = sl.Xbp.get()
                    k.copy("scalar" if i % 2 == 0 else "vector", f2(Xb), px[:, 0:256], reads=[px], writes=[Xb])
                    pn = k.ps()
                    pnv = pn[:].rearrange("p (h m t) -> p h m t", h=2, m=2)
                    for h in range(2):
                        k.mm(pnv[:, h, 1, :], MN[:, h, 0, :], MN[:, h, 1, :], reads=[MN], writes=[pn])
                        k.mm(pnv[:, h, 0, :], MN[:, h, 1, :], MN[:, h, 0, :], reads=[MN], writes=[pn])
                    MNn = sl.MNp.get()
                    k.copy("scalar", f2(MNn), pn[:], reads=[pn], writes=[MNn])
                    MN = MNn
                else:
                    Xf = sl.Xfp.get()
                    k.copy("vector", f2(Xf), px[:, 0:256], reads=[px], writes=[Xf])
                yield
            X = Xf
            Ahat = X[:, 0].rearrange("p h v -> p (h v)")
            U0 = X[:, 1].rearrange("p h v -> p (h v)")
            pg = k.ps()
            k.mm(pg[:, 0:128], Ahat, tm[:, 0, :], reads=[X, tm], writes=[pg])
            gt1 = sl.sq.get()
            k.tt("vector", gt1[:], pg[:, 0:128], bones, ALU.mult, reads=[pg, cst], writes=[gt1])
            GT = sl.sq.get()
            k.stt("vector", GT[:], identf, e4[:, 127:128], gt1[:], ALU.mult, ALU.add, reads=[cst, e4, gt1], writes=[GT])
            ph = k.ps()
            k.mm(ph[:, 0:128], tm[:, 0, :], U0, start=True, stop=False, reads=[X, tm], writes=[ph])
            k.mm(ph[:, 0:128], tm[:, 1, :], tm[:, 2, :], start=False, stop=True, reads=[tm], writes=[ph])
            Hp = sl.Hpp.get()
            k.tt("vector", Hp[:], ph[:, 0:128], bones, ALU.mult, reads=[ph, cst], writes=[Hp])
            QT = sl.sq.get()
            for h in range(2):
                hr = slice(h * 64, (h + 1) * 64)
                pq = k.ps()
                k.mm(pq[:, 0:128], Ahat, wr[:, h, 0, :], reads=[X, wr], writes=[pq])
                k.tt("vector", QT[hr, :], pq[hr, 0:128], RT[hr, :], ALU.add, reads=[pq, RT], writes=[QT])
            yield
            py = k.ps()
            So = ST[cp]
            k.mm(py[:, 0:128], QT[:], So[:], start=True, stop=False, reads=[QT, So], writes=[py])
            for h in range(2):
                hr = slice(h * 64, (h + 1) * 64)
                ys = py[:, h * 64:(h + 1) * 64]
                k.mm(ys, wr[:, h, 0, :], X[:, 1, h, :], start=False, stop=False, reads=[wr, X], writes=[py])
                k.mm(ys, wr[:, h, 1, :], tm[:, 2, hr], start=False, stop=(h == 1), reads=[wr, tm], writes=[py])
            pS = k.ps()
            k.mm(pS[:, 0:128], GT[:], So[:], reads=[GT, So], writes=[pS])
            Sn = STp[cp].get()
            k.tt("vector", Sn[:], pS[:, 0:128], Hp[:], ALU.add, reads=[pS, Hp], writes=[Sn])
            ST[cp] = Sn
            ysb = sl.sq.get()
            k.copy("scalar", ysb[:], py[:, 0:128], reads=[py], writes=[ysb])
            pb_ = k.ps()
            k.mm(pb_[:, 0:2], RKR[:, cc], cst[:, C_BO:C_BO + 128:64], reads=[RKR, cst], writes=[pb_])
            bon = sl.smp.get()
            k.copy("scalar", bon[:], pb_[:, 0:2], reads=[pb_], writes=[bon])
            yield
            yc = sl.sq.get()
            yn = sl.sq.get()
            v3 = lambda t: t[:].rearrange("p (h v) -> p h v", h=2)
            st_ = sl.smp.get()
            k.S.op("vector", (lambda e, o=st_[:, 0:2], i_=v3(ysb): e.tensor_reduce(out=o, in_=i_, axis=AX.X, op=ALU.add)),
                   reads=[ysb], writes=[st_])
            nm = sl.smp.get()
            k.ts("vector", nm[:, 0:2], st_[:, 0:2], -1.0 / 64, None, ALU.mult, reads=[st_], writes=[nm])
            k.tt("vector", v3(yc), v3(ysb), bc(nm[:, 0:2], 64), ALU.add, reads=[ysb, nm], writes=[yc])
            s2 = sl.smp.get()
            jk = sl.ep.get()
            for h in range(2):
                hs = slice(h * 64, (h + 1) * 64)
                k.act(jk[:, hs], yc[:, hs], AF.Square, accum_out=s2[:, h:h + 1], reads=[yc], writes=[jk, s2])
            rs2 = sl.smp.get()
            k.act(rs2[:, 0:2], s2[:, 0:2], AF.Sqrt, bias=GN_EPS, scale=1.0 / 64, reads=[s2], writes=[rs2])
            k.recip(rs2[:, 0:2], rs2[:, 0:2], reads=[rs2], writes=[rs2])
            hg0 = cp * 2
            k.tt("vector", v3(yn), v3(yc), bc(rs2[:, 0:2], 64), ALU.mult, reads=[yc, rs2], writes=[yn])
            k.tt("gpsimd", yn[:], yn[:], lnwb[:, hg0 * 64:(hg0 + 2) * 64], ALU.mult, reads=[yn, lnwb], writes=[yn])
            k.tt("gpsimd", yn[:], yn[:], lnwb[:, 256 + hg0 * 64:256 + (hg0 + 2) * 64], ALU.add, reads=[yn, lnwb], writes=[yn])
            bv = sl.sq.get()
            k.tt("gpsimd", v3(bv), tm[:, 2, :].rearrange("p (h v) -> p h v", h=2), bc(bon[:, 0:2], 64), ALU.mult,
                 reads=[tm, bon], writes=[bv])
            k.tt("vector", yn[:], yn[:], bv[:], ALU.add, reads=[yn, bv], writes=[yn])
            pgt = k.ps()
            pcs = slice(cp * 128, (cp + 1) * 128)
            k.mm(pgt[:, 0:128], ho.SX7[:, cc], gateA[:, pcs], start=True, stop=False, reads=[ho.SX7, gateA], writes=[pgt])
            k.mm(pgt[:, 0:128], ho.SX8[:, cc], gateB[:, pcs], start=False, stop=True, reads=[ho.SX8, gateB], writes=[pgt])
            ya = sl.sq.get()
            k.tt("vector", ya[:], yn[:], pgt[:, 0:128], ALU.mult, reads=[yn, pgt], writes=[ya])
            pT2 = k.ps()
            k.tr(pT2[:, 0:128], ya[:], identf, reads=[ya, cst], writes=[pT2])
            k.copy("scalar", cA[:, cp, cc], pT2[:, 0:128], reads=[pT2], writes=[cA])

        tasks = [(g, c4) for g in range(NG) for c4 in range(4)]
        a1 = {}
        a1_finished = set()

        def finish_a1(g):
            if g not in a1_finished:
                for _ in a1[g]:
                    pass
                a1_finished.add(g)
        a1[0] = a1gen(0, HO[0])
        finish_a1(0)
        if NG > 1:
            a1[1] = a1gen(1, HO[1])
        bg = [1] if NG > 1 else []
        cAs = {}
        active = []
        steps = {}
        remaining = {}
        done = set()
        nxt = 0
        while True:
            while nxt < len(tasks):
                g, c4 = tasks[nxt]
                ok_slots = nxt < 2 or (nxt - 2) in done
                ok_stag = nxt == 0 or (nxt - 1) in done or steps.get(nxt - 1, 0) >= 8
                if not (ok_slots and ok_stag):
                    break
                if c4 == 0:
                    finish_a1(g)
                    if g in bg:
                        bg.remove(g)
                    cAs[g] = catA.get()
                lane = nxt % 2
                for cp in range(2):
                    active.append((nxt, unit(slots[lane * 2 + cp], HO[g % 2], cp, c4, cAs[g])))
                steps[nxt] = 0
                remaining[nxt] = 2
                nxt += 1
            if not active:
                break
            for item in list(active):
                idx, gen = item
                try:
                    next(gen)
                except StopIteration:
                    active.remove(item)
                    remaining[idx] -= 1
                    if remaining[idx] == 0:
                        done.add(idx)
                        g, c4 = tasks[idx]
                        if c4 == 3:
                            k.dma("gpsimd", P.cat[0:256, g * 512:(g + 1) * 512].rearrange("(c p) t -> p c t", p=128),
                                  cAs[g][:], reads=[cAs[g]])
                            if g + 2 < NG:
                                a1[g + 2] = a1gen(g + 2, HO[g % 2])
                                bg.append(g + 2)
            for idx in set(i for i, _ in active):
                steps[idx] += 1
            if bg:
                gb_ = bg[0]
                try:
                    next(a1[gb_])
                except StopIteration:
                    a1_finished.add(gb_)
                    bg.remove(gb_)
        k.rot = None


def phase_tail(P, l, xsrc, last):
    k, cst = P.k, P.cst
    with k.scope() as wsc:
        woutb = wsc.tile([128, 8, 1024], BF16)
        wgb = wsc.tile([128, 8, DFF], BF16)
        wub = wsc.tile([128, 8, DFF], BF16)
        wdb = wsc.tile([128, 22, 1024], BF16)
        load_w2(P, woutb, P.w_out[l], 8, 2)
        wg_c = [Tl(wgb.t) for _ in range(22)]
        wu_c = [Tl(wub.t) for _ in range(22)]
        for c in range(22):
            cs = slice(c * 128, (c + 1) * 128)
            k.dma("gpsimd", wgb[:, :, cs], P.w_gate[l][:, :, cs], writes=[wg_c[c]])
            k.dma("gpsimd", wub[:, :, cs], P.w_up[l][:, :, cs], writes=[wu_c[c]])
        with k.scope() as sc:
            vec = sc.tile([128, NV], F32)
            k.dma("sync", vec[:], P.vecs[l], writes=[vec])
            gb = make_gb(P, sc, vec, V_GFFN)
            xts = sc.pool(2, [128, D], F32)
            x1s = sc.pool(2, [128, D], F32)
            cts = sc.pool(1, [128, 8, 512], BF16)
            catv = P.cat.rearrange("(k p) t -> p k t", p=128)
            ss = sc.tile([128, 1], F32)
            rstd = sc.tile([128, 1], F32)
            xn = sc.tile([128, D], BF16)
            junk = xn
            hTs = sc.pool(2, [128, 8, 512], BF16)
            sg = sc.pool(1, [128, 512], F32)
            ab = sc.pool(3, [128, 512], BF16)

            def norm_gen(g_, hT_):
                ct = cts.get()
                k.dma("sync", ct[:], catv[:, :, g_ * 512:(g_ + 1) * 512], writes=[ct])
                for tt in range(4):
                    rows = slice(g_ * 512 + tt * 128, g_ * 512 + (tt + 1) * 128)
                    xt = xts.get()
                    k.dma("sync", xt[:], xsrc[rows, :], writes=[xt])
                    x1t = x1s.get()
                    for nb in range(2):
                        ps = k.ps()
                        for kk in range(8):
                            k.mm(ps[:], ct[:, kk, tt * 128:(tt + 1) * 128], woutb[:, kk, nb * 512:(nb + 1) * 512],
                                 start=(kk == 0), stop=(kk == 7), reads=[ct, woutb], writes=[ps])
                        k.tt("vector", x1t[:, nb * 512:(nb + 1) * 512], ps[:], xt[:, nb * 512:(nb + 1) * 512], ALU.add,
                             reads=[ps, xt], writes=[x1t])
                    k.dma("gpsimd", P.x1[rows, :], x1t[:], reads=[x1t])
                    rmsnorm_T(P, x1t, junk, ss, rstd, xn, P.pTt, gb, hT_[:, :, tt * 128:(tt + 1) * 128], hT_)
                    yield
            hT_next = hTs.get()
            for _ in norm_gen(0, hT_next):
                pass
            for g in range(NG):
                cols = slice(g * 512, (g + 1) * 512)
                hT = hT_next
                gen = None
                if g + 1 < NG:
                    hT_next = hTs.get()
                    gen = norm_gen(g + 1, hT_next)
                if g == 1:
                    load_w2(P, wdb, P.w_down[l], 22, 2)
                for c in range(22):
                    if gen is not None and c in (4, 9, 14, 19):
                        next(gen, None)
                    cs = slice(c * 128, (c + 1) * 128)
                    pg, pu = k.ps(), k.ps()
                    for kk in range(8):
                        k.mm(pg[:], wgb[:, kk, cs], hT[:, kk, :], start=(kk == 0), stop=(kk == 7), reads=[wg_c[c], hT], writes=[pg])
                    for kk in range(8):
                        k.mm(pu[:], wub[:, kk, cs], hT[:, kk, :], start=(kk == 0), stop=(kk == 7), reads=[wu_c[c], hT], writes=[pu])
                    s_ = sg.get()
                    k.act(s_[:], pg[:], AF.Silu, reads=[pg], writes=[s_])
                    a_ = ab.get()
                    k.tt("vector", a_[:], s_[:], pu[:], ALU.mult, reads=[s_, pu], writes=[a_])
                    k.dma("gpsimd", P.aT[cs, cols], a_[:], reads=[a_])
        with k.scope() as sc:
            gfb = sc.tile([128, D], F32)
            k.dma("sync", gfb[:], P.gfinb[:, :], writes=[gfb])
            ats = sc.pool(2, [128, 22, 256], BF16)
            xts = sc.pool(2, [128, D], F32)
            x2s = sc.pool(2, [128, D], F32)
            junk = sc.tile([128, D], BF16)
            ss = sc.pool(2, [128, 1], F32)
            aTv = P.aT.rearrange("(k p) t -> p k t", p=128)
            for hg in range(2 * NG):
                at = ats.get()
                k.dma("sync", at[:], aTv[:, :, hg * 256:(hg + 1) * 256], writes=[at])
                for tt in range(2):
                    rows = slice(hg * 256 + tt * 128, hg * 256 + (tt + 1) * 128)
                    xt = xts.get()
                    k.dma("sync", xt[:], P.x1[rows, :], writes=[xt])
                    x2 = x2s.get()
                    for nb in range(2):
                        ps = k.ps()
                        for c in range(22):
                            k.mm(ps[:], at[:, c, tt * 128:(tt + 1) * 128], wdb[:, c, nb * 512:(nb + 1) * 512],
                                 start=(c == 0), stop=(c == 21), reads=[at, wdb], writes=[ps])
                        k.tt("vector", x2[:, nb * 512:(nb + 1) * 512], ps[:], xt[:, nb * 512:(nb + 1) * 512], ALU.add,
                             reads=[ps, xt], writes=[x2])
                    if not last:
                        k.dma("gpsimd", P.xr[rows, :], x2[:], reads=[x2])
                    else:
                        s1 = ss.get()
                        k.act(junk[:], x2[:], AF.Square, accum_out=s1[:], reads=[x2], writes=[junk, s1])
                        k.act(s1[:], s1[:], AF.Sqrt, bias=EPS, scale=1.0 / D, reads=[s1], writes=[s1])
                        k.recip(s1[:], s1[:], reads=[s1], writes=[s1])
                        k.stt("vector", x2[:], x2[:], s1[:, 0:1], gfb[:], ALU.mult, ALU.mult, reads=[x2, s1, gfb], writes=[x2])
                        k.dma("gpsimd", P.out[rows, :], x2[:], reads=[x2])


def make_in_maps(inputs):
    f = lambda a: np.ascontiguousarray(np.asarray(a))
    inp = {kk: f(v) for kk, v in inputs.items()}
    shared = {
        "consts": make_consts(),
        "vecs": np.stack([make_vecs(inp, l) for l in range(L)]),
        "lnwb": np.stack([np.broadcast_to(np.concatenate([inp["a_ln_w"][l], inp["a_ln_b"][l]])[None, :], (128, 512)).copy()
                          for l in range(L)]),
        "gfinb": np.broadcast_to(inp["final_norm_g"][None, :], (128, D)).copy(),
        "w_in": np.stack([ktile(inp["w_in"][l]) for l in range(L)]),
        "w_uq": np.stack([ktile(inp["c_w_uq"][l]) for l in range(L)]),
        "w_ukv": np.stack([ktile(np.concatenate(
            [inp["c_w_ukv"][l].reshape(128, 4, 256)[:, :, :128].reshape(128, 512),
             inp["c_w_ukv"][l].reshape(128, 4, 256)[:, :, 128:].reshape(128, 512)], axis=1)) for l in range(L)]),
        "w_out": np.stack([ktile(inp["w_out"][l]) for l in range(L)]),
        "w_gate": np.stack([ktile(inp["ffn_w_gate"][l]) for l in range(L)]),
        "w_up": np.stack([ktile(inp["ffn_w_up"][l]) for l in range(L)]),
        "w_down": np.stack([ktile(inp["ffn_w_down"][l]) for l in range(L)]),
        "a_dec": inp["a_decay_up"], "a_icl": inp["a_iclr_up"], "a_gate": inp["a_gate_up"],
    }
    maps = []
    for b in range(8):
        m = dict(shared)
        m["x"] = inp["x"][b]
        m["pos"] = inp["positions"][b].reshape(1, S_LEN).astype(np.int32)
        maps.append(m)
    return maps


def kernel(**inputs):
    nc = build()
    maps = make_in_maps(inputs)
    res = run_bass_kernel_spmd(nc, maps, core_ids=list(range(8)))
    return np.stack([np.asarray(r["out"]) for r in res.results]).astype(np.float32)
```

```python
import contextlib
import math
import numpy as np
import concourse.bass as bass
import concourse.mybir as mybir
from concourse.bass_utils import run_bass_kernel_spmd

F32 = mybir.dt.float32
BF16 = mybir.dt.bfloat16
I32 = mybir.dt.int32
AF = mybir.ActivationFunctionType
ALU = mybir.AluOpType

S_LEN = 4096
D = 1024
DFF = 2816
L = 2
NG = 8
EPS = 1e-6
THETA = 500000.0
MAGIC = 12582912.0

ENGS = ("tensor", "vector", "scalar", "gpsimd", "sync")
NSLOTS = {"sync": 12, "gpsimd": 48}


class Buf:
    __slots__ = ("w", "r")

    def __init__(self):
        self.w = {}
        self.r = []


class Tl:
    __slots__ = ("t", "b")

    def __init__(self, t):
        self.t = t
        self.b = Buf()

    def __getitem__(self, idx):
        return self.t[idx]


class Sched:
    def __init__(self, nc, sems):
        self.nc = nc
        self.lists = {k: [] for k in ENGS}
        self.cnt = {k: 0 for k in ENGS}
        self.seen = {k: {} for k in ENGS}
        self.semh = sems
        self.dq = {q: {"n": 0, "slots": [f"dma_{q}_{i}" for i in range(NSLOTS[q])]} for q in ("sync", "gpsimd")}

    def _waits(self, eng, reads, writes, merge=False):
        deps = {}

        def need(ev):
            if ev is None:
                return
            k, v = ev
            if k == "tensor" and eng == "tensor":
                return
            if deps.get(k, 0) < v:
                deps[k] = v
        for b in reads:
            for ev in b.w.items():
                need(ev)
        for b in writes:
            if not merge:
                for ev in b.w.items():
                    need(ev)
            for ev in b.r:
                need(ev)
        out = []
        seen = self.seen[eng]
        for k, v in deps.items():
            if seen.get(k, 0) < v:
                seen[k] = v
                out.append((k, v))
        return out

    def _mark(self, ev, reads, writes, merge=False):
        for b in reads:
            b.r.append(ev)
        for b in writes:
            if merge:
                b.w[ev[0]] = max(b.w.get(ev[0], 0), ev[1])
            else:
                b.w = {ev[0]: ev[1]}
                b.r = []

    def op(self, eng, emit, reads=(), writes=()):
        reads = [x.b if isinstance(x, Tl) else x for x in reads]
        writes = [x.b if isinstance(x, Tl) else x for x in writes]
        waits = self._waits(eng, reads, writes)
        self.cnt[eng] += 1
        ev = (eng, self.cnt[eng])
        semh = self.semh
        mysem = semh[eng]

        def run(e):
            for k, v in waits:
                e.wait_ge(semh[k], v)
            emit(e).then_inc(mysem, 1)
        self.lists[eng].append(run)
        self._mark(ev, reads, writes)

    def dma(self, q, out, in_, reads=(), writes=(), merge=False):
        reads = [x.b if isinstance(x, Tl) else x for x in reads]
        writes = [x.b if isinstance(x, Tl) else x for x in writes]
        st = self.dq[q]
        n = st["n"]
        st["n"] += 1
        slots = st["slots"]
        key = slots[n % len(slots)]
        use = n // len(slots)
        waits = self._waits(q, reads, writes, merge)
        seen = self.seen[q]
        if use > 0 and seen.get(key, 0) < 16 * use:
            seen[key] = 16 * use
            waits.append((key, 16 * use))
        ev = (key, 16 * (use + 1))
        semh = self.semh

        def run(e):
            for k, v in waits:
                e.wait_ge(semh[k], v)
            e.dma_start(out=out, in_=in_).then_inc(semh[key], 16)
        self.lists[q].append(run)
        self._mark(ev, reads, writes, merge)

    def barrier(self):
        targets = {k: self.cnt[k] for k in ENGS if self.cnt[k] > 0}
        for q, st in self.dq.items():
            n = st["n"]
            ns = len(st["slots"])
            for i, key in enumerate(st["slots"]):
                uses = (n - i + ns - 1) // ns if n > i else 0
                if uses > 0:
                    targets[key] = 16 * uses
        semh = self.semh
        for eng in ENGS:
            seen = self.seen[eng]
            waits = []
            for k, v in targets.items():
                if k == eng:
                    continue
                if seen.get(k, 0) < v:
                    seen[k] = v
                    waits.append((k, v))

            def run(e, waits=waits):
                for k, v in waits:
                    e.wait_ge(semh[k], v)
            self.lists[eng].append(run)

    def emit(self):
        with self.nc.Block() as block:
            for name in ENGS:
                lst = self.lists[name]

                def body(e, lst=lst):
                    for f in lst:
                        f(e)
                getattr(block, name)(body)


class K:
    def __init__(self, nc, S):
        self.nc = nc
        self.S = S
        self.uid = 0
        self.psum = []
        self.pi = 0
        self.rot = None

    def name(self, p="t"):
        self.uid += 1
        return f"{p}{self.uid}"

    @contextlib.contextmanager
    def scope(self):
        es = contextlib.ExitStack()
        sc = Scope(self, es)
        try:
            yield sc
            self.S.barrier()
        finally:
            es.close()

    def ps(self):
        rot = self.rot if self.rot else list(range(7))
        t = self.psum[rot[self.pi % len(rot)]]
        self.pi += 1
        return t

    def mm(self, out, lhsT, rhs, start=True, stop=True, reads=(), writes=(), skip=False):
        if skip:
            self.S.op("tensor", lambda e: e.matmul(out, lhsT=lhsT, rhs=rhs, start=start, stop=stop, skip_group_check=True), reads, writes)
        else:
            self.S.op("tensor", lambda e: e.matmul(out, lhsT=lhsT, rhs=rhs, start=start, stop=stop), reads, writes)

    def tr(self, out, in_, ident, reads=(), writes=()):
        self.S.op("tensor", lambda e: e.transpose(out=out, in_=in_, identity=ident), reads, writes)

    def act(self, out, in_, func, bias=0.0, scale=1.0, accum_out=None, reads=(), writes=()):
        if accum_out is None:
            self.S.op("scalar", lambda e: e.activation(out=out, in_=in_, func=func, bias=bias, scale=scale), reads, writes)
        else:
            self.S.op("scalar", lambda e: e.activation(out=out, in_=in_, func=func, bias=bias, scale=scale,
                                                       accum_out=accum_out), reads, writes)

    def tt(self, eng, out, in0, in1, op, reads=(), writes=()):
        self.S.op(eng, lambda e: e.tensor_tensor(out=out, in0=in0, in1=in1, op=op), reads, writes)

    def ts(self, eng, out, in0, s1, s2=None, op0=ALU.mult, op1=None, reads=(), writes=()):
        if op1 is None:
            self.S.op(eng, lambda e: e.tensor_scalar(out=out, in0=in0, scalar1=s1, scalar2=None, op0=op0), reads, writes)
        else:
            self.S.op(eng, lambda e: e.tensor_scalar(out=out, in0=in0, scalar1=s1, scalar2=s2, op0=op0, op1=op1), reads, writes)

    def stt(self, eng, out, in0, scalar, in1, op0, op1, reads=(), writes=()):
        self.S.op(eng, lambda e: e.scalar_tensor_tensor(out=out, in0=in0, scalar=scalar, in1=in1, op0=op0, op1=op1), reads, writes)

    def copy(self, eng, out, in_, reads=(), writes=()):
        if eng == "scalar":
            self.S.op(eng, lambda e: e.activation(out=out, in_=in_, func=AF.Copy), reads, writes)
        else:
            self.S.op(eng, lambda e: e.tensor_copy(out=out, in_=in_), reads, writes)

    def recip(self, out, in_, reads=(), writes=()):
        self.S.op("vector", lambda e: e.reciprocal(out=out, in_=in_), reads, writes)

    def memset(self, eng, out, val, writes=()):
        self.S.op(eng, lambda e: e.memset(out, val), (), writes)

    def dma(self, q, out, in_, reads=(), writes=(), merge=False):
        self.S.dma(q, out, in_, reads, writes, merge)


class Scope:
    def __init__(self, k, es):
        self.k = k
        self.es = es

    def tile(self, shape, dtype, name="t"):
        t = self.es.enter_context(self.k.nc.sbuf_tensor(self.k.name(name), list(shape), dtype))
        return Tl(t)

    def pool(self, n, shape, dtype, name="p"):
        return Pool([self.tile(shape, dtype, name) for _ in range(n)])


class Pool:
    def __init__(self, tiles):
        self.tiles = tiles
        self.i = 0

    def get(self):
        t = self.tiles[self.i % len(self.tiles)]
        self.i += 1
        return t


C_ID, C_TRI, C_PB, C_PC, C_FC, C_BO, C_ONE, C_NB = 0, 128, 640, 768, 896, 898, 1026, 1154
NCONST = 1154 + 256
NEG = -30000.0


def make_consts():
    c = np.zeros((128, NCONST), np.float32)
    c[:, C_ID:C_ID + 128] = np.eye(128)
    r = np.arange(128)[:, None]
    q = np.arange(128)[None, :]
    c[:, C_TRI + 0:C_TRI + 128] = (q >= r)
    c[:, C_TRI + 128:C_TRI + 256] = (q <= r)
    c[:, C_TRI + 256:C_TRI + 384] = (q > r)
    c[:, C_TRI + 384:C_TRI + 512] = (q < r)
    pb = np.zeros((128, 128), np.float32)
    for rr in range(128):
        d = rr % 64
        if d < 8:
            pb[rr + 8, rr] = -1.0
        elif d < 16:
            pb[rr - 8, rr] = 1.0
    c[:, C_PB:C_PB + 128] = pb
    pc = np.zeros((128, 128), np.float32)
    for rr in range(64):
        if rr < 32:
            pc[rr + 32, rr] = -1.0
        else:
            pc[rr - 32, rr] = 1.0
    c[:, C_PC:C_PC + 128] = pc
    invb = 1.0 / (THETA ** (np.arange(0, 16, 2, dtype=np.float32) / 16))
    invc = 1.0 / (THETA ** (np.arange(0, 64, 2, dtype=np.float32) / 64))
    for rr in range(128):
        d = rr % 64
        c[rr, C_FC] = invb[d % 8] / (2 * math.pi) if d < 16 else 0.0
        c[rr, C_FC + 1] = invc[rr % 32] / (2 * math.pi)
    bo = np.zeros((128, 128), np.float32)
    bo[:64, :64] = 1
    bo[64:, 64:] = 1
    c[:, C_BO:C_BO + 128] = bo
    c[:, C_ONE:C_ONE + 128] = 1.0
    c[:, C_NB:C_NB + 128] = np.where(q >= r, 0.0, NEG)
    c[:, C_NB + 128:C_NB + 256] = np.where(q <= r, 0.0, NEG)
    return c


V_GATT, V_GFFN, V_GFIN, V_CQG, V_CKVG, V_MU, V_W0, V_A0, V_KK, V_KA, V_RK = 0, 8, 16, 24, 26, 27, 36, 38, 40, 42, 44
NV = 46


def col(v, n):
    return np.ascontiguousarray(v.reshape(n, 128).T)


def make_vecs(inp, l):
    v = np.zeros((128, NV), np.float32)
    v[:, V_GATT:V_GATT + 8] = col(inp["attn_norm_g"][l], 8)
    v[:, V_GFFN:V_GFFN + 8] = col(inp["ffn_norm_g"][l], 8)
    v[:, V_GFIN:V_GFIN + 8] = col(inp["final_norm_g"], 8)
    v[:, V_CQG:V_CQG + 2] = col(inp["c_q_norm_g"][l], 2)
    v[:, V_CKVG:V_CKVG + 1] = col(inp["c_kv_norm_g"][l], 1)
    mu = np.zeros(9 * 128, np.float32)
    mu[:1056] = inp["a_mu"][l]
    v[:, V_MU:V_MU + 9] = col(mu, 9)
    v[:, V_W0:V_W0 + 2] = col(inp["a_w0"][l], 2)
    v[:, V_A0:V_A0 + 2] = col(inp["a_a0"][l], 2)
    v[:, V_KK:V_KK + 2] = col(inp["a_k_k"][l], 2)
    v[:, V_KA:V_KA + 2] = col(inp["a_k_a"][l], 2)
    v[:, V_RK:V_RK + 2] = col(inp["a_r_k"][l].reshape(-1), 2)
    return v


def ktile(w):
    kk = w.shape[0] // 128
    return np.ascontiguousarray(w.reshape(kk, 128, w.shape[1]).transpose(1, 0, 2))


class Prog:
    pass


def build(debug=False, stop=None, nlayers=L, skip=()):
    nc = bass.Bass("TRN2", target_bir_lowering=False)
    P = Prog()
    din = {}

    def inp(name, shape, dt=F32):
        din[name] = nc.dram_tensor(name, list(shape), dt, kind="ExternalInput").ap()
        return din[name]

    skind = "ExternalOutput" if debug else "Internal"
    dscr = {}

    def scr(name, shape, dt):
        dscr[name] = nc.dram_tensor(name, list(shape), dt, kind=skind).ap()
        return dscr[name]

    x_in = inp("x", [S_LEN, D])
    pos = inp("pos", [1, S_LEN], I32)
    consts = inp("consts", [128, NCONST])
    vecs = inp("vecs", [L, 128, NV])
    lnwb = inp("lnwb", [L, 128, 512])
    gfinb = inp("gfinb", [128, D])
    w_in = inp("w_in", [L, 128, 8, 2272])
    w_uq = inp("w_uq", [L, 128, 2, 768])
    w_ukv = inp("w_ukv", [L, 128, 1, 1024])
    w_out = inp("w_out", [L, 128, 8, 1024])
    w_gate = inp("w_gate", [L, 128, 8, DFF])
    w_up = inp("w_up", [L, 128, 8, DFF])
    w_down = inp("w_down", [L, 128, 22, 1024])
    a_dec = inp("a_dec", [L, 64, 256])
    a_icl = inp("a_icl", [L, 64, 256])
    a_gate = inp("a_gate", [L, 160, 256])
    out = nc.dram_tensor("out", [S_LEN, D], F32, kind="ExternalOutput").ap()

    pA = scr("pA", [1056, S_LEN], F32)
    qB = [scr(f"qB{i}", [256, S_LEN], BF16) for i in range(3)]
    kB = [scr(f"kB{i}", [256, S_LEN], BF16) for i in range(3)]
    vB = scr("vB", [S_LEN, 256], BF16)
    qn = scr("qn", [512, S_LEN], BF16)
    qr = scr("qr", [256, S_LEN], BF16)
    kn = scr("kn", [512, S_LEN], BF16)
    kr = scr("kr", [64, S_LEN], BF16)
    vC = scr("vC", [S_LEN, 512], BF16)
    cat = scr("cat", [1024, S_LEN], BF16)
    x1 = scr("x1", [S_LEN, D], F32)
    xr = scr("xr", [S_LEN, D], F32)
    aT = scr("aT", [DFF, S_LEN], BF16)
    tabB = scr("tabB", [2, 128, S_LEN], F32)
    tabC = scr("tabC", [2, 64, S_LEN], F32)

    with contextlib.ExitStack() as es:
        E = es.enter_context
        names = list(ENGS) + [f"dma_{q}_{i}" for q in ("sync", "gpsimd") for i in range(NSLOTS[q])]
        sems = {n: E(nc.semaphore(n)) for n in names}
        S = Sched(nc, sems)
        k = K(nc, S)
        k.psum = [Tl(E(nc.psum_tensor(f"ps{i}", [128, 512], F32))) for i in range(8)]
        pTt = Tl(k.psum[7][:].bitcast(BF16).rearrange("p (a b) -> p a b", a=8))
        pTt.b = k.psum[7].b
        cst = Tl(E(nc.sbuf_tensor("cst", [128, NCONST], F32)))
        cstb = Tl(E(nc.sbuf_tensor("cstb", [128, NCONST], BF16)))
        k.dma("sync", cst[:], consts[:, :], writes=[cst])
        k.copy("vector", cstb[:], cst[:], reads=[cst], writes=[cstb])
        P.__dict__.update(locals())

        phase_tables(P)
        done = (stop == "tables")
        for l in range(nlayers):
            if done:
                break
            xsrc = x_in if l == 0 else xr
            phase_p1(P, l, xsrc)
            if stop == f"p1_{l}":
                break
            if "mla" not in skip:
                phase_mla(P, l)
            if stop == f"mla_{l}":
                break
            if "dil" not in skip:
                phase_dil(P, l)
            if stop == f"dil_{l}":
                break
            phase_rwkv(P, l)
            if stop == f"rwkv_{l}":
                break
            phase_tail(P, l, xsrc, last=(l == nlayers - 1))
        S.barrier()
        S.emit()
    return nc


def phase_tables(P):
    k, cst = P.k, P.cst
    with k.scope() as sc:
        posi = sc.tile([128, S_LEN], I32)
        posf = sc.tile([128, S_LEN], F32)
        y = sc.tile([128, S_LEN], F32)
        t = sc.tile([128, S_LEN], F32)
        r = sc.tile([128, S_LEN], F32)
        o = sc.tile([128, S_LEN], F32)
        src = bass.AP(P.pos.tensor, 0, [[0, 128], [1, S_LEN]])
        k.dma("sync", posi[:], src, writes=[posi])
        k.copy("vector", posf[:], posi[:], reads=[posi], writes=[posf])
        for ti, (fc, rows, dst) in enumerate(((C_FC, 128, P.tabB), (C_FC + 1, 64, P.tabC))):
            for cs in range(2):
                add = 0.25 if cs == 0 else 0.0
                k.ts("vector", y[:rows], posf[:rows], cst[:rows, fc:fc + 1], add, ALU.mult, ALU.add,
                     reads=[posf, cst], writes=[y])
                k.ts("vector", t[:rows], y[:rows], MAGIC, None, ALU.add, reads=[y], writes=[t])
                k.ts("vector", t[:rows], t[:rows], -MAGIC, None, ALU.add, reads=[t], writes=[t])
                k.tt("vector", r[:rows], y[:rows], t[:rows], ALU.subtract, reads=[y, t], writes=[r])
                k.act(o[:rows], r[:rows], AF.Sin, scale=6.28318, reads=[r], writes=[o])
                k.dma("sync", dst[cs, :, :], o[:rows], reads=[o])


def load_w(P, sc, dst, src, kc, n, stage, eng="gpsimd"):
    k = P.k
    for i in range(kc):
        st = stage.get()
        k.dma("sync", st[:, 0:n], src[:, i, :], writes=[st])
        k.copy(eng, dst[:, i, :], st[:, 0:n], reads=[st], writes=[dst])


def load_w2(P, dst, src, kc, nsplit):
    step = (kc + nsplit - 1) // nsplit
    for i in range(0, kc, step):
        j = min(kc, i + step)
        P.k.dma("gpsimd", dst[:, i:j, :], src[:, i:j, :], writes=[dst], merge=True)


def rmsnorm_T(P, xt, junk, ss, rstd, xn, pT, gb, hT_ap, hT):
    k = P.k
    k.act(junk[:], xt[:], AF.Square, accum_out=ss[:], reads=[xt], writes=[junk, ss])
    k.act(rstd[:], ss[:], AF.Sqrt, bias=EPS, scale=1.0 / D, reads=[ss], writes=[rstd])
    k.recip(rstd[:], rstd[:], reads=[rstd], writes=[rstd])
    k.ts("vector", xn[:], xt[:], rstd[:, 0:1], None, ALU.mult, reads=[xt, rstd], writes=[xn])
    for kk in range(8):
        k.tr(pT[:, kk, :], xn[:, kk * 128:(kk + 1) * 128], P.cstb[:, C_ID:C_ID + 128], reads=[xn, P.cstb], writes=[pT])
    k.tt("vector", hT_ap, pT[:], gb[:], ALU.mult, reads=[pT, gb], writes=[hT])


def make_gb(P, sc, vec, c0):
    k = P.k
    gb = sc.tile([128, 8, 128], BF16)
    for kk in range(8):
        k.ts("vector", gb[:, kk, :], P.cst[:, C_ONE:C_ONE + 128], vec[:, c0 + kk:c0 + kk + 1], None, ALU.mult,
             reads=[P.cst, vec], writes=[gb])
    return gb


def rope(P, ps, rows, permT, cos, sin, scale, tmp, outs, srcs, after=None):
    k = P.k
    qs = tmp.get()
    k.act(qs[:rows], ps[:rows], AF.Copy, scale=scale, reads=[ps], writes=[qs])

    def second():
        rope_b(P, qs, rows, permT, cos, sin, tmp, outs, srcs)
        if after is not None:
            after()
    return second


def rope_b(P, qs, rows, permT, cos, sin, tmp, outs, srcs):
    k = P.k
    pp = k.ps()
    k.mm(pp[:rows], permT, qs[:rows], reads=[qs, P.cst], writes=[pp])
    t1 = tmp.get()
    k.tt("vector", t1[:rows], qs[:rows], cos, ALU.mult, reads=[qs] + srcs, writes=[t1])
    t2 = tmp.get()
    k.tt("vector", t2[:rows], pp[:rows], sin, ALU.mult, reads=[pp] + srcs, writes=[t2])
    for eng, out_ap, view, tl in outs:
        k.tt(eng, out_ap, view(t1[:rows]), view(t2[:rows]), ALU.add, reads=[t1, t2], writes=[tl])


def phase_p1(P, l, xsrc):
    k, cst, cstb = P.k, P.cst, P.cstb
    with k.scope() as sc:
        winb = sc.tile([128, 8, 2272], BF16)
        wuqb = sc.tile([128, 2, 768], BF16)
        wukvb = sc.tile([128, 1, 1024], BF16)
        vec = sc.tile([128, NV], F32)
        k.dma("sync", vec[:], P.vecs[l], writes=[vec])
        load_w2(P, winb, P.w_in[l], 8, 4)
        load_w2(P, wuqb, P.w_uq[l], 2, 1)
        load_w2(P, wukvb, P.w_ukv[l], 1, 1)
        gb = make_gb(P, sc, vec, V_GATT)
        xts = sc.pool(3, [128, D], F32)
        junk = sc.tile([128, D], BF16)
        ss = sc.tile([128, 1], F32)
        rstd = sc.tile([128, 1], F32)
        xn = sc.tile([128, D], BF16)
        hTs = sc.pool(2, [128, 8, 512], BF16)
        pTt = P.pTt
        stA = sc.pool(3, [128, 512], F32)
        tmp = sc.pool(6, [128, 512], F32)
        ob = sc.pool(8, [128, 512], BF16)
        o16 = [[sc.tile([128, 16, 128], BF16) for _ in range(4)] for _ in range(2)]
        tabs = sc.pool(2, [128, 4, 512], F32)
        vbt = sc.pool(2, [128, 4, 256], BF16)
        vct = sc.pool(2, [128, 4, 512], BF16)
        csb = sc.tile([128, 3, 512], F32)
        sqb = sc.tile([128, 3, 512], F32)
        rs = sc.pool(2, [128, 512], F32)
        cqn = sc.tile([128, 3, 512], BF16)
        ones = cst[:, C_ONE:C_ONE + 128]
        A_CH = [(i * 128, 128) for i in range(8)] + [(1024, 32)]

        def norm_gen(g_, hT_):
            for tt in range(4):
                xt = xts.get()
                k.dma("sync", xt[:], xsrc[g_ * 512 + tt * 128: g_ * 512 + (tt + 1) * 128, :], writes=[xt])
                rmsnorm_T(P, xt, junk, ss, rstd, xn, pTt, gb, hT_[:, :, tt * 128:(tt + 1) * 128], hT_)
                yield
        hT_next = hTs.get()
        for _ in norm_gen(0, hT_next):
            pass
        for g in range(NG):
            cols = slice(g * 512, (g + 1) * 512)
            hT = hT_next
            gen = None
            if g + 1 < NG:
                hT_next = hTs.get()
                gen = norm_gen(g + 1, hT_next)

            def step(gen=gen):
                if gen is not None:
                    next(gen, None)
            tb = tabs.get()
            k.dma("sync", tb[:, 0:2, :], P.tabB[:, :, cols].rearrange("c p t -> p c t"), writes=[tb])
            k.dma("sync", tb[0:64, 2:4, :], P.tabC[:, :, cols].rearrange("c p t -> p c t"), writes=[tb])

            def proj(c0, m):
                ps = k.ps()
                for kk in range(8):
                    k.mm(ps[:m], winb[:, kk, c0:c0 + m], hT[:, kk, :], start=(kk == 0), stop=(kk == 7),
                         reads=[winb, hT], writes=[ps])
                return ps
            pend = []

            def flush():
                while pend:
                    pend.pop(0)()

            def a_chunk(ci):
                c0, m = A_CH[ci]
                ps = proj(c0, m)
                st = stA.get()
                k.copy("scalar", st[:m], ps[:m], reads=[ps], writes=[st])
                k.dma("gpsimd", P.pA[c0:c0 + m, cols], st[:m], reads=[st])
                if ci in (2, 5, 8):
                    step()
            for j, (c0, m) in enumerate(((1824, 128), (1952, 128), (2080, 128))):
                ps = proj(c0, m)
                k.copy("scalar", csb[:, j, :], ps[:], reads=[ps], writes=[csb])
                k.tt("gpsimd", sqb[:, j, :], csb[:, j, :], csb[:, j, :], ALU.mult, reads=[csb], writes=[sqb])
            for ci in range(3):
                a_chunk(ci)
            for which in range(2):
                js = (0, 1) if which == 0 else (2,)
                nfeat = 256.0 if which == 0 else 128.0
                ps = k.ps()
                for i, j in enumerate(js):
                    k.mm(ps[:], ones, sqb[:, j, :], start=(i == 0), stop=(i == len(js) - 1), reads=[cst, sqb], writes=[ps])
                r_ = rs.get()
                k.act(r_[:], ps[:], AF.Sqrt, bias=EPS, scale=1.0 / nfeat, reads=[ps], writes=[r_])
                k.recip(r_[:], r_[:], reads=[r_], writes=[r_])
                for j in js:
                    gcol = vec[:, V_CQG + j:V_CQG + j + 1]
                    k.stt("vector", cqn[:, j, :], csb[:, j, :], gcol, r_[:], ALU.mult, ALU.mult,
                          reads=[csb, vec, r_], writes=[cqn])
            for ci in range(3, 9):
                a_chunk(ci)
            for qk in range(2):
                for ch in range(2):
                    c0 = 1056 + qk * 256 + ch * 128
                    ps = proj(c0, 128)
                    flush()
                    o1 = ob.get()
                    o4 = ob.get()
                    o16t = o16[qk][ch * 2 + 0]
                    off = (g % 4) * 32
                    outs = [
                        ("vector", o1[:], (lambda v: v), o1),
                        ("gpsimd", o4[:].rearrange("p (r l) -> p r l", r=4), (lambda v: v.rearrange("p (l r) -> p r l", r=4)), o4),
                        ("gpsimd", o16t[:, :, off:off + 32], (lambda v: v.rearrange("p (l r) -> p r l", r=16)), o16t),
                    ]
                    dst = P.qB if qk == 0 else P.kB
                    rows = slice(ch * 128, (ch + 1) * 128)

                    def after(dst=dst, rows=rows, o1=o1, o4=o4, o16t=o16t):
                        k.dma("gpsimd", dst[0][rows, cols], o1[:], reads=[o1])
                        k.dma("gpsimd", dst[1][rows, cols], o4[:], reads=[o4])
                        if g % 4 == 3:
                            sp = g // 4
                            k.dma("gpsimd", dst[2][rows, sp * 2048:(sp + 1) * 2048], o16t[:].rearrange("p r l -> p (r l)"), reads=[o16t])
                    pend.append(rope(P, ps, 128, cst[:, C_PB:C_PB + 128], tb[:, 0, :], tb[:, 1, :], (0.125 if qk == 0 else 1.0),
                                     tmp, outs, [tb], after))
            step()
            vb = vbt.get()
            for tt in range(4):
                ps = k.ps()
                for kk in range(8):
                    k.mm(ps[:, 0:256], hT[:, kk, tt * 128:(tt + 1) * 128], winb[:, kk, 1568:1824], start=(kk == 0), stop=(kk == 7),
                         reads=[winb, hT], writes=[ps])
                if tt == 0:
                    flush()
                k.copy("scalar", vb[:, tt, :], ps[:, 0:256], reads=[ps], writes=[vb])
            k.dma("gpsimd", P.vB[cols, :].rearrange("(t p) c -> p t c", p=128), vb[:], reads=[vb])
            qscale = 192.0 ** -0.5
            for h in range(4):
                ps = k.ps()
                for kc in range(2):
                    k.mm(ps[:], wuqb[:, kc, h * 192:h * 192 + 128], cqn[:, kc, :], start=(kc == 0), stop=(kc == 1),
                         reads=[wuqb, cqn], writes=[ps])
                o = ob.get()
                k.act(o[:], ps[:], AF.Copy, scale=qscale, reads=[ps], writes=[o])
                k.dma("gpsimd", P.qn[h * 128:(h + 1) * 128, cols], o[:], reads=[o])
                ps = k.ps()
                for kc in range(2):
                    k.mm(ps[0:64], wuqb[:, kc, h * 192 + 128:h * 192 + 192], cqn[:, kc, :], start=(kc == 0), stop=(kc == 1),
                         reads=[wuqb, cqn], writes=[ps])
                flush()
                o = ob.get()

                def after(o=o, h=h):
                    k.dma("gpsimd", P.qr[h * 64:(h + 1) * 64, cols], o[0:64], reads=[o])
                pend.append(rope(P, ps, 64, cst[0:64, C_PC:C_PC + 64], tb[0:64, 2, :], tb[0:64, 3, :], qscale, tmp,
                                 [("vector", o[0:64], (lambda v: v), o)], [tb], after))
            for h in range(4):
                ps = k.ps()
                k.mm(ps[:], wukvb[:, 0, h * 128:(h + 1) * 128], cqn[:, 2, :], reads=[wukvb, cqn], writes=[ps])
                if h == 0:
                    flush()
                o = ob.get()
                k.copy("scalar", o[:], ps[:], reads=[ps], writes=[o])
                k.dma("gpsimd", P.kn[h * 128:(h + 1) * 128, cols], o[:], reads=[o])
            ps = proj(2208, 64)
            o = ob.get()

            def after(o=o):
                k.dma("gpsimd", P.kr[:, cols], o[0:64], reads=[o])
            pend.append(rope(P, ps, 64, cst[0:64, C_PC:C_PC + 64], tb[0:64, 2, :], tb[0:64, 3, :], 1.0, tmp,
                             [("vector", o[0:64], (lambda v: v), o)], [tb], after))
            vc = vct.get()
            for tt in range(4):
                ps = k.ps()
                k.mm(ps[:], cqn[:, 2, tt * 128:(tt + 1) * 128], wukvb[:, 0, 512:1024], reads=[wukvb, cqn], writes=[ps])
                if tt == 0:
                    flush()
                k.copy("scalar", vc[:, tt, :], ps[:], reads=[ps], writes=[vc])
            k.dma("gpsimd", P.vC[cols, :].rearrange("(t p) c -> p t c", p=128), vc[:], reads=[vc])
            flush()


def phase_mla(P, l):
    k, cst, cstb = P.k, P.cst, P.cstb
    with k.scope() as sc:
        knp = sc.pool(2, [128, S_LEN], BF16)
        qnp = sc.pool(2, [128, S_LEN], BF16)
        qrp = sc.pool(2, [128, S_LEN], BF16)
        vp = sc.pool(2, [128, 32, 128], BF16)
        krt = sc.tile([128, S_LEN], BF16)
        for t_ in qrp.tiles + [krt]:
            k.memset("gpsimd", t_[64:128, :], 0.0, writes=[t_])
        pacc1p = sc.pool(3, [128, 512], F32)
        pTp = sc.pool(8, [128, 512], BF16)
        ob = sc.pool(2, [128, 512], BF16)
        rd = sc.pool(2, [128, 512], F32)
        paccp = sc.pool(3, [128, 512], F32)
        k.dma("sync", krt[0:64, :], P.kr[:, :], writes=[krt])
        ones = cst[:, C_ONE:C_ONE + 128]
        identb = cstb[:, C_ID:C_ID + 128]
        nbias = cstb[:, C_NB:C_NB + 128]
        onesb = cstb[:, C_ONE:C_ONE + 128]
        pending = []
        for h in range(4):
            knh, qnh, qrh, vh = knp.get(), qnp.get(), qrp.get(), vp.get()
            k.dma("sync", knh[:], P.kn[h * 128:(h + 1) * 128, :], writes=[knh])
            k.dma("sync", qnh[:], P.qn[h * 128:(h + 1) * 128, :], writes=[qnh])
            k.dma("sync", qrh[0:64, :], P.qr[h * 64:(h + 1) * 64, :], writes=[qrh])
            k.dma("sync", vh[:], P.vC[:, h * 128:(h + 1) * 128].rearrange("(t p) c -> p t c", p=128), writes=[vh])
            for g in range(NG):
                k.rot = [0, 1, 2, 3]
                po = k.psum[4 + (g % 2)]
                pden = k.psum[6 + (g % 2)]
                pe_den = [kt for kt in range(4 * g + 4) if kt % 3 == 1] if g > 0 else []
                pacc = paccp.get()
                pacc1 = pacc1p.get()
                k.memset("gpsimd", pacc1[:], 0.0, writes=[pacc1])
                nkt = 4 * g + 4

                def s_stage(kt, g=g, pacc=pacc, pacc1=pacc1, pe_den=pe_den):
                    o = kt - 4 * g
                    c0 = max(o, 0) * 128
                    ks = slice(kt * 128, (kt + 1) * 128)
                    qs = slice(g * 512 + c0, (g + 1) * 512)
                    pS = k.ps()
                    k.mm(pS[:, c0:512], knh[:, ks], qnh[:, qs], start=True, stop=False, reads=[knh, qnh], writes=[pS])
                    k.mm(pS[:, c0:512], krt[:, ks], qrh[:, qs], start=False, stop=(o < 0), reads=[krt, qrh], writes=[pS])
                    if o >= 0:
                        k.mm(pS[:, c0:c0 + 128], identb, nbias, start=False, stop=True, reads=[cstb], writes=[pS])
                    pT = pTp.get()
                    k.act(pT[:, c0:512], pS[:, c0:512], AF.Exp, reads=[pS], writes=[pT])
                    if kt == 0:
                        k.copy("vector", pacc[:], pT[:], reads=[pT], writes=[pacc])
                    elif kt in pe_den:
                        pass
                    elif kt % 3 == 2:
                        k.tt("gpsimd", pacc1[:, c0:512], pacc1[:, c0:512], pT[:, c0:512], ALU.add, reads=[pT, pacc1], writes=[pacc1])
                    else:
                        k.tt("vector", pacc[:, c0:512], pacc[:, c0:512], pT[:, c0:512], ALU.add, reads=[pT, pacc], writes=[pacc])
                    return kt, c0, pT

                def pv_stage(st, po=po, nkt=nkt, pden=pden, pe_den=pe_den):
                    kt, c0, pT = st
                    k.mm(po[:, c0:512], vh[:, kt, :], pT[:, c0:512], start=(kt == 0), stop=(kt == nkt - 1), reads=[vh, pT], writes=[po])
                    if kt in pe_den:
                        k.mm(pden[:, c0:512], onesb, pT[:, c0:512], start=(kt == pe_den[0]), stop=False, reads=[cstb, pT], writes=[pden],
                             skip=True)

                def finalize(g=g, h=h, po=po, pacc=pacc, pacc1=pacc1, pd=pden, pe_den=pe_den):
                    k.mm(pd[:], ones, pacc[:], start=(not pe_den), stop=False, reads=[cst, pacc], writes=[pd], skip=True)
                    k.mm(pd[:], ones, pacc1[:], start=False, stop=True, reads=[cst, pacc1], writes=[pd], skip=True)
                    r = rd.get()
                    k.recip(r[:], pd[:], reads=[pd], writes=[r])
                    oo = ob.get()
                    k.tt("vector", oo[:], po[:], r[:], ALU.mult, reads=[po, r], writes=[oo])
                    k.dma("gpsimd", P.cat[512 + h * 128:512 + (h + 1) * 128, g * 512:(g + 1) * 512], oo[:], reads=[oo])
                DEPTH = 2
                q_ = []
                for kt in range(nkt):
                    q_.append(s_stage(kt))
                    if kt == 2 and pending:
                        pending.pop()()
                    if len(q_) > DEPTH:
                        pv_stage(q_.pop(0))
                while q_:
                    pv_stage(q_.pop(0))
                pending.append(finalize)
        while pending:
            pending.pop()()
        k.rot = None


def phase_dil(P, l):
    k, cst, cstb = P.k, P.cst, P.cstb
    with k.scope() as sc:
        vts = [sc.tile([128, 32, 256], BF16) for _ in range(3)]
        vap = sc.pool(2, [128, 32, 128], BF16)
        qp = [sc.pool(2, [128, S_LEN], BF16) for _ in range(2)]
        kp = sc.pool(2, [128, S_LEN], BF16)
        acc = sc.tile([128, S_LEN], F32)
        pTp = sc.pool(8, [128, 512], BF16)
        ob = sc.pool(2, [64, S_LEN], BF16)
        rdp = sc.pool(2, [64, 512], F32)
        for par in range(2):
            for t_ in qp[par].tiles:
                k.memset("gpsimd", t_[(1 - par) * 64:(2 - par) * 64, :], 0.0, writes=[t_])
        for t_ in vap.tiles:
            k.memset("gpsimd", t_[:, :, 64:128], 1.0, writes=[t_])
        DS = (1, 4, 16)
        for pi, d in enumerate(DS):
            if d == 1:
                src = P.vB[:, :].rearrange("(s l) c -> l s c", l=128)
                k.dma("sync", vts[pi][:], src, writes=[vts[pi]])
            else:
                for r in range(d):
                    src = P.vB[:, :].rearrange("(s l r) c -> r l s c", l=128, r=d)[r]
                    dstv = vts[pi][:].rearrange("p (s r) c -> p r s c", r=d)[:, r]
                    k.dma("sync", dstv, src, writes=[vts[pi]], merge=True)
        identb = cstb[:, C_ID:C_ID + 128]
        nb_cur = cstb[:, C_NB:C_NB + 128]
        nb_prev = cstb[:, C_NB + 128:C_NB + 256]
        shiftT = cst[:, C_ID + 64:C_ID + 128]
        k.rot = [0, 1, 2, 3, 4, 5, 6]
        for h in range(4):
            hc = slice(h * 64, (h + 1) * 64)
            par = h % 2
            pr = slice((h // 2) * 128, (h // 2 + 1) * 128)
            for pi, d in enumerate(DS):
                qh, kh = qp[par].get(), kp.get()
                k.dma("sync", qh[par * 64:(par + 1) * 64, :], P.qB[pi][hc, :], writes=[qh])
                k.dma("sync", kh[:], P.kB[pi][pr, :], writes=[kh])
                vt = vap.get()
                k.copy("gpsimd", vt[:, :, 0:64], vts[pi][:, :, hc], reads=[vts[pi]], writes=[vt])

                def s_stage(bt, d=d, qh=qh, kh=kh):
                    b0 = bt * 4
                    pc, pp = k.ps(), k.ps()
                    hasp = [(b0 + j - d) >= 0 for j in range(4)]
                    for j in range(4):
                        cs = slice((b0 + j) * 128, (b0 + j + 1) * 128)
                        js = slice(j * 128, (j + 1) * 128)
                        k.mm(pc[:, js], kh[:, cs], qh[:, cs], start=True, stop=False, reads=[kh, qh], writes=[pc])
                        k.mm(pc[:, js], identb, nb_cur, start=False, stop=True, reads=[cstb], writes=[pc])
                        if hasp[j]:
                            ps_ = slice((b0 + j - d) * 128, (b0 + j - d + 1) * 128)
                            k.mm(pp[:, js], kh[:, ps_], qh[:, cs], start=True, stop=False, reads=[kh, qh], writes=[pp])
                            k.mm(pp[:, js], identb, nb_prev, start=False, stop=True, reads=[cstb], writes=[pp])
                    tc_ = pTp.get()
                    k.act(tc_[:], pc[:], AF.Exp, reads=[pc], writes=[tc_])
                    tp_ = None
                    if any(hasp):
                        j0 = hasp.index(True)
                        tp_ = pTp.get()
                        k.act(tp_[:, j0 * 128:512], pp[:, j0 * 128:512], AF.Exp, reads=[pp], writes=[tp_])
                    return bt, hasp, tc_, tp_

                def pv_stage(st, d=d, pi=pi, vt=vt):
                    bt, hasp, tc_, tp_ = st
                    b0 = bt * 4
                    pn = k.ps()
                    for j in range(4):
                        js = slice(j * 128, (j + 1) * 128)
                        if hasp[j]:
                            k.mm(pn[:, js], vt[:, b0 + j - d, :], tp_[:, js], start=True, stop=False, reads=[vt, tp_], writes=[pn])
                        k.mm(pn[:, js], vt[:, b0 + j, :], tc_[:, js], start=(not hasp[j]), stop=True, reads=[vt, tc_], writes=[pn])
                    if d == 1:
                        va = acc[:, b0 * 128:b0 * 128 + 512]
                        vpz = pn[:, :]
                    elif d == 4:
                        va = acc[:, bt * 512:(bt + 1) * 512].rearrange("p (l r) -> p r l", r=4)
                        vpz = pn[:, :].rearrange("p (r l) -> p r l", r=4)
                    else:
                        sp, r0 = b0 // 16, b0 % 16
                        va = acc[:, sp * 2048:(sp + 1) * 2048].rearrange("p (l r) -> p r l", r=16)[:, r0:r0 + 4, :]
                        vpz = pn[:, :].rearrange("p (r l) -> p r l", r=4)
                    if pi == 0:
                        k.copy("vector", va, vpz, reads=[pn], writes=[acc])
                    else:
                        k.tt("vector", va, va, vpz, ALU.add, reads=[pn, acc], writes=[acc])
                DEPTH = 2
                q_ = []
                for bt in range(8):
                    q_.append(s_stage(bt))
                    if len(q_) > DEPTH:
                        pv_stage(q_.pop(0))
                while q_:
                    pv_stage(q_.pop(0))
            oo = ob.get()
            for c8 in range(8):
                cs8 = slice(c8 * 512, (c8 + 1) * 512)
                pd = k.ps()
                k.mm(pd[0:64, :], shiftT, acc[:, cs8], reads=[cst, acc], writes=[pd])
                r_ = rdp.get()
                k.recip(r_[:], pd[0:64, :], reads=[pd], writes=[r_])
                k.tt("gpsimd", oo[:, cs8], acc[0:64, cs8], r_[:], ALU.mult, reads=[acc, r_], writes=[oo])
            k.dma("gpsimd", P.cat[256 + h * 64:256 + (h + 1) * 64, :], oo[:], reads=[oo])
        k.rot = None


C0 = 0.6065306597126334
GN_EPS = 64e-5


def bc(ap, n):
    return bass.AP(ap.tensor, ap.offset, [list(x) for x in ap.ap] + [[0, n]])


def phase_rwkv(P, l):
    k, cst, cstb = P.k, P.cst, P.cstb
    AX = mybir.AxisListType
    with k.scope() as sc:
        identf = cst[:, C_ID:C_ID + 128]
        bones = cst[:, C_BO:C_BO + 128]
        ones = cst[:, C_ONE:C_ONE + 128]
        vec = sc.tile([128, NV], F32)
        lnwb = sc.tile([128, 512], F32)
        dec = sc.tile([128, 256], F32)
        icl = sc.tile([128, 256], F32)
        gateA = sc.tile([128, 256], F32)
        gateB = sc.tile([128, 256], F32)
        for t_ in (dec, icl, gateB):
            k.memset("gpsimd", t_[:], 0.0, writes=[t_])
        k.dma("sync", vec[:], P.vecs[l], writes=[vec])
        k.dma("sync", lnwb[:], P.lnwb[l], writes=[lnwb])
        k.dma("sync", dec[0:64, :], P.a_dec[l], writes=[dec])
        k.dma("sync", icl[64:128, :], P.a_icl[l], writes=[icl])
        k.dma("sync", gateA[:], P.a_gate[l, 0:128, :], writes=[gateA])
        k.dma("sync", gateB[0:32, :], P.a_gate[l, 128:160, :], writes=[gateB])
        msk4 = sc.tile([128, 512], F32)
        miu4 = sc.tile([128, 512], F32)
        msu2 = sc.tile([128, 256], F32)
        for j in range(4):
            src = C_TRI + (384 if j % 2 == 0 else 256)
            k.copy("vector", msk4[:, j * 128:(j + 1) * 128], cst[:, src:src + 128], reads=[cst], writes=[msk4])
            k.copy("vector", miu4[:, j * 128:(j + 1) * 128], cst[:, C_TRI:C_TRI + 128], reads=[cst], writes=[miu4])
        for j in range(2):
            k.copy("vector", msu2[:, j * 128:(j + 1) * 128], cst[:, C_TRI + 256:C_TRI + 384], reads=[cst], writes=[msu2])
        rawp = sc.pool(2, [128, 513], F32)
        dtmp = sc.pool(2, [128, 512], F32)
        PFk = [sc.tile([128, 512], F32) for _ in range(2)]
        PF6 = sc.tile([128, 512], F32)
        PF7 = sc.tile([128, 512], F32)
        PF8 = sc.tile([128, 512], F32)
        tmpf = sc.pool(3, [128, 512], F32)
        th = sc.tile([128, 512], F32)
        SG = sc.tile([128, 512], F32)
        Aic = sc.tile([128, 512], F32)

        class HOc:
            pass
        HO = []
        for i in range(2):
            ho = HOc()
            for nm_ in ("R", "V", "CS", "CSX", "KN", "KF", "Bv", "RKR"):
                setattr(ho, nm_, [sc.tile([128, 512], F32) for _ in range(2)])
            ho.SX7 = sc.tile([128, 512], F32)
            ho.SX8 = sc.tile([128, 512], F32)
            k.memset("gpsimd", ho.SX8[:], 0.0, writes=[ho.SX8])
            HO.append(ho)

        class Slot:
            pass
        slots = []
        for s_ in range(4):
            sl = Slot()
            sl.ep = sc.pool(5, [128, 128], F32)
            sl.dp = sc.pool(4, [128, 128], F32)
            sl.APz = [sc.tile([128, 128], BF16) for _ in range(2)]
            sl.bkb = sc.pool(2, [128, 2, 128], BF16)
            sl.RPz = [sc.tile([128, 128], F32) for _ in range(2)]
            for t_ in sl.APz + sl.RPz:
                k.memset("gpsimd", t_[:], 0.0, writes=[t_])
            sl.smp = sc.pool(8, [128, 2], F32)
            sl.tmp = sc.pool(1, [128, 3, 128], F32)
            sl.MNp = sc.pool(2, [128, 2, 2, 128], BF16)
            sl.mkp = sc.pool(1, [128, 2, 128], F32)
            sl.wrp = sc.pool(1, [128, 2, 2, 128], F32)
            sl.Xbp = sc.pool(2, [128, 2, 2, 64], BF16)
            sl.Xfp = sc.pool(1, [128, 2, 2, 64], F32)
            sl.sq = sc.pool(6, [128, 128], F32)
            sl.Hpp = sc.pool(1, [128, 128], F32)
            sl.px = k.psum[4 + s_]
            slots.append(sl)
        k.rot = [0, 1, 2, 3]
        STp = [sc.pool(2, [128, 128], F32) for _ in range(2)]
        catA = sc.pool(3, [128, 2, 512], BF16)
        ST = []
        for cp in range(2):
            t = STp[cp].get()
            k.memset("gpsimd", t[:], 0.0, writes=[t])
            ST.append(t)
        f2 = lambda t: t[:].rearrange("p a b c -> p (a b c)")

        def a1gen(g, ho):
            lo = g * 512 - 1

            def mix(ch, dst_ap, dst_tl):
                rows = 128 if ch < 8 else 32
                rw = rawp.get()
                rsl = slice(ch * 128, ch * 128 + rows)
                if g == 0:
                    k.memset("gpsimd", rw[:, 0:1], 0.0, writes=[rw])
                    k.dma("sync", rw[:rows, 1:513], P.pA[rsl, 0:512], writes=[rw])
                else:
                    k.dma("sync", rw[:rows, :], P.pA[rsl, lo:lo + 513], writes=[rw])
                d_ = dtmp.get()
                k.tt("gpsimd", d_[:rows], rw[:rows, 0:512], rw[:rows, 1:513], ALU.subtract, reads=[rw], writes=[d_])
                k.stt("vector", dst_ap, d_[:rows], vec[:rows, V_MU + ch:V_MU + ch + 1], rw[:rows, 1:513],
                      ALU.mult, ALU.add, reads=[d_, vec, rw], writes=[dst_tl])
            dsts = [ho.R[0], ho.R[1], PFk[0], PFk[1], ho.V[0], ho.V[1], PF6, PF7, PF8]
            for ch in range(9):
                d = dsts[ch]
                mix(ch, d[:] if ch < 8 else d[0:32, :], d)
                if ch % 2 == 1:
                    yield
            k.act(th[:], PF6[:], AF.Tanh, reads=[PF6], writes=[th])
            k.act(ho.SX7[:], PF7[:], AF.Sigmoid, reads=[PF7], writes=[ho.SX7])
            k.act(ho.SX8[0:32, :], PF8[0:32, :], AF.Sigmoid, reads=[PF8], writes=[ho.SX8])
            yield
            for cp in range(2):
                pcs = slice(cp * 128, (cp + 1) * 128)
                CS, CSX, KN, KF, Bv, RKR = ho.CS[cp], ho.CSX[cp], ho.KN[cp], ho.KF[cp], ho.Bv[cp], ho.RKR[cp]
                pz = k.ps()
                k.mm(pz[:], dec[:, pcs], th[:, :], reads=[dec, th], writes=[pz])
                k.act(SG[:], pz[:], AF.Sigmoid, bias=vec[:, V_W0 + cp:V_W0 + cp + 1], reads=[pz, vec], writes=[SG])
                pa_ = k.ps()
                k.mm(pa_[:], icl[:, pcs], PF6[:, :], reads=[icl, PF6], writes=[pa_])
                k.act(Aic[:], pa_[:], AF.Sigmoid, bias=vec[:, V_A0 + cp:V_A0 + cp + 1], reads=[pa_, vec], writes=[Aic])
                yield
                for c4 in range(4):
                    cc = slice(c4 * 128, (c4 + 1) * 128)
                    k.S.op("vector", (lambda e, o=CS[:, cc], d1=SG[:, cc]: e.tensor_tensor_scan(
                        out=o, data0=ones, data1=d1, initial=0.0, op0=ALU.mult, op1=ALU.add)),
                        reads=[cst, SG], writes=[CS])
                k.tt("gpsimd", CSX[:], CS[:], SG[:], ALU.subtract, reads=[CS, SG], writes=[CSX])
                yield
                kraw = PFk[cp]
                kkr = tmpf.get()
                k.ts("vector", kkr[:], kraw[:], vec[:, V_KK + cp:V_KK + cp + 1], None, ALU.mult, reads=[kraw, vec], writes=[kkr])
                sq = tmpf.get()
                k.tt("gpsimd", sq[:], kkr[:], kkr[:], ALU.mult, reads=[kkr], writes=[sq])
                pn_ = k.ps()
                k.mm(pn_[:], bones, sq[:], reads=[cst, sq], writes=[pn_])
                nrm = tmpf.get()
                k.act(nrm[:], pn_[:], AF.Sqrt, reads=[pn_], writes=[nrm])
                yield
                k.ts("vector", nrm[:], nrm[:], 1e-12, None, ALU.max, reads=[nrm], writes=[nrm])
                k.recip(nrm[:], nrm[:], reads=[nrm], writes=[nrm])
                k.tt("vector", KN[:], kkr[:], nrm[:], ALU.mult, reads=[kkr, nrm], writes=[KN])
                yield
                t1 = tmpf.get()
                k.ts("vector", t1[:], Aic[:], -1.0, vec[:, V_KA + cp:V_KA + cp + 1], ALU.add, ALU.mult, reads=[Aic, vec], writes=[t1])
                k.stt("vector", KF[:], t1[:], 1.0, kraw[:], ALU.add, ALU.mult, reads=[t1, kraw], writes=[KF])
                k.tt("gpsimd", Bv[:], KN[:], Aic[:], ALU.mult, reads=[KN, Aic], writes=[Bv])
                k.stt("vector", RKR[:], ho.R[cp][:], vec[:, V_RK + cp:V_RK + cp + 1], KF[:], ALU.mult, ALU.mult,
                      reads=[ho.R[cp], vec, KF], writes=[RKR])
                yield

        def unit(sl, ho, cp, c4, cA):
            cc = slice(c4 * 128, (c4 + 1) * 128)
            Rt_, Vt_ = ho.R[cp], ho.V[cp]
            CS, CSX, KN, KF, Bv, RKR = ho.CS[cp], ho.CSX[cp], ho.KN[cp], ho.KF[cp], ho.Bv[cp], ho.RKR[cp]
            rfe = Rt_[:, cc]
            vfe = Vt_[:, cc]
            cs_ = CS[:, cc]
            csx = CSX[:, cc]
            cend = CS[:, c4 * 128 + 127:c4 * 128 + 128]
            sm = sl.smp.get()
            k.ts("vector", sm[:, 0:1], cend, C0, None, ALU.mult, reads=[CS], writes=[sm])
            k.ts("vector", sm[:, 1:2], cend, -C0, None, ALU.mult, reads=[CS], writes=[sm])
            pce, nce = sm[:, 0:1], sm[:, 1:2]
            e1, e2, e3, e4, e5 = (sl.ep.get() for _ in range(5))
            k.act(e1[:], csx, AF.Exp, scale=-C0, reads=[CSX], writes=[e1])
            k.act(e2[:], csx, AF.Exp, scale=-C0, bias=pce, reads=[CSX, sm], writes=[e2])
            k.act(e3[:], cs_, AF.Exp, scale=C0, bias=nce, reads=[CS, sm], writes=[e3])
            k.act(e4[:], cs_, AF.Exp, scale=-C0, reads=[CS], writes=[e4])
            k.act(e5[:], cs_, AF.Exp, scale=-C0, bias=pce, reads=[CS, sm], writes=[e5])
            AT, BT, KT, RT = (sl.dp.get() for _ in range(4))
            APz, RPz = sl.APz, sl.RPz
            k.stt("vector", AT[:], KN[:, cc], -1.0, e1[:], ALU.mult, ALU.mult, reads=[KN, e1], writes=[AT])
            for h in range(2):
                hr = slice(h * 64, (h + 1) * 64)
                k.stt("vector", APz[h][hr, :], KN[hr, cc], -1.0, e2[hr, :], ALU.mult, ALU.mult, reads=[KN, e2], writes=[APz[h]])
                k.tt("gpsimd", RPz[h][hr, :], Rt_[hr, cc], e5[hr, :], ALU.mult, reads=[Rt_, e5], writes=[RPz[h]])
            k.tt("gpsimd", BT[:], Bv[:, cc], e3[:], ALU.mult, reads=[Bv, e3], writes=[BT])
            k.tt("gpsimd", KT[:], KF[:, cc], e3[:], ALU.mult, reads=[KF, e3], writes=[KT])
            k.tt("vector", RT[:], rfe, e4[:], ALU.mult, reads=[Rt_, e4], writes=[RT])
            bkb = sl.bkb.get()
            k.copy("gpsimd", bkb[:, 0, :], BT[:], reads=[BT], writes=[bkb])
            k.copy("gpsimd", bkb[:, 1, :], KT[:], reads=[KT], writes=[bkb])
            yield
            pt = k.ps()
            for q, (srcT, stl) in enumerate(((BT[:], BT), (KT[:], KT), (vfe, Vt_))):
                k.tr(pt[:, q * 128:(q + 1) * 128], srcT, identf, reads=[stl, cst], writes=[pt])
            tm = sl.tmp.get()
            k.copy("scalar", tm[:].rearrange("p a b -> p (a b)"), pt[:, 0:384], reads=[pt], writes=[tm])
            yield
            pmA, pmB, pw = k.ps(), k.ps(), k.ps()
            for h in range(2):
                k.mm(pmA[:, h * 256:h * 256 + 128], APz[h][:], bkb[:, 0, :], reads=[APz[h], bkb], writes=[pmA])
                k.mm(pmA[:, h * 256 + 128:h * 256 + 256], bkb[:, 0, :], APz[h][:], reads=[APz[h], bkb], writes=[pmA])
                k.mm(pmB[:, h * 128:(h + 1) * 128], bkb[:, 1, :], APz[h][:], reads=[APz[h], bkb], writes=[pmB])
                k.mm(pw[:, h * 256:h * 256 + 128], BT[:], RPz[h][:], reads=[RPz[h], BT], writes=[pw])
                k.mm(pw[:, h * 256 + 128:h * 256 + 256], KT[:], RPz[h][:], reads=[RPz[h], KT], writes=[pw])
            MN = sl.MNp.get()
            k.tt("vector", f2(MN), pmA[:], msk4[:], ALU.mult, reads=[pmA, msk4], writes=[MN])
            mk = sl.mkp.get()
            k.tt("vector", mk[:].rearrange("p a b -> p (a b)"), pmB[:, 0:256], msu2[:], ALU.mult, reads=[pmB, msu2], writes=[mk])
            wr = sl.wrp.get()
            k.tt("vector", f2(wr), pw[:], miu4[:], ALU.mult, reads=[pw, miu4], writes=[wr])
            yield
            px = sl.px
            pxv = px[:, 0:256].rearrange("p (a h v) -> p a h v", a=2, h=2)
            for h in range(2):
                k.mm(pxv[:, 1, h, :], mk[:, h, :], tm[:, 2, h * 64:(h + 1) * 64], start=(h == 0), stop=False,
                     reads=[mk, tm], writes=[px], skip=True)
            k.mm(px[:, 0:128], AT[:], identf, start=False, stop=False, reads=[AT, cst], writes=[px], skip=True)
            Xb = sl.Xbp.get()
            k.copy("vector", f2(Xb), px[:, 0:256], reads=[px], writes=[Xb])
            yield
            Xf = None
            for i in range(7):
                for h in range(2):
                    k.mm(pxv[:, :, h, :], MN[:, h, 1, :], Xb[:, :, h, :], start=False, stop=(i == 6 and h == 1),
                         reads=[MN, Xb], writes=[px], skip=True)
                if i < 6:
                    Xb = sl.Xbp.get()
                    k.copy("scalar" if i % 2 == 0 else "vector", f2(Xb), px[:, 0:256], reads=[px], writes=[Xb])
                    pn = k.ps()
                    pnv = pn[:].rearrange("p (h m t) -> p h m t", h=2, m=2)
                    for h in range(2):
                        k.mm(pnv[:, h, 1, :], MN[:, h, 0, :], MN[:, h, 1, :], reads=[MN], writes=[pn])
                        k.mm(pnv[:, h, 0, :], MN[:, h, 1, :], MN[:, h, 0, :], reads=[MN], writes=[pn])
                    MNn = sl.MNp.get()
                    k.copy("scalar", f2(MNn), pn[:], reads=[pn], writes=[MNn])
                    MN = MNn
                else:
                    Xf = sl.Xfp.get()
                    k.copy("vector", f2(Xf), px[:, 0:256], reads=[px], writes=[Xf])
                yield
            X = Xf
            Ahat = X[:, 0].rearrange("p h v -> p (h v)")
            U0 = X[:, 1].rearrange("p h v -> p (h v)")
            pg = k.ps()
            k.mm(pg[:, 0:128], Ahat, tm[:, 0, :], reads=[X, tm], writes=[pg])
            gt1 = sl.sq.get()
            k.tt("vector", gt1[:], pg[:, 0:128], bones, ALU.mult, reads=[pg, cst], writes=[gt1])
            GT = sl.sq.get()
            k.stt("vector", GT[:], identf, e4[:, 127:128], gt1[:], ALU.mult, ALU.add, reads=[cst, e4, gt1], writes=[GT])
            ph = k.ps()
            k.mm(ph[:, 0:128], tm[:, 0, :], U0, start=True, stop=False, reads=[X, tm], writes=[ph])
            k.mm(ph[:, 0:128], tm[:, 1, :], tm[:, 2, :], start=False, stop=True, reads=[tm], writes=[ph])
            Hp = sl.Hpp.get()
            k.tt("vector", Hp[:], ph[:, 0:128], bones, ALU.mult, reads=[ph, cst], writes=[Hp])
            QT = sl.sq.get()
            for h in range(2):
                hr = slice(h * 64, (h + 1) * 64)
                pq = k.ps()
                k.mm(pq[:, 0:128], Ahat, wr[:, h, 0, :], reads=[X, wr], writes=[pq])
                k.tt("vector", QT[hr, :], pq[hr, 0:128], RT[hr, :], ALU.add, reads=[pq, RT], writes=[QT])
            yield
            py = k.ps()
            So = ST[cp]
            k.mm(py[:, 0:128], QT[:], So[:], start=True, stop=False, reads=[QT, So], writes=[py])
            for h in range(2):
                hr = slice(h * 64, (h + 1) * 64)
                ys = py[:, h * 64:(h + 1) * 64]
                k.mm(ys, wr[:, h, 0, :], X[:, 1, h, :], start=False, stop=False, reads=[wr, X], writes=[py])
                k.mm(ys, wr[:, h, 1, :], tm[:, 2, hr], start=False, stop=(h == 1), reads=[wr, tm], writes=[py])
            pS = k.ps()
            k.mm(pS[:, 0:128], GT[:], So[:], reads=[GT, So], writes=[pS])
            Sn = STp[cp].get()
            k.tt("vector", Sn[:], pS[:, 0:128], Hp[:], ALU.add, reads=[pS, Hp], writes=[Sn])
            ST[cp] = Sn
            ysb = sl.sq.get()
            k.copy("scalar", ysb[:], py[:, 0:128], reads=[py], writes=[ysb])
            pb_ = k.ps()
            k.mm(pb_[:, 0:2], RKR[:, cc], cst[:, C_BO:C_BO + 128:64], reads=[RKR, cst], writes=[pb_])
            bon = sl.smp.get()
            k.copy("scalar", bon[:], pb_[:, 0:2], reads=[pb_], writes=[bon])
            yield
            yc = sl.sq.get()
            yn = sl.sq.get()
            v3 = lambda t: t[:].rearrange("p (h v) -> p h v", h=2)
            st_ = sl.smp.get()
            k.S.op("vector", (lambda e, o=st_[:, 0:2], i_=v3(ysb): e.tensor_reduce(out=o, in_=i_, axis=AX.X, op=ALU.add)),
                   reads=[ysb], writes=[st_])
            nm = sl.smp.get()
            k.ts("vector", nm[:, 0:2], st_[:, 0:2], -1.0 / 64, None, ALU.mult, reads=[st_], writes=[nm])
            k.tt("vector", v3(yc), v3(ysb), bc(nm[:, 0:2], 64), ALU.add, reads=[ysb, nm], writes=[yc])
            s2 = sl.smp.get()
            jk = sl.ep.get()
            for h in range(2):
                hs = slice(h * 64, (h + 1) * 64)
                k.act(jk[:, hs], yc[:, hs], AF.Square, accum_out=s2[:, h:h + 1], reads=[yc], writes=[jk, s2])
            rs2 = sl.smp.get()
            k.act(rs2[:, 0:2], s2[:, 0:2], AF.Sqrt, bias=GN_EPS, scale=1.0 / 64, reads=[s2], writes=[rs2])
            k.recip(rs2[:, 0:2], rs2[:, 0:2], reads=[rs2], writes=[rs2])
            hg0 = cp * 2
            k.tt("vector", v3(yn), v3(yc), bc(rs2[:, 0:2], 64), ALU.mult, reads=[yc, rs2], writes=[yn])
            k.tt("gpsimd", yn[:], yn[:], lnwb[:, hg0 * 64:(hg0 + 2) * 64], ALU.mult, reads=[yn, lnwb], writes=[yn])
            k.tt("gpsimd", yn[:], yn[:], lnwb[:, 256 + hg0 * 64:256 + (hg0 + 2) * 64], ALU.add, reads=[yn, lnwb], writes=[yn])
            bv = sl.sq.get()
            k.tt("gpsimd", v3(bv), tm[:, 2, :].rearrange("p (h v) -> p h v", h=2), bc(bon[:, 0:2], 64), ALU.mult,
                 reads=[tm, bon], writes=[bv])
            k.tt("vector", yn[:], yn[:], bv[:], ALU.add, reads=[yn, bv], writes=[yn])
            pgt = k.ps()
            pcs = slice(cp * 128, (cp + 1) * 128)
            k.mm(pgt[:, 0:128], ho.SX7[:, cc], gateA[:, pcs], start=True, stop=False, reads=[ho.SX7, gateA], writes=[pgt])
            k.mm(pgt[:, 0:128], ho.SX8[:, cc], gateB[:, pcs], start=False, stop=True, reads=[ho.SX8, gateB], writes=[pgt])
            ya = sl.sq.get()
            k.tt("vector", ya[:], yn[:], pgt[:, 0:128], ALU.mult, reads=[yn, pgt], writes=[ya])
            pT2 = k.ps()
            k.tr(pT2[:, 0:128], ya[:], identf, reads=[ya, cst], writes=[pT2])
            k.copy("scalar", cA[:, cp, cc], pT2[:, 0:128], reads=[pT2], writes=[cA])

        tasks = [(g, c4) for g in range(NG) for c4 in range(4)]
        a1 = {}
        a1_finished = set()

        def finish_a1(g):
            if g not in a1_finished:
                for _ in a1[g]:
                    pass
                a1_finished.add(g)
        a1[0] = a1gen(0, HO[0])
        finish_a1(0)
        if NG > 1:
            a1[1] = a1gen(1, HO[1])
        bg = [1] if NG > 1 else []
        cAs = {}
        active = []
        steps = {}
        remaining = {}
        done = set()
        nxt = 0
        while True:
            while nxt < len(tasks):
                g, c4 = tasks[nxt]
                ok_slots = nxt < 2 or (nxt - 2) in done
                ok_stag = nxt == 0 or (nxt - 1) in done or steps.get(nxt - 1, 0) >= 6
                if not (ok_slots and ok_stag):
                    break
                if c4 == 0:
                    finish_a1(g)
                    if g in bg:
                        bg.remove(g)
                    cAs[g] = catA.get()
                lane = nxt % 2
                for cp in range(2):
                    active.append((nxt, unit(slots[lane * 2 + cp], HO[g % 2], cp, c4, cAs[g])))
                steps[nxt] = 0
                remaining[nxt] = 2
                nxt += 1
            if not active:
                break
            for item in list(active):
                idx, gen = item
                try:
                    next(gen)
                except StopIteration:
                    active.remove(item)
                    remaining[idx] -= 1
                    if remaining[idx] == 0:
                        done.add(idx)
                        g, c4 = tasks[idx]
                        if c4 == 3:
                            k.dma("gpsimd", P.cat[0:256, g * 512:(g + 1) * 512].rearrange("(c p) t -> p c t", p=128),
                                  cAs[g][:], reads=[cAs[g]])
                            if g + 2 < NG:
                                a1[g + 2] = a1gen(g + 2, HO[g % 2])
                                bg.append(g + 2)
            for idx in set(i for i, _ in active):
                steps[idx] += 1
            if bg:
                gb_ = bg[0]
                try:
                    next(a1[gb_])
                except StopIteration:
                    a1_finished.add(gb_)
                    bg.remove(gb_)
        k.rot = None


def phase_tail(P, l, xsrc, last):
    k, cst = P.k, P.cst
    with k.scope() as wsc:
        woutb = wsc.tile([128, 8, 1024], BF16)
        wgb = wsc.tile([128, 8, DFF], BF16)
        wub = wsc.tile([128, 8, DFF], BF16)
        wdb = wsc.tile([128, 22, 1024], BF16)
        load_w2(P, woutb, P.w_out[l], 8, 2)
        wg_c = [Tl(wgb.t) for _ in range(22)]
        wu_c = [Tl(wub.t) for _ in range(22)]
        for c in range(22):
            cs = slice(c * 128, (c + 1) * 128)
            k.dma("gpsimd", wgb[:, :, cs], P.w_gate[l][:, :, cs], writes=[wg_c[c]])
            k.dma("gpsimd", wub[:, :, cs], P.w_up[l][:, :, cs], writes=[wu_c[c]])
        with k.scope() as sc:
            vec = sc.tile([128, NV], F32)
            k.dma("sync", vec[:], P.vecs[l], writes=[vec])
            gb = make_gb(P, sc, vec, V_GFFN)
            xts = sc.pool(2, [128, D], F32)
            x1s = sc.pool(2, [128, D], F32)
            cts = sc.pool(1, [128, 8, 512], BF16)
            catv = P.cat.rearrange("(k p) t -> p k t", p=128)
            ss = sc.tile([128, 1], F32)
            rstd = sc.tile([128, 1], F32)
            xn = sc.tile([128, D], BF16)
            junk = xn
            hTs = sc.pool(2, [128, 8, 512], BF16)
            sg = sc.pool(1, [128, 512], F32)
            ab = sc.pool(3, [128, 512], BF16)

            def norm_gen(g_, hT_):
                ct = cts.get()
                k.dma("sync", ct[:], catv[:, :, g_ * 512:(g_ + 1) * 512], writes=[ct])
                for tt in range(4):
                    rows = slice(g_ * 512 + tt * 128, g_ * 512 + (tt + 1) * 128)
                    xt = xts.get()
                    k.dma("sync", xt[:], xsrc[rows, :], writes=[xt])
                    x1t = x1s.get()
                    for nb in range(2):
                        ps = k.ps()
                        for kk in range(8):
                            k.mm(ps[:], ct[:, kk, tt * 128:(tt + 1) * 128], woutb[:, kk, nb * 512:(nb + 1) * 512],
                                 start=(kk == 0), stop=(kk == 7), reads=[ct, woutb], writes=[ps])
                        k.tt("vector", x1t[:, nb * 512:(nb + 1) * 512], ps[:], xt[:, nb * 512:(nb + 1) * 512], ALU.add,
                             reads=[ps, xt], writes=[x1t])
                    k.dma("sync", P.x1[rows, :], x1t[:], reads=[x1t])
                    rmsnorm_T(P, x1t, junk, ss, rstd, xn, P.pTt, gb, hT_[:, :, tt * 128:(tt + 1) * 128], hT_)
                    yield
            hT_next = hTs.get()
            for _ in norm_gen(0, hT_next):
                pass
            for g in range(NG):
                cols = slice(g * 512, (g + 1) * 512)
                hT = hT_next
                gen = None
                if g + 1 < NG:
                    hT_next = hTs.get()
                    gen = norm_gen(g + 1, hT_next)
                if g == 1:
                    load_w2(P, wdb, P.w_down[l], 22, 2)
                for c in range(22):
                    if gen is not None and c in (4, 9, 14, 19):
                        next(gen, None)
                    cs = slice(c * 128, (c + 1) * 128)
                    pg, pu = k.ps(), k.ps()
                    for kk in range(8):
                        k.mm(pg[:], wgb[:, kk, cs], hT[:, kk, :], start=(kk == 0), stop=(kk == 7), reads=[wg_c[c], hT], writes=[pg])
                    for kk in range(8):
                        k.mm(pu[:], wub[:, kk, cs], hT[:, kk, :], start=(kk == 0), stop=(kk == 7), reads=[wu_c[c], hT], writes=[pu])
                    s_ = sg.get()
                    k.act(s_[:], pg[:], AF.Silu, reads=[pg], writes=[s_])
                    a_ = ab.get()
                    k.tt("vector", a_[:], s_[:], pu[:], ALU.mult, reads=[s_, pu], writes=[a_])
                    k.dma("gpsimd", P.aT[cs, cols], a_[:], reads=[a_])
        with k.scope() as sc:
            gfb = sc.tile([128, D], F32)
            k.dma("sync", gfb[:], P.gfinb[:, :], writes=[gfb])
            ats = sc.pool(2, [128, 22, 256], BF16)
            xts = sc.pool(2, [128, D], F32)
            x2s = sc.pool(2, [128, D], F32)
            junk = sc.tile([128, D], BF16)
            ss = sc.pool(2, [128, 1], F32)
            aTv = P.aT.rearrange("(k p) t -> p k t", p=128)
            for hg in range(2 * NG):
                at = ats.get()
                k.dma("sync", at[:], aTv[:, :, hg * 256:(hg + 1) * 256], writes=[at])
                for tt in range(2):
                    rows = slice(hg * 256 + tt * 128, hg * 256 + (tt + 1) * 128)
                    xt = xts.get()
                    k.dma("sync", xt[:], P.x1[rows, :], writes=[xt])
                    x2 = x2s.get()
                    for nb in range(2):
                        ps = k.ps()
                        for c in range(22):
                            k.mm(ps[:], at[:, c, tt * 128:(tt + 1) * 128], wdb[:, c, nb * 512:(nb + 1) * 512],
                                 start=(c == 0), stop=(c == 21), reads=[at, wdb], writes=[ps])
                        k.tt("vector", x2[:, nb * 512:(nb + 1) * 512], ps[:], xt[:, nb * 512:(nb + 1) * 512], ALU.add,
                             reads=[ps, xt], writes=[x2])
                    if not last:
                        k.dma("gpsimd", P.xr[rows, :], x2[:], reads=[x2])
                    else:
                        s1 = ss.get()
                        k.act(junk[:], x2[:], AF.Square, accum_out=s1[:], reads=[x2], writes=[junk, s1])
                        k.act(s1[:], s1[:], AF.Sqrt, bias=EPS, scale=1.0 / D, reads=[s1], writes=[s1])
                        k.recip(s1[:], s1[:], reads=[s1], writes=[s1])
                        k.stt("vector", x2[:], x2[:], s1[:, 0:1], gfb[:], ALU.mult, ALU.mult, reads=[x2, s1, gfb], writes=[x2])
                        k.dma("gpsimd", P.out[rows, :], x2[:], reads=[x2])


def make_in_maps(inputs):
    f = lambda a: np.ascontiguousarray(np.asarray(a))
    inp = {kk: f(v) for kk, v in inputs.items()}
    shared = {
        "consts": make_consts(),
        "vecs": np.stack([make_vecs(inp, l) for l in range(L)]),
        "lnwb": np.stack([np.broadcast_to(np.concatenate([inp["a_ln_w"][l], inp["a_ln_b"][l]])[None, :], (128, 512)).copy()
                          for l in range(L)]),
        "gfinb": np.broadcast_to(inp["final_norm_g"][None, :], (128, D)).copy(),
        "w_in": np.stack([ktile(inp["w_in"][l]) for l in range(L)]),
        "w_uq": np.stack([ktile(inp["c_w_uq"][l]) for l in range(L)]),
        "w_ukv": np.stack([ktile(np.concatenate(
            [inp["c_w_ukv"][l].reshape(128, 4, 256)[:, :, :128].reshape(128, 512),
             inp["c_w_ukv"][l].reshape(128, 4, 256)[:, :, 128:].reshape(128, 512)], axis=1)) for l in range(L)]),
        "w_out": np.stack([ktile(inp["w_out"][l]) for l in range(L)]),
        "w_gate": np.stack([ktile(inp["ffn_w_gate"][l]) for l in range(L)]),
        "w_up": np.stack([ktile(inp["ffn_w_up"][l]) for l in range(L)]),
        "w_down": np.stack([ktile(inp["ffn_w_down"][l]) for l in range(L)]),
        "a_dec": inp["a_decay_up"], "a_icl": inp["a_iclr_up"], "a_gate": inp["a_gate_up"],
    }
    maps = []
    for b in range(8):
        m = dict(shared)
        m["x"] = inp["x"][b]
        m["pos"] = inp["positions"][b].reshape(1, S_LEN).astype(np.int32)
        maps.append(m)
    return maps


def kernel(**inputs):
    nc = build()
    maps = make_in_maps(inputs)
    res = run_bass_kernel_spmd(nc, maps, core_ids=list(range(8)))
    return np.stack([np.asarray(r["out"]) for r in res.results]).astype(np.float32)
```

```python
import contextlib
import math
import numpy as np
import concourse.bass as bass
import concourse.mybir as mybir
from concourse.bass_utils import run_bass_kernel_spmd

F32 = mybir.dt.float32
BF16 = mybir.dt.bfloat16
I32 = mybir.dt.int32
AF = mybir.ActivationFunctionType
ALU = mybir.AluOpType

S_LEN = 4096
D = 1024
DFF = 2816
L = 2
NG = 8
EPS = 1e-6
THETA = 500000.0
MAGIC = 12582912.0

ENGS = ("tensor", "vector", "scalar", "gpsimd", "sync")
NSLOTS = {"sync": 12, "gpsimd": 48}


class Buf:
    __slots__ = ("w", "r")

    def __init__(self):
        self.w = {}
        self.r = []


class Tl:
    __slots__ = ("t", "b")

    def __init__(self, t):
        self.t = t
        self.b = Buf()

    def __getitem__(self, idx):
        return self.t[idx]


class Sched:
    def __init__(self, nc, sems):
        self.nc = nc
        self.lists = {k: [] for k in ENGS}
        self.cnt = {k: 0 for k in ENGS}
        self.seen = {k: {} for k in ENGS}
        self.semh = sems
        self.dq = {q: {"n": 0, "slots": [f"dma_{q}_{i}" for i in range(NSLOTS[q])]} for q in ("sync", "gpsimd")}

    def _waits(self, eng, reads, writes, merge=False):
        deps = {}

        def need(ev):
            if ev is None:
                return
            k, v = ev
            if k == "tensor" and eng == "tensor":
                return
            if deps.get(k, 0) < v:
                deps[k] = v
        for b in reads:
            for ev in b.w.items():
                need(ev)
        for b in writes:
            if not merge:
                for ev in b.w.items():
                    need(ev)
            for ev in b.r:
                need(ev)
        out = []
        seen = self.seen[eng]
        for k, v in deps.items():
            if seen.get(k, 0) < v:
                seen[k] = v
                out.append((k, v))
        return out

    def _mark(self, ev, reads, writes, merge=False):
        for b in reads:
            b.r.append(ev)
        for b in writes:
            if merge:
                b.w[ev[0]] = max(b.w.get(ev[0], 0), ev[1])
            else:
                b.w = {ev[0]: ev[1]}
                b.r = []

    def op(self, eng, emit, reads=(), writes=()):
        reads = [x.b if isinstance(x, Tl) else x for x in reads]
        writes = [x.b if isinstance(x, Tl) else x for x in writes]
        waits = self._waits(eng, reads, writes)
        self.cnt[eng] += 1
        ev = (eng, self.cnt[eng])
        semh = self.semh
        mysem = semh[eng]

        def run(e):
            for k, v in waits:
                e.wait_ge(semh[k], v)
            emit(e).then_inc(mysem, 1)
        self.lists[eng].append(run)
        self._mark(ev, reads, writes)

    def dma(self, q, out, in_, reads=(), writes=(), merge=False):
        reads = [x.b if isinstance(x, Tl) else x for x in reads]
        writes = [x.b if isinstance(x, Tl) else x for x in writes]
        st = self.dq[q]
        n = st["n"]
        st["n"] += 1
        slots = st["slots"]
        key = slots[n % len(slots)]
        use = n // len(slots)
        waits = self._waits(q, reads, writes, merge)
        seen = self.seen[q]
        if use > 0 and seen.get(key, 0) < 16 * use:
            seen[key] = 16 * use
            waits.append((key, 16 * use))
        ev = (key, 16 * (use + 1))
        semh = self.semh

        def run(e):
            for k, v in waits:
                e.wait_ge(semh[k], v)
            e.dma_start(out=out, in_=in_).then_inc(semh[key], 16)
        self.lists[q].append(run)
        self._mark(ev, reads, writes, merge)

    def barrier(self):
        targets = {k: self.cnt[k] for k in ENGS if self.cnt[k] > 0}
        for q, st in self.dq.items():
            n = st["n"]
            ns = len(st["slots"])
            for i, key in enumerate(st["slots"]):
                uses = (n - i + ns - 1) // ns if n > i else 0
                if uses > 0:
                    targets[key] = 16 * uses
        semh = self.semh
        for eng in ENGS:
            seen = self.seen[eng]
            waits = []
            for k, v in targets.items():
                if k == eng:
                    continue
                if seen.get(k, 0) < v:
                    seen[k] = v
                    waits.append((k, v))

            def run(e, waits=waits):
                for k, v in waits:
                    e.wait_ge(semh[k], v)
            self.lists[eng].append(run)

    def emit(self):
        with self.nc.Block() as block:
            for name in ENGS:
                lst = self.lists[name]

                def body(e, lst=lst):
                    for f in lst:
                        f(e)
                getattr(block, name)(body)


class K:
    def __init__(self, nc, S):
        self.nc = nc
        self.S = S
        self.uid = 0
        self.psum = []
        self.pi = 0
        self.rot = None

    def name(self, p="t"):
        self.uid += 1
        return f"{p}{self.uid}"

    @contextlib.contextmanager
    def scope(self):
        es = contextlib.ExitStack()
        sc = Scope(self, es)
        try:
            yield sc
            self.S.barrier()
        finally:
            es.close()

    def ps(self):
        rot = self.rot if self.rot else list(range(7))
        t = self.psum[rot[self.pi % len(rot)]]
        self.pi += 1
        return t

    def mm(self, out, lhsT, rhs, start=True, stop=True, reads=(), writes=(), skip=False):
        if skip:
            self.S.op("tensor", lambda e: e.matmul(out, lhsT=lhsT, rhs=rhs, start=start, stop=stop, skip_group_check=True), reads, writes)
        else:
            self.S.op("tensor", lambda e: e.matmul(out, lhsT=lhsT, rhs=rhs, start=start, stop=stop), reads, writes)

    def tr(self, out, in_, ident, reads=(), writes=()):
        self.S.op("tensor", lambda e: e.transpose(out=out, in_=in_, identity=ident), reads, writes)

    def act(self, out, in_, func, bias=0.0, scale=1.0, accum_out=None, reads=(), writes=()):
        if accum_out is None:
            self.S.op("scalar", lambda e: e.activation(out=out, in_=in_, func=func, bias=bias, scale=scale), reads, writes)
        else:
            self.S.op("scalar", lambda e: e.activation(out=out, in_=in_, func=func, bias=bias, scale=scale,
                                                       accum_out=accum_out), reads, writes)

    def tt(self, eng, out, in0, in1, op, reads=(), writes=()):
        self.S.op(eng, lambda e: e.tensor_tensor(out=out, in0=in0, in1=in1, op=op), reads, writes)

    def ts(self, eng, out, in0, s1, s2=None, op0=ALU.mult, op1=None, reads=(), writes=()):
        if op1 is None:
            self.S.op(eng, lambda e: e.tensor_scalar(out=out, in0=in0, scalar1=s1, scalar2=None, op0=op0), reads, writes)
        else:
            self.S.op(eng, lambda e: e.tensor_scalar(out=out, in0=in0, scalar1=s1, scalar2=s2, op0=op0, op1=op1), reads, writes)

    def stt(self, eng, out, in0, scalar, in1, op0, op1, reads=(), writes=()):
        self.S.op(eng, lambda e: e.scalar_tensor_tensor(out=out, in0=in0, scalar=scalar, in1=in1, op0=op0, op1=op1), reads, writes)

    def copy(self, eng, out, in_, reads=(), writes=()):
        if eng == "scalar":
            self.S.op(eng, lambda e: e.activation(out=out, in_=in_, func=AF.Copy), reads, writes)
        else:
            self.S.op(eng, lambda e: e.tensor_copy(out=out, in_=in_), reads, writes)

    def recip(self, out, in_, reads=(), writes=()):
        self.S.op("vector", lambda e: e.reciprocal(out=out, in_=in_), reads, writes)

    def memset(self, eng, out, val, writes=()):
        self.S.op(eng, lambda e: e.memset(out, val), (), writes)

    def dma(self, q, out, in_, reads=(), writes=(), merge=False):
        self.S.dma(q, out, in_, reads, writes, merge)


class Scope:
    def __init__(self, k, es):
        self.k = k
        self.es = es

    def tile(self, shape, dtype, name="t"):
        t = self.es.enter_context(self.k.nc.sbuf_tensor(self.k.name(name), list(shape), dtype))
        return Tl(t)

    def pool(self, n, shape, dtype, name="p"):
        return Pool([self.tile(shape, dtype, name) for _ in range(n)])


class Pool:
    def __init__(self, tiles):
        self.tiles = tiles
        self.i = 0

    def get(self):
        t = self.tiles[self.i % len(self.tiles)]
        self.i += 1
        return t


C_ID, C_TRI, C_PB, C_PC, C_FC, C_BO, C_ONE, C_NB = 0, 128, 640, 768, 896, 898, 1026, 1154
NCONST = 1154 + 256
NEG = -30000.0


def make_consts():
    c = np.zeros((128, NCONST), np.float32)
    c[:, C_ID:C_ID + 128] = np.eye(128)
    r = np.arange(128)[:, None]
    q = np.arange(128)[None, :]
    c[:, C_TRI + 0:C_TRI + 128] = (q >= r)
    c[:, C_TRI + 128:C_TRI + 256] = (q <= r)
    c[:, C_TRI + 256:C_TRI + 384] = (q > r)
    c[:, C_TRI + 384:C_TRI + 512] = (q < r)
    pb = np.zeros((128, 128), np.float32)
    for rr in range(128):
        d = rr % 64
        if d < 8:
            pb[rr + 8, rr] = -1.0
        elif d < 16:
            pb[rr - 8, rr] = 1.0
    c[:, C_PB:C_PB + 128] = pb
    pc = np.zeros((128, 128), np.float32)
    for rr in range(64):
        if rr < 32:
            pc[rr + 32, rr] = -1.0
        else:
            pc[rr - 32, rr] = 1.0
    c[:, C_PC:C_PC + 128] = pc
    invb = 1.0 / (THETA ** (np.arange(0, 16, 2, dtype=np.float32) / 16))
    invc = 1.0 / (THETA ** (np.arange(0, 64, 2, dtype=np.float32) / 64))
    for rr in range(128):
        d = rr % 64
        c[rr, C_FC] = invb[d % 8] / (2 * math.pi) if d < 16 else 0.0
        c[rr, C_FC + 1] = invc[rr % 32] / (2 * math.pi)
    bo = np.zeros((128, 128), np.float32)
    bo[:64, :64] = 1
    bo[64:, 64:] = 1
    c[:, C_BO:C_BO + 128] = bo
    c[:, C_ONE:C_ONE + 128] = 1.0
    c[:, C_NB:C_NB + 128] = np.where(q >= r, 0.0, NEG)
    c[:, C_NB + 128:C_NB + 256] = np.where(q <= r, 0.0, NEG)
    return c


V_GATT, V_GFFN, V_GFIN, V_CQG, V_CKVG, V_MU, V_W0, V_A0, V_KK, V_KA, V_RK = 0, 8, 16, 24, 26, 27, 36, 38, 40, 42, 44
NV = 46


def col(v, n):
    return np.ascontiguousarray(v.reshape(n, 128).T)


def make_vecs(inp, l):
    v = np.zeros((128, NV), np.float32)
    v[:, V_GATT:V_GATT + 8] = col(inp["attn_norm_g"][l], 8)
    v[:, V_GFFN:V_GFFN + 8] = col(inp["ffn_norm_g"][l], 8)
    v[:, V_GFIN:V_GFIN + 8] = col(inp["final_norm_g"], 8)
    v[:, V_CQG:V_CQG + 2] = col(inp["c_q_norm_g"][l], 2)
    v[:, V_CKVG:V_CKVG + 1] = col(inp["c_kv_norm_g"][l], 1)
    mu = np.zeros(9 * 128, np.float32)
    mu[:1056] = inp["a_mu"][l]
    v[:, V_MU:V_MU + 9] = col(mu, 9)
    v[:, V_W0:V_W0 + 2] = col(inp["a_w0"][l], 2)
    v[:, V_A0:V_A0 + 2] = col(inp["a_a0"][l], 2)
    v[:, V_KK:V_KK + 2] = col(inp["a_k_k"][l], 2)
    v[:, V_KA:V_KA + 2] = col(inp["a_k_a"][l], 2)
    v[:, V_RK:V_RK + 2] = col(inp["a_r_k"][l].reshape(-1), 2)
    return v


def ktile(w):
    kk = w.shape[0] // 128
    return np.ascontiguousarray(w.reshape(kk, 128, w.shape[1]).transpose(1, 0, 2))


class Prog:
    pass


def build(debug=False, stop=None, nlayers=L, skip=()):
    nc = bass.Bass("TRN2", target_bir_lowering=False)
    P = Prog()
    din = {}

    def inp(name, shape, dt=F32):
        din[name] = nc.dram_tensor(name, list(shape), dt, kind="ExternalInput").ap()
        return din[name]

    skind = "ExternalOutput" if debug else "Internal"
    dscr = {}

    def scr(name, shape, dt):
        dscr[name] = nc.dram_tensor(name, list(shape), dt, kind=skind).ap()
        return dscr[name]

    x_in = inp("x", [S_LEN, D])
    pos = inp("pos", [1, S_LEN], I32)
    consts = inp("consts", [128, NCONST])
    vecs = inp("vecs", [L, 128, NV])
    lnwb = inp("lnwb", [L, 128, 512])
    gfinb = inp("gfinb", [128, D])
    w_in = inp("w_in", [L, 128, 8, 2272])
    w_uq = inp("w_uq", [L, 128, 2, 768])
    w_ukv = inp("w_ukv", [L, 128, 1, 1024])
    w_out = inp("w_out", [L, 128, 8, 1024])
    w_gate = inp("w_gate", [L, 128, 8, DFF])
    w_up = inp("w_up", [L, 128, 8, DFF])
    w_down = inp("w_down", [L, 128, 22, 1024])
    a_dec = inp("a_dec", [L, 64, 256])
    a_icl = inp("a_icl", [L, 64, 256])
    a_gate = inp("a_gate", [L, 160, 256])
    out = nc.dram_tensor("out", [S_LEN, D], F32, kind="ExternalOutput").ap()

    pA = scr("pA", [1056, S_LEN], F32)
    qB = [scr(f"qB{i}", [256, S_LEN], BF16) for i in range(3)]
    kB = [scr(f"kB{i}", [256, S_LEN], BF16) for i in range(3)]
    vB = scr("vB", [S_LEN, 256], BF16)
    qn = scr("qn", [512, S_LEN], BF16)
    qr = scr("qr", [256, S_LEN], BF16)
    kn = scr("kn", [512, S_LEN], BF16)
    kr = scr("kr", [64, S_LEN], BF16)
    vC = scr("vC", [S_LEN, 512], BF16)
    cat = scr("cat", [1024, S_LEN], BF16)
    x1 = scr("x1", [S_LEN, D], F32)
    xr = scr("xr", [S_LEN, D], F32)
    aT = scr("aT", [DFF, S_LEN], BF16)
    tabB = scr("tabB", [2, 128, S_LEN], F32)
    tabC = scr("tabC", [2, 64, S_LEN], F32)

    with contextlib.ExitStack() as es:
        E = es.enter_context
        names = list(ENGS) + [f"dma_{q}_{i}" for q in ("sync", "gpsimd") for i in range(NSLOTS[q])]
        sems = {n: E(nc.semaphore(n)) for n in names}
        S = Sched(nc, sems)
        k = K(nc, S)
        k.psum = [Tl(E(nc.psum_tensor(f"ps{i}", [128, 512], F32))) for i in range(8)]
        pTt = Tl(k.psum[7][:].bitcast(BF16).rearrange("p (a b) -> p a b", a=8))
        pTt.b = k.psum[7].b
        cst = Tl(E(nc.sbuf_tensor("cst", [128, NCONST], F32)))
        cstb = Tl(E(nc.sbuf_tensor("cstb", [128, NCONST], BF16)))
        k.dma("sync", cst[:], consts[:, :], writes=[cst])
        k.copy("vector", cstb[:], cst[:], reads=[cst], writes=[cstb])
        P.__dict__.update(locals())

        phase_tables(P)
        done = (stop == "tables")
        for l in range(nlayers):
            if done:
                break
            xsrc = x_in if l == 0 else xr
            phase_p1(P, l, xsrc)
            if stop == f"p1_{l}":
                break
            if "mla" not in skip:
                phase_mla(P, l)
            if stop == f"mla_{l}":
                break
            if "dil" not in skip:
                phase_dil(P, l)
            if stop == f"dil_{l}":
                break
            phase_rwkv(P, l)
            if stop == f"rwkv_{l}":
                break
            phase_tail(P, l, xsrc, last=(l == nlayers - 1))
        S.barrier()
        S.emit()
    return nc


def phase_tables(P):
    k, cst = P.k, P.cst
    with k.scope() as sc:
        posi = sc.tile([128, S_LEN], I32)
        posf = sc.tile([128, S_LEN], F32)
        y = sc.tile([128, S_LEN], F32)
        t = sc.tile([128, S_LEN], F32)
        r = sc.tile([128, S_LEN], F32)
        o = sc.tile([128, S_LEN], F32)
        src = bass.AP(P.pos.tensor, 0, [[0, 128], [1, S_LEN]])
        k.dma("sync", posi[:], src, writes=[posi])
        k.copy("vector", posf[:], posi[:], reads=[posi], writes=[posf])
        for ti, (fc, rows, dst) in enumerate(((C_FC, 128, P.tabB), (C_FC + 1, 64, P.tabC))):
            for cs in range(2):
                add = 0.25 if cs == 0 else 0.0
                k.ts("vector", y[:rows], posf[:rows], cst[:rows, fc:fc + 1], add, ALU.mult, ALU.add,
                     reads=[posf, cst], writes=[y])
                k.ts("vector", t[:rows], y[:rows], MAGIC, None, ALU.add, reads=[y], writes=[t])
                k.ts("vector", t[:rows], t[:rows], -MAGIC, None, ALU.add, reads=[t], writes=[t])
                k.tt("vector", r[:rows], y[:rows], t[:rows], ALU.subtract, reads=[y, t], writes=[r])
                k.act(o[:rows], r[:rows], AF.Sin, scale=6.28318, reads=[r], writes=[o])
                k.dma("sync", dst[cs, :, :], o[:rows], reads=[o])


def load_w(P, sc, dst, src, kc, n, stage, eng="gpsimd"):
    k = P.k
    for i in range(kc):
        st = stage.get()
        k.dma("sync", st[:, 0:n], src[:, i, :], writes=[st])
        k.copy(eng, dst[:, i, :], st[:, 0:n], reads=[st], writes=[dst])


def load_w2(P, dst, src, kc, nsplit):
    step = (kc + nsplit - 1) // nsplit
    for i in range(0, kc, step):
        j = min(kc, i + step)
        P.k.dma("gpsimd", dst[:, i:j, :], src[:, i:j, :], writes=[dst], merge=True)


def rmsnorm_T(P, xt, junk, ss, rstd, xn, pT, gb, hT_ap, hT):
    k = P.k
    k.act(junk[:], xt[:], AF.Square, accum_out=ss[:], reads=[xt], writes=[junk, ss])
    k.act(rstd[:], ss[:], AF.Sqrt, bias=EPS, scale=1.0 / D, reads=[ss], writes=[rstd])
    k.recip(rstd[:], rstd[:], reads=[rstd], writes=[rstd])
    k.ts("vector", xn[:], xt[:], rstd[:, 0:1], None, ALU.mult, reads=[xt, rstd], writes=[xn])
    for kk in range(8):
        k.tr(pT[:, kk, :], xn[:, kk * 128:(kk + 1) * 128], P.cstb[:, C_ID:C_ID + 128], reads=[xn, P.cstb], writes=[pT])
    k.tt("vector", hT_ap, pT[:], gb[:], ALU.mult, reads=[pT, gb], writes=[hT])


def make_gb(P, sc, vec, c0):
    k = P.k
    gb = sc.tile([128, 8, 128], BF16)
    for kk in range(8):
        k.ts("vector", gb[:, kk, :], P.cst[:, C_ONE:C_ONE + 128], vec[:, c0 + kk:c0 + kk + 1], None, ALU.mult,
             reads=[P.cst, vec], writes=[gb])
    return gb


def rope(P, ps, rows, permT, cos, sin, scale, tmp, outs, srcs, after=None):
    k = P.k
    qs = tmp.get()
    k.act(qs[:rows], ps[:rows], AF.Copy, scale=scale, reads=[ps], writes=[qs])

    def second():
        rope_b(P, qs, rows, permT, cos, sin, tmp, outs, srcs)
        if after is not None:
            after()
    return second


def rope_b(P, qs, rows, permT, cos, sin, tmp, outs, srcs):
    k = P.k
    pp = k.ps()
    k.mm(pp[:rows], permT, qs[:rows], reads=[qs, P.cst], writes=[pp])
    t1 = tmp.get()
    k.tt("vector", t1[:rows], qs[:rows], cos, ALU.mult, reads=[qs] + srcs, writes=[t1])
    t2 = tmp.get()
    k.tt("vector", t2[:rows], pp[:rows], sin, ALU.mult, reads=[pp] + srcs, writes=[t2])
    for eng, out_ap, view, tl in outs:
        k.tt(eng, out_ap, view(t1[:rows]), view(t2[:rows]), ALU.add, reads=[t1, t2], writes=[tl])


def phase_p1(P, l, xsrc):
    k, cst, cstb = P.k, P.cst, P.cstb
    with k.scope() as sc:
        winb = sc.tile([128, 8, 2272], BF16)
        wuqb = sc.tile([128, 2, 768], BF16)
        wukvb = sc.tile([128, 1, 1024], BF16)
        vec = sc.tile([128, NV], F32)
        k.dma("sync", vec[:], P.vecs[l], writes=[vec])
        load_w2(P, winb, P.w_in[l], 8, 4)
        load_w2(P, wuqb, P.w_uq[l], 2, 1)
        load_w2(P, wukvb, P.w_ukv[l], 1, 1)
        gb = make_gb(P, sc, vec, V_GATT)
        xts = sc.pool(3, [128, D], F32)
        junk = sc.tile([128, D], BF16)
        ss = sc.tile([128, 1], F32)
        rstd = sc.tile([128, 1], F32)
        xn = sc.tile([128, D], BF16)
        hTs = sc.pool(2, [128, 8, 512], BF16)
        pTt = P.pTt
        stA = sc.pool(3, [128, 512], F32)
        tmp = sc.pool(6, [128, 512], F32)
        ob = sc.pool(8, [128, 512], BF16)
        o16 = [[sc.tile([128, 16, 128], BF16) for _ in range(4)] for _ in range(2)]
        tabs = sc.pool(2, [128, 4, 512], F32)
        vbt = sc.pool(2, [128, 4, 256], BF16)
        vct = sc.pool(2, [128, 4, 512], BF16)
        csb = sc.tile([128, 3, 512], F32)
        sqb = sc.tile([128, 3, 512], F32)
        rs = sc.pool(2, [128, 512], F32)
        cqn = sc.tile([128, 3, 512], BF16)
        ones = cst[:, C_ONE:C_ONE + 128]
        A_CH = [(i * 128, 128) for i in range(8)] + [(1024, 32)]

        def norm_gen(g_, hT_):
            for tt in range(4):
                xt = xts.get()
                k.dma("sync", xt[:], xsrc[g_ * 512 + tt * 128: g_ * 512 + (tt + 1) * 128, :], writes=[xt])
                rmsnorm_T(P, xt, junk, ss, rstd, xn, pTt, gb, hT_[:, :, tt * 128:(tt + 1) * 128], hT_)
                yield
        hT_next = hTs.get()
        for _ in norm_gen(0, hT_next):
            pass
        for g in range(NG):
            cols = slice(g * 512, (g + 1) * 512)
            hT = hT_next
            gen = None
            if g + 1 < NG:
                hT_next = hTs.get()
                gen = norm_gen(g + 1, hT_next)

            def step(gen=gen):
                if gen is not None:
                    next(gen, None)
            tb = tabs.get()
            k.dma("sync", tb[:, 0:2, :], P.tabB[:, :, cols].rearrange("c p t -> p c t"), writes=[tb])
            k.dma("sync", tb[0:64, 2:4, :], P.tabC[:, :, cols].rearrange("c p t -> p c t"), writes=[tb])

            def proj(c0, m):
                ps = k.ps()
                for kk in range(8):
                    k.mm(ps[:m], winb[:, kk, c0:c0 + m], hT[:, kk, :], start=(kk == 0), stop=(kk == 7),
                         reads=[winb, hT], writes=[ps])
                return ps
            pend = []

            def flush():
                while pend:
                    pend.pop(0)()

            def a_chunk(ci):
                c0, m = A_CH[ci]
                ps = proj(c0, m)
                st = stA.get()
                k.copy("scalar", st[:m], ps[:m], reads=[ps], writes=[st])
                k.dma("gpsimd", P.pA[c0:c0 + m, cols], st[:m], reads=[st])
                if ci in (2, 5, 8):
                    step()
            for j, (c0, m) in enumerate(((1824, 128), (1952, 128), (2080, 128))):
                ps = proj(c0, m)
                k.copy("scalar", csb[:, j, :], ps[:], reads=[ps], writes=[csb])
                k.tt("gpsimd", sqb[:, j, :], csb[:, j, :], csb[:, j, :], ALU.mult, reads=[csb], writes=[sqb])
            for ci in range(3):
                a_chunk(ci)
            for which in range(2):
                js = (0, 1) if which == 0 else (2,)
                nfeat = 256.0 if which == 0 else 128.0
                ps = k.ps()
                for i, j in enumerate(js):
                    k.mm(ps[:], ones, sqb[:, j, :], start=(i == 0), stop=(i == len(js) - 1), reads=[cst, sqb], writes=[ps])
                r_ = rs.get()
                k.act(r_[:], ps[:], AF.Sqrt, bias=EPS, scale=1.0 / nfeat, reads=[ps], writes=[r_])
                k.recip(r_[:], r_[:], reads=[r_], writes=[r_])
                for j in js:
                    gcol = vec[:, V_CQG + j:V_CQG + j + 1]
                    k.stt("vector", cqn[:, j, :], csb[:, j, :], gcol, r_[:], ALU.mult, ALU.mult,
                          reads=[csb, vec, r_], writes=[cqn])
            for ci in range(3, 9):
                a_chunk(ci)
            for qk in range(2):
                for ch in range(2):
                    c0 = 1056 + qk * 256 + ch * 128
                    ps = proj(c0, 128)
                    flush()
                    o1 = ob.get()
                    o4 = ob.get()
                    o16t = o16[qk][ch * 2 + 0]
                    off = (g % 4) * 32
                    outs = [
                        ("vector", o1[:], (lambda v: v), o1),
                        ("gpsimd", o4[:].rearrange("p (r l) -> p r l", r=4), (lambda v: v.rearrange("p (l r) -> p r l", r=4)), o4),
                        ("gpsimd", o16t[:, :, off:off + 32], (lambda v: v.rearrange("p (l r) -> p r l", r=16)), o16t),
                    ]
                    dst = P.qB if qk == 0 else P.kB
                    rows = slice(ch * 128, (ch + 1) * 128)

                    def after(dst=dst, rows=rows, o1=o1, o4=o4, o16t=o16t):
                        k.dma("gpsimd", dst[0][rows, cols], o1[:], reads=[o1])
                        k.dma("gpsimd", dst[1][rows, cols], o4[:], reads=[o4])
                        if g % 4 == 3:
                            sp = g // 4
                            k.dma("gpsimd", dst[2][rows, sp * 2048:(sp + 1) * 2048], o16t[:].rearrange("p r l -> p (r l)"), reads=[o16t])
                    pend.append(rope(P, ps, 128, cst[:, C_PB:C_PB + 128], tb[:, 0, :], tb[:, 1, :], (0.125 if qk == 0 else 1.0),
                                     tmp, outs, [tb], after))
            step()
            vb = vbt.get()
            for tt in range(4):
                ps = k.ps()
                for kk in range(8):
                    k.mm(ps[:, 0:256], hT[:, kk, tt * 128:(tt + 1) * 128], winb[:, kk, 1568:1824], start=(kk == 0), stop=(kk == 7),
                         reads=[winb, hT], writes=[ps])
                if tt == 0:
                    flush()
                k.copy("scalar", vb[:, tt, :], ps[:, 0:256], reads=[ps], writes=[vb])
            k.dma("gpsimd", P.vB[cols, :].rearrange("(t p) c -> p t c", p=128), vb[:], reads=[vb])
            qscale = 192.0 ** -0.5
            for h in range(4):
                ps = k.ps()
                for kc in range(2):
                    k.mm(ps[:], wuqb[:, kc, h * 192:h * 192 + 128], cqn[:, kc, :], start=(kc == 0), stop=(kc == 1),
                         reads=[wuqb, cqn], writes=[ps])
                o = ob.get()
                k.act(o[:], ps[:], AF.Copy, scale=qscale, reads=[ps], writes=[o])
                k.dma("gpsimd", P.qn[h * 128:(h + 1) * 128, cols], o[:], reads=[o])
                ps = k.ps()
                for kc in range(2):
                    k.mm(ps[0:64], wuqb[:, kc, h * 192 + 128:h * 192 + 192], cqn[:, kc, :], start=(kc == 0), stop=(kc == 1),
                         reads=[wuqb, cqn], writes=[ps])
                flush()
                o = ob.get()

                def after(o=o, h=h):
                    k.dma("gpsimd", P.qr[h * 64:(h + 1) * 64, cols], o[0:64], reads=[o])
                pend.append(rope(P, ps, 64, cst[0:64, C_PC:C_PC + 64], tb[0:64, 2, :], tb[0:64, 3, :], qscale, tmp,
                                 [("vector", o[0:64], (lambda v: v), o)], [tb], after))
            for h in range(4):
                ps = k.ps()
                k.mm(ps[:], wukvb[:, 0, h * 128:(h + 1) * 128], cqn[:, 2, :], reads=[wukvb, cqn], writes=[ps])
                if h == 0:
                    flush()
                o = ob.get()
                k.copy("scalar", o[:], ps[:], reads=[ps], writes=[o])
                k.dma("gpsimd", P.kn[h * 128:(h + 1) * 128, cols], o[:], reads=[o])
            ps = proj(2208, 64)
            o = ob.get()

            def after(o=o):
                k.dma("gpsimd", P.kr[:, cols], o[0:64], reads=[o])
            pend.append(rope(P, ps, 64, cst[0:64, C_PC:C_PC + 64], tb[0:64, 2, :], tb[0:64, 3, :], 1.0, tmp,
                             [("vector", o[0:64], (lambda v: v), o)], [tb], after))
            vc = vct.get()
            for tt in range(4):
                ps = k.ps()
                k.mm(ps[:], cqn[:, 2, tt * 128:(tt + 1) * 128], wukvb[:, 0, 512:1024], reads=[wukvb, cqn], writes=[ps])
                if tt == 0:
                    flush()
                k.copy("scalar", vc[:, tt, :], ps[:], reads=[ps], writes=[vc])
            k.dma("gpsimd", P.vC[cols, :].rearrange("(t p) c -> p t c", p=128), vc[:], reads=[vc])
            flush()


def phase_mla(P, l):
    k, cst, cstb = P.k, P.cst, P.cstb
    with k.scope() as sc:
        knp = sc.pool(2, [128, S_LEN], BF16)
        qnp = sc.pool(2, [128, S_LEN], BF16)
        qrp = sc.pool(2, [128, S_LEN], BF16)
        vp = sc.pool(2, [128, 32, 128], BF16)
        krt = sc.tile([128, S_LEN], BF16)
        for t_ in qrp.tiles + [krt]:
            k.memset("gpsimd", t_[64:128, :], 0.0, writes=[t_])
        pacc1p = sc.pool(3, [128, 512], F32)
        pTp = sc.pool(8, [128, 512], BF16)
        ob = sc.pool(2, [128, 512], BF16)
        rd = sc.pool(2, [128, 512], F32)
        paccp = sc.pool(3, [128, 512], F32)
        k.dma("sync", krt[0:64, :], P.kr[:, :], writes=[krt])
        ones = cst[:, C_ONE:C_ONE + 128]
        identb = cstb[:, C_ID:C_ID + 128]
        nbias = cstb[:, C_NB:C_NB + 128]
        onesb = cstb[:, C_ONE:C_ONE + 128]
        pending = []
        for h in range(4):
            knh, qnh, qrh, vh = knp.get(), qnp.get(), qrp.get(), vp.get()
            k.dma("sync", knh[:], P.kn[h * 128:(h + 1) * 128, :], writes=[knh])
            k.dma("sync", qnh[:], P.qn[h * 128:(h + 1) * 128, :], writes=[qnh])
            k.dma("sync", qrh[0:64, :], P.qr[h * 64:(h + 1) * 64, :], writes=[qrh])
            k.dma("sync", vh[:], P.vC[:, h * 128:(h + 1) * 128].rearrange("(t p) c -> p t c", p=128), writes=[vh])
            for g in range(NG):
                k.rot = [0, 1, 2, 3]
                po = k.psum[4 + (g % 2)]
                pden = k.psum[6 + (g % 2)]
                pe_den = [kt for kt in range(4 * g + 4) if kt % 3 == 1] if g > 0 else []
                pacc = paccp.get()
                pacc1 = pacc1p.get()
                k.memset("gpsimd", pacc1[:], 0.0, writes=[pacc1])
                nkt = 4 * g + 4

                def s_stage(kt, g=g, pacc=pacc, pacc1=pacc1, pe_den=pe_den):
                    o = kt - 4 * g
                    c0 = max(o, 0) * 128
                    ks = slice(kt * 128, (kt + 1) * 128)
                    qs = slice(g * 512 + c0, (g + 1) * 512)
                    pS = k.ps()
                    k.mm(pS[:, c0:512], knh[:, ks], qnh[:, qs], start=True, stop=False, reads=[knh, qnh], writes=[pS])
                    k.mm(pS[:, c0:512], krt[:, ks], qrh[:, qs], start=False, stop=(o < 0), reads=[krt, qrh], writes=[pS])
                    if o >= 0:
                        k.mm(pS[:, c0:c0 + 128], identb, nbias, start=False, stop=True, reads=[cstb], writes=[pS])
                    pT = pTp.get()
                    k.act(pT[:, c0:512], pS[:, c0:512], AF.Exp, reads=[pS], writes=[pT])
                    if kt == 0:
                        k.copy("vector", pacc[:], pT[:], reads=[pT], writes=[pacc])
                    elif kt in pe_den:
                        pass
                    elif kt % 3 == 2:
                        k.tt("gpsimd", pacc1[:, c0:512], pacc1[:, c0:512], pT[:, c0:512], ALU.add, reads=[pT, pacc1], writes=[pacc1])
                    else:
                        k.tt("vector", pacc[:, c0:512], pacc[:, c0:512], pT[:, c0:512], ALU.add, reads=[pT, pacc], writes=[pacc])
                    return kt, c0, pT

                def pv_stage(st, po=po, nkt=nkt, pden=pden, pe_den=pe_den):
                    kt, c0, pT = st
                    k.mm(po[:, c0:512], vh[:, kt, :], pT[:, c0:512], start=(kt == 0), stop=(kt == nkt - 1), reads=[vh, pT], writes=[po])
                    if kt in pe_den:
                        k.mm(pden[:, c0:512], onesb, pT[:, c0:512], start=(kt == pe_den[0]), stop=False, reads=[cstb, pT], writes=[pden],
                             skip=True)

                def finalize(g=g, h=h, po=po, pacc=pacc, pacc1=pacc1, pd=pden, pe_den=pe_den):
                    k.mm(pd[:], ones, pacc[:], start=(not pe_den), stop=False, reads=[cst, pacc], writes=[pd], skip=True)
                    k.mm(pd[:], ones, pacc1[:], start=False, stop=True, reads=[cst, pacc1], writes=[pd], skip=True)
                    r = rd.get()
                    k.recip(r[:], pd[:], reads=[pd], writes=[r])
                    oo = ob.get()
                    k.tt("vector", oo[:], po[:], r[:], ALU.mult, reads=[po, r], writes=[oo])
                    k.dma("gpsimd", P.cat[512 + h * 128:512 + (h + 1) * 128, g * 512:(g + 1) * 512], oo[:], reads=[oo])
                DEPTH = 3
                q_ = []
                for kt in range(nkt):
                    q_.append(s_stage(kt))
                    if kt == 2 and pending:
                        pending.pop()()
                    if len(q_) > DEPTH:
                        pv_stage(q_.pop(0))
                while q_:
                    pv_stage(q_.pop(0))
                pending.append(finalize)
        while pending:
            pending.pop()()
        k.rot = None


def phase_dil(P, l):
    k, cst, cstb = P.k, P.cst, P.cstb
    with k.scope() as sc:
        vts = [sc.tile([128, 32, 256], BF16) for _ in range(3)]
        vap = sc.pool(2, [128, 32, 128], BF16)
        qp = [sc.pool(2, [128, S_LEN], BF16) for _ in range(2)]
        kp = sc.pool(2, [128, S_LEN], BF16)
        acc = sc.tile([128, S_LEN], F32)
        pTp = sc.pool(10, [128, 512], BF16)
        ob = sc.pool(2, [64, S_LEN], BF16)
        rdp = sc.pool(2, [64, 512], F32)
        for par in range(2):
            for t_ in qp[par].tiles:
                k.memset("gpsimd", t_[(1 - par) * 64:(2 - par) * 64, :], 0.0, writes=[t_])
        for t_ in vap.tiles:
            k.memset("gpsimd", t_[:, :, 64:128], 1.0, writes=[t_])
        DS = (1, 4, 16)
        for pi, d in enumerate(DS):
            if d == 1:
                src = P.vB[:, :].rearrange("(s l) c -> l s c", l=128)
                k.dma("sync", vts[pi][:], src, writes=[vts[pi]])
            else:
                for r in range(d):
                    src = P.vB[:, :].rearrange("(s l r) c -> r l s c", l=128, r=d)[r]
                    dstv = vts[pi][:].rearrange("p (s r) c -> p r s c", r=d)[:, r]
                    k.dma("sync", dstv, src, writes=[vts[pi]], merge=True)
        identb = cstb[:, C_ID:C_ID + 128]
        nb_cur = cstb[:, C_NB:C_NB + 128]
        nb_prev = cstb[:, C_NB + 128:C_NB + 256]
        shiftT = cst[:, C_ID + 64:C_ID + 128]
        k.rot = [0, 1, 2, 3, 4, 5, 6]
        for h in range(4):
            hc = slice(h * 64, (h + 1) * 64)
            par = h % 2
            pr = slice((h // 2) * 128, (h // 2 + 1) * 128)
            for pi, d in enumerate(DS):
                qh, kh = qp[par].get(), kp.get()
                k.dma("sync", qh[par * 64:(par + 1) * 64, :], P.qB[pi][hc, :], writes=[qh])
                k.dma("sync", kh[:], P.kB[pi][pr, :], writes=[kh])
                vt = vap.get()
                k.copy("gpsimd", vt[:, :, 0:64], vts[pi][:, :, hc], reads=[vts[pi]], writes=[vt])

                def s_stage(bt, d=d, qh=qh, kh=kh):
                    b0 = bt * 4
                    pc, pp = k.ps(), k.ps()
                    hasp = [(b0 + j - d) >= 0 for j in range(4)]
                    for j in range(4):
                        cs = slice((b0 + j) * 128, (b0 + j + 1) * 128)
                        js = slice(j * 128, (j + 1) * 128)
                        k.mm(pc[:, js], kh[:, cs], qh[:, cs], start=True, stop=False, reads=[kh, qh], writes=[pc])
                        k.mm(pc[:, js], identb, nb_cur, start=False, stop=True, reads=[cstb], writes=[pc])
                        if hasp[j]:
                            ps_ = slice((b0 + j - d) * 128, (b0 + j - d + 1) * 128)
                            k.mm(pp[:, js], kh[:, ps_], qh[:, cs], start=True, stop=False, reads=[kh, qh], writes=[pp])
                            k.mm(pp[:, js], identb, nb_prev, start=False, stop=True, reads=[cstb], writes=[pp])
                    tc_ = pTp.get()
                    k.act(tc_[:], pc[:], AF.Exp, reads=[pc], writes=[tc_])
                    tp_ = None
                    if any(hasp):
                        j0 = hasp.index(True)
                        tp_ = pTp.get()
                        k.act(tp_[:, j0 * 128:512], pp[:, j0 * 128:512], AF.Exp, reads=[pp], writes=[tp_])
                    return bt, hasp, tc_, tp_

                def pv_stage(st, d=d, pi=pi, vt=vt):
                    bt, hasp, tc_, tp_ = st
                    b0 = bt * 4
                    pn = k.ps()
                    for j in range(4):
                        js = slice(j * 128, (j + 1) * 128)
                        if hasp[j]:
                            k.mm(pn[:, js], vt[:, b0 + j - d, :], tp_[:, js], start=True, stop=False, reads=[vt, tp_], writes=[pn])
                        k.mm(pn[:, js], vt[:, b0 + j, :], tc_[:, js], start=(not hasp[j]), stop=True, reads=[vt, tc_], writes=[pn])
                    if d == 1:
                        va = acc[:, b0 * 128:b0 * 128 + 512]
                        vpz = pn[:, :]
                    elif d == 4:
                        va = acc[:, bt * 512:(bt + 1) * 512].rearrange("p (l r) -> p r l", r=4)
                        vpz = pn[:, :].rearrange("p (r l) -> p r l", r=4)
                    else:
                        sp, r0 = b0 // 16, b0 % 16
                        va = acc[:, sp * 2048:(sp + 1) * 2048].rearrange("p (l r) -> p r l", r=16)[:, r0:r0 + 4, :]
                        vpz = pn[:, :].rearrange("p (r l) -> p r l", r=4)
                    if pi == 0:
                        k.copy("vector", va, vpz, reads=[pn], writes=[acc])
                    else:
                        k.tt("vector", va, va, vpz, ALU.add, reads=[pn, acc], writes=[acc])
                DEPTH = 3
                q_ = []
                for bt in range(8):
                    q_.append(s_stage(bt))
                    if len(q_) > DEPTH:
                        pv_stage(q_.pop(0))
                while q_:
                    pv_stage(q_.pop(0))
            oo = ob.get()
            for c8 in range(8):
                cs8 = slice(c8 * 512, (c8 + 1) * 512)
                pd = k.ps()
                k.mm(pd[0:64, :], shiftT, acc[:, cs8], reads=[cst, acc], writes=[pd])
                r_ = rdp.get()
                k.recip(r_[:], pd[0:64, :], reads=[pd], writes=[r_])
                k.tt("gpsimd", oo[:, cs8], acc[0:64, cs8], r_[:], ALU.mult, reads=[acc, r_], writes=[oo])
            k.dma("gpsimd", P.cat[256 + h * 64:256 + (h + 1) * 64, :], oo[:], reads=[oo])
        k.rot = None


C0 = 0.6065306597126334
GN_EPS = 64e-5


def bc(ap, n):
    return bass.AP(ap.tensor, ap.offset, [list(x) for x in ap.ap] + [[0, n]])


def phase_rwkv(P, l):
    k, cst, cstb = P.k, P.cst, P.cstb
    AX = mybir.AxisListType
    with k.scope() as sc:
        identf = cst[:, C_ID:C_ID + 128]
        bones = cst[:, C_BO:C_BO + 128]
        ones = cst[:, C_ONE:C_ONE + 128]
        vec = sc.tile([128, NV], F32)
        lnwb = sc.tile([128, 512], F32)
        dec = sc.tile([128, 256], F32)
        icl = sc.tile([128, 256], F32)
        gateA = sc.tile([128, 256], F32)
        gateB = sc.tile([128, 256], F32)
        for t_ in (dec, icl, gateB):
            k.memset("gpsimd", t_[:], 0.0, writes=[t_])
        k.dma("sync", vec[:], P.vecs[l], writes=[vec])
        k.dma("sync", lnwb[:], P.lnwb[l], writes=[lnwb])
        k.dma("sync", dec[0:64, :], P.a_dec[l], writes=[dec])
        k.dma("sync", icl[64:128, :], P.a_icl[l], writes=[icl])
        k.dma("sync", gateA[:], P.a_gate[l, 0:128, :], writes=[gateA])
        k.dma("sync", gateB[0:32, :], P.a_gate[l, 128:160, :], writes=[gateB])
        msk4 = sc.tile([128, 512], F32)
        miu4 = sc.tile([128, 512], F32)
        msu2 = sc.tile([128, 256], F32)
        for j in range(4):
            src = C_TRI + (384 if j % 2 == 0 else 256)
            k.copy("vector", msk4[:, j * 128:(j + 1) * 128], cst[:, src:src + 128], reads=[cst], writes=[msk4])
            k.copy("vector", miu4[:, j * 128:(j + 1) * 128], cst[:, C_TRI:C_TRI + 128], reads=[cst], writes=[miu4])
        for j in range(2):
            k.copy("vector", msu2[:, j * 128:(j + 1) * 128], cst[:, C_TRI + 256:C_TRI + 384], reads=[cst], writes=[msu2])
        rawp = sc.pool(2, [128, 513], F32)
        dtmp = sc.pool(2, [128, 512], F32)
        PFk = [sc.tile([128, 512], F32) for _ in range(2)]
        PF6 = sc.tile([128, 512], F32)
        PF7 = sc.tile([128, 512], F32)
        PF8 = sc.tile([128, 512], F32)
        tmpf = sc.pool(3, [128, 512], F32)
        th = sc.tile([128, 512], F32)
        SG = sc.tile([128, 512], F32)
        Aic = sc.tile([128, 512], F32)

        class HOc:
            pass
        HO = []
        for i in range(2):
            ho = HOc()
            for nm_ in ("R", "V", "CS", "CSX", "KN", "KF", "Bv", "RKR"):
                setattr(ho, nm_, [sc.tile([128, 512], F32) for _ in range(2)])
            ho.SX7 = sc.tile([128, 512], F32)
            ho.SX8 = sc.tile([128, 512], F32)
            k.memset("gpsimd", ho.SX8[:], 0.0, writes=[ho.SX8])
            HO.append(ho)

        class Slot:
            pass
        slots = []
        for s_ in range(4):
            sl = Slot()
            sl.ep = sc.pool(5, [128, 128], F32)
            sl.dp = sc.pool(4, [128, 128], F32)
            sl.APz = [sc.tile([128, 128], BF16) for _ in range(2)]
            sl.bkb = sc.pool(2, [128, 2, 128], BF16)
            sl.RPz = [sc.tile([128, 128], F32) for _ in range(2)]
            for t_ in sl.APz + sl.RPz:
                k.memset("gpsimd", t_[:], 0.0, writes=[t_])
            sl.smp = sc.pool(8, [128, 2], F32)
            sl.tmp = sc.pool(1, [128, 3, 128], F32)
            sl.MNp = sc.pool(2, [128, 2, 2, 128], BF16)
            sl.mkp = sc.pool(1, [128, 2, 128], F32)
            sl.wrp = sc.pool(1, [128, 2, 2, 128], F32)
            sl.Xbp = sc.pool(2, [128, 2, 2, 64], BF16)
            sl.Xfp = sc.pool(1, [128, 2, 2, 64], F32)
            sl.sq = sc.pool(6, [128, 128], F32)
            sl.Hpp = sc.pool(1, [128, 128], F32)
            sl.px = k.psum[4 + s_]
            slots.append(sl)
        k.rot = [0, 1, 2, 3]
        STp = [sc.pool(2, [128, 128], F32) for _ in range(2)]
        catA = sc.pool(3, [128, 2, 512], BF16)
        ST = []
        for cp in range(2):
            t = STp[cp].get()
            k.memset("gpsimd", t[:], 0.0, writes=[t])
            ST.append(t)
        f2 = lambda t: t[:].rearrange("p a b c -> p (a b c)")

        def a1gen(g, ho):
            lo = g * 512 - 1

            def mix(ch, dst_ap, dst_tl):
                rows = 128 if ch < 8 else 32
                rw = rawp.get()
                rsl = slice(ch * 128, ch * 128 + rows)
                if g == 0:
                    k.memset("gpsimd", rw[:, 0:1], 0.0, writes=[rw])
                    k.dma("sync", rw[:rows, 1:513], P.pA[rsl, 0:512], writes=[rw])
                else:
                    k.dma("sync", rw[:rows, :], P.pA[rsl, lo:lo + 513], writes=[rw])
                d_ = dtmp.get()
                k.tt("gpsimd", d_[:rows], rw[:rows, 0:512], rw[:rows, 1:513], ALU.subtract, reads=[rw], writes=[d_])
                k.stt("vector", dst_ap, d_[:rows], vec[:rows, V_MU + ch:V_MU + ch + 1], rw[:rows, 1:513],
                      ALU.mult, ALU.add, reads=[d_, vec, rw], writes=[dst_tl])
            dsts = [ho.R[0], ho.R[1], PFk[0], PFk[1], ho.V[0], ho.V[1], PF6, PF7, PF8]
            for ch in range(9):
                d = dsts[ch]
                mix(ch, d[:] if ch < 8 else d[0:32, :], d)
                if ch % 2 == 1:
                    yield
            k.act(th[:], PF6[:], AF.Tanh, reads=[PF6], writes=[th])
            k.act(ho.SX7[:], PF7[:], AF.Sigmoid, reads=[PF7], writes=[ho.SX7])
            k.act(ho.SX8[0:32, :], PF8[0:32, :], AF.Sigmoid, reads=[PF8], writes=[ho.SX8])
            yield
            for cp in range(2):
                pcs = slice(cp * 128, (cp + 1) * 128)
                CS, CSX, KN, KF, Bv, RKR = ho.CS[cp], ho.CSX[cp], ho.KN[cp], ho.KF[cp], ho.Bv[cp], ho.RKR[cp]
                pz = k.ps()
                k.mm(pz[:], dec[:, pcs], th[:, :], reads=[dec, th], writes=[pz])
                k.act(SG[:], pz[:], AF.Sigmoid, bias=vec[:, V_W0 + cp:V_W0 + cp + 1], reads=[pz, vec], writes=[SG])
                pa_ = k.ps()
                k.mm(pa_[:], icl[:, pcs], PF6[:, :], reads=[icl, PF6], writes=[pa_])
                k.act(Aic[:], pa_[:], AF.Sigmoid, bias=vec[:, V_A0 + cp:V_A0 + cp + 1], reads=[pa_, vec], writes=[Aic])
                yield
                for c4 in range(4):
                    cc = slice(c4 * 128, (c4 + 1) * 128)
                    k.S.op("vector", (lambda e, o=CS[:, cc], d1=SG[:, cc]: e.tensor_tensor_scan(
                        out=o, data0=ones, data1=d1, initial=0.0, op0=ALU.mult, op1=ALU.add)),
                        reads=[cst, SG], writes=[CS])
                k.tt("gpsimd", CSX[:], CS[:], SG[:], ALU.subtract, reads=[CS, SG], writes=[CSX])
                yield
                kraw = PFk[cp]
                kkr = tmpf.get()
                k.ts("vector", kkr[:], kraw[:], vec[:, V_KK + cp:V_KK + cp + 1], None, ALU.mult, reads=[kraw, vec], writes=[kkr])
                sq = tmpf.get()
                k.tt("gpsimd", sq[:], kkr[:], kkr[:], ALU.mult, reads=[kkr], writes=[sq])
                pn_ = k.ps()
                k.mm(pn_[:], bones, sq[:], reads=[cst, sq], writes=[pn_])
                nrm = tmpf.get()
                k.act(nrm[:], pn_[:], AF.Sqrt, reads=[pn_], writes=[nrm])
                yield
                k.ts("vector", nrm[:], nrm[:], 1e-12, None, ALU.max, reads=[nrm], writes=[nrm])
                k.recip(nrm[:], nrm[:], reads=[nrm], writes=[nrm])
                k.tt("vector", KN[:], kkr[:], nrm[:], ALU.mult, reads=[kkr, nrm], writes=[KN])
                yield
                t1 = tmpf.get()
                k.ts("vector", t1[:], Aic[:], -1.0, vec[:, V_KA + cp:V_KA + cp + 1], ALU.add, ALU.mult, reads=[Aic, vec], writes=[t1])
                k.stt("vector", KF[:], t1[:], 1.0, kraw[:], ALU.add, ALU.mult, reads=[t1, kraw], writes=[KF])
                k.tt("gpsimd", Bv[:], KN[:], Aic[:], ALU.mult, reads=[KN, Aic], writes=[Bv])
                k.stt("vector", RKR[:], ho.R[cp][:], vec[:, V_RK + cp:V_RK + cp + 1], KF[:], ALU.mult, ALU.mult,
                      reads=[ho.R[cp], vec, KF], writes=[RKR])
                yield

        def unit(sl, ho, cp, c4, cA):
            cc = slice(c4 * 128, (c4 + 1) * 128)
            Rt_, Vt_ = ho.R[cp], ho.V[cp]
            CS, CSX, KN, KF, Bv, RKR = ho.CS[cp], ho.CSX[cp], ho.KN[cp], ho.KF[cp], ho.Bv[cp], ho.RKR[cp]
            rfe = Rt_[:, cc]
            vfe = Vt_[:, cc]
            cs_ = CS[:, cc]
            csx = CSX[:, cc]
            cend = CS[:, c4 * 128 + 127:c4 * 128 + 128]
            sm = sl.smp.get()
            k.ts("vector", sm[:, 0:1], cend, C0, None, ALU.mult, reads=[CS], writes=[sm])
            k.ts("vector", sm[:, 1:2], cend, -C0, None, ALU.mult, reads=[CS], writes=[sm])
            pce, nce = sm[:, 0:1], sm[:, 1:2]
            e1, e2, e3, e4, e5 = (sl.ep.get() for _ in range(5))
            k.act(e1[:], csx, AF.Exp, scale=-C0, reads=[CSX], writes=[e1])
            k.act(e2[:], csx, AF.Exp, scale=-C0, bias=pce, reads=[CSX, sm], writes=[e2])
            k.act(e3[:], cs_, AF.Exp, scale=C0, bias=nce, reads=[CS, sm], writes=[e3])
            k.act(e4[:], cs_, AF.Exp, scale=-C0, reads=[CS], writes=[e4])
            k.act(e5[:], cs_, AF.Exp, scale=-C0, bias=pce, reads=[CS, sm], writes=[e5])
            AT, BT, KT, RT = (sl.dp.get() for _ in range(4))
            APz, RPz = sl.APz, sl.RPz
            k.stt("vector", AT[:], KN[:, cc], -1.0, e1[:], ALU.mult, ALU.mult, reads=[KN, e1], writes=[AT])
            for h in range(2):
                hr = slice(h * 64, (h + 1) * 64)
                k.stt("vector", APz[h][hr, :], KN[hr, cc], -1.0, e2[hr, :], ALU.mult, ALU.mult, reads=[KN, e2], writes=[APz[h]])
                k.tt("gpsimd", RPz[h][hr, :], Rt_[hr, cc], e5[hr, :], ALU.mult, reads=[Rt_, e5], writes=[RPz[h]])
            k.tt("gpsimd", BT[:], Bv[:, cc], e3[:], ALU.mult, reads=[Bv, e3], writes=[BT])
            k.tt("gpsimd", KT[:], KF[:, cc], e3[:], ALU.mult, reads=[KF, e3], writes=[KT])
            k.tt("vector", RT[:], rfe, e4[:], ALU.mult, reads=[Rt_, e4], writes=[RT])
            bkb = sl.bkb.get()
            k.copy("gpsimd", bkb[:, 0, :], BT[:], reads=[BT], writes=[bkb])
            k.copy("gpsimd", bkb[:, 1, :], KT[:], reads=[KT], writes=[bkb])
            yield
            pt = k.ps()
            for q, (srcT, stl) in enumerate(((BT[:], BT), (KT[:], KT), (vfe, Vt_))):
                k.tr(pt[:, q * 128:(q + 1) * 128], srcT, identf, reads=[stl, cst], writes=[pt])
            tm = sl.tmp.get()
            k.copy("scalar", tm[:].rearrange("p a b -> p (a b)"), pt[:, 0:384], reads=[pt], writes=[tm])
            yield
            pmA, pmB, pw = k.ps(), k.ps(), k.ps()
            for h in range(2):
                k.mm(pmA[:, h * 256:h * 256 + 128], APz[h][:], bkb[:, 0, :], reads=[APz[h], bkb], writes=[pmA])
                k.mm(pmA[:, h * 256 + 128:h * 256 + 256], bkb[:, 0, :], APz[h][:], reads=[APz[h], bkb], writes=[pmA])
                k.mm(pmB[:, h * 128:(h + 1) * 128], bkb[:, 1, :], APz[h][:], reads=[APz[h], bkb], writes=[pmB])
                k.mm(pw[:, h * 256:h * 256 + 128], BT[:], RPz[h][:], reads=[RPz[h], BT], writes=[pw])
                k.mm(pw[:, h * 256 + 128:h * 256 + 256], KT[:], RPz[h][:], reads=[RPz[h], KT], writes=[pw])
            MN = sl.MNp.get()
            k.tt("vector", f2(MN), pmA[:], msk4[:], ALU.mult, reads=[pmA, msk4], writes=[MN])
            mk = sl.mkp.get()
            k.tt("vector", mk[:].rearrange("p a b -> p (a b)"), pmB[:, 0:256], msu2[:], ALU.mult, reads=[pmB, msu2], writes=[mk])
            wr = sl.wrp.get()
            k.tt("vector", f2(wr), pw[:], miu4[:], ALU.mult, reads=[pw, miu4], writes=[wr])
            yield
            px = sl.px
            pxv = px[:, 0:256].rearrange("p (a h v) -> p a h v", a=2, h=2)
            for h in range(2):
                k.mm(pxv[:, 1, h, :], mk[:, h, :], tm[:, 2, h * 64:(h + 1) * 64], start=(h == 0), stop=False,
                     reads=[mk, tm], writes=[px], skip=True)
            k.mm(px[:, 0:128], AT[:], identf, start=False, stop=False, reads=[AT, cst], writes=[px], skip=True)
            Xb = sl.Xbp.get()
            k.copy("vector", f2(Xb), px[:, 0:256], reads=[px], writes=[Xb])
            yield
            Xf = None
            for i in range(7):
                for h in range(2):
                    k.mm(pxv[:, :, h, :], MN[:, h, 1, :], Xb[:, :, h, :], start=False, stop=(i == 6 and h == 1),
                         reads=[MN, Xb], writes=[px], skip=True)
                if i < 6:
                    Xb = sl.Xbp.get()
                    k.copy("scalar" if i % 2 == 0 else "vector", f2(Xb), px[:, 0:256], reads=[px], writes=[Xb])
                    pn = k.ps()
                    pnv = pn[:].rearrange("p (h m t) -> p h m t", h=2, m=2)
                    for h in range(2):
                        k.mm(pnv[:, h, 1, :], MN[:, h, 0, :], MN[:, h, 1, :], reads=[MN], writes=[pn])
                        k.mm(pnv[:, h, 0, :], MN[:, h, 1, :], MN[:, h, 0, :], reads=[MN], writes=[pn])
                    MNn = sl.MNp.get()
                    k.copy("scalar", f2(MNn), pn[:], reads=[pn], writes=[MNn])
                    MN = MNn
                else:
                    Xf = sl.Xfp.get()
                    k.copy("vector", f2(Xf), px[:, 0:256], reads=[px], writes=[Xf])
                yield
            X = Xf
            Ahat = X[:, 0].rearrange("p h v -> p (h v)")
            U0 = X[:, 1].rearrange("p h v -> p (h v)")
            pg = k.ps()
            k.mm(pg[:, 0:128], Ahat, tm[:, 0, :], reads=[X, tm], writes=[pg])
            gt1 = sl.sq.get()
            k.tt("vector", gt1[:], pg[:, 0:128], bones, ALU.mult, reads=[pg, cst], writes=[gt1])
            GT = sl.sq.get()
            k.stt("vector", GT[:], identf, e4[:, 127:128], gt1[:], ALU.mult, ALU.add, reads=[cst, e4, gt1], writes=[GT])
            ph = k.ps()
            k.mm(ph[:, 0:128], tm[:, 0, :], U0, start=True, stop=False, reads=[X, tm], writes=[ph])
            k.mm(ph[:, 0:128], tm[:, 1, :], tm[:, 2, :], start=False, stop=True, reads=[tm], writes=[ph])
            Hp = sl.Hpp.get()
            k.tt("vector", Hp[:], ph[:, 0:128], bones, ALU.mult, reads=[ph, cst], writes=[Hp])
            QT = sl.sq.get()
            for h in range(2):
                hr = slice(h * 64, (h + 1) * 64)
                pq = k.ps()
                k.mm(pq[:, 0:128], Ahat, wr[:, h, 0, :], reads=[X, wr], writes=[pq])
                k.tt("vector", QT[hr, :], pq[hr, 0:128], RT[hr, :], ALU.add, reads=[pq, RT], writes=[QT])
            yield
            py = k.ps()
            So = ST[cp]
            k.mm(py[:, 0:128], QT[:], So[:], start=True, stop=False, reads=[QT, So], writes=[py])
            for h in range(2):
                hr = slice(h * 64, (h + 1) * 64)
                ys = py[:, h * 64:(h + 1) * 64]
                k.mm(ys, wr[:, h, 0, :], X[:, 1, h, :], start=False, stop=False, reads=[wr, X], writes=[py])
                k.mm(ys, wr[:, h, 1, :], tm[:, 2, hr], start=False, stop=(h == 1), reads=[wr, tm], writes=[py])
            pS = k.ps()
            k.mm(pS[:, 0:128], GT[:], So[:], reads=[GT, So], writes=[pS])
            Sn = STp[cp].get()
            k.tt("vector", Sn[:], pS[:, 0:128], Hp[:], ALU.add, reads=[pS, Hp], writes=[Sn])
            ST[cp] = Sn
            ysb = sl.sq.get()
            k.copy("scalar", ysb[:], py[:, 0:128], reads=[py], writes=[ysb])
            pb_ = k.ps()
            k.mm(pb_[:, 0:2], RKR[:, cc], cst[:, C_BO:C_BO + 128:64], reads=[RKR, cst], writes=[pb_])
            bon = sl.smp.get()
            k.copy("scalar", bon[:], pb_[:, 0:2], reads=[pb_], writes=[bon])
            yield
            yc = sl.sq.get()
            yn = sl.sq.get()
            v3 = lambda t: t[:].rearrange("p (h v) -> p h v", h=2)
            st_ = sl.smp.get()
            k.S.op("vector", (lambda e, o=st_[:, 0:2], i_=v3(ysb): e.tensor_reduce(out=o, in_=i_, axis=AX.X, op=ALU.add)),
                   reads=[ysb], writes=[st_])
            nm = sl.smp.get()
            k.ts("vector", nm[:, 0:2], st_[:, 0:2], -1.0 / 64, None, ALU.mult, reads=[st_], writes=[nm])
            k.tt("vector", v3(yc), v3(ysb), bc(nm[:, 0:2], 64), ALU.add, reads=[ysb, nm], writes=[yc])
            s2 = sl.smp.get()
            jk = sl.ep.get()
            for h in range(2):
                hs = slice(h * 64, (h + 1) * 64)
                k.act(jk[:, hs], yc[:, hs], AF.Square, accum_out=s2[:, h:h + 1], reads=[yc], writes=[jk, s2])
            rs2 = sl.smp.get()
            k.act(rs2[:, 0:2], s2[:, 0:2], AF.Sqrt, bias=GN_EPS, scale=1.0 / 64, reads=[s2], writes=[rs2])
            k.recip(rs2[:, 0:2], rs2[:, 0:2], reads=[rs2], writes=[rs2])
            hg0 = cp * 2
            k.tt("vector", v3(yn), v3(yc), bc(rs2[:, 0:2], 64), ALU.mult, reads=[yc, rs2], writes=[yn])
            k.tt("gpsimd", yn[:], yn[:], lnwb[:, hg0 * 64:(hg0 + 2) * 64], ALU.mult, reads=[yn, lnwb], writes=[yn])
            k.tt("gpsimd", yn[:], yn[:], lnwb[:, 256 + hg0 * 64:256 + (hg0 + 2) * 64], ALU.add, reads=[yn, lnwb], writes=[yn])
            bv = sl.sq.get()
            k.tt("gpsimd", v3(bv), tm[:, 2, :].rearrange("p (h v) -> p h v", h=2), bc(bon[:, 0:2], 64), ALU.mult,
                 reads=[tm, bon], writes=[bv])
            k.tt("vector", yn[:], yn[:], bv[:], ALU.add, reads=[yn, bv], writes=[yn])
            pgt = k.ps()
            pcs = slice(cp * 128, (cp + 1) * 128)
            k.mm(pgt[:, 0:128], ho.SX7[:, cc], gateA[:, pcs], start=True, stop=False, reads=[ho.SX7, gateA], writes=[pgt])
            k.mm(pgt[:, 0:128], ho.SX8[:, cc], gateB[:, pcs], start=False, stop=True, reads=[ho.SX8, gateB], writes=[pgt])
            ya = sl.sq.get()
            k.tt("vector", ya[:], yn[:], pgt[:, 0:128], ALU.mult, reads=[yn, pgt], writes=[ya])
            pT2 = k.ps()
            k.tr(pT2[:, 0:128], ya[:], identf, reads=[ya, cst], writes=[pT2])
            k.copy("scalar", cA[:, cp, cc], pT2[:, 0:128], reads=[pT2], writes=[cA])

        tasks = [(g, c4) for g in range(NG) for c4 in range(4)]
        a1 = {}
        a1_finished = set()

        def finish_a1(g):
            if g not in a1_finished:
                for _ in a1[g]:
                    pass
                a1_finished.add(g)
        a1[0] = a1gen(0, HO[0])
        finish_a1(0)
        if NG > 1:
            a1[1] = a1gen(1, HO[1])
        bg = [1] if NG > 1 else []
        cAs = {}
        active = []
        steps = {}
        remaining = {}
        done = set()
        nxt = 0
        while True:
            while nxt < len(tasks):
                g, c4 = tasks[nxt]
                ok_slots = nxt < 2 or (nxt - 2) in done
                ok_stag = nxt == 0 or (nxt - 1) in done or steps.get(nxt - 1, 0) >= 6
                if not (ok_slots and ok_stag):
                    break
                if c4 == 0:
                    finish_a1(g)
                    if g in bg:
                        bg.remove(g)
                    cAs[g] = catA.get()
                lane = nxt % 2
                for cp in range(2):
                    active.append((nxt, unit(slots[lane * 2 + cp], HO[g % 2], cp, c4, cAs[g])))
                steps[nxt] = 0
                remaining[nxt] = 2
                nxt += 1
            if not active:
                break
            for item in list(active):
                idx, gen = item
                try:
                    next(gen)
                except StopIteration:
                    active.remove(item)
                    remaining[idx] -= 1
                    if remaining[idx] == 0:
                        done.add(idx)
                        g, c4 = tasks[idx]
                        if c4 == 3:
                            k.dma("gpsimd", P.cat[0:256, g * 512:(g + 1) * 512].rearrange("(c p) t -> p c t", p=128),
                                  cAs[g][:], reads=[cAs[g]])
                            if g + 2 < NG:
                                a1[g + 2] = a1gen(g + 2, HO[g % 2])
                                bg.append(g + 2)
            for idx in set(i for i, _ in active):
                steps[idx] += 1
            if bg:
                gb_ = bg[0]
                try:
                    next(a1[gb_])
                except StopIteration:
                    a1_finished.add(gb_)
                    bg.remove(gb_)
        k.rot = None


def phase_tail(P, l, xsrc, last):
    k, cst = P.k, P.cst
    with k.scope() as wsc:
        woutb = wsc.tile([128, 8, 1024], BF16)
        wgb = wsc.tile([128, 8, DFF], BF16)
        wub = wsc.tile([128, 8, DFF], BF16)
        wdb = wsc.tile([128, 22, 1024], BF16)
        load_w2(P, woutb, P.w_out[l], 8, 2)
        wg_c = [Tl(wgb.t) for _ in range(22)]
        wu_c = [Tl(wub.t) for _ in range(22)]
        for c in range(22):
            cs = slice(c * 128, (c + 1) * 128)
            k.dma("gpsimd", wgb[:, :, cs], P.w_gate[l][:, :, cs], writes=[wg_c[c]])
            k.dma("gpsimd", wub[:, :, cs], P.w_up[l][:, :, cs], writes=[wu_c[c]])
        with k.scope() as sc:
            vec = sc.tile([128, NV], F32)
            k.dma("sync", vec[:], P.vecs[l], writes=[vec])
            gb = make_gb(P, sc, vec, V_GFFN)
            xts = sc.pool(2, [128, D], F32)
            x1s = sc.pool(2, [128, D], F32)
            cts = sc.pool(1, [128, 8, 512], BF16)
            catv = P.cat.rearrange("(k p) t -> p k t", p=128)
            ss = sc.tile([128, 1], F32)
            rstd = sc.tile([128, 1], F32)
            xn = sc.tile([128, D], BF16)
            junk = xn
            hTs = sc.pool(2, [128, 8, 512], BF16)
            sg = sc.pool(1, [128, 512], F32)
            ab = sc.pool(3, [128, 512], BF16)

            def norm_gen(g_, hT_):
                ct = cts.get()
                k.dma("sync", ct[:], catv[:, :, g_ * 512:(g_ + 1) * 512], writes=[ct])
                for tt in range(4):
                    rows = slice(g_ * 512 + tt * 128, g_ * 512 + (tt + 1) * 128)
                    xt = xts.get()
                    k.dma("sync", xt[:], xsrc[rows, :], writes=[xt])
                    x1t = x1s.get()
                    for nb in range(2):
                        ps = k.ps()
                        for kk in range(8):
                            k.mm(ps[:], ct[:, kk, tt * 128:(tt + 1) * 128], woutb[:, kk, nb * 512:(nb + 1) * 512],
                                 start=(kk == 0), stop=(kk == 7), reads=[ct, woutb], writes=[ps])
                        k.tt("vector", x1t[:, nb * 512:(nb + 1) * 512], ps[:], xt[:, nb * 512:(nb + 1) * 512], ALU.add,
                             reads=[ps, xt], writes=[x1t])
                    k.dma("sync", P.x1[rows, :], x1t[:], reads=[x1t])
                    rmsnorm_T(P, x1t, junk, ss, rstd, xn, P.pTt, gb, hT_[:, :, tt * 128:(tt + 1) * 128], hT_)
                    yield
            hT_next = hTs.get()
            for _ in norm_gen(0, hT_next):
                pass
            for g in range(NG):
                cols = slice(g * 512, (g + 1) * 512)
                hT = hT_next
                gen = None
                if g + 1 < NG:
                    hT_next = hTs.get()
                    gen = norm_gen(g + 1, hT_next)
                if g == 1:
                    load_w2(P, wdb, P.w_down[l], 22, 2)
                for c in range(22):
                    if gen is not None and c in (4, 9, 14, 19):
                        next(gen, None)
                    cs = slice(c * 128, (c + 1) * 128)
                    pg, pu = k.ps(), k.ps()
                    for kk in range(8):
                        k.mm(pg[:], wgb[:, kk, cs], hT[:, kk, :], start=(kk == 0), stop=(kk == 7), reads=[wg_c[c], hT], writes=[pg])
                    for kk in range(8):
                        k.mm(pu[:], wub[:, kk, cs], hT[:, kk, :], start=(kk == 0), stop=(kk == 7), reads=[wu_c[c], hT], writes=[pu])
                    s_ = sg.get()
                    k.act(s_[:], pg[:], AF.Silu, reads=[pg], writes=[s_])
                    a_ = ab.get()
                    k.tt("vector", a_[:], s_[:], pu[:], ALU.mult, reads=[s_, pu], writes=[a_])
                    k.dma("gpsimd", P.aT[cs, cols], a_[:], reads=[a_])
        with k.scope() as sc:
            gfb = sc.tile([128, D], F32)
            k.dma("sync", gfb[:], P.gfinb[:, :], writes=[gfb])
            ats = sc.pool(2, [128, 22, 256], BF16)
            xts = sc.pool(2, [128, D], F32)
            x2s = sc.pool(2, [128, D], F32)
            junk = sc.tile([128, D], BF16)
            ss = sc.pool(2, [128, 1], F32)
            aTv = P.aT.rearrange("(k p) t -> p k t", p=128)
            for hg in range(2 * NG):
                at = ats.get()
                k.dma("sync", at[:], aTv[:, :, hg * 256:(hg + 1) * 256], writes=[at])
                for tt in range(2):
                    rows = slice(hg * 256 + tt * 128, hg * 256 + (tt + 1) * 128)
                    xt = xts.get()
                    k.dma("sync", xt[:], P.x1[rows, :], writes=[xt])
                    x2 = x2s.get()
                    for nb in range(2):
                        ps = k.ps()
                        for c in range(22):
                            k.mm(ps[:], at[:, c, tt * 128:(tt + 1) * 128], wdb[:, c, nb * 512:(nb + 1) * 512],
                                 start=(c == 0), stop=(c == 21), reads=[at, wdb], writes=[ps])
                        k.tt("vector", x2[:, nb * 512:(nb + 1) * 512], ps[:], xt[:, nb * 512:(nb + 1) * 512], ALU.add,
                             reads=[ps, xt], writes=[x2])
                    if not last:
                        k.dma("gpsimd", P.xr[rows, :], x2[:], reads=[x2])
                    else:
                        s1 = ss.get()
                        k.act(junk[:], x2[:], AF.Square, accum_out=s1[:], reads=[x2], writes=[junk, s1])
                        k.act(s1[:], s1[:], AF.Sqrt, bias=EPS, scale=1.0 / D, reads=[s1], writes=[s1])
                        k.recip(s1[:], s1[:], reads=[s1], writes=[s1])
                        k.stt("vector", x2[:], x2[:], s1[:, 0:1], gfb[:], ALU.mult, ALU.mult, reads=[x2, s1, gfb], writes=[x2])
                        k.dma("gpsimd", P.out[rows, :], x2[:], reads=[x2])


def make_in_maps(inputs):
    f = lambda a: np.ascontiguousarray(np.asarray(a))
    inp = {kk: f(v) for kk, v in inputs.items()}
    shared = {
        "consts": make_consts(),
        "vecs": np.stack([make_vecs(inp, l) for l in range(L)]),
        "lnwb": np.stack([np.broadcast_to(np.concatenate([inp["a_ln_w"][l], inp["a_ln_b"][l]])[None, :], (128, 512)).copy()
                          for l in range(L)]),
        "gfinb": np.broadcast_to(inp["final_norm_g"][None, :], (128, D)).copy(),
        "w_in": np.stack([ktile(inp["w_in"][l]) for l in range(L)]),
        "w_uq": np.stack([ktile(inp["c_w_uq"][l]) for l in range(L)]),
        "w_ukv": np.stack([ktile(np.concatenate(
            [inp["c_w_ukv"][l].reshape(128, 4, 256)[:, :, :128].reshape(128, 512),
             inp["c_w_ukv"][l].reshape(128, 4, 256)[:, :, 128:].reshape(128, 512)], axis=1)) for l in range(L)]),
        "w_out": np.stack([ktile(inp["w_out"][l]) for l in range(L)]),
        "w_gate": np.stack([ktile(inp["ffn_w_gate"][l]) for l in range(L)]),
        "w_up": np.stack([ktile(inp["ffn_w_up"][l]) for l in range(L)]),
        "w_down": np.stack([ktile(inp["ffn_w_down"][l]) for l in range(L)]),
        "a_dec": inp["a_decay_up"], "a_icl": inp["a_iclr_up"], "a_gate": inp["a_gate_up"],
    }
    maps = []
    for b in range(8):
        m = dict(shared)
        m["x"] = inp["x"][b]
        m["pos"] = inp["positions"][b].reshape(1, S_LEN).astype(np.int32)
        maps.append(m)
    return maps


def kernel(**inputs):
    nc = build()
    maps = make_in_maps(inputs)
    res = run_bass_kernel_spmd(nc, maps, core_ids=list(range(8)))
    return np.stack([np.asarray(r["out"]) for r in res.results]).astype(np.float32)
```

```python
import contextlib
import math
import numpy as np
import concourse.bass as bass
import concourse.mybir as mybir
from concourse.bass_utils import run_bass_kernel_spmd

F32 = mybir.dt.float32
BF16 = mybir.dt.bfloat16
I32 = mybir.dt.int32
AF = mybir.ActivationFunctionType
ALU = mybir.AluOpType

S_LEN = 4096
D = 1024
DFF = 2816
L = 2
NG = 8
EPS = 1e-6
THETA = 500000.0
MAGIC = 12582912.0

ENGS = ("tensor", "vector", "scalar", "gpsimd", "sync")
NSLOTS = {"sync": 12, "gpsimd": 48}


class Buf:
    __slots__ = ("w", "r")

    def __init__(self):
        self.w = {}
        self.r = []


class Tl:
    __slots__ = ("t", "b")

    def __init__(self, t):
        self.t = t
        self.b = Buf()

    def __getitem__(self, idx):
        return self.t[idx]


class Sched:
    def __init__(self, nc, sems):
        self.nc = nc
        self.lists = {k: [] for k in ENGS}
        self.cnt = {k: 0 for k in ENGS}
        self.seen = {k: {} for k in ENGS}
        self.semh = sems
        self.dq = {q: {"n": 0, "slots": [f"dma_{q}_{i}" for i in range(NSLOTS[q])]} for q in ("sync", "gpsimd")}

    def _waits(self, eng, reads, writes, merge=False):
        deps = {}

        def need(ev):
            if ev is None:
                return
            k, v = ev
            if k == "tensor" and eng == "tensor":
                return
            if deps.get(k, 0) < v:
                deps[k] = v
        for b in reads:
            for ev in b.w.items():
                need(ev)
        for b in writes:
            if not merge:
                for ev in b.w.items():
                    need(ev)
            for ev in b.r:
                need(ev)
        out = []
        seen = self.seen[eng]
        for k, v in deps.items():
            if seen.get(k, 0) < v:
                seen[k] = v
                out.append((k, v))
        return out

    def _mark(self, ev, reads, writes, merge=False):
        for b in reads:
            b.r.append(ev)
        for b in writes:
            if merge:
                b.w[ev[0]] = max(b.w.get(ev[0], 0), ev[1])
            else:
                b.w = {ev[0]: ev[1]}
                b.r = []

    def op(self, eng, emit, reads=(), writes=()):
        reads = [x.b if isinstance(x, Tl) else x for x in reads]
        writes = [x.b if isinstance(x, Tl) else x for x in writes]
        waits = self._waits(eng, reads, writes)
        self.cnt[eng] += 1
        ev = (eng, self.cnt[eng])
        semh = self.semh
        mysem = semh[eng]

        def run(e):
            for k, v in waits:
                e.wait_ge(semh[k], v)
            emit(e).then_inc(mysem, 1)
        self.lists[eng].append(run)
        self._mark(ev, reads, writes)

    def dma(self, q, out, in_, reads=(), writes=(), merge=False):
        reads = [x.b if isinstance(x, Tl) else x for x in reads]
        writes = [x.b if isinstance(x, Tl) else x for x in writes]
        st = self.dq[q]
        n = st["n"]
        st["n"] += 1
        slots = st["slots"]
        key = slots[n % len(slots)]
        use = n // len(slots)
        waits = self._waits(q, reads, writes, merge)
        seen = self.seen[q]
        if use > 0 and seen.get(key, 0) < 16 * use:
            seen[key] = 16 * use
            waits.append((key, 16 * use))
        ev = (key, 16 * (use + 1))
        semh = self.semh

        def run(e):
            for k, v in waits:
                e.wait_ge(semh[k], v)
            e.dma_start(out=out, in_=in_).then_inc(semh[key], 16)
        self.lists[q].append(run)
        self._mark(ev, reads, writes, merge)

    def barrier(self):
        targets = {k: self.cnt[k] for k in ENGS if self.cnt[k] > 0}
        for q, st in self.dq.items():
            n = st["n"]
            ns = len(st["slots"])
            for i, key in enumerate(st["slots"]):
                uses = (n - i + ns - 1) // ns if n > i else 0
                if uses > 0:
                    targets[key] = 16 * uses
        semh = self.semh
        for eng in ENGS:
            seen = self.seen[eng]
            waits = []
            for k, v in targets.items():
                if k == eng:
                    continue
                if seen.get(k, 0) < v:
                    seen[k] = v
                    waits.append((k, v))

            def run(e, waits=waits):
                for k, v in waits:
                    e.wait_ge(semh[k], v)
            self.lists[eng].append(run)

    def emit(self):
        with self.nc.Block() as block:
            for name in ENGS:
                lst = self.lists[name]

                def body(e, lst=lst):
                    for f in lst:
                        f(e)
                getattr(block, name)(body)


class K:
    def __init__(self, nc, S):
        self.nc = nc
        self.S = S
        self.uid = 0
        self.psum = []
        self.pi = 0
        self.rot = None

    def name(self, p="t"):
        self.uid += 1
        return f"{p}{self.uid}"

    @contextlib.contextmanager
    def scope(self):
        es = contextlib.ExitStack()
        sc = Scope(self, es)
        try:
            yield sc
            self.S.barrier()
        finally:
            es.close()

    def ps(self):
        rot = self.rot if self.rot else list(range(7))
        t = self.psum[rot[self.pi % len(rot)]]
        self.pi += 1
        return t

    def mm(self, out, lhsT, rhs, start=True, stop=True, reads=(), writes=(), skip=False):
        if skip:
            self.S.op("tensor", lambda e: e.matmul(out, lhsT=lhsT, rhs=rhs, start=start, stop=stop, skip_group_check=True), reads, writes)
        else:
            self.S.op("tensor", lambda e: e.matmul(out, lhsT=lhsT, rhs=rhs, start=start, stop=stop), reads, writes)

    def tr(self, out, in_, ident, reads=(), writes=()):
        self.S.op("tensor", lambda e: e.transpose(out=out, in_=in_, identity=ident), reads, writes)

    def act(self, out, in_, func, bias=0.0, scale=1.0, accum_out=None, reads=(), writes=()):
        if accum_out is None:
            self.S.op("scalar", lambda e: e.activation(out=out, in_=in_, func=func, bias=bias, scale=scale), reads, writes)
        else:
            self.S.op("scalar", lambda e: e.activation(out=out, in_=in_, func=func, bias=bias, scale=scale,
                                                       accum_out=accum_out), reads, writes)

    def tt(self, eng, out, in0, in1, op, reads=(), writes=()):
        self.S.op(eng, lambda e: e.tensor_tensor(out=out, in0=in0, in1=in1, op=op), reads, writes)

    def ts(self, eng, out, in0, s1, s2=None, op0=ALU.mult, op1=None, reads=(), writes=()):
        if op1 is None:
            self.S.op(eng, lambda e: e.tensor_scalar(out=out, in0=in0, scalar1=s1, scalar2=None, op0=op0), reads, writes)
        else:
            self.S.op(eng, lambda e: e.tensor_scalar(out=out, in0=in0, scalar1=s1, scalar2=s2, op0=op0, op1=op1), reads, writes)

    def stt(self, eng, out, in0, scalar, in1, op0, op1, reads=(), writes=()):
        self.S.op(eng, lambda e: e.scalar_tensor_tensor(out=out, in0=in0, scalar=scalar, in1=in1, op0=op0, op1=op1), reads, writes)

    def copy(self, eng, out, in_, reads=(), writes=()):
        if eng == "scalar":
            self.S.op(eng, lambda e: e.activation(out=out, in_=in_, func=AF.Copy), reads, writes)
        else:
            self.S.op(eng, lambda e: e.tensor_copy(out=out, in_=in_), reads, writes)

    def recip(self, out, in_, reads=(), writes=()):
        self.S.op("vector", lambda e: e.reciprocal(out=out, in_=in_), reads, writes)

    def memset(self, eng, out, val, writes=()):
        self.S.op(eng, lambda e: e.memset(out, val), (), writes)

    def dma(self, q, out, in_, reads=(), writes=(), merge=False):
        self.S.dma(q, out, in_, reads, writes, merge)


class Scope:
    def __init__(self, k, es):
        self.k = k
        self.es = es

    def tile(self, shape, dtype, name="t"):
        t = self.es.enter_context(self.k.nc.sbuf_tensor(self.k.name(name), list(shape), dtype))
        return Tl(t)

    def pool(self, n, shape, dtype, name="p"):
        return Pool([self.tile(shape, dtype, name) for _ in range(n)])


class Pool:
    def __init__(self, tiles):
        self.tiles = tiles
        self.i = 0

    def get(self):
        t = self.tiles[self.i % len(self.tiles)]
        self.i += 1
        return t


C_ID, C_TRI, C_PB, C_PC, C_FC, C_BO, C_ONE, C_NB = 0, 128, 640, 768, 896, 898, 1026, 1154
NCONST = 1154 + 256
NEG = -30000.0


def make_consts():
    c = np.zeros((128, NCONST), np.float32)
    c[:, C_ID:C_ID + 128] = np.eye(128)
    r = np.arange(128)[:, None]
    q = np.arange(128)[None, :]
    c[:, C_TRI + 0:C_TRI + 128] = (q >= r)
    c[:, C_TRI + 128:C_TRI + 256] = (q <= r)
    c[:, C_TRI + 256:C_TRI + 384] = (q > r)
    c[:, C_TRI + 384:C_TRI + 512] = (q < r)
    pb = np.zeros((128, 128), np.float32)
    for rr in range(128):
        d = rr % 64
        if d < 8:
            pb[rr + 8, rr] = -1.0
        elif d < 16:
            pb[rr - 8, rr] = 1.0
    c[:, C_PB:C_PB + 128] = pb
    pc = np.zeros((128, 128), np.float32)
    for rr in range(64):
        if rr < 32:
            pc[rr + 32, rr] = -1.0
        else:
            pc[rr - 32, rr] = 1.0
    c[:, C_PC:C_PC + 128] = pc
    invb = 1.0 / (THETA ** (np.arange(0, 16, 2, dtype=np.float32) / 16))
    invc = 1.0 / (THETA ** (np.arange(0, 64, 2, dtype=np.float32) / 64))
    for rr in range(128):
        d = rr % 64
        c[rr, C_FC] = invb[d % 8] / (2 * math.pi) if d < 16 else 0.0
        c[rr, C_FC + 1] = invc[rr % 32] / (2 * math.pi)
    bo = np.zeros((128, 128), np.float32)
    bo[:64, :64] = 1
    bo[64:, 64:] = 1
    c[:, C_BO:C_BO + 128] = bo
    c[:, C_ONE:C_ONE + 128] = 1.0
    c[:, C_NB:C_NB + 128] = np.where(q >= r, 0.0, NEG)
    c[:, C_NB + 128:C_NB + 256] = np.where(q <= r, 0.0, NEG)
    return c


V_GATT, V_GFFN, V_GFIN, V_CQG, V_CKVG, V_MU, V_W0, V_A0, V_KK, V_KA, V_RK = 0, 8, 16, 24, 26, 27, 36, 38, 40, 42, 44
NV = 46


def col(v, n):
    return np.ascontiguousarray(v.reshape(n, 128).T)


def make_vecs(inp, l):
    v = np.zeros((128, NV), np.float32)
    v[:, V_GATT:V_GATT + 8] = col(inp["attn_norm_g"][l], 8)
    v[:, V_GFFN:V_GFFN + 8] = col(inp["ffn_norm_g"][l], 8)
    v[:, V_GFIN:V_GFIN + 8] = col(inp["final_norm_g"], 8)
    v[:, V_CQG:V_CQG + 2] = col(inp["c_q_norm_g"][l], 2)
    v[:, V_CKVG:V_CKVG + 1] = col(inp["c_kv_norm_g"][l], 1)
    mu = np.zeros(9 * 128, np.float32)
    mu[:1056] = inp["a_mu"][l]
    v[:, V_MU:V_MU + 9] = col(mu, 9)
    v[:, V_W0:V_W0 + 2] = col(inp["a_w0"][l], 2)
    v[:, V_A0:V_A0 + 2] = col(inp["a_a0"][l], 2)
    v[:, V_KK:V_KK + 2] = col(inp["a_k_k"][l], 2)
    v[:, V_KA:V_KA + 2] = col(inp["a_k_a"][l], 2)
    v[:, V_RK:V_RK + 2] = col(inp["a_r_k"][l].reshape(-1), 2)
    return v


def ktile(w):
    kk = w.shape[0] // 128
    return np.ascontiguousarray(w.reshape(kk, 128, w.shape[1]).transpose(1, 0, 2))


class Prog:
    pass


def build(debug=False, stop=None, nlayers=L, skip=()):
    nc = bass.Bass("TRN2", target_bir_lowering=False)
    P = Prog()
    din = {}

    def inp(name, shape, dt=F32):
        din[name] = nc.dram_tensor(name, list(shape), dt, kind="ExternalInput").ap()
        return din[name]

    skind = "ExternalOutput" if debug else "Internal"
    dscr = {}

    def scr(name, shape, dt):
        dscr[name] = nc.dram_tensor(name, list(shape), dt, kind=skind).ap()
        return dscr[name]

    x_in = inp("x", [S_LEN, D])
    pos = inp("pos", [1, S_LEN], I32)
    consts = inp("consts", [128, NCONST])
    vecs = inp("vecs", [L, 128, NV])
    lnwb = inp("lnwb", [L, 128, 512])
    gfinb = inp("gfinb", [128, D])
    w_in = inp("w_in", [L, 128, 8, 2272])
    w_uq = inp("w_uq", [L, 128, 2, 768])
    w_ukv = inp("w_ukv", [L, 128, 1, 1024])
    w_out = inp("w_out", [L, 128, 8, 1024])
    w_gate = inp("w_gate", [L, 128, 8, DFF])
    w_up = inp("w_up", [L, 128, 8, DFF])
    w_down = inp("w_down", [L, 128, 22, 1024])
    a_dec = inp("a_dec", [L, 64, 256])
    a_icl = inp("a_icl", [L, 64, 256])
    a_gate = inp("a_gate", [L, 160, 256])
    out = nc.dram_tensor("out", [S_LEN, D], F32, kind="ExternalOutput").ap()

    pA = scr("pA", [1056, S_LEN], F32)
    qB = [scr(f"qB{i}", [256, S_LEN], BF16) for i in range(3)]
    kB = [scr(f"kB{i}", [256, S_LEN], BF16) for i in range(3)]
    vB = scr("vB", [S_LEN, 256], BF16)
    qn = scr("qn", [512, S_LEN], BF16)
    qr = scr("qr", [256, S_LEN], BF16)
    kn = scr("kn", [512, S_LEN], BF16)
    kr = scr("kr", [64, S_LEN], BF16)
    vC = scr("vC", [S_LEN, 512], BF16)
    cat = scr("cat", [1024, S_LEN], BF16)
    x1 = scr("x1", [S_LEN, D], F32)
    xr = scr("xr", [S_LEN, D], F32)
    aT = scr("aT", [DFF, S_LEN], BF16)
    tabB = scr("tabB", [2, 128, S_LEN], F32)
    tabC = scr("tabC", [2, 64, S_LEN], F32)

    with contextlib.ExitStack() as es:
        E = es.enter_context
        names = list(ENGS) + [f"dma_{q}_{i}" for q in ("sync", "gpsimd") for i in range(NSLOTS[q])]
        sems = {n: E(nc.semaphore(n)) for n in names}
        S = Sched(nc, sems)
        k = K(nc, S)
        k.psum = [Tl(E(nc.psum_tensor(f"ps{i}", [128, 512], F32))) for i in range(8)]
        pTt = Tl(k.psum[7][:].bitcast(BF16).rearrange("p (a b) -> p a b", a=8))
        pTt.b = k.psum[7].b
        cst = Tl(E(nc.sbuf_tensor("cst", [128, NCONST], F32)))
        cstb = Tl(E(nc.sbuf_tensor("cstb", [128, NCONST], BF16)))
        k.dma("sync", cst[:], consts[:, :], writes=[cst])
        k.copy("vector", cstb[:], cst[:], reads=[cst], writes=[cstb])
        epsc = Tl(E(nc.sbuf_tensor("epsc", [128, 1], F32)))
        k.memset("vector", epsc[:], EPS, writes=[epsc])
        P.__dict__.update(locals())

        phase_tables(P)
        done = (stop == "tables")
        for l in range(nlayers):
            if done:
                break
            xsrc = x_in if l == 0 else xr
            phase_p1(P, l, xsrc)
            if stop == f"p1_{l}":
                break
            if "mla" not in skip:
                phase_mla(P, l)
            if stop == f"mla_{l}":
                break
            if "dil" not in skip:
                phase_dil(P, l)
            if stop == f"dil_{l}":
                break
            phase_rwkv(P, l)
            if stop == f"rwkv_{l}":
                break
            phase_tail(P, l, xsrc, last=(l == nlayers - 1))
        S.barrier()
        S.emit()
    return nc


def phase_tables(P):
    k, cst = P.k, P.cst
    with k.scope() as sc:
        posi = sc.tile([128, S_LEN], I32)
        posf = sc.tile([128, S_LEN], F32)
        y = sc.tile([128, S_LEN], F32)
        t = sc.tile([128, S_LEN], F32)
        r = sc.tile([128, S_LEN], F32)
        o = sc.tile([128, S_LEN], F32)
        src = bass.AP(P.pos.tensor, 0, [[0, 128], [1, S_LEN]])
        k.dma("sync", posi[:], src, writes=[posi])
        k.copy("vector", posf[:], posi[:], reads=[posi], writes=[posf])
        for ti, (fc, rows, dst) in enumerate(((C_FC, 128, P.tabB), (C_FC + 1, 64, P.tabC))):
            for cs in range(2):
                add = 0.25 if cs == 0 else 0.0
                k.ts("vector", y[:rows], posf[:rows], cst[:rows, fc:fc + 1], add, ALU.mult, ALU.add,
                     reads=[posf, cst], writes=[y])
                k.ts("vector", t[:rows], y[:rows], MAGIC, None, ALU.add, reads=[y], writes=[t])
                k.ts("vector", t[:rows], t[:rows], -MAGIC, None, ALU.add, reads=[t], writes=[t])
                k.tt("vector", r[:rows], y[:rows], t[:rows], ALU.subtract, reads=[y, t], writes=[r])
                k.act(o[:rows], r[:rows], AF.Sin, scale=6.28318, reads=[r], writes=[o])
                k.dma("sync", dst[cs, :, :], o[:rows], reads=[o])


def load_w(P, sc, dst, src, kc, n, stage, eng="gpsimd"):
    k = P.k
    for i in range(kc):
        st = stage.get()
        k.dma("sync", st[:, 0:n], src[:, i, :], writes=[st])
        k.copy(eng, dst[:, i, :], st[:, 0:n], reads=[st], writes=[dst])


def load_w2(P, dst, src, kc, nsplit):
    step = (kc + nsplit - 1) // nsplit
    for i in range(0, kc, step):
        j = min(kc, i + step)
        P.k.dma("gpsimd", dst[:, i:j, :], src[:, i:j, :], writes=[dst], merge=True)


def rmsnorm_T(P, xt, junk, ss, rstd, xn, pT, gb, hT_ap, hT):
    k = P.k
    k.act(junk[:], xt[:], AF.Square, accum_out=ss[:], reads=[xt], writes=[junk, ss])
    k.act(rstd[:], ss[:], AF.Sqrt, bias=EPS, scale=1.0 / D, reads=[ss], writes=[rstd])
    k.recip(rstd[:], rstd[:], reads=[rstd], writes=[rstd])
    k.ts("vector", xn[:], xt[:], rstd[:, 0:1], None, ALU.mult, reads=[xt, rstd], writes=[xn])
    for kk in range(8):
        k.tr(pT[:, kk, :], xn[:, kk * 128:(kk + 1) * 128], P.cstb[:, C_ID:C_ID + 128], reads=[xn, P.cstb], writes=[pT])
    k.tt("vector", hT_ap, pT[:], gb[:], ALU.mult, reads=[pT, gb], writes=[hT])


def make_gb(P, sc, vec, c0):
    k = P.k
    gb = sc.tile([128, 8, 128], BF16)
    for kk in range(8):
        k.ts("vector", gb[:, kk, :], P.cst[:, C_ONE:C_ONE + 128], vec[:, c0 + kk:c0 + kk + 1], None, ALU.mult,
             reads=[P.cst, vec], writes=[gb])
    return gb


def rope(P, ps, rows, permT, cos, sin, scale, tmp, outs, srcs, after=None):
    k = P.k
    qs = tmp.get()
    k.act(qs[:rows], ps[:rows], AF.Copy, scale=scale, reads=[ps], writes=[qs])

    def second():
        rope_b(P, qs, rows, permT, cos, sin, tmp, outs, srcs)
        if after is not None:
            after()
    return second


def rope_b(P, qs, rows, permT, cos, sin, tmp, outs, srcs):
    k = P.k
    pp = k.ps()
    k.mm(pp[:rows], permT, qs[:rows], reads=[qs, P.cst], writes=[pp])
    t1 = tmp.get()
    k.tt("vector", t1[:rows], qs[:rows], cos, ALU.mult, reads=[qs] + srcs, writes=[t1])
    t2 = tmp.get()
    k.tt("vector", t2[:rows], pp[:rows], sin, ALU.mult, reads=[pp] + srcs, writes=[t2])
    for eng, out_ap, view, tl in outs:
        k.tt(eng, out_ap, view(t1[:rows]), view(t2[:rows]), ALU.add, reads=[t1, t2], writes=[tl])


def phase_p1(P, l, xsrc):
    k, cst, cstb = P.k, P.cst, P.cstb
    with k.scope() as sc:
        winb = sc.tile([128, 8, 2272], BF16)
        wuqb = sc.tile([128, 2, 768], BF16)
        wukvb = sc.tile([128, 1, 1024], BF16)
        vec = sc.tile([128, NV], F32)
        k.dma("sync", vec[:], P.vecs[l], writes=[vec])
        load_w2(P, winb, P.w_in[l], 8, 4)
        load_w2(P, wuqb, P.w_uq[l], 2, 1)
        load_w2(P, wukvb, P.w_ukv[l], 1, 1)
        gb = make_gb(P, sc, vec, V_GATT)
        xts = sc.pool(3, [128, D], F32)
        junk = sc.tile([128, D], BF16)
        ss = sc.tile([128, 1], F32)
        rstd = sc.tile([128, 1], F32)
        xn = sc.tile([128, D], BF16)
        hTs = sc.pool(2, [128, 8, 512], BF16)
        pTt = P.pTt
        stA = sc.pool(3, [128, 512], F32)
        tmp = sc.pool(6, [128, 512], F32)
        ob = sc.pool(8, [128, 512], BF16)
        o16 = [[sc.tile([128, 16, 128], BF16) for _ in range(4)] for _ in range(2)]
        tabs = sc.pool(2, [128, 4, 512], F32)
        vbt = sc.pool(2, [128, 4, 256], BF16)
        vct = sc.pool(2, [128, 4, 512], BF16)
        csb = sc.tile([128, 3, 512], F32)
        sqb = sc.tile([128, 3, 512], F32)
        rs = sc.pool(2, [128, 512], F32)
        cqn = sc.tile([128, 3, 512], BF16)
        ones = cst[:, C_ONE:C_ONE + 128]
        A_CH = [(i * 128, 128) for i in range(8)] + [(1024, 32)]

        def norm_gen(g_, hT_):
            for tt in range(4):
                xt = xts.get()
                k.dma("sync", xt[:], xsrc[g_ * 512 + tt * 128: g_ * 512 + (tt + 1) * 128, :], writes=[xt])
                rmsnorm_T(P, xt, junk, ss, rstd, xn, pTt, gb, hT_[:, :, tt * 128:(tt + 1) * 128], hT_)
                yield
        hT_next = hTs.get()
        for _ in norm_gen(0, hT_next):
            pass
        for g in range(NG):
            cols = slice(g * 512, (g + 1) * 512)
            hT = hT_next
            gen = None
            if g + 1 < NG:
                hT_next = hTs.get()
                gen = norm_gen(g + 1, hT_next)

            def step(gen=gen):
                if gen is not None:
                    next(gen, None)
            tb = tabs.get()
            k.dma("sync", tb[:, 0:2, :], P.tabB[:, :, cols].rearrange("c p t -> p c t"), writes=[tb])
            k.dma("sync", tb[0:64, 2:4, :], P.tabC[:, :, cols].rearrange("c p t -> p c t"), writes=[tb])

            def proj(c0, m):
                ps = k.ps()
                for kk in range(8):
                    k.mm(ps[:m], winb[:, kk, c0:c0 + m], hT[:, kk, :], start=(kk == 0), stop=(kk == 7),
                         reads=[winb, hT], writes=[ps])
                return ps
            pend = []

            def flush():
                while pend:
                    pend.pop(0)()

            def a_chunk(ci):
                c0, m = A_CH[ci]
                ps = proj(c0, m)
                st = stA.get()
                k.copy("scalar", st[:m], ps[:m], reads=[ps], writes=[st])
                k.dma("gpsimd", P.pA[c0:c0 + m, cols], st[:m], reads=[st])
                if ci in (2, 5, 8):
                    step()
            for j, (c0, m) in enumerate(((1824, 128), (1952, 128), (2080, 128))):
                ps = proj(c0, m)
                k.copy("scalar", csb[:, j, :], ps[:], reads=[ps], writes=[csb])
                k.tt("gpsimd", sqb[:, j, :], csb[:, j, :], csb[:, j, :], ALU.mult, reads=[csb], writes=[sqb])
            for ci in range(3):
                a_chunk(ci)
            for which in range(2):
                js = (0, 1) if which == 0 else (2,)
                nfeat = 256.0 if which == 0 else 128.0
                ps = k.ps()
                for i, j in enumerate(js):
                    k.mm(ps[:], ones, sqb[:, j, :], start=(i == 0), stop=(i == len(js) - 1), reads=[cst, sqb], writes=[ps])
                r_ = rs.get()
                k.act(r_[:], ps[:], AF.Ln, bias=P.epsc[:, 0:1], scale=1.0 / nfeat, reads=[ps, P.epsc], writes=[r_])
                k.act(r_[:], r_[:], AF.Exp, scale=-0.5, reads=[r_], writes=[r_])
                for j in js:
                    gcol = vec[:, V_CQG + j:V_CQG + j + 1]
                    k.stt("vector", cqn[:, j, :], csb[:, j, :], gcol, r_[:], ALU.mult, ALU.mult,
                          reads=[csb, vec, r_], writes=[cqn])
            for ci in range(3, 9):
                a_chunk(ci)
            for qk in range(2):
                for ch in range(2):
                    c0 = 1056 + qk * 256 + ch * 128
                    ps = proj(c0, 128)
                    flush()
                    o1 = ob.get()
                    o4 = ob.get()
                    o16t = o16[qk][ch * 2 + 0]
                    off = (g % 4) * 32
                    outs = [
                        ("vector", o1[:], (lambda v: v), o1),
                        ("gpsimd", o4[:].rearrange("p (r l) -> p r l", r=4), (lambda v: v.rearrange("p (l r) -> p r l", r=4)), o4),
                        ("gpsimd", o16t[:, :, off:off + 32], (lambda v: v.rearrange("p (l r) -> p r l", r=16)), o16t),
                    ]
                    dst = P.qB if qk == 0 else P.kB
                    rows = slice(ch * 128, (ch + 1) * 128)

                    def after(dst=dst, rows=rows, o1=o1, o4=o4, o16t=o16t):
                        k.dma("gpsimd", dst[0][rows, cols], o1[:], reads=[o1])
                        k.dma("gpsimd", dst[1][rows, cols], o4[:], reads=[o4])
                        if g % 4 == 3:
                            sp = g // 4
                            k.dma("gpsimd", dst[2][rows, sp * 2048:(sp + 1) * 2048], o16t[:].rearrange("p r l -> p (r l)"), reads=[o16t])
                    pend.append(rope(P, ps, 128, cst[:, C_PB:C_PB + 128], tb[:, 0, :], tb[:, 1, :], (0.125 if qk == 0 else 1.0),
                                     tmp, outs, [tb], after))
            step()
            vb = vbt.get()
            for tt in range(4):
                ps = k.ps()
                for kk in range(8):
                    k.mm(ps[:, 0:256], hT[:, kk, tt * 128:(tt + 1) * 128], winb[:, kk, 1568:1824], start=(kk == 0), stop=(kk == 7),
                         reads=[winb, hT], writes=[ps])
                if tt == 0:
                    flush()
                k.copy("scalar", vb[:, tt, :], ps[:, 0:256], reads=[ps], writes=[vb])
            k.dma("gpsimd", P.vB[cols, :].rearrange("(t p) c -> p t c", p=128), vb[:], reads=[vb])
            qscale = 192.0 ** -0.5
            for h in range(4):
                ps = k.ps()
                for kc in range(2):
                    k.mm(ps[:], wuqb[:, kc, h * 192:h * 192 + 128], cqn[:, kc, :], start=(kc == 0), stop=(kc == 1),
                         reads=[wuqb, cqn], writes=[ps])
                o = ob.get()
                k.act(o[:], ps[:], AF.Copy, scale=qscale, reads=[ps], writes=[o])
                k.dma("gpsimd", P.qn[h * 128:(h + 1) * 128, cols], o[:], reads=[o])
                ps = k.ps()
                for kc in range(2):
                    k.mm(ps[0:64], wuqb[:, kc, h * 192 + 128:h * 192 + 192], cqn[:, kc, :], start=(kc == 0), stop=(kc == 1),
                         reads=[wuqb, cqn], writes=[ps])
                flush()
                o = ob.get()

                def after(o=o, h=h):
                    k.dma("gpsimd", P.qr[h * 64:(h + 1) * 64, cols], o[0:64], reads=[o])
                pend.append(rope(P, ps, 64, cst[0:64, C_PC:C_PC + 64], tb[0:64, 2, :], tb[0:64, 3, :], qscale, tmp,
                                 [("vector", o[0:64], (lambda v: v), o)], [tb], after))
            for h in range(4):
                ps = k.ps()
                k.mm(ps[:], wukvb[:, 0, h * 128:(h + 1) * 128], cqn[:, 2, :], reads=[wukvb, cqn], writes=[ps])
                if h == 0:
                    flush()
                o = ob.get()
                k.copy("scalar", o[:], ps[:], reads=[ps], writes=[o])
                k.dma("gpsimd", P.kn[h * 128:(h + 1) * 128, cols], o[:], reads=[o])
            ps = proj(2208, 64)
            o = ob.get()

            def after(o=o):
                k.dma("gpsimd", P.kr[:, cols], o[0:64], reads=[o])
            pend.append(rope(P, ps, 64, cst[0:64, C_PC:C_PC + 64], tb[0:64, 2, :], tb[0:64, 3, :], 1.0, tmp,
                             [("vector", o[0:64], (lambda v: v), o)], [tb], after))
            vc = vct.get()
            for tt in range(4):
                ps = k.ps()
                k.mm(ps[:], cqn[:, 2, tt * 128:(tt + 1) * 128], wukvb[:, 0, 512:1024], reads=[wukvb, cqn], writes=[ps])
                if tt == 0:
                    flush()
                k.copy("scalar", vc[:, tt, :], ps[:], reads=[ps], writes=[vc])
            k.dma("gpsimd", P.vC[cols, :].rearrange("(t p) c -> p t c", p=128), vc[:], reads=[vc])
            flush()


def phase_mla(P, l):
    k, cst, cstb = P.k, P.cst, P.cstb
    with k.scope() as sc:
        knp = sc.pool(2, [128, S_LEN], BF16)
        qnp = sc.pool(2, [128, S_LEN], BF16)
        qrp = sc.pool(2, [128, S_LEN], BF16)
        vp = sc.pool(2, [128, 32, 128], BF16)
        krt = sc.tile([128, S_LEN], BF16)
        for t_ in qrp.tiles + [krt]:
            k.memset("gpsimd", t_[64:128, :], 0.0, writes=[t_])
        pacc1p = sc.pool(3, [128, 512], F32)
        pTp = sc.pool(8, [128, 512], BF16)
        ob = sc.pool(2, [128, 512], BF16)
        rd = sc.pool(2, [128, 512], F32)
        paccp = sc.pool(3, [128, 512], F32)
        k.dma("sync", krt[0:64, :], P.kr[:, :], writes=[krt])
        ones = cst[:, C_ONE:C_ONE + 128]
        identb = cstb[:, C_ID:C_ID + 128]
        nbias = cstb[:, C_NB:C_NB + 128]
        onesb = cstb[:, C_ONE:C_ONE + 128]
        pending = []
        for h in range(4):
            knh, qnh, qrh, vh = knp.get(), qnp.get(), qrp.get(), vp.get()
            k.dma("sync", knh[:], P.kn[h * 128:(h + 1) * 128, :], writes=[knh])
            k.dma("sync", qnh[:], P.qn[h * 128:(h + 1) * 128, :], writes=[qnh])
            k.dma("sync", qrh[0:64, :], P.qr[h * 64:(h + 1) * 64, :], writes=[qrh])
            k.dma("sync", vh[:], P.vC[:, h * 128:(h + 1) * 128].rearrange("(t p) c -> p t c", p=128), writes=[vh])
            for g in range(NG):
                k.rot = [0, 1, 2, 3]
                po = k.psum[4 + (g % 2)]
                pden = k.psum[6 + (g % 2)]
                pe_den = [kt for kt in range(4 * g + 4) if kt % 3 == 1] if g > 0 else []
                pacc = paccp.get()
                pacc1 = pacc1p.get()
                k.memset("gpsimd", pacc1[:], 0.0, writes=[pacc1])
                nkt = 4 * g + 4

                def s_stage(kt, g=g, pacc=pacc, pacc1=pacc1, pe_den=pe_den):
                    o = kt - 4 * g
                    c0 = max(o, 0) * 128
                    ks = slice(kt * 128, (kt + 1) * 128)
                    qs = slice(g * 512 + c0, (g + 1) * 512)
                    pS = k.ps()
                    k.mm(pS[:, c0:512], knh[:, ks], qnh[:, qs], start=True, stop=False, reads=[knh, qnh], writes=[pS])
                    k.mm(pS[:, c0:512], krt[:, ks], qrh[:, qs], start=False, stop=(o < 0), reads=[krt, qrh], writes=[pS])
                    if o >= 0:
                        k.mm(pS[:, c0:c0 + 128], identb, nbias, start=False, stop=True, reads=[cstb], writes=[pS])
                    pT = pTp.get()
                    k.act(pT[:, c0:512], pS[:, c0:512], AF.Exp, reads=[pS], writes=[pT])
                    if kt == 0:
                        k.copy("vector", pacc[:], pT[:], reads=[pT], writes=[pacc])
                    elif kt in pe_den:
                        pass
                    elif kt % 3 == 2:
                        k.tt("gpsimd", pacc1[:, c0:512], pacc1[:, c0:512], pT[:, c0:512], ALU.add, reads=[pT, pacc1], writes=[pacc1])
                    else:
                        k.tt("vector", pacc[:, c0:512], pacc[:, c0:512], pT[:, c0:512], ALU.add, reads=[pT, pacc], writes=[pacc])
                    return kt, c0, pT

                def pv_stage(st, po=po, nkt=nkt, pden=pden, pe_den=pe_den):
                    kt, c0, pT = st
                    k.mm(po[:, c0:512], vh[:, kt, :], pT[:, c0:512], start=(kt == 0), stop=(kt == nkt - 1), reads=[vh, pT], writes=[po])
                    if kt in pe_den:
                        k.mm(pden[:, c0:512], onesb, pT[:, c0:512], start=(kt == pe_den[0]), stop=False, reads=[cstb, pT], writes=[pden],
                             skip=True)

                def finalize(g=g, h=h, po=po, pacc=pacc, pacc1=pacc1, pd=pden, pe_den=pe_den):
                    k.mm(pd[:], ones, pacc[:], start=(not pe_den), stop=False, reads=[cst, pacc], writes=[pd], skip=True)
                    k.mm(pd[:], ones, pacc1[:], start=False, stop=True, reads=[cst, pacc1], writes=[pd], skip=True)
                    r = rd.get()
                    k.act(r[:], pd[:], AF.Ln, reads=[pd], writes=[r])
                    k.act(r[:], r[:], AF.Exp, scale=-1.0, reads=[r], writes=[r])
                    oo = ob.get()
                    k.tt("vector", oo[:], po[:], r[:], ALU.mult, reads=[po, r], writes=[oo])
                    k.dma("gpsimd", P.cat[512 + h * 128:512 + (h + 1) * 128, g * 512:(g + 1) * 512], oo[:], reads=[oo])
                DEPTH = 2
                q_ = []
                for kt in range(nkt):
                    q_.append(s_stage(kt))
                    if kt == 2 and pending:
                        pending.pop()()
                    if len(q_) > DEPTH:
                        pv_stage(q_.pop(0))
                while q_:
                    pv_stage(q_.pop(0))
                pending.append(finalize)
        while pending:
            pending.pop()()
        k.rot = None


def phase_dil(P, l):
    k, cst, cstb = P.k, P.cst, P.cstb
    with k.scope() as sc:
        vts = [sc.tile([128, 32, 256], BF16) for _ in range(3)]
        vap = sc.pool(2, [128, 32, 128], BF16)
        qp = [sc.pool(2, [128, S_LEN], BF16) for _ in range(2)]
        kp = sc.pool(2, [128, S_LEN], BF16)
        acc = sc.tile([128, S_LEN], F32)
        pTp = sc.pool(8, [128, 512], BF16)
        ob = sc.pool(2, [64, S_LEN], BF16)
        rdp = sc.pool(2, [64, 512], F32)
        for par in range(2):
            for t_ in qp[par].tiles:
                k.memset("gpsimd", t_[(1 - par) * 64:(2 - par) * 64, :], 0.0, writes=[t_])
        for t_ in vap.tiles:
            k.memset("gpsimd", t_[:, :, 64:128], 1.0, writes=[t_])
        DS = (1, 4, 16)
        for pi, d in enumerate(DS):
            if d == 1:
                src = P.vB[:, :].rearrange("(s l) c -> l s c", l=128)
                k.dma("sync", vts[pi][:], src, writes=[vts[pi]])
            else:
                for r in range(d):
                    src = P.vB[:, :].rearrange("(s l r) c -> r l s c", l=128, r=d)[r]
                    dstv = vts[pi][:].rearrange("p (s r) c -> p r s c", r=d)[:, r]
                    k.dma("sync", dstv, src, writes=[vts[pi]], merge=True)
        identb = cstb[:, C_ID:C_ID + 128]
        nb_cur = cstb[:, C_NB:C_NB + 128]
        nb_prev = cstb[:, C_NB + 128:C_NB + 256]
        shiftT = cst[:, C_ID + 64:C_ID + 128]
        k.rot = [0, 1, 2, 3, 4, 5, 6]
        for h in range(4):
            hc = slice(h * 64, (h + 1) * 64)
            par = h % 2
            pr = slice((h // 2) * 128, (h // 2 + 1) * 128)
            for pi, d in enumerate(DS):
                qh, kh = qp[par].get(), kp.get()
                k.dma("sync", qh[par * 64:(par + 1) * 64, :], P.qB[pi][hc, :], writes=[qh])
                k.dma("sync", kh[:], P.kB[pi][pr, :], writes=[kh])
                vt = vap.get()
                k.copy("gpsimd", vt[:, :, 0:64], vts[pi][:, :, hc], reads=[vts[pi]], writes=[vt])

                def s_stage(bt, d=d, qh=qh, kh=kh):
                    b0 = bt * 4
                    pc, pp = k.ps(), k.ps()
                    hasp = [(b0 + j - d) >= 0 for j in range(4)]
                    for j in range(4):
                        cs = slice((b0 + j) * 128, (b0 + j + 1) * 128)
                        js = slice(j * 128, (j + 1) * 128)
                        k.mm(pc[:, js], kh[:, cs], qh[:, cs], start=True, stop=False, reads=[kh, qh], writes=[pc])
                        k.mm(pc[:, js], identb, nb_cur, start=False, stop=True, reads=[cstb], writes=[pc])
                        if hasp[j]:
                            ps_ = slice((b0 + j - d) * 128, (b0 + j - d + 1) * 128)
                            k.mm(pp[:, js], kh[:, ps_], qh[:, cs], start=True, stop=False, reads=[kh, qh], writes=[pp])
                            k.mm(pp[:, js], identb, nb_prev, start=False, stop=True, reads=[cstb], writes=[pp])
                    tc_ = pTp.get()
                    k.act(tc_[:], pc[:], AF.Exp, reads=[pc], writes=[tc_])
                    tp_ = None
                    if any(hasp):
                        j0 = hasp.index(True)
                        tp_ = pTp.get()
                        k.act(tp_[:, j0 * 128:512], pp[:, j0 * 128:512], AF.Exp, reads=[pp], writes=[tp_])
                    return bt, hasp, tc_, tp_

                def pv_stage(st, d=d, pi=pi, vt=vt):
                    bt, hasp, tc_, tp_ = st
                    b0 = bt * 4
                    pn = k.ps()
                    for j in range(4):
                        js = slice(j * 128, (j + 1) * 128)
                        if hasp[j]:
                            k.mm(pn[:, js], vt[:, b0 + j - d, :], tp_[:, js], start=True, stop=False, reads=[vt, tp_], writes=[pn])
                        k.mm(pn[:, js], vt[:, b0 + j, :], tc_[:, js], start=(not hasp[j]), stop=True, reads=[vt, tc_], writes=[pn])
                    if d == 1:
                        va = acc[:, b0 * 128:b0 * 128 + 512]
                        vpz = pn[:, :]
                    elif d == 4:
                        va = acc[:, bt * 512:(bt + 1) * 512].rearrange("p (l r) -> p r l", r=4)
                        vpz = pn[:, :].rearrange("p (r l) -> p r l", r=4)
                    else:
                        sp, r0 = b0 // 16, b0 % 16
                        va = acc[:, sp * 2048:(sp + 1) * 2048].rearrange("p (l r) -> p r l", r=16)[:, r0:r0 + 4, :]
                        vpz = pn[:, :].rearrange("p (r l) -> p r l", r=4)
                    if pi == 0:
                        k.copy("vector", va, vpz, reads=[pn], writes=[acc])
                    else:
                        k.tt("vector", va, va, vpz, ALU.add, reads=[pn, acc], writes=[acc])
                DEPTH = 2
                q_ = []
                for bt in range(8):
                    q_.append(s_stage(bt))
                    if len(q_) > DEPTH:
                        pv_stage(q_.pop(0))
                while q_:
                    pv_stage(q_.pop(0))
            oo = ob.get()
            for c8 in range(8):
                cs8 = slice(c8 * 512, (c8 + 1) * 512)
                pd = k.ps()
                k.mm(pd[0:64, :], shiftT, acc[:, cs8], reads=[cst, acc], writes=[pd])
                r_ = rdp.get()
                k.act(r_[:], pd[0:64, :], AF.Ln, reads=[pd], writes=[r_])
                k.act(r_[:], r_[:], AF.Exp, scale=-1.0, reads=[r_], writes=[r_])
                k.tt("gpsimd", oo[:, cs8], acc[0:64, cs8], r_[:], ALU.mult, reads=[acc, r_], writes=[oo])
            k.dma("gpsimd", P.cat[256 + h * 64:256 + (h + 1) * 64, :], oo[:], reads=[oo])
        k.rot = None


C0 = 0.6065306597126334
GN_EPS = 64e-5


def bc(ap, n):
    return bass.AP(ap.tensor, ap.offset, [list(x) for x in ap.ap] + [[0, n]])


def phase_rwkv(P, l):
    k, cst, cstb = P.k, P.cst, P.cstb
    AX = mybir.AxisListType
    with k.scope() as sc:
        identf = cst[:, C_ID:C_ID + 128]
        bones = cst[:, C_BO:C_BO + 128]
        ones = cst[:, C_ONE:C_ONE + 128]
        vec = sc.tile([128, NV], F32)
        lnwb = sc.tile([128, 512], F32)
        dec = sc.tile([128, 256], F32)
        icl = sc.tile([128, 256], F32)
        gateA = sc.tile([128, 256], F32)
        gateB = sc.tile([128, 256], F32)
        for t_ in (dec, icl, gateB):
            k.memset("gpsimd", t_[:], 0.0, writes=[t_])
        k.dma("sync", vec[:], P.vecs[l], writes=[vec])
        k.dma("sync", lnwb[:], P.lnwb[l], writes=[lnwb])
        k.dma("sync", dec[0:64, :], P.a_dec[l], writes=[dec])
        k.dma("sync", icl[64:128, :], P.a_icl[l], writes=[icl])
        k.dma("sync", gateA[:], P.a_gate[l, 0:128, :], writes=[gateA])
        k.dma("sync", gateB[0:32, :], P.a_gate[l, 128:160, :], writes=[gateB])
        msk4 = sc.tile([128, 512], F32)
        miu4 = sc.tile([128, 512], F32)
        msu2 = sc.tile([128, 256], F32)
        for j in range(4):
            src = C_TRI + (384 if j % 2 == 0 else 256)
            k.copy("vector", msk4[:, j * 128:(j + 1) * 128], cst[:, src:src + 128], reads=[cst], writes=[msk4])
            k.copy("vector", miu4[:, j * 128:(j + 1) * 128], cst[:, C_TRI:C_TRI + 128], reads=[cst], writes=[miu4])
        for j in range(2):
            k.copy("vector", msu2[:, j * 128:(j + 1) * 128], cst[:, C_TRI + 256:C_TRI + 384], reads=[cst], writes=[msu2])
        rawp = sc.pool(2, [128, 513], F32)
        dtmp = sc.pool(2, [128, 512], F32)
        PFk = [sc.tile([128, 512], F32) for _ in range(2)]
        PF6 = sc.tile([128, 512], F32)
        PF7 = sc.tile([128, 512], F32)
        PF8 = sc.tile([128, 512], F32)
        tmpf = sc.pool(3, [128, 512], F32)
        th = sc.tile([128, 512], F32)
        SG = sc.tile([128, 512], F32)
        Aic = sc.tile([128, 512], F32)

        class HOc:
            pass
        HO = []
        for i in range(2):
            ho = HOc()
            for nm_ in ("R", "V", "CS", "CSX", "KN", "KF", "Bv", "RKR"):
                setattr(ho, nm_, [sc.tile([128, 512], F32) for _ in range(2)])
            ho.SX7 = sc.tile([128, 512], F32)
            ho.SX8 = sc.tile([128, 512], F32)
            k.memset("gpsimd", ho.SX8[:], 0.0, writes=[ho.SX8])
            HO.append(ho)

        class Slot:
            pass
        slots = []
        for s_ in range(4):
            sl = Slot()
            sl.ep = sc.pool(5, [128, 128], F32)
            sl.dp = sc.pool(4, [128, 128], F32)
            sl.APz = [sc.tile([128, 128], BF16) for _ in range(2)]
            sl.bkb = sc.pool(2, [128, 2, 128], BF16)
            sl.RPz = [sc.tile([128, 128], F32) for _ in range(2)]
            for t_ in sl.APz + sl.RPz:
                k.memset("gpsimd", t_[:], 0.0, writes=[t_])
            sl.smp = sc.pool(8, [128, 2], F32)
            sl.tmp = sc.pool(1, [128, 3, 128], F32)
            sl.MNp = sc.pool(2, [128, 2, 2, 128], BF16)
            sl.mkp = sc.pool(1, [128, 2, 128], F32)
            sl.wrp = sc.pool(1, [128, 2, 2, 128], F32)
            sl.Xbp = sc.pool(2, [128, 2, 2, 64], BF16)
            sl.Xfp = sc.pool(1, [128, 2, 2, 64], F32)
            sl.sq = sc.pool(6, [128, 128], F32)
            sl.Hpp = sc.pool(1, [128, 128], F32)
            sl.px = k.psum[4 + s_]
            slots.append(sl)
        k.rot = [0, 1, 2, 3]
        STp = [sc.pool(2, [128, 128], F32) for _ in range(2)]
        catA = sc.pool(3, [128, 2, 512], BF16)
        ST = []
        for cp in range(2):
            t = STp[cp].get()
            k.memset("gpsimd", t[:], 0.0, writes=[t])
            ST.append(t)
        f2 = lambda t: t[:].rearrange("p a b c -> p (a b c)")

        def a1gen(g, ho):
            lo = g * 512 - 1

            def mix(ch, dst_ap, dst_tl):
                rows = 128 if ch < 8 else 32
                rw = rawp.get()
                rsl = slice(ch * 128, ch * 128 + rows)
                if g == 0:
                    k.memset("gpsimd", rw[:, 0:1], 0.0, writes=[rw])
                    k.dma("sync", rw[:rows, 1:513], P.pA[rsl, 0:512], writes=[rw])
                else:
                    k.dma("sync", rw[:rows, :], P.pA[rsl, lo:lo + 513], writes=[rw])
                d_ = dtmp.get()
                k.tt("gpsimd", d_[:rows], rw[:rows, 0:512], rw[:rows, 1:513], ALU.subtract, reads=[rw], writes=[d_])
                k.stt("vector", dst_ap, d_[:rows], vec[:rows, V_MU + ch:V_MU + ch + 1], rw[:rows, 1:513],
                      ALU.mult, ALU.add, reads=[d_, vec, rw], writes=[dst_tl])
            dsts = [ho.R[0], ho.R[1], PFk[0], PFk[1], ho.V[0], ho.V[1], PF6, PF7, PF8]
            for ch in range(9):
                d = dsts[ch]
                mix(ch, d[:] if ch < 8 else d[0:32, :], d)
                if ch % 2 == 1:
                    yield
            k.act(th[:], PF6[:], AF.Tanh, reads=[PF6], writes=[th])
            k.act(ho.SX7[:], PF7[:], AF.Sigmoid, reads=[PF7], writes=[ho.SX7])
            k.act(ho.SX8[0:32, :], PF8[0:32, :], AF.Sigmoid, reads=[PF8], writes=[ho.SX8])
            yield
            for cp in range(2):
                pcs = slice(cp * 128, (cp + 1) * 128)
                CS, CSX, KN, KF, Bv, RKR = ho.CS[cp], ho.CSX[cp], ho.KN[cp], ho.KF[cp], ho.Bv[cp], ho.RKR[cp]
                pz = k.ps()
                k.mm(pz[:], dec[:, pcs], th[:, :], reads=[dec, th], writes=[pz])
                k.act(SG[:], pz[:], AF.Sigmoid, bias=vec[:, V_W0 + cp:V_W0 + cp + 1], reads=[pz, vec], writes=[SG])
                pa_ = k.ps()
                k.mm(pa_[:], icl[:, pcs], PF6[:, :], reads=[icl, PF6], writes=[pa_])
                k.act(Aic[:], pa_[:], AF.Sigmoid, bias=vec[:, V_A0 + cp:V_A0 + cp + 1], reads=[pa_, vec], writes=[Aic])
                yield
                for c4 in range(4):
                    cc = slice(c4 * 128, (c4 + 1) * 128)
                    k.S.op("vector", (lambda e, o=CS[:, cc], d1=SG[:, cc]: e.tensor_tensor_scan(
                        out=o, data0=ones, data1=d1, initial=0.0, op0=ALU.mult, op1=ALU.add)),
                        reads=[cst, SG], writes=[CS])
                k.tt("gpsimd", CSX[:], CS[:], SG[:], ALU.subtract, reads=[CS, SG], writes=[CSX])
                yield
                kraw = PFk[cp]
                kkr = tmpf.get()
                k.ts("vector", kkr[:], kraw[:], vec[:, V_KK + cp:V_KK + cp + 1], None, ALU.mult, reads=[kraw, vec], writes=[kkr])
                sq = tmpf.get()
                k.tt("gpsimd", sq[:], kkr[:], kkr[:], ALU.mult, reads=[kkr], writes=[sq])
                pn_ = k.ps()
                k.mm(pn_[:], bones, sq[:], reads=[cst, sq], writes=[pn_])
                nrm = tmpf.get()
                k.act(nrm[:], pn_[:], AF.Sqrt, reads=[pn_], writes=[nrm])
                yield
                k.ts("vector", nrm[:], nrm[:], 1e-12, None, ALU.max, reads=[nrm], writes=[nrm])
                k.recip(nrm[:], nrm[:], reads=[nrm], writes=[nrm])
                k.tt("vector", KN[:], kkr[:], nrm[:], ALU.mult, reads=[kkr, nrm], writes=[KN])
                yield
                t1 = tmpf.get()
                k.ts("vector", t1[:], Aic[:], -1.0, vec[:, V_KA + cp:V_KA + cp + 1], ALU.add, ALU.mult, reads=[Aic, vec], writes=[t1])
                k.stt("vector", KF[:], t1[:], 1.0, kraw[:], ALU.add, ALU.mult, reads=[t1, kraw], writes=[KF])
                k.tt("gpsimd", Bv[:], KN[:], Aic[:], ALU.mult, reads=[KN, Aic], writes=[Bv])
                k.stt("vector", RKR[:], ho.R[cp][:], vec[:, V_RK + cp:V_RK + cp + 1], KF[:], ALU.mult, ALU.mult,
                      reads=[ho.R[cp], vec, KF], writes=[RKR])
                yield

        def unit(sl, ho, cp, c4, cA):
            cc = slice(c4 * 128, (c4 + 1) * 128)
            Rt_, Vt_ = ho.R[cp], ho.V[cp]
            CS, CSX, KN, KF, Bv, RKR = ho.CS[cp], ho.CSX[cp], ho.KN[cp], ho.KF[cp], ho.Bv[cp], ho.RKR[cp]
            rfe = Rt_[:, cc]
            vfe = Vt_[:, cc]
            cs_ = CS[:, cc]
            csx = CSX[:, cc]
            cend = CS[:, c4 * 128 + 127:c4 * 128 + 128]
            sm = sl.smp.get()
            k.ts("vector", sm[:, 0:1], cend, C0, None, ALU.mult, reads=[CS], writes=[sm])
            k.ts("vector", sm[:, 1:2], cend, -C0, None, ALU.mult, reads=[CS], writes=[sm])
            pce, nce = sm[:, 0:1], sm[:, 1:2]
            e1, e2, e3, e4, e5 = (sl.ep.get() for _ in range(5))
            k.act(e1[:], csx, AF.Exp, scale=-C0, reads=[CSX], writes=[e1])
            k.act(e2[:], csx, AF.Exp, scale=-C0, bias=pce, reads=[CSX, sm], writes=[e2])
            k.act(e3[:], cs_, AF.Exp, scale=C0, bias=nce, reads=[CS, sm], writes=[e3])
            k.act(e4[:], cs_, AF.Exp, scale=-C0, reads=[CS], writes=[e4])
            k.act(e5[:], cs_, AF.Exp, scale=-C0, bias=pce, reads=[CS, sm], writes=[e5])
            AT, BT, KT, RT = (sl.dp.get() for _ in range(4))
            APz, RPz = sl.APz, sl.RPz
            k.stt("vector", AT[:], KN[:, cc], -1.0, e1[:], ALU.mult, ALU.mult, reads=[KN, e1], writes=[AT])
            for h in range(2):
                hr = slice(h * 64, (h + 1) * 64)
                k.stt("vector", APz[h][hr, :], KN[hr, cc], -1.0, e2[hr, :], ALU.mult, ALU.mult, reads=[KN, e2], writes=[APz[h]])
                k.tt("gpsimd", RPz[h][hr, :], Rt_[hr, cc], e5[hr, :], ALU.mult, reads=[Rt_, e5], writes=[RPz[h]])
            k.tt("gpsimd", BT[:], Bv[:, cc], e3[:], ALU.mult, reads=[Bv, e3], writes=[BT])
            k.tt("gpsimd", KT[:], KF[:, cc], e3[:], ALU.mult, reads=[KF, e3], writes=[KT])
            k.tt("vector", RT[:], rfe, e4[:], ALU.mult, reads=[Rt_, e4], writes=[RT])
            bkb = sl.bkb.get()
            k.copy("gpsimd", bkb[:, 0, :], BT[:], reads=[BT], writes=[bkb])
            k.copy("gpsimd", bkb[:, 1, :], KT[:], reads=[KT], writes=[bkb])
            yield
            pt = k.ps()
            for q, (srcT, stl) in enumerate(((BT[:], BT), (KT[:], KT), (vfe, Vt_))):
                k.tr(pt[:, q * 128:(q + 1) * 128], srcT, identf, reads=[stl, cst], writes=[pt])
            tm = sl.tmp.get()
            k.copy("scalar", tm[:].rearrange("p a b -> p (a b)"), pt[:, 0:384], reads=[pt], writes=[tm])
            yield
            pmA, pmB, pw = k.ps(), k.ps(), k.ps()
            for h in range(2):
                k.mm(pmA[:, h * 256:h * 256 + 128], APz[h][:], bkb[:, 0, :], reads=[APz[h], bkb], writes=[pmA])
                k.mm(pmA[:, h * 256 + 128:h * 256 + 256], bkb[:, 0, :], APz[h][:], reads=[APz[h], bkb], writes=[pmA])
                k.mm(pmB[:, h * 128:(h + 1) * 128], bkb[:, 1, :], APz[h][:], reads=[APz[h], bkb], writes=[pmB])
                k.mm(pw[:, h * 256:h * 256 + 128], BT[:], RPz[h][:], reads=[RPz[h], BT], writes=[pw])
                k.mm(pw[:, h * 256 + 128:h * 256 + 256], KT[:], RPz[h][:], reads=[RPz[h], KT], writes=[pw])
            MN = sl.MNp.get()
            k.tt("vector", f2(MN), pmA[:], msk4[:], ALU.mult, reads=[pmA, msk4], writes=[MN])
            mk = sl.mkp.get()
            k.tt("vector", mk[:].rearrange("p a b -> p (a b)"), pmB[:, 0:256], msu2[:], ALU.mult, reads=[pmB, msu2], writes=[mk])
            wr = sl.wrp.get()
            k.tt("vector", f2(wr), pw[:], miu4[:], ALU.mult, reads=[pw, miu4], writes=[wr])
            yield
            px = sl.px
            pxv = px[:, 0:256].rearrange("p (a h v) -> p a h v", a=2, h=2)
            for h in range(2):
                k.mm(pxv[:, 1, h, :], mk[:, h, :], tm[:, 2, h * 64:(h + 1) * 64], start=(h == 0), stop=False,
                     reads=[mk, tm], writes=[px], skip=True)
            k.mm(px[:, 0:128], AT[:], identf, start=False, stop=False, reads=[AT, cst], writes=[px], skip=True)
            Xb = sl.Xbp.get()
            k.copy("vector", f2(Xb), px[:, 0:256], reads=[px], writes=[Xb])
            yield
            Xf = None
            for i in range(7):
                for h in range(2):
                    k.mm(pxv[:, :, h, :], MN[:, h, 1, :], Xb[:, :, h, :], start=False, stop=(i == 6 and h == 1),
                         reads=[MN, Xb], writes=[px], skip=True)
                if i < 6:
                    Xb = sl.Xbp.get()
                    k.copy("scalar" if i % 2 == 0 else "vector", f2(Xb), px[:, 0:256], reads=[px], writes=[Xb])
                    pn = k.ps()
                    pnv = pn[:].rearrange("p (h m t) -> p h m t", h=2, m=2)
                    for h in range(2):
                        k.mm(pnv[:, h, 1, :], MN[:, h, 0, :], MN[:, h, 1, :], reads=[MN], writes=[pn])
                        k.mm(pnv[:, h, 0, :], MN[:, h, 1, :], MN[:, h, 0, :], reads=[MN], writes=[pn])
                    MNn = sl.MNp.get()
                    k.copy("scalar", f2(MNn), pn[:], reads=[pn], writes=[MNn])
                    MN = MNn
                else:
                    Xf = sl.Xfp.get()
                    k.copy("vector", f2(Xf), px[:, 0:256], reads=[px], writes=[Xf])
                yield
            X = Xf
            Ahat = X[:, 0].rearrange("p h v -> p (h v)")
            U0 = X[:, 1].rearrange("p h v -> p (h v)")
            pg = k.ps()
            k.mm(pg[:, 0:128], Ahat, tm[:, 0, :], reads=[X, tm], writes=[pg])
            gt1 = sl.sq.get()
            k.tt("vector", gt1[:], pg[:, 0:128], bones, ALU.mult, reads=[pg, cst], writes=[gt1])
            GT = sl.sq.get()
            k.stt("vector", GT[:], identf, e4[:, 127:128], gt1[:], ALU.mult, ALU.add, reads=[cst, e4, gt1], writes=[GT])
            ph = k.ps()
            k.mm(ph[:, 0:128], tm[:, 0, :], U0, start=True, stop=False, reads=[X, tm], writes=[ph])
            k.mm(ph[:, 0:128], tm[:, 1, :], tm[:, 2, :], start=False, stop=True, reads=[tm], writes=[ph])
            Hp = sl.Hpp.get()
            k.tt("vector", Hp[:], ph[:, 0:128], bones, ALU.mult, reads=[ph, cst], writes=[Hp])
            QT = sl.sq.get()
            for h in range(2):
                hr = slice(h * 64, (h + 1) * 64)
                pq = k.ps()
                k.mm(pq[:, 0:128], Ahat, wr[:, h, 0, :], reads=[X, wr], writes=[pq])
                k.tt("vector", QT[hr, :], pq[hr, 0:128], RT[hr, :], ALU.add, reads=[pq, RT], writes=[QT])
            yield
            py = k.ps()
            So = ST[cp]
            k.mm(py[:, 0:128], QT[:], So[:], start=True, stop=False, reads=[QT, So], writes=[py])
            for h in range(2):
                hr = slice(h * 64, (h + 1) * 64)
                ys = py[:, h * 64:(h + 1) * 64]
                k.mm(ys, wr[:, h, 0, :], X[:, 1, h, :], start=False, stop=False, reads=[wr, X], writes=[py])
                k.mm(ys, wr[:, h, 1, :], tm[:, 2, hr], start=False, stop=(h == 1), reads=[wr, tm], writes=[py])
            pS = k.ps()
            k.mm(pS[:, 0:128], GT[:], So[:], reads=[GT, So], writes=[pS])
            Sn = STp[cp].get()
            k.tt("vector", Sn[:], pS[:, 0:128], Hp[:], ALU.add, reads=[pS, Hp], writes=[Sn])
            ST[cp] = Sn
            ysb = sl.sq.get()
            k.copy("scalar", ysb[:], py[:, 0:128], reads=[py], writes=[ysb])
            pb_ = k.ps()
            k.mm(pb_[:, 0:2], RKR[:, cc], cst[:, C_BO:C_BO + 128:64], reads=[RKR, cst], writes=[pb_])
            bon = sl.smp.get()
            k.copy("scalar", bon[:], pb_[:, 0:2], reads=[pb_], writes=[bon])
            yield
            yc = sl.sq.get()
            yn = sl.sq.get()
            v3 = lambda t: t[:].rearrange("p (h v) -> p h v", h=2)
            st_ = sl.smp.get()
            k.S.op("vector", (lambda e, o=st_[:, 0:2], i_=v3(ysb): e.tensor_reduce(out=o, in_=i_, axis=AX.X, op=ALU.add)),
                   reads=[ysb], writes=[st_])
            nm = sl.smp.get()
            k.ts("vector", nm[:, 0:2], st_[:, 0:2], -1.0 / 64, None, ALU.mult, reads=[st_], writes=[nm])
            k.tt("vector", v3(yc), v3(ysb), bc(nm[:, 0:2], 64), ALU.add, reads=[ysb, nm], writes=[yc])
            s2 = sl.smp.get()
            jk = sl.ep.get()
            for h in range(2):
                hs = slice(h * 64, (h + 1) * 64)
                k.act(jk[:, hs], yc[:, hs], AF.Square, accum_out=s2[:, h:h + 1], reads=[yc], writes=[jk, s2])
            rs2 = sl.smp.get()
            k.act(rs2[:, 0:2], s2[:, 0:2], AF.Sqrt, bias=GN_EPS, scale=1.0 / 64, reads=[s2], writes=[rs2])
            k.recip(rs2[:, 0:2], rs2[:, 0:2], reads=[rs2], writes=[rs2])
            hg0 = cp * 2
            k.tt("vector", v3(yn), v3(yc), bc(rs2[:, 0:2], 64), ALU.mult, reads=[yc, rs2], writes=[yn])
            k.tt("gpsimd", yn[:], yn[:], lnwb[:, hg0 * 64:(hg0 + 2) * 64], ALU.mult, reads=[yn, lnwb], writes=[yn])
            k.tt("gpsimd", yn[:], yn[:], lnwb[:, 256 + hg0 * 64:256 + (hg0 + 2) * 64], ALU.add, reads=[yn, lnwb], writes=[yn])
            bv = sl.sq.get()
            k.tt("gpsimd", v3(bv), tm[:, 2, :].rearrange("p (h v) -> p h v", h=2), bc(bon[:, 0:2], 64), ALU.mult,
                 reads=[tm, bon], writes=[bv])
            k.tt("vector", yn[:], yn[:], bv[:], ALU.add, reads=[yn, bv], writes=[yn])
            pgt = k.ps()
            pcs = slice(cp * 128, (cp + 1) * 128)
            k.mm(pgt[:, 0:128], ho.SX7[:, cc], gateA[:, pcs], start=True, stop=False, reads=[ho.SX7, gateA], writes=[pgt])
            k.mm(pgt[:, 0:128], ho.SX8[:, cc], gateB[:, pcs], start=False, stop=True, reads=[ho.SX8, gateB], writes=[pgt])
            ya = sl.sq.get()
            k.tt("vector", ya[:], yn[:], pgt[:, 0:128], ALU.mult, reads=[yn, pgt], writes=[ya])
            pT2 = k.ps()
            k.tr(pT2[:, 0:128], ya[:], identf, reads=[ya, cst], writes=[pT2])
            k.copy("scalar", cA[:, cp, cc], pT2[:, 0:128], reads=[pT2], writes=[cA])

        tasks = [(g, c4) for g in range(NG) for c4 in range(4)]
        a1 = {}
        a1_finished = set()

        def finish_a1(g):
            if g not in a1_finished:
                for _ in a1[g]:
                    pass
                a1_finished.add(g)
        a1[0] = a1gen(0, HO[0])
        finish_a1(0)
        if NG > 1:
            a1[1] = a1gen(1, HO[1])
        bg = [1] if NG > 1 else []
        cAs = {}
        active = []
        steps = {}
        remaining = {}
        done = set()
        nxt = 0
        while True:
            while nxt < len(tasks):
                g, c4 = tasks[nxt]
                ok_slots = nxt < 2 or (nxt - 2) in done
                ok_stag = nxt == 0 or (nxt - 1) in done or steps.get(nxt - 1, 0) >= 6
                if not (ok_slots and ok_stag):
                    break
                if c4 == 0:
                    finish_a1(g)
                    if g in bg:
                        bg.remove(g)
                    cAs[g] = catA.get()
                lane = nxt % 2
                for cp in range(2):
                    active.append((nxt, unit(slots[lane * 2 + cp], HO[g % 2], cp, c4, cAs[g])))
                steps[nxt] = 0
                remaining[nxt] = 2
                nxt += 1
            if not active:
                break
            for item in list(active):
                idx, gen = item
                try:
                    next(gen)
                except StopIteration:
                    active.remove(item)
                    remaining[idx] -= 1
                    if remaining[idx] == 0:
                        done.add(idx)
                        g, c4 = tasks[idx]
                        if c4 == 3:
                            k.dma("gpsimd", P.cat[0:256, g * 512:(g + 1) * 512].rearrange("(c p) t -> p c t", p=128),
                                  cAs[g][:], reads=[cAs[g]])
                            if g + 2 < NG:
                                a1[g + 2] = a1gen(g + 2, HO[g % 2])
                                bg.append(g + 2)
            for idx in set(i for i, _ in active):
                steps[idx] += 1
            if bg:
                gb_ = bg[0]
                try:
                    next(a1[gb_])
                except StopIteration:
                    a1_finished.add(gb_)
                    bg.remove(gb_)
        k.rot = None


def phase_tail(P, l, xsrc, last):
    k, cst = P.k, P.cst
    with k.scope() as wsc:
        woutb = wsc.tile([128, 8, 1024], BF16)
        wgb = wsc.tile([128, 8, DFF], BF16)
        wub = wsc.tile([128, 8, DFF], BF16)
        wdb = wsc.tile([128, 22, 1024], BF16)
        load_w2(P, woutb, P.w_out[l], 8, 2)
        wg_c = [Tl(wgb.t) for _ in range(22)]
        wu_c = [Tl(wub.t) for _ in range(22)]
        for c in range(22):
            cs = slice(c * 128, (c + 1) * 128)
            k.dma("gpsimd", wgb[:, :, cs], P.w_gate[l][:, :, cs], writes=[wg_c[c]])
            k.dma("gpsimd", wub[:, :, cs], P.w_up[l][:, :, cs], writes=[wu_c[c]])
        with k.scope() as sc:
            vec = sc.tile([128, NV], F32)
            k.dma("sync", vec[:], P.vecs[l], writes=[vec])
            gb = make_gb(P, sc, vec, V_GFFN)
            xts = sc.pool(2, [128, D], F32)
            x1s = sc.pool(2, [128, D], F32)
            cts = sc.pool(1, [128, 8, 512], BF16)
            catv = P.cat.rearrange("(k p) t -> p k t", p=128)
            ss = sc.tile([128, 1], F32)
            rstd = sc.tile([128, 1], F32)
            xn = sc.tile([128, D], BF16)
            junk = xn
            hTs = sc.pool(2, [128, 8, 512], BF16)
            sg = sc.pool(1, [128, 512], F32)
            ab = sc.pool(3, [128, 512], BF16)

            def norm_gen(g_, hT_):
                ct = cts.get()
                k.dma("sync", ct[:], catv[:, :, g_ * 512:(g_ + 1) * 512], writes=[ct])
                for tt in range(4):
                    rows = slice(g_ * 512 + tt * 128, g_ * 512 + (tt + 1) * 128)
                    xt = xts.get()
                    k.dma("sync", xt[:], xsrc[rows, :], writes=[xt])
                    x1t = x1s.get()
                    for nb in range(2):
                        ps = k.ps()
                        for kk in range(8):
                            k.mm(ps[:], ct[:, kk, tt * 128:(tt + 1) * 128], woutb[:, kk, nb * 512:(nb + 1) * 512],
                                 start=(kk == 0), stop=(kk == 7), reads=[ct, woutb], writes=[ps])
                        k.tt("vector", x1t[:, nb * 512:(nb + 1) * 512], ps[:], xt[:, nb * 512:(nb + 1) * 512], ALU.add,
                             reads=[ps, xt], writes=[x1t])
                    k.dma("sync", P.x1[rows, :], x1t[:], reads=[x1t])
                    rmsnorm_T(P, x1t, junk, ss, rstd, xn, P.pTt, gb, hT_[:, :, tt * 128:(tt + 1) * 128], hT_)
                    yield
            hT_next = hTs.get()
            for _ in norm_gen(0, hT_next):
                pass
            for g in range(NG):
                cols = slice(g * 512, (g + 1) * 512)
                hT = hT_next
                gen = None
                if g + 1 < NG:
                    hT_next = hTs.get()
                    gen = norm_gen(g + 1, hT_next)
                if g == 1:
                    load_w2(P, wdb, P.w_down[l], 22, 2)
                for c in range(22):
                    if gen is not None and c in (4, 9, 14, 19):
                        next(gen, None)
                    cs = slice(c * 128, (c + 1) * 128)
                    pg, pu = k.ps(), k.ps()
                    for kk in range(8):
                        k.mm(pg[:], wgb[:, kk, cs], hT[:, kk, :], start=(kk == 0), stop=(kk == 7), reads=[wg_c[c], hT], writes=[pg])
                    for kk in range(8):
                        k.mm(pu[:], wub[:, kk, cs], hT[:, kk, :], start=(kk == 0), stop=(kk == 7), reads=[wu_c[c], hT], writes=[pu])
                    s_ = sg.get()
                    k.act(s_[:], pg[:], AF.Silu, reads=[pg], writes=[s_])
                    a_ = ab.get()
                    k.tt("vector", a_[:], s_[:], pu[:], ALU.mult, reads=[s_, pu], writes=[a_])
                    k.dma("gpsimd", P.aT[cs, cols], a_[:], reads=[a_])
        with k.scope() as sc:
            gfb = sc.tile([128, D], F32)
            k.dma("sync", gfb[:], P.gfinb[:, :], writes=[gfb])
            ats = sc.pool(2, [128, 22, 256], BF16)
            xts = sc.pool(2, [128, D], F32)
            x2s = sc.pool(2, [128, D], F32)
            junk = sc.tile([128, D], BF16)
            ss = sc.pool(2, [128, 1], F32)
            aTv = P.aT.rearrange("(k p) t -> p k t", p=128)
            for hg in range(2 * NG):
                at = ats.get()
                k.dma("sync", at[:], aTv[:, :, hg * 256:(hg + 1) * 256], writes=[at])
                for tt in range(2):
                    rows = slice(hg * 256 + tt * 128, hg * 256 + (tt + 1) * 128)
                    xt = xts.get()
                    k.dma("sync", xt[:], P.x1[rows, :], writes=[xt])
                    x2 = x2s.get()
                    for nb in range(2):
                        ps = k.ps()
                        for c in range(22):
                            k.mm(ps[:], at[:, c, tt * 128:(tt + 1) * 128], wdb[:, c, nb * 512:(nb + 1) * 512],
                                 start=(c == 0), stop=(c == 21), reads=[at, wdb], writes=[ps])
                        k.tt("vector", x2[:, nb * 512:(nb + 1) * 512], ps[:], xt[:, nb * 512:(nb + 1) * 512], ALU.add,
                             reads=[ps, xt], writes=[x2])
                    if not last:
                        k.dma("gpsimd", P.xr[rows, :], x2[:], reads=[x2])
                    else:
                        s1 = ss.get()
                        k.act(junk[:], x2[:], AF.Square, accum_out=s1[:], reads=[x2], writes=[junk, s1])
                        k.act(s1[:], s1[:], AF.Sqrt, bias=EPS, scale=1.0 / D, reads=[s1], writes=[s1])
                        k.recip(s1[:], s1[:], reads=[s1], writes=[s1])
                        k.stt("vector", x2[:], x2[:], s1[:, 0:1], gfb[:], ALU.mult, ALU.mult, reads=[x2, s1, gfb], writes=[x2])
                        k.dma("gpsimd", P.out[rows, :], x2[:], reads=[x2])


def make_in_maps(inputs):
    f = lambda a: np.ascontiguousarray(np.asarray(a))
    inp = {kk: f(v) for kk, v in inputs.items()}
    shared = {
        "consts": make_consts(),
        "vecs": np.stack([make_vecs(inp, l) for l in range(L)]),
        "lnwb": np.stack([np.broadcast_to(np.concatenate([inp["a_ln_w"][l], inp["a_ln_b"][l]])[None, :], (128, 512)).copy()
                          for l in range(L)]),
        "gfinb": np.broadcast_to(inp["final_norm_g"][None, :], (128, D)).copy(),
        "w_in": np.stack([ktile(inp["w_in"][l]) for l in range(L)]),
        "w_uq": np.stack([ktile(inp["c_w_uq"][l]) for l in range(L)]),
        "w_ukv": np.stack([ktile(np.concatenate(
            [inp["c_w_ukv"][l].reshape(128, 4, 256)[:, :, :128].reshape(128, 512),
             inp["c_w_ukv"][l].reshape(128, 4, 256)[:, :, 128:].reshape(128, 512)], axis=1)) for l in range(L)]),
        "w_out": np.stack([ktile(inp["w_out"][l]) for l in range(L)]),
        "w_gate": np.stack([ktile(inp["ffn_w_gate"][l]) for l in range(L)]),
        "w_up": np.stack([ktile(inp["ffn_w_up"][l]) for l in range(L)]),
        "w_down": np.stack([ktile(inp["ffn_w_down"][l]) for l in range(L)]),
        "a_dec": inp["a_decay_up"], "a_icl": inp["a_iclr_up"], "a_gate": inp["a_gate_up"],
    }
    maps = []
    for b in range(8):
        m = dict(shared)
        m["x"] = inp["x"][b]
        m["pos"] = inp["positions"][b].reshape(1, S_LEN).astype(np.int32)
        maps.append(m)
    return maps


def kernel(**inputs):
    nc = build()
    maps = make_in_maps(inputs)
    res = run_bass_kernel_spmd(nc, maps, core_ids=list(range(8)))
    return np.stack([np.asarray(r["out"]) for r in res.results]).astype(np.float32)
```

```python
import contextlib
import math
import numpy as np
import concourse.bass as bass
import concourse.mybir as mybir
from concourse.bass_utils import run_bass_kernel_spmd

F32 = mybir.dt.float32
BF16 = mybir.dt.bfloat16
I32 = mybir.dt.int32
AF = mybir.ActivationFunctionType
ALU = mybir.AluOpType

S_LEN = 4096
D = 1024
DFF = 2816
L = 2
NG = 8
EPS = 1e-6
THETA = 500000.0
MAGIC = 12582912.0

ENGS = ("tensor", "vector", "scalar", "gpsimd", "sync")
NSLOTS = {"sync": 12, "gpsimd": 48}


class Buf:
    __slots__ = ("w", "r")

    def __init__(self):
        self.w = {}
        self.r = []


class Tl:
    __slots__ = ("t", "b")

    def __init__(self, t):
        self.t = t
        self.b = Buf()

    def __getitem__(self, idx):
        return self.t[idx]


class Sched:
    def __init__(self, nc, sems):
        self.nc = nc
        self.lists = {k: [] for k in ENGS}
        self.cnt = {k: 0 for k in ENGS}
        self.seen = {k: {} for k in ENGS}
        self.semh = sems
        self.dq = {q: {"n": 0, "slots": [f"dma_{q}_{i}" for i in range(NSLOTS[q])]} for q in ("sync", "gpsimd")}

    def _waits(self, eng, reads, writes, merge=False):
        deps = {}

        def need(ev):
            if ev is None:
                return
            k, v = ev
            if k == "tensor" and eng == "tensor":
                return
            if deps.get(k, 0) < v:
                deps[k] = v
        for b in reads:
            for ev in b.w.items():
                need(ev)
        for b in writes:
            if not merge:
                for ev in b.w.items():
                    need(ev)
            for ev in b.r:
                need(ev)
        out = []
        seen = self.seen[eng]
        for k, v in deps.items():
            if seen.get(k, 0) < v:
                seen[k] = v
                out.append((k, v))
        return out

    def _mark(self, ev, reads, writes, merge=False):
        for b in reads:
            b.r.append(ev)
        for b in writes:
            if merge:
                b.w[ev[0]] = max(b.w.get(ev[0], 0), ev[1])
            else:
                b.w = {ev[0]: ev[1]}
                b.r = []

    def op(self, eng, emit, reads=(), writes=()):
        reads = [x.b if isinstance(x, Tl) else x for x in reads]
        writes = [x.b if isinstance(x, Tl) else x for x in writes]
        waits = self._waits(eng, reads, writes)
        self.cnt[eng] += 1
        ev = (eng, self.cnt[eng])
        semh = self.semh
        mysem = semh[eng]

        def run(e):
            for k, v in waits:
                e.wait_ge(semh[k], v)
            emit(e).then_inc(mysem, 1)
        self.lists[eng].append(run)
        self._mark(ev, reads, writes)

    def dma(self, q, out, in_, reads=(), writes=(), merge=False):
        reads = [x.b if isinstance(x, Tl) else x for x in reads]
        writes = [x.b if isinstance(x, Tl) else x for x in writes]
        st = self.dq[q]
        n = st["n"]
        st["n"] += 1
        slots = st["slots"]
        key = slots[n % len(slots)]
        use = n // len(slots)
        waits = self._waits(q, reads, writes, merge)
        seen = self.seen[q]
        if use > 0 and seen.get(key, 0) < 16 * use:
            seen[key] = 16 * use
            waits.append((key, 16 * use))
        ev = (key, 16 * (use + 1))
        semh = self.semh

        def run(e):
            for k, v in waits:
                e.wait_ge(semh[k], v)
            e.dma_start(out=out, in_=in_).then_inc(semh[key], 16)
        self.lists[q].append(run)
        self._mark(ev, reads, writes, merge)

    def barrier(self):
        targets = {k: self.cnt[k] for k in ENGS if self.cnt[k] > 0}
        for q, st in self.dq.items():
            n = st["n"]
            ns = len(st["slots"])
            for i, key in enumerate(st["slots"]):
                uses = (n - i + ns - 1) // ns if n > i else 0
                if uses > 0:
                    targets[key] = 16 * uses
        semh = self.semh
        for eng in ENGS:
            seen = self.seen[eng]
            waits = []
            for k, v in targets.items():
                if k == eng:
                    continue
                if seen.get(k, 0) < v:
                    seen[k] = v
                    waits.append((k, v))

            def run(e, waits=waits):
                for k, v in waits:
                    e.wait_ge(semh[k], v)
            self.lists[eng].append(run)

    def emit(self):
        with self.nc.Block() as block:
            for name in ENGS:
                lst = self.lists[name]

                def body(e, lst=lst):
                    for f in lst:
                        f(e)
                getattr(block, name)(body)


class K:
    def __init__(self, nc, S):
        self.nc = nc
        self.S = S
        self.uid = 0
        self.psum = []
        self.pi = 0
        self.rot = None

    def name(self, p="t"):
        self.uid += 1
        return f"{p}{self.uid}"

    @contextlib.contextmanager
    def scope(self):
        es = contextlib.ExitStack()
        sc = Scope(self, es)
        try:
            yield sc
            self.S.barrier()
        finally:
            es.close()

    def ps(self):
        rot = self.rot if self.rot else list(range(7))
        t = self.psum[rot[self.pi % len(rot)]]
        self.pi += 1
        return t

    def mm(self, out, lhsT, rhs, start=True, stop=True, reads=(), writes=(), skip=False):
        if skip:
            self.S.op("tensor", lambda e: e.matmul(out, lhsT=lhsT, rhs=rhs, start=start, stop=stop, skip_group_check=True), reads, writes)
        else:
            self.S.op("tensor", lambda e: e.matmul(out, lhsT=lhsT, rhs=rhs, start=start, stop=stop), reads, writes)

    def tr(self, out, in_, ident, reads=(), writes=()):
        self.S.op("tensor", lambda e: e.transpose(out=out, in_=in_, identity=ident), reads, writes)

    def act(self, out, in_, func, bias=0.0, scale=1.0, accum_out=None, reads=(), writes=()):
        if accum_out is None:
            self.S.op("scalar", lambda e: e.activation(out=out, in_=in_, func=func, bias=bias, scale=scale), reads, writes)
        else:
            self.S.op("scalar", lambda e: e.activation(out=out, in_=in_, func=func, bias=bias, scale=scale,
                                                       accum_out=accum_out), reads, writes)

    def tt(self, eng, out, in0, in1, op, reads=(), writes=()):
        self.S.op(eng, lambda e: e.tensor_tensor(out=out, in0=in0, in1=in1, op=op), reads, writes)

    def ts(self, eng, out, in0, s1, s2=None, op0=ALU.mult, op1=None, reads=(), writes=()):
        if op1 is None:
            self.S.op(eng, lambda e: e.tensor_scalar(out=out, in0=in0, scalar1=s1, scalar2=None, op0=op0), reads, writes)
        else:
            self.S.op(eng, lambda e: e.tensor_scalar(out=out, in0=in0, scalar1=s1, scalar2=s2, op0=op0, op1=op1), reads, writes)

    def stt(self, eng, out, in0, scalar, in1, op0, op1, reads=(), writes=()):
        self.S.op(eng, lambda e: e.scalar_tensor_tensor(out=out, in0=in0, scalar=scalar, in1=in1, op0=op0, op1=op1), reads, writes)

    def copy(self, eng, out, in_, reads=(), writes=()):
        if eng == "scalar":
            self.S.op(eng, lambda e: e.activation(out=out, in_=in_, func=AF.Copy), reads, writes)
        else:
            self.S.op(eng, lambda e: e.tensor_copy(out=out, in_=in_), reads, writes)

    def recip(self, out, in_, reads=(), writes=()):
        self.S.op("vector", lambda e: e.reciprocal(out=out, in_=in_), reads, writes)

    def memset(self, eng, out, val, writes=()):
        self.S.op(eng, lambda e: e.memset(out, val), (), writes)

    def dma(self, q, out, in_, reads=(), writes=(), merge=False):
        self.S.dma(q, out, in_, reads, writes, merge)


class Scope:
    def __init__(self, k, es):
        self.k = k
        self.es = es

    def tile(self, shape, dtype, name="t"):
        t = self.es.enter_context(self.k.nc.sbuf_tensor(self.k.name(name), list(shape), dtype))
        return Tl(t)

    def pool(self, n, shape, dtype, name="p"):
        return Pool([self.tile(shape, dtype, name) for _ in range(n)])


class Pool:
    def __init__(self, tiles):
        self.tiles = tiles
        self.i = 0

    def get(self):
        t = self.tiles[self.i % len(self.tiles)]
        self.i += 1
        return t


C_ID, C_TRI, C_PB, C_PC, C_FC, C_BO, C_ONE, C_NB = 0, 128, 640, 768, 896, 898, 1026, 1154
NCONST = 1154 + 256
NEG = -30000.0


def make_consts():
    c = np.zeros((128, NCONST), np.float32)
    c[:, C_ID:C_ID + 128] = np.eye(128)
    r = np.arange(128)[:, None]
    q = np.arange(128)[None, :]
    c[:, C_TRI + 0:C_TRI + 128] = (q >= r)
    c[:, C_TRI + 128:C_TRI + 256] = (q <= r)
    c[:, C_TRI + 256:C_TRI + 384] = (q > r)
    c[:, C_TRI + 384:C_TRI + 512] = (q < r)
    pb = np.zeros((128, 128), np.float32)
    for rr in range(128):
        d = rr % 64
        if d < 8:
            pb[rr + 8, rr] = -1.0
        elif d < 16:
            pb[rr - 8, rr] = 1.0
    c[:, C_PB:C_PB + 128] = pb
    pc = np.zeros((128, 128), np.float32)
    for rr in range(64):
        if rr < 32:
            pc[rr + 32, rr] = -1.0
        else:
            pc[rr - 32, rr] = 1.0
    c[:, C_PC:C_PC + 128] = pc
    invb = 1.0 / (THETA ** (np.arange(0, 16, 2, dtype=np.float32) / 16))
    invc = 1.0 / (THETA ** (np.arange(0, 64, 2, dtype=np.float32) / 64))
    for rr in range(128):
        d = rr % 64
        c[rr, C_FC] = invb[d % 8] / (2 * math.pi) if d < 16 else 0.0
        c[rr, C_FC + 1] = invc[rr % 32] / (2 * math.pi)
    bo = np.zeros((128, 128), np.float32)
    bo[:64, :64] = 1
    bo[64:, 64:] = 1
    c[:, C_BO:C_BO + 128] = bo
    c[:, C_ONE:C_ONE + 128] = 1.0
    c[:, C_NB:C_NB + 128] = np.where(q >= r, 0.0, NEG)
    c[:, C_NB + 128:C_NB + 256] = np.where(q <= r, 0.0, NEG)
    return c


V_GATT, V_GFFN, V_GFIN, V_CQG, V_CKVG, V_MU, V_W0, V_A0, V_KK, V_KA, V_RK = 0, 8, 16, 24, 26, 27, 36, 38, 40, 42, 44
NV = 46


def col(v, n):
    return np.ascontiguousarray(v.reshape(n, 128).T)


def make_vecs(inp, l):
    v = np.zeros((128, NV), np.float32)
    v[:, V_GATT:V_GATT + 8] = col(inp["attn_norm_g"][l], 8)
    v[:, V_GFFN:V_GFFN + 8] = col(inp["ffn_norm_g"][l], 8)
    v[:, V_GFIN:V_GFIN + 8] = col(inp["final_norm_g"], 8)
    v[:, V_CQG:V_CQG + 2] = col(inp["c_q_norm_g"][l], 2)
    v[:, V_CKVG:V_CKVG + 1] = col(inp["c_kv_norm_g"][l], 1)
    mu = np.zeros(9 * 128, np.float32)
    mu[:1056] = inp["a_mu"][l]
    v[:, V_MU:V_MU + 9] = col(mu, 9)
    v[:, V_W0:V_W0 + 2] = col(inp["a_w0"][l], 2)
    v[:, V_A0:V_A0 + 2] = col(inp["a_a0"][l], 2)
    v[:, V_KK:V_KK + 2] = col(inp["a_k_k"][l], 2)
    v[:, V_KA:V_KA + 2] = col(inp["a_k_a"][l], 2)
    v[:, V_RK:V_RK + 2] = col(inp["a_r_k"][l].reshape(-1), 2)
    return v


def ktile(w):
    kk = w.shape[0] // 128
    return np.ascontiguousarray(w.reshape(kk, 128, w.shape[1]).transpose(1, 0, 2))


class Prog:
    pass


def build(debug=False, stop=None, nlayers=L, skip=()):
    nc = bass.Bass("TRN2", target_bir_lowering=False)
    P = Prog()
    din = {}

    def inp(name, shape, dt=F32):
        din[name] = nc.dram_tensor(name, list(shape), dt, kind="ExternalInput").ap()
        return din[name]

    skind = "ExternalOutput" if debug else "Internal"
    dscr = {}

    def scr(name, shape, dt):
        dscr[name] = nc.dram_tensor(name, list(shape), dt, kind=skind).ap()
        return dscr[name]

    x_in = inp("x", [S_LEN, D])
    pos = inp("pos", [1, S_LEN], I32)
    consts = inp("consts", [128, NCONST])
    vecs = inp("vecs", [L, 128, NV])
    lnwb = inp("lnwb", [L, 128, 512])
    gfinb = inp("gfinb", [128, D])
    w_in = inp("w_in", [L, 128, 8, 2272])
    w_uq = inp("w_uq", [L, 128, 2, 768])
    w_ukv = inp("w_ukv", [L, 128, 1, 1024])
    w_out = inp("w_out", [L, 128, 8, 1024])
    w_gate = inp("w_gate", [L, 128, 8, DFF])
    w_up = inp("w_up", [L, 128, 8, DFF])
    w_down = inp("w_down", [L, 128, 22, 1024])
    a_dec = inp("a_dec", [L, 64, 256])
    a_icl = inp("a_icl", [L, 64, 256])
    a_gate = inp("a_gate", [L, 160, 256])
    out = nc.dram_tensor("out", [S_LEN, D], F32, kind="ExternalOutput").ap()

    pA = scr("pA", [1056, S_LEN], F32)
    qB = [scr(f"qB{i}", [256, S_LEN], BF16) for i in range(3)]
    kB = [scr(f"kB{i}", [256, S_LEN], BF16) for i in range(3)]
    vB = scr("vB", [S_LEN, 256], BF16)
    qn = scr("qn", [512, S_LEN], BF16)
    qr = scr("qr", [256, S_LEN], BF16)
    kn = scr("kn", [512, S_LEN], BF16)
    kr = scr("kr", [64, S_LEN], BF16)
    vC = scr("vC", [S_LEN, 512], BF16)
    cat = scr("cat", [1024, S_LEN], BF16)
    x1 = scr("x1", [S_LEN, D], F32)
    xr = scr("xr", [S_LEN, D], F32)
    aT = scr("aT", [DFF, S_LEN], BF16)
    tabB = scr("tabB", [2, 128, S_LEN], F32)
    tabC = scr("tabC", [2, 64, S_LEN], F32)

    with contextlib.ExitStack() as es:
        E = es.enter_context
        names = list(ENGS) + [f"dma_{q}_{i}" for q in ("sync", "gpsimd") for i in range(NSLOTS[q])]
        sems = {n: E(nc.semaphore(n)) for n in names}
        S = Sched(nc, sems)
        k = K(nc, S)
        k.psum = [Tl(E(nc.psum_tensor(f"ps{i}", [128, 512], F32))) for i in range(8)]
        pTt = Tl(k.psum[7][:].bitcast(BF16).rearrange("p (a b) -> p a b", a=8))
        pTt.b = k.psum[7].b
        cst = Tl(E(nc.sbuf_tensor("cst", [128, NCONST], F32)))
        cstb = Tl(E(nc.sbuf_tensor("cstb", [128, NCONST], BF16)))
        k.dma("sync", cst[:], consts[:, :], writes=[cst])
        k.copy("vector", cstb[:], cst[:], reads=[cst], writes=[cstb])
        epsc = Tl(E(nc.sbuf_tensor("epsc", [128, 1], F32)))
        k.memset("vector", epsc[:], EPS, writes=[epsc])
        P.__dict__.update(locals())

        phase_tables(P)
        done = (stop == "tables")
        for l in range(nlayers):
            if done:
                break
            xsrc = x_in if l == 0 else xr
            phase_p1(P, l, xsrc)
            if stop == f"p1_{l}":
                break
            if "mla" not in skip:
                phase_mla(P, l)
            if stop == f"mla_{l}":
                break
            if "dil" not in skip:
                phase_dil(P, l)
            if stop == f"dil_{l}":
                break
            phase_rwkv(P, l)
            if stop == f"rwkv_{l}":
                break
            phase_tail(P, l, xsrc, last=(l == nlayers - 1))
        S.barrier()
        S.emit()
    return nc


def phase_tables(P):
    k, cst = P.k, P.cst
    with k.scope() as sc:
        posi = sc.tile([128, S_LEN], I32)
        posf = sc.tile([128, S_LEN], F32)
        y = sc.tile([128, S_LEN], F32)
        t = sc.tile([128, S_LEN], F32)
        r = sc.tile([128, S_LEN], F32)
        o = sc.tile([128, S_LEN], F32)
        src = bass.AP(P.pos.tensor, 0, [[0, 128], [1, S_LEN]])
        k.dma("sync", posi[:], src, writes=[posi])
        k.copy("vector", posf[:], posi[:], reads=[posi], writes=[posf])
        for ti, (fc, rows, dst) in enumerate(((C_FC, 128, P.tabB), (C_FC + 1, 64, P.tabC))):
            for cs in range(2):
                add = 0.25 if cs == 0 else 0.0
                k.ts("vector", y[:rows], posf[:rows], cst[:rows, fc:fc + 1], add, ALU.mult, ALU.add,
                     reads=[posf, cst], writes=[y])
                k.ts("vector", t[:rows], y[:rows], MAGIC, None, ALU.add, reads=[y], writes=[t])
                k.ts("vector", t[:rows], t[:rows], -MAGIC, None, ALU.add, reads=[t], writes=[t])
                k.tt("vector", r[:rows], y[:rows], t[:rows], ALU.subtract, reads=[y, t], writes=[r])
                k.act(o[:rows], r[:rows], AF.Sin, scale=6.28318, reads=[r], writes=[o])
                k.dma("sync", dst[cs, :, :], o[:rows], reads=[o])


def load_w(P, sc, dst, src, kc, n, stage, eng="gpsimd"):
    k = P.k
    for i in range(kc):
        st = stage.get()
        k.dma("sync", st[:, 0:n], src[:, i, :], writes=[st])
        k.copy(eng, dst[:, i, :], st[:, 0:n], reads=[st], writes=[dst])


def load_w2(P, dst, src, kc, nsplit):
    step = (kc + nsplit - 1) // nsplit
    for i in range(0, kc, step):
        j = min(kc, i + step)
        P.k.dma("gpsimd", dst[:, i:j, :], src[:, i:j, :], writes=[dst], merge=True)


def rmsnorm_T(P, xt, junk, ss, rstd, xn, pT, gb, hT_ap, hT):
    k = P.k
    k.act(junk[:], xt[:], AF.Square, accum_out=ss[:], reads=[xt], writes=[junk, ss])
    k.act(rstd[:], ss[:], AF.Sqrt, bias=EPS, scale=1.0 / D, reads=[ss], writes=[rstd])
    k.recip(rstd[:], rstd[:], reads=[rstd], writes=[rstd])
    k.ts("vector", xn[:], xt[:], rstd[:, 0:1], None, ALU.mult, reads=[xt, rstd], writes=[xn])
    for kk in range(8):
        k.tr(pT[:, kk, :], xn[:, kk * 128:(kk + 1) * 128], P.cstb[:, C_ID:C_ID + 128], reads=[xn, P.cstb], writes=[pT])
    k.tt("vector", hT_ap, pT[:], gb[:], ALU.mult, reads=[pT, gb], writes=[hT])


def make_gb(P, sc, vec, c0):
    k = P.k
    gb = sc.tile([128, 8, 128], BF16)
    for kk in range(8):
        k.ts("vector", gb[:, kk, :], P.cst[:, C_ONE:C_ONE + 128], vec[:, c0 + kk:c0 + kk + 1], None, ALU.mult,
             reads=[P.cst, vec], writes=[gb])
    return gb


def rope(P, ps, rows, permT, cos, sin, scale, tmp, outs, srcs, after=None):
    k = P.k
    qs = tmp.get()
    k.act(qs[:rows], ps[:rows], AF.Copy, scale=scale, reads=[ps], writes=[qs])

    def second():
        rope_b(P, qs, rows, permT, cos, sin, tmp, outs, srcs)
        if after is not None:
            after()
    return second


def rope_b(P, qs, rows, permT, cos, sin, tmp, outs, srcs):
    k = P.k
    pp = k.ps()
    k.mm(pp[:rows], permT, qs[:rows], reads=[qs, P.cst], writes=[pp])
    t1 = tmp.get()
    k.tt("vector", t1[:rows], qs[:rows], cos, ALU.mult, reads=[qs] + srcs, writes=[t1])
    t2 = tmp.get()
    k.tt("vector", t2[:rows], pp[:rows], sin, ALU.mult, reads=[pp] + srcs, writes=[t2])
    for eng, out_ap, view, tl in outs:
        k.tt(eng, out_ap, view(t1[:rows]), view(t2[:rows]), ALU.add, reads=[t1, t2], writes=[tl])


def phase_p1(P, l, xsrc):
    k, cst, cstb = P.k, P.cst, P.cstb
    with k.scope() as sc:
        winb = sc.tile([128, 8, 2272], BF16)
        wuqb = sc.tile([128, 2, 768], BF16)
        wukvb = sc.tile([128, 1, 1024], BF16)
        vec = sc.tile([128, NV], F32)
        k.dma("sync", vec[:], P.vecs[l], writes=[vec])
        load_w2(P, winb, P.w_in[l], 8, 4)
        load_w2(P, wuqb, P.w_uq[l], 2, 1)
        load_w2(P, wukvb, P.w_ukv[l], 1, 1)
        gb = make_gb(P, sc, vec, V_GATT)
        xts = sc.pool(3, [128, D], F32)
        junk = sc.tile([128, D], BF16)
        ss = sc.tile([128, 1], F32)
        rstd = sc.tile([128, 1], F32)
        xn = sc.tile([128, D], BF16)
        hTs = sc.pool(2, [128, 8, 512], BF16)
        pTt = P.pTt
        stA = sc.pool(3, [128, 512], F32)
        tmp = sc.pool(6, [128, 512], F32)
        ob = sc.pool(8, [128, 512], BF16)
        o16 = [[sc.tile([128, 16, 128], BF16) for _ in range(4)] for _ in range(2)]
        tabs = sc.pool(2, [128, 4, 512], F32)
        vbt = sc.pool(2, [128, 4, 256], BF16)
        vct = sc.pool(2, [128, 4, 512], BF16)
        csb = sc.tile([128, 3, 512], F32)
        sqb = sc.tile([128, 3, 512], F32)
        rs = sc.pool(2, [128, 512], F32)
        cqn = sc.tile([128, 3, 512], BF16)
        ones = cst[:, C_ONE:C_ONE + 128]
        A_CH = [(i * 128, 128) for i in range(8)] + [(1024, 32)]

        def norm_gen(g_, hT_):
            for tt in range(4):
                xt = xts.get()
                k.dma("sync", xt[:], xsrc[g_ * 512 + tt * 128: g_ * 512 + (tt + 1) * 128, :], writes=[xt])
                rmsnorm_T(P, xt, junk, ss, rstd, xn, pTt, gb, hT_[:, :, tt * 128:(tt + 1) * 128], hT_)
                yield
        hT_next = hTs.get()
        for _ in norm_gen(0, hT_next):
            pass
        for g in range(NG):
            cols = slice(g * 512, (g + 1) * 512)
            hT = hT_next
            gen = None
            if g + 1 < NG:
                hT_next = hTs.get()
                gen = norm_gen(g + 1, hT_next)

            def step(gen=gen):
                if gen is not None:
                    next(gen, None)
            tb = tabs.get()
            k.dma("sync", tb[:, 0:2, :], P.tabB[:, :, cols].rearrange("c p t -> p c t"), writes=[tb])
            k.dma("sync", tb[0:64, 2:4, :], P.tabC[:, :, cols].rearrange("c p t -> p c t"), writes=[tb])

            def proj(c0, m):
                ps = k.ps()
                for kk in range(8):
                    k.mm(ps[:m], winb[:, kk, c0:c0 + m], hT[:, kk, :], start=(kk == 0), stop=(kk == 7),
                         reads=[winb, hT], writes=[ps])
                return ps
            pend = []

            def flush():
                while pend:
                    pend.pop(0)()

            def a_chunk(ci):
                c0, m = A_CH[ci]
                ps = proj(c0, m)
                st = stA.get()
                k.copy("scalar", st[:m], ps[:m], reads=[ps], writes=[st])
                k.dma("gpsimd", P.pA[c0:c0 + m, cols], st[:m], reads=[st])
                if ci in (2, 5, 8):
                    step()
            for j, (c0, m) in enumerate(((1824, 128), (1952, 128), (2080, 128))):
                ps = proj(c0, m)
                k.copy("scalar", csb[:, j, :], ps[:], reads=[ps], writes=[csb])
                k.tt("gpsimd", sqb[:, j, :], csb[:, j, :], csb[:, j, :], ALU.mult, reads=[csb], writes=[sqb])
            for ci in range(3):
                a_chunk(ci)
            for which in range(2):
                js = (0, 1) if which == 0 else (2,)
                nfeat = 256.0 if which == 0 else 128.0
                ps = k.ps()
                for i, j in enumerate(js):
                    k.mm(ps[:], ones, sqb[:, j, :], start=(i == 0), stop=(i == len(js) - 1), reads=[cst, sqb], writes=[ps])
                r_ = rs.get()
                k.act(r_[:], ps[:], AF.Ln, bias=P.epsc[:, 0:1], scale=1.0 / nfeat, reads=[ps, P.epsc], writes=[r_])
                k.act(r_[:], r_[:], AF.Exp, scale=-0.5, reads=[r_], writes=[r_])
                for j in js:
                    gcol = vec[:, V_CQG + j:V_CQG + j + 1]
                    k.stt("vector", cqn[:, j, :], csb[:, j, :], gcol, r_[:], ALU.mult, ALU.mult,
                          reads=[csb, vec, r_], writes=[cqn])
            for ci in range(3, 9):
                a_chunk(ci)
            for qk in range(2):
                for ch in range(2):
                    c0 = 1056 + qk * 256 + ch * 128
                    ps = proj(c0, 128)
                    flush()
                    o1 = ob.get()
                    o4 = ob.get()
                    o16t = o16[qk][ch * 2 + 0]
                    off = (g % 4) * 32
                    outs = [
                        ("vector", o1[:], (lambda v: v), o1),
                        ("gpsimd", o4[:].rearrange("p (r l) -> p r l", r=4), (lambda v: v.rearrange("p (l r) -> p r l", r=4)), o4),
                        ("gpsimd", o16t[:, :, off:off + 32], (lambda v: v.rearrange("p (l r) -> p r l", r=16)), o16t),
                    ]
                    dst = P.qB if qk == 0 else P.kB
                    rows = slice(ch * 128, (ch + 1) * 128)

                    def after(dst=dst, rows=rows, o1=o1, o4=o4, o16t=o16t):
                        k.dma("gpsimd", dst[0][rows, cols], o1[:], reads=[o1])
                        k.dma("gpsimd", dst[1][rows, cols], o4[:], reads=[o4])
                        if g % 4 == 3:
                            sp = g // 4
                            k.dma("gpsimd", dst[2][rows, sp * 2048:(sp + 1) * 2048], o16t[:].rearrange("p r l -> p (r l)"), reads=[o16t])
                    pend.append(rope(P, ps, 128, cst[:, C_PB:C_PB + 128], tb[:, 0, :], tb[:, 1, :], (0.125 if qk == 0 else 1.0),
                                     tmp, outs, [tb], after))
            step()
            vb = vbt.get()
            for tt in range(4):
                ps = k.ps()
                for kk in range(8):
                    k.mm(ps[:, 0:256], hT[:, kk, tt * 128:(tt + 1) * 128], winb[:, kk, 1568:1824], start=(kk == 0), stop=(kk == 7),
                         reads=[winb, hT], writes=[ps])
                if tt == 0:
                    flush()
                k.copy("scalar", vb[:, tt, :], ps[:, 0:256], reads=[ps], writes=[vb])
            k.dma("gpsimd", P.vB[cols, :].rearrange("(t p) c -> p t c", p=128), vb[:], reads=[vb])
            qscale = 192.0 ** -0.5
            for h in range(4):
                ps = k.ps()
                for kc in range(2):
                    k.mm(ps[:], wuqb[:, kc, h * 192:h * 192 + 128], cqn[:, kc, :], start=(kc == 0), stop=(kc == 1),
                         reads=[wuqb, cqn], writes=[ps])
                o = ob.get()
                k.act(o[:], ps[:], AF.Copy, scale=qscale, reads=[ps], writes=[o])
                k.dma("gpsimd", P.qn[h * 128:(h + 1) * 128, cols], o[:], reads=[o])
                ps = k.ps()
                for kc in range(2):
                    k.mm(ps[0:64], wuqb[:, kc, h * 192 + 128:h * 192 + 192], cqn[:, kc, :], start=(kc == 0), stop=(kc == 1),
                         reads=[wuqb, cqn], writes=[ps])
                flush()
                o = ob.get()

                def after(o=o, h=h):
                    k.dma("gpsimd", P.qr[h * 64:(h + 1) * 64, cols], o[0:64], reads=[o])
                pend.append(rope(P, ps, 64, cst[0:64, C_PC:C_PC + 64], tb[0:64, 2, :], tb[0:64, 3, :], qscale, tmp,
                                 [("vector", o[0:64], (lambda v: v), o)], [tb], after))
            for h in range(4):
                ps = k.ps()
                k.mm(ps[:], wukvb[:, 0, h * 128:(h + 1) * 128], cqn[:, 2, :], reads=[wukvb, cqn], writes=[ps])
                if h == 0:
                    flush()
                o = ob.get()
                k.copy("scalar", o[:], ps[:], reads=[ps], writes=[o])
                k.dma("gpsimd", P.kn[h * 128:(h + 1) * 128, cols], o[:], reads=[o])
            ps = proj(2208, 64)
            o = ob.get()

            def after(o=o):
                k.dma("gpsimd", P.kr[:, cols], o[0:64], reads=[o])
            pend.append(rope(P, ps, 64, cst[0:64, C_PC:C_PC + 64], tb[0:64, 2, :], tb[0:64, 3, :], 1.0, tmp,
                             [("vector", o[0:64], (lambda v: v), o)], [tb], after))
            vc = vct.get()
            for tt in range(4):
                ps = k.ps()
                k.mm(ps[:], cqn[:, 2, tt * 128:(tt + 1) * 128], wukvb[:, 0, 512:1024], reads=[wukvb, cqn], writes=[ps])
                if tt == 0:
                    flush()
                k.copy("scalar", vc[:, tt, :], ps[:], reads=[ps], writes=[vc])
            k.dma("gpsimd", P.vC[cols, :].rearrange("(t p) c -> p t c", p=128), vc[:], reads=[vc])
            flush()


def phase_mla(P, l):
    k, cst, cstb = P.k, P.cst, P.cstb
    with k.scope() as sc:
        knp = sc.pool(2, [128, S_LEN], BF16)
        qnp = sc.pool(2, [128, S_LEN], BF16)
        qrp = sc.pool(2, [128, S_LEN], BF16)
        vp = sc.pool(2, [128, 32, 128], BF16)
        krt = sc.tile([128, S_LEN], BF16)
        for t_ in qrp.tiles + [krt]:
            k.memset("gpsimd", t_[64:128, :], 0.0, writes=[t_])
        pacc1p = sc.pool(3, [128, 512], F32)
        pTp = sc.pool(8, [128, 512], BF16)
        ob = sc.pool(2, [128, 512], BF16)
        rd = sc.pool(2, [128, 512], F32)
        paccp = sc.pool(3, [128, 512], F32)
        k.dma("sync", krt[0:64, :], P.kr[:, :], writes=[krt])
        ones = cst[:, C_ONE:C_ONE + 128]
        identb = cstb[:, C_ID:C_ID + 128]
        nbias = cstb[:, C_NB:C_NB + 128]
        onesb = cstb[:, C_ONE:C_ONE + 128]
        pending = []
        for h in range(4):
            knh, qnh, qrh, vh = knp.get(), qnp.get(), qrp.get(), vp.get()
            k.dma("sync", knh[:], P.kn[h * 128:(h + 1) * 128, :], writes=[knh])
            k.dma("sync", qnh[:], P.qn[h * 128:(h + 1) * 128, :], writes=[qnh])
            k.dma("sync", qrh[0:64, :], P.qr[h * 64:(h + 1) * 64, :], writes=[qrh])
            k.dma("sync", vh[:], P.vC[:, h * 128:(h + 1) * 128].rearrange("(t p) c -> p t c", p=128), writes=[vh])
            for g in range(NG):
                k.rot = [0, 1, 2, 3]
                po = k.psum[4 + (g % 2)]
                pden = k.psum[6 + (g % 2)]
                pe_den = [kt for kt in range(4 * g + 4) if kt % 3 == 1] if g > 0 else []
                pacc = paccp.get()
                pacc1 = pacc1p.get()
                k.memset("gpsimd", pacc1[:], 0.0, writes=[pacc1])
                nkt = 4 * g + 4

                def s_stage(kt, g=g, pacc=pacc, pacc1=pacc1, pe_den=pe_den):
                    o = kt - 4 * g
                    c0 = max(o, 0) * 128
                    ks = slice(kt * 128, (kt + 1) * 128)
                    qs = slice(g * 512 + c0, (g + 1) * 512)
                    pS = k.ps()
                    k.mm(pS[:, c0:512], knh[:, ks], qnh[:, qs], start=True, stop=False, reads=[knh, qnh], writes=[pS])
                    k.mm(pS[:, c0:512], krt[:, ks], qrh[:, qs], start=False, stop=(o < 0), reads=[krt, qrh], writes=[pS])
                    if o >= 0:
                        k.mm(pS[:, c0:c0 + 128], identb, nbias, start=False, stop=True, reads=[cstb], writes=[pS])
                    pT = pTp.get()
                    k.act(pT[:, c0:512], pS[:, c0:512], AF.Exp, reads=[pS], writes=[pT])
                    if kt == 0:
                        k.copy("vector", pacc[:], pT[:], reads=[pT], writes=[pacc])
                    elif kt in pe_den:
                        pass
                    elif kt % 3 == 2:
                        k.tt("gpsimd", pacc1[:, c0:512], pacc1[:, c0:512], pT[:, c0:512], ALU.add, reads=[pT, pacc1], writes=[pacc1])
                    else:
                        k.tt("vector", pacc[:, c0:512], pacc[:, c0:512], pT[:, c0:512], ALU.add, reads=[pT, pacc], writes=[pacc])
                    return kt, c0, pT

                def pv_stage(st, po=po, nkt=nkt, pden=pden, pe_den=pe_den):
                    kt, c0, pT = st
                    k.mm(po[:, c0:512], vh[:, kt, :], pT[:, c0:512], start=(kt == 0), stop=(kt == nkt - 1), reads=[vh, pT], writes=[po])
                    if kt in pe_den:
                        k.mm(pden[:, c0:512], onesb, pT[:, c0:512], start=(kt == pe_den[0]), stop=False, reads=[cstb, pT], writes=[pden],
                             skip=True)

                def finalize(g=g, h=h, po=po, pacc=pacc, pacc1=pacc1, pd=pden, pe_den=pe_den):
                    k.mm(pd[:], ones, pacc[:], start=(not pe_den), stop=False, reads=[cst, pacc], writes=[pd], skip=True)
                    k.mm(pd[:], ones, pacc1[:], start=False, stop=True, reads=[cst, pacc1], writes=[pd], skip=True)
                    r = rd.get()
                    k.act(r[:], pd[:], AF.Ln, reads=[pd], writes=[r])
                    k.act(r[:], r[:], AF.Exp, scale=-1.0, reads=[r], writes=[r])
                    oo = ob.get()
                    k.tt("vector", oo[:], po[:], r[:], ALU.mult, reads=[po, r], writes=[oo])
                    k.dma("gpsimd", P.cat[512 + h * 128:512 + (h + 1) * 128, g * 512:(g + 1) * 512], oo[:], reads=[oo])
                DEPTH = 2
                q_ = []
                for kt in range(nkt):
                    q_.append(s_stage(kt))
                    if kt == 2 and pending:
                        pending.pop()()
                    if len(q_) > DEPTH:
                        pv_stage(q_.pop(0))
                while q_:
                    pv_stage(q_.pop(0))
                pending.append(finalize)
        while pending:
            pending.pop()()
        k.rot = None


def phase_dil(P, l):
    k, cst, cstb = P.k, P.cst, P.cstb
    with k.scope() as sc:
        vts = [sc.tile([128, 32, 256], BF16) for _ in range(3)]
        vap = sc.pool(2, [128, 32, 128], BF16)
        qp = [sc.pool(2, [128, S_LEN], BF16) for _ in range(2)]
        kp = sc.pool(2, [128, S_LEN], BF16)
        acc = sc.tile([128, S_LEN], F32)
        pTp = sc.pool(8, [128, 512], BF16)
        ob = sc.pool(2, [64, S_LEN], BF16)
        rdp = sc.pool(2, [64, 512], F32)
        for par in range(2):
            for t_ in qp[par].tiles:
                k.memset("gpsimd", t_[(1 - par) * 64:(2 - par) * 64, :], 0.0, writes=[t_])
        for t_ in vap.tiles:
            k.memset("gpsimd", t_[:, :, 64:128], 1.0, writes=[t_])
        DS = (1, 4, 16)
        for pi, d in enumerate(DS):
            if d == 1:
                src = P.vB[:, :].rearrange("(s l) c -> l s c", l=128)
                k.dma("sync", vts[pi][:], src, writes=[vts[pi]])
            else:
                for r in range(d):
                    src = P.vB[:, :].rearrange("(s l r) c -> r l s c", l=128, r=d)[r]
                    dstv = vts[pi][:].rearrange("p (s r) c -> p r s c", r=d)[:, r]
                    k.dma("sync", dstv, src, writes=[vts[pi]], merge=True)
        identb = cstb[:, C_ID:C_ID + 128]
        nb_cur = cstb[:, C_NB:C_NB + 128]
        nb_prev = cstb[:, C_NB + 128:C_NB + 256]
        shiftT = cst[:, C_ID + 64:C_ID + 128]
        k.rot = [0, 1, 2, 3, 4, 5, 6]
        for h in range(4):
            hc = slice(h * 64, (h + 1) * 64)
            par = h % 2
            pr = slice((h // 2) * 128, (h // 2 + 1) * 128)
            for pi, d in enumerate(DS):
                qh, kh = qp[par].get(), kp.get()
                k.dma("sync", qh[par * 64:(par + 1) * 64, :], P.qB[pi][hc, :], writes=[qh])
                k.dma("sync", kh[:], P.kB[pi][pr, :], writes=[kh])
                vt = vap.get()
                k.copy("gpsimd", vt[:, :, 0:64], vts[pi][:, :, hc], reads=[vts[pi]], writes=[vt])

                def s_stage(bt, d=d, qh=qh, kh=kh):
                    b0 = bt * 4
                    pc, pp = k.ps(), k.ps()
                    hasp = [(b0 + j - d) >= 0 for j in range(4)]
                    for j in range(4):
                        cs = slice((b0 + j) * 128, (b0 + j + 1) * 128)
                        js = slice(j * 128, (j + 1) * 128)
                        k.mm(pc[:, js], kh[:, cs], qh[:, cs], start=True, stop=False, reads=[kh, qh], writes=[pc])
                        k.mm(pc[:, js], identb, nb_cur, start=False, stop=True, reads=[cstb], writes=[pc])
                        if hasp[j]:
                            ps_ = slice((b0 + j - d) * 128, (b0 + j - d + 1) * 128)
                            k.mm(pp[:, js], kh[:, ps_], qh[:, cs], start=True, stop=False, reads=[kh, qh], writes=[pp])
                            k.mm(pp[:, js], identb, nb_prev, start=False, stop=True, reads=[cstb], writes=[pp])
                    tc_ = pTp.get()
                    k.act(tc_[:], pc[:], AF.Exp, reads=[pc], writes=[tc_])
                    tp_ = None
                    if any(hasp):
                        j0 = hasp.index(True)
                        tp_ = pTp.get()
                        k.act(tp_[:, j0 * 128:512], pp[:, j0 * 128:512], AF.Exp, reads=[pp], writes=[tp_])
                    return bt, hasp, tc_, tp_

                def pv_stage(st, d=d, pi=pi, vt=vt):
                    bt, hasp, tc_, tp_ = st
                    b0 = bt * 4
                    pn = k.ps()
                    for j in range(4):
                        js = slice(j * 128, (j + 1) * 128)
                        if hasp[j]:
                            k.mm(pn[:, js], vt[:, b0 + j - d, :], tp_[:, js], start=True, stop=False, reads=[vt, tp_], writes=[pn])
                        k.mm(pn[:, js], vt[:, b0 + j, :], tc_[:, js], start=(not hasp[j]), stop=True, reads=[vt, tc_], writes=[pn])
                    if d == 1:
                        va = acc[:, b0 * 128:b0 * 128 + 512]
                        vpz = pn[:, :]
                    elif d == 4:
                        va = acc[:, bt * 512:(bt + 1) * 512].rearrange("p (l r) -> p r l", r=4)
                        vpz = pn[:, :].rearrange("p (r l) -> p r l", r=4)
                    else:
                        sp, r0 = b0 // 16, b0 % 16
                        va = acc[:, sp * 2048:(sp + 1) * 2048].rearrange("p (l r) -> p r l", r=16)[:, r0:r0 + 4, :]
                        vpz = pn[:, :].rearrange("p (r l) -> p r l", r=4)
                    if pi == 0:
                        k.copy("vector", va, vpz, reads=[pn], writes=[acc])
                    else:
                        k.tt("vector", va, va, vpz, ALU.add, reads=[pn, acc], writes=[acc])
                DEPTH = 2
                q_ = []
                for bt in range(8):
                    q_.append(s_stage(bt))
                    if len(q_) > DEPTH:
                        pv_stage(q_.pop(0))
                while q_:
                    pv_stage(q_.pop(0))
            oo = ob.get()
            for c8 in range(8):
                cs8 = slice(c8 * 512, (c8 + 1) * 512)
                pd = k.ps()
                k.mm(pd[0:64, :], shiftT, acc[:, cs8], reads=[cst, acc], writes=[pd])
                r_ = rdp.get()
                k.act(r_[:], pd[0:64, :], AF.Ln, reads=[pd], writes=[r_])
                k.act(r_[:], r_[:], AF.Exp, scale=-1.0, reads=[r_], writes=[r_])
                k.tt("gpsimd", oo[:, cs8], acc[0:64, cs8], r_[:], ALU.mult, reads=[acc, r_], writes=[oo])
            k.dma("gpsimd", P.cat[256 + h * 64:256 + (h + 1) * 64, :], oo[:], reads=[oo])
        k.rot = None


C0 = 0.6065306597126334
GN_EPS = 64e-5


def bc(ap, n):
    return bass.AP(ap.tensor, ap.offset, [list(x) for x in ap.ap] + [[0, n]])


def phase_rwkv(P, l):
    k, cst, cstb = P.k, P.cst, P.cstb
    AX = mybir.AxisListType
    with k.scope() as sc:
        identf = cst[:, C_ID:C_ID + 128]
        bones = cst[:, C_BO:C_BO + 128]
        ones = cst[:, C_ONE:C_ONE + 128]
        vec = sc.tile([128, NV], F32)
        lnwb = sc.tile([128, 512], F32)
        dec = sc.tile([128, 256], F32)
        icl = sc.tile([128, 256], F32)
        gateA = sc.tile([128, 256], F32)
        gateB = sc.tile([128, 256], F32)
        for t_ in (dec, icl, gateB):
            k.memset("gpsimd", t_[:], 0.0, writes=[t_])
        k.dma("sync", vec[:], P.vecs[l], writes=[vec])
        k.dma("sync", lnwb[:], P.lnwb[l], writes=[lnwb])
        k.dma("sync", dec[0:64, :], P.a_dec[l], writes=[dec])
        k.dma("sync", icl[64:128, :], P.a_icl[l], writes=[icl])
        k.dma("sync", gateA[:], P.a_gate[l, 0:128, :], writes=[gateA])
        k.dma("sync", gateB[0:32, :], P.a_gate[l, 128:160, :], writes=[gateB])
        msk4 = sc.tile([128, 512], F32)
        miu4 = sc.tile([128, 512], F32)
        msu2 = sc.tile([128, 256], F32)
        for j in range(4):
            src = C_TRI + (384 if j % 2 == 0 else 256)
            k.copy("vector", msk4[:, j * 128:(j + 1) * 128], cst[:, src:src + 128], reads=[cst], writes=[msk4])
            k.copy("vector", miu4[:, j * 128:(j + 1) * 128], cst[:, C_TRI:C_TRI + 128], reads=[cst], writes=[miu4])
        for j in range(2):
            k.copy("vector", msu2[:, j * 128:(j + 1) * 128], cst[:, C_TRI + 256:C_TRI + 384], reads=[cst], writes=[msu2])
        rawp = sc.pool(2, [128, 513], F32)
        dtmp = sc.pool(2, [128, 512], F32)
        PFk = [sc.tile([128, 512], F32) for _ in range(2)]
        PF6 = sc.tile([128, 512], F32)
        PF7 = sc.tile([128, 512], F32)
        PF8 = sc.tile([128, 512], F32)
        tmpf = sc.pool(3, [128, 512], F32)
        th = sc.tile([128, 512], F32)
        SG = sc.tile([128, 512], F32)
        Aic = sc.tile([128, 512], F32)

        class HOc:
            pass
        HO = []
        for i in range(2):
            ho = HOc()
            for nm_ in ("R", "V", "CS", "CSX", "KN", "KF", "Bv", "RKR"):
                setattr(ho, nm_, [sc.tile([128, 512], F32) for _ in range(2)])
            ho.SX7 = sc.tile([128, 512], F32)
            ho.SX8 = sc.tile([128, 512], F32)
            k.memset("gpsimd", ho.SX8[:], 0.0, writes=[ho.SX8])
            HO.append(ho)

        class Slot:
            pass
        slots = []
        for s_ in range(4):
            sl = Slot()
            sl.ep = sc.pool(5, [128, 128], F32)
            sl.dp = sc.pool(4, [128, 128], F32)
            sl.APz = [sc.tile([128, 128], BF16) for _ in range(2)]
            sl.bkb = sc.pool(2, [128, 2, 128], BF16)
            sl.RPz = [sc.tile([128, 128], F32) for _ in range(2)]
            for t_ in sl.APz + sl.RPz:
                k.memset("gpsimd", t_[:], 0.0, writes=[t_])
            sl.smp = sc.pool(8, [128, 2], F32)
            sl.tmp = sc.pool(1, [128, 3, 128], F32)
            sl.MNp = sc.pool(2, [128, 2, 2, 128], BF16)
            sl.mkp = sc.pool(1, [128, 2, 128], F32)
            sl.wrp = sc.pool(1, [128, 2, 2, 128], F32)
            sl.Xbp = sc.pool(2, [128, 2, 2, 64], BF16)
            sl.Xfp = sc.pool(1, [128, 2, 2, 64], F32)
            sl.sq = sc.pool(6, [128, 128], F32)
            sl.Hpp = sc.pool(1, [128, 128], F32)
            sl.px = k.psum[4 + s_]
            slots.append(sl)
        k.rot = [0, 1, 2, 3]
        STp = [sc.pool(2, [128, 128], F32) for _ in range(2)]
        catA = sc.pool(3, [128, 2, 512], BF16)
        ST = []
        for cp in range(2):
            t = STp[cp].get()
            k.memset("gpsimd", t[:], 0.0, writes=[t])
            ST.append(t)
        f2 = lambda t: t[:].rearrange("p a b c -> p (a b c)")

        def a1gen(g, ho):
            lo = g * 512 - 1

            def mix(ch, dst_ap, dst_tl):
                rows = 128 if ch < 8 else 32
                rw = rawp.get()
                rsl = slice(ch * 128, ch * 128 + rows)
                if g == 0:
                    k.memset("gpsimd", rw[:, 0:1], 0.0, writes=[rw])
                    k.dma("sync", rw[:rows, 1:513], P.pA[rsl, 0:512], writes=[rw])
                else:
                    k.dma("sync", rw[:rows, :], P.pA[rsl, lo:lo + 513], writes=[rw])
                d_ = dtmp.get()
                k.tt("gpsimd", d_[:rows], rw[:rows, 0:512], rw[:rows, 1:513], ALU.subtract, reads=[rw], writes=[d_])
                k.stt("vector", dst_ap, d_[:rows], vec[:rows, V_MU + ch:V_MU + ch + 1], rw[:rows, 1:513],
                      ALU.mult, ALU.add, reads=[d_, vec, rw], writes=[dst_tl])
            dsts = [ho.R[0], ho.R[1], PFk[0], PFk[1], ho.V[0], ho.V[1], PF6, PF7, PF8]
            for ch in range(9):
                d = dsts[ch]
                mix(ch, d[:] if ch < 8 else d[0:32, :], d)
                if ch % 2 == 1:
                    yield
            k.act(th[:], PF6[:], AF.Tanh, reads=[PF6], writes=[th])
            k.act(ho.SX7[:], PF7[:], AF.Sigmoid, reads=[PF7], writes=[ho.SX7])
            k.act(ho.SX8[0:32, :], PF8[0:32, :], AF.Sigmoid, reads=[PF8], writes=[ho.SX8])
            yield
            for cp in range(2):
                pcs = slice(cp * 128, (cp + 1) * 128)
                CS, CSX, KN, KF, Bv, RKR = ho.CS[cp], ho.CSX[cp], ho.KN[cp], ho.KF[cp], ho.Bv[cp], ho.RKR[cp]
                pz = k.ps()
                k.mm(pz[:], dec[:, pcs], th[:, :], reads=[dec, th], writes=[pz])
                k.act(SG[:], pz[:], AF.Sigmoid, bias=vec[:, V_W0 + cp:V_W0 + cp + 1], reads=[pz, vec], writes=[SG])
                pa_ = k.ps()
                k.mm(pa_[:], icl[:, pcs], PF6[:, :], reads=[icl, PF6], writes=[pa_])
                k.act(Aic[:], pa_[:], AF.Sigmoid, bias=vec[:, V_A0 + cp:V_A0 + cp + 1], reads=[pa_, vec], writes=[Aic])
                yield
                for c4 in range(4):
                    cc = slice(c4 * 128, (c4 + 1) * 128)
                    k.S.op("vector", (lambda e, o=CS[:, cc], d1=SG[:, cc]: e.tensor_tensor_scan(
                        out=o, data0=ones, data1=d1, initial=0.0, op0=ALU.mult, op1=ALU.add)),
                        reads=[cst, SG], writes=[CS])
                k.tt("gpsimd", CSX[:], CS[:], SG[:], ALU.subtract, reads=[CS, SG], writes=[CSX])
                yield
                kraw = PFk[cp]
                kkr = tmpf.get()
                k.ts("vector", kkr[:], kraw[:], vec[:, V_KK + cp:V_KK + cp + 1], None, ALU.mult, reads=[kraw, vec], writes=[kkr])
                sq = tmpf.get()
                k.tt("gpsimd", sq[:], kkr[:], kkr[:], ALU.mult, reads=[kkr], writes=[sq])
                pn_ = k.ps()
                k.mm(pn_[:], bones, sq[:], reads=[cst, sq], writes=[pn_])
                nrm = tmpf.get()
                k.ts("vector", nrm[:], pn_[:], 1e-24, None, ALU.max, reads=[pn_], writes=[nrm])
                yield
                k.act(nrm[:], nrm[:], AF.Ln, reads=[nrm], writes=[nrm])
                k.act(nrm[:], nrm[:], AF.Exp, scale=-0.5, reads=[nrm], writes=[nrm])
                k.tt("vector", KN[:], kkr[:], nrm[:], ALU.mult, reads=[kkr, nrm], writes=[KN])
                yield
                t1 = tmpf.get()
                k.ts("vector", t1[:], Aic[:], -1.0, vec[:, V_KA + cp:V_KA + cp + 1], ALU.add, ALU.mult, reads=[Aic, vec], writes=[t1])
                k.stt("vector", KF[:], t1[:], 1.0, kraw[:], ALU.add, ALU.mult, reads=[t1, kraw], writes=[KF])
                k.tt("gpsimd", Bv[:], KN[:], Aic[:], ALU.mult, reads=[KN, Aic], writes=[Bv])
                k.stt("vector", RKR[:], ho.R[cp][:], vec[:, V_RK + cp:V_RK + cp + 1], KF[:], ALU.mult, ALU.mult,
                      reads=[ho.R[cp], vec, KF], writes=[RKR])
                yield

        def unit(sl, ho, cp, c4, cA):
            cc = slice(c4 * 128, (c4 + 1) * 128)
            Rt_, Vt_ = ho.R[cp], ho.V[cp]
            CS, CSX, KN, KF, Bv, RKR = ho.CS[cp], ho.CSX[cp], ho.KN[cp], ho.KF[cp], ho.Bv[cp], ho.RKR[cp]
            rfe = Rt_[:, cc]
            vfe = Vt_[:, cc]
            cs_ = CS[:, cc]
            csx = CSX[:, cc]
            cend = CS[:, c4 * 128 + 127:c4 * 128 + 128]
            sm = sl.smp.get()
            k.ts("vector", sm[:, 0:1], cend, C0, None, ALU.mult, reads=[CS], writes=[sm])
            k.ts("vector", sm[:, 1:2], cend, -C0, None, ALU.mult, reads=[CS], writes=[sm])
            pce, nce = sm[:, 0:1], sm[:, 1:2]
            e1, e2, e3, e4, e5 = (sl.ep.get() for _ in range(5))
            k.act(e1[:], csx, AF.Exp, scale=-C0, reads=[CSX], writes=[e1])
            k.act(e2[:], csx, AF.Exp, scale=-C0, bias=pce, reads=[CSX, sm], writes=[e2])
            k.act(e3[:], cs_, AF.Exp, scale=C0, bias=nce, reads=[CS, sm], writes=[e3])
            k.act(e4[:], cs_, AF.Exp, scale=-C0, reads=[CS], writes=[e4])
            k.act(e5[:], cs_, AF.Exp, scale=-C0, bias=pce, reads=[CS, sm], writes=[e5])
            AT, BT, KT, RT = (sl.dp.get() for _ in range(4))
            APz, RPz = sl.APz, sl.RPz
            k.stt("vector", AT[:], KN[:, cc], -1.0, e1[:], ALU.mult, ALU.mult, reads=[KN, e1], writes=[AT])
            for h in range(2):
                hr = slice(h * 64, (h + 1) * 64)
                k.stt("vector", APz[h][hr, :], KN[hr, cc], -1.0, e2[hr, :], ALU.mult, ALU.mult, reads=[KN, e2], writes=[APz[h]])
                k.tt("gpsimd", RPz[h][hr, :], Rt_[hr, cc], e5[hr, :], ALU.mult, reads=[Rt_, e5], writes=[RPz[h]])
            k.tt("gpsimd", BT[:], Bv[:, cc], e3[:], ALU.mult, reads=[Bv, e3], writes=[BT])
            k.tt("gpsimd", KT[:], KF[:, cc], e3[:], ALU.mult, reads=[KF, e3], writes=[KT])
            k.tt("vector", RT[:], rfe, e4[:], ALU.mult, reads=[Rt_, e4], writes=[RT])
            bkb = sl.bkb.get()
            k.copy("gpsimd", bkb[:, 0, :], BT[:], reads=[BT], writes=[bkb])
            k.copy("gpsimd", bkb[:, 1, :], KT[:], reads=[KT], writes=[bkb])
            yield
            pt = k.ps()
            for q, (srcT, stl) in enumerate(((BT[:], BT), (KT[:], KT), (vfe, Vt_))):
                k.tr(pt[:, q * 128:(q + 1) * 128], srcT, identf, reads=[stl, cst], writes=[pt])
            tm = sl.tmp.get()
            k.copy("scalar", tm[:].rearrange("p a b -> p (a b)"), pt[:, 0:384], reads=[pt], writes=[tm])
            yield
            pmA, pmB, pw = k.ps(), k.ps(), k.ps()
            for h in range(2):
                k.mm(pmA[:, h * 256:h * 256 + 128], APz[h][:], bkb[:, 0, :], reads=[APz[h], bkb], writes=[pmA])
                k.mm(pmA[:, h * 256 + 128:h * 256 + 256], bkb[:, 0, :], APz[h][:], reads=[APz[h], bkb], writes=[pmA])
                k.mm(pmB[:, h * 128:(h + 1) * 128], bkb[:, 1, :], APz[h][:], reads=[APz[h], bkb], writes=[pmB])
                k.mm(pw[:, h * 256:h * 256 + 128], BT[:], RPz[h][:], reads=[RPz[h], BT], writes=[pw])
                k.mm(pw[:, h * 256 + 128:h * 256 + 256], KT[:], RPz[h][:], reads=[RPz[h], KT], writes=[pw])
            MN = sl.MNp.get()
            k.tt("vector", f2(MN), pmA[:], msk4[:], ALU.mult, reads=[pmA, msk4], writes=[MN])
            mk = sl.mkp.get()
            k.tt("vector", mk[:].rearrange("p a b -> p (a b)"), pmB[:, 0:256], msu2[:], ALU.mult, reads=[pmB, msu2], writes=[mk])
            wr = sl.wrp.get()
            k.tt("vector", f2(wr), pw[:], miu4[:], ALU.mult, reads=[pw, miu4], writes=[wr])
            yield
            px = sl.px
            pxv = px[:, 0:256].rearrange("p (a h v) -> p a h v", a=2, h=2)
            for h in range(2):
                k.mm(pxv[:, 1, h, :], mk[:, h, :], tm[:, 2, h * 64:(h + 1) * 64], start=(h == 0), stop=False,
                     reads=[mk, tm], writes=[px], skip=True)
            k.mm(px[:, 0:128], AT[:], identf, start=False, stop=False, reads=[AT, cst], writes=[px], skip=True)
            Xb = sl.Xbp.get()
            k.copy("vector", f2(Xb), px[:, 0:256], reads=[px], writes=[Xb])
            yield
            Xf = None
            for i in range(7):
                for h in range(2):
                    k.mm(pxv[:, :, h, :], MN[:, h, 1, :], Xb[:, :, h, :], start=False, stop=(i == 6 and h == 1),
                         reads=[MN, Xb], writes=[px], skip=True)
                if i < 6:
                    Xb = sl.Xbp.get()
                    k.copy("scalar" if i % 2 == 0 else "vector", f2(Xb), px[:, 0:256], reads=[px], writes=[Xb])
                    pn = k.ps()
                    pnv = pn[:].rearrange("p (h m t) -> p h m t", h=2, m=2)
                    for h in range(2):
                        k.mm(pnv[:, h, 1, :], MN[:, h, 0, :], MN[:, h, 1, :], reads=[MN], writes=[pn])
                        k.mm(pnv[:, h, 0, :], MN[:, h, 1, :], MN[:, h, 0, :], reads=[MN], writes=[pn])
                    MNn = sl.MNp.get()
                    k.copy("scalar", f2(MNn), pn[:], reads=[pn], writes=[MNn])
                    MN = MNn
                else:
                    Xf = sl.Xfp.get()
                    k.copy("vector", f2(Xf), px[:, 0:256], reads=[px], writes=[Xf])
                yield
            X = Xf
            Ahat = X[:, 0].rearrange("p h v -> p (h v)")
            U0 = X[:, 1].rearrange("p h v -> p (h v)")
            pg = k.ps()
            k.mm(pg[:, 0:128], Ahat, tm[:, 0, :], reads=[X, tm], writes=[pg])
            gt1 = sl.sq.get()
            k.tt("vector", gt1[:], pg[:, 0:128], bones, ALU.mult, reads=[pg, cst], writes=[gt1])
            GT = sl.sq.get()
            k.stt("vector", GT[:], identf, e4[:, 127:128], gt1[:], ALU.mult, ALU.add, reads=[cst, e4, gt1], writes=[GT])
            ph = k.ps()
            k.mm(ph[:, 0:128], tm[:, 0, :], U0, start=True, stop=False, reads=[X, tm], writes=[ph])
            k.mm(ph[:, 0:128], tm[:, 1, :], tm[:, 2, :], start=False, stop=True, reads=[tm], writes=[ph])
            Hp = sl.Hpp.get()
            k.tt("vector", Hp[:], ph[:, 0:128], bones, ALU.mult, reads=[ph, cst], writes=[Hp])
            QT = sl.sq.get()
            for h in range(2):
                hr = slice(h * 64, (h + 1) * 64)
                pq = k.ps()
                k.mm(pq[:, 0:128], Ahat, wr[:, h, 0, :], reads=[X, wr], writes=[pq])
                k.tt("vector", QT[hr, :], pq[hr, 0:128], RT[hr, :], ALU.add, reads=[pq, RT], writes=[QT])
            yield
            py = k.ps()
            So = ST[cp]
            k.mm(py[:, 0:128], QT[:], So[:], start=True, stop=False, reads=[QT, So], writes=[py])
            for h in range(2):
                hr = slice(h * 64, (h + 1) * 64)
                ys = py[:, h * 64:(h + 1) * 64]
                k.mm(ys, wr[:, h, 0, :], X[:, 1, h, :], start=False, stop=False, reads=[wr, X], writes=[py])
                k.mm(ys, wr[:, h, 1, :], tm[:, 2, hr], start=False, stop=(h == 1), reads=[wr, tm], writes=[py])
            pS = k.ps()
            k.mm(pS[:, 0:128], GT[:], So[:], reads=[GT, So], writes=[pS])
            Sn = STp[cp].get()
            k.tt("vector", Sn[:], pS[:, 0:128], Hp[:], ALU.add, reads=[pS, Hp], writes=[Sn])
            ST[cp] = Sn
            ysb = sl.sq.get()
            k.copy("scalar", ysb[:], py[:, 0:128], reads=[py], writes=[ysb])
            pb_ = k.ps()
            k.mm(pb_[:, 0:2], RKR[:, cc], cst[:, C_BO:C_BO + 128:64], reads=[RKR, cst], writes=[pb_])
            bon = sl.smp.get()
            k.copy("scalar", bon[:], pb_[:, 0:2], reads=[pb_], writes=[bon])
            yield
            yc = sl.sq.get()
            yn = sl.sq.get()
            v3 = lambda t: t[:].rearrange("p (h v) -> p h v", h=2)
            st_ = sl.smp.get()
            k.S.op("vector", (lambda e, o=st_[:, 0:2], i_=v3(ysb): e.tensor_reduce(out=o, in_=i_, axis=AX.X, op=ALU.add)),
                   reads=[ysb], writes=[st_])
            nm = sl.smp.get()
            k.ts("vector", nm[:, 0:2], st_[:, 0:2], -1.0 / 64, None, ALU.mult, reads=[st_], writes=[nm])
            k.tt("vector", v3(yc), v3(ysb), bc(nm[:, 0:2], 64), ALU.add, reads=[ysb, nm], writes=[yc])
            s2 = sl.smp.get()
            jk = sl.ep.get()
            for h in range(2):
                hs = slice(h * 64, (h + 1) * 64)
                k.act(jk[:, hs], yc[:, hs], AF.Square, accum_out=s2[:, h:h + 1], reads=[yc], writes=[jk, s2])
            rs2 = sl.smp.get()
            k.act(rs2[:, 0:2], s2[:, 0:2], AF.Sqrt, bias=GN_EPS, scale=1.0 / 64, reads=[s2], writes=[rs2])
            k.recip(rs2[:, 0:2], rs2[:, 0:2], reads=[rs2], writes=[rs2])
            hg0 = cp * 2
            k.tt("vector", v3(yn), v3(yc), bc(rs2[:, 0:2], 64), ALU.mult, reads=[yc, rs2], writes=[yn])
            k.tt("gpsimd", yn[:], yn[:], lnwb[:, hg0 * 64:(hg0 + 2) * 64], ALU.mult, reads=[yn, lnwb], writes=[yn])
            k.tt("gpsimd", yn[:], yn[:], lnwb[:, 256 + hg0 * 64:256 + (hg0 + 2) * 64], ALU.add, reads=[yn, lnwb], writes=[yn])
            bv = sl.sq.get()
            k.tt("gpsimd", v3(bv), tm[:, 2, :].rearrange("p (h v) -> p h v", h=2), bc(bon[:, 0:2], 64), ALU.mult,
                 reads=[tm, bon], writes=[bv])
            k.tt("vector", yn[:], yn[:], bv[:], ALU.add, reads=[yn, bv], writes=[yn])
            pgt = k.ps()
            pcs = slice(cp * 128, (cp + 1) * 128)
            k.mm(pgt[:, 0:128], ho.SX7[:, cc], gateA[:, pcs], start=True, stop=False, reads=[ho.SX7, gateA], writes=[pgt])
            k.mm(pgt[:, 0:128], ho.SX8[:, cc], gateB[:, pcs], start=False, stop=True, reads=[ho.SX8, gateB], writes=[pgt])
            ya = sl.sq.get()
            k.tt("vector", ya[:], yn[:], pgt[:, 0:128], ALU.mult, reads=[yn, pgt], writes=[ya])
            pT2 = k.ps()
            k.tr(pT2[:, 0:128], ya[:], identf, reads=[ya, cst], writes=[pT2])
            k.copy("scalar", cA[:, cp, cc], pT2[:, 0:128], reads=[pT2], writes=[cA])

        tasks = [(g, c4) for g in range(NG) for c4 in range(4)]
        a1 = {}
        a1_finished = set()

        def finish_a1(g):
            if g not in a1_finished:
                for _ in a1[g]:
                    pass
                a1_finished.add(g)
        a1[0] = a1gen(0, HO[0])
        finish_a1(0)
        if NG > 1:
            a1[1] = a1gen(1, HO[1])
        bg = [1] if NG > 1 else []
        cAs = {}
        active = []
        steps = {}
        remaining = {}
        done = set()
        nxt = 0
        while True:
            while nxt < len(tasks):
                g, c4 = tasks[nxt]
                ok_slots = nxt < 2 or (nxt - 2) in done
                ok_stag = nxt == 0 or (nxt - 1) in done or steps.get(nxt - 1, 0) >= 6
                if not (ok_slots and ok_stag):
                    break
                if c4 == 0:
                    finish_a1(g)
                    if g in bg:
                        bg.remove(g)
                    cAs[g] = catA.get()
                lane = nxt % 2
                for cp in range(2):
                    active.append((nxt, unit(slots[lane * 2 + cp], HO[g % 2], cp, c4, cAs[g])))
                steps[nxt] = 0
                remaining[nxt] = 2
                nxt += 1
            if not active:
                break
            for item in list(active):
                idx, gen = item
                try:
                    next(gen)
                except StopIteration:
                    active.remove(item)
                    remaining[idx] -= 1
                    if remaining[idx] == 0:
                        done.add(idx)
                        g, c4 = tasks[idx]
                        if c4 == 3:
                            k.dma("gpsimd", P.cat[0:256, g * 512:(g + 1) * 512].rearrange("(c p) t -> p c t", p=128),
                                  cAs[g][:], reads=[cAs[g]])
                            if g + 2 < NG:
                                a1[g + 2] = a1gen(g + 2, HO[g % 2])
                                bg.append(g + 2)
            for idx in set(i for i, _ in active):
                steps[idx] += 1
            if bg:
                gb_ = bg[0]
                try:
                    next(a1[gb_])
                except StopIteration:
                    a1_finished.add(gb_)
                    bg.remove(gb_)
        k.rot = None


def phase_tail(P, l, xsrc, last):
    k, cst = P.k, P.cst
    with k.scope() as wsc:
        woutb = wsc.tile([128, 8, 1024], BF16)
        wgb = wsc.tile([128, 8, DFF], BF16)
        wub = wsc.tile([128, 8, DFF], BF16)
        wdb = wsc.tile([128, 22, 1024], BF16)
        load_w2(P, woutb, P.w_out[l], 8, 2)
        wg_c = [Tl(wgb.t) for _ in range(22)]
        wu_c = [Tl(wub.t) for _ in range(22)]
        for c in range(22):
            cs = slice(c * 128, (c + 1) * 128)
            k.dma("gpsimd", wgb[:, :, cs], P.w_gate[l][:, :, cs], writes=[wg_c[c]])
            k.dma("gpsimd", wub[:, :, cs], P.w_up[l][:, :, cs], writes=[wu_c[c]])
        with k.scope() as sc:
            vec = sc.tile([128, NV], F32)
            k.dma("sync", vec[:], P.vecs[l], writes=[vec])
            gb = make_gb(P, sc, vec, V_GFFN)
            xts = sc.pool(2, [128, D], F32)
            x1s = sc.pool(2, [128, D], F32)
            cts = sc.pool(1, [128, 8, 512], BF16)
            catv = P.cat.rearrange("(k p) t -> p k t", p=128)
            ss = sc.tile([128, 1], F32)
            rstd = sc.tile([128, 1], F32)
            xn = sc.tile([128, D], BF16)
            junk = xn
            hTs = sc.pool(2, [128, 8, 512], BF16)
            sg = sc.pool(1, [128, 512], F32)
            ab = sc.pool(3, [128, 512], BF16)

            def norm_gen(g_, hT_):
                ct = cts.get()
                k.dma("sync", ct[:], catv[:, :, g_ * 512:(g_ + 1) * 512], writes=[ct])
                for tt in range(4):
                    rows = slice(g_ * 512 + tt * 128, g_ * 512 + (tt + 1) * 128)
                    xt = xts.get()
                    k.dma("sync", xt[:], xsrc[rows, :], writes=[xt])
                    x1t = x1s.get()
                    for nb in range(2):
                        ps = k.ps()
                        for kk in range(8):
                            k.mm(ps[:], ct[:, kk, tt * 128:(tt + 1) * 128], woutb[:, kk, nb * 512:(nb + 1) * 512],
                                 start=(kk == 0), stop=(kk == 7), reads=[ct, woutb], writes=[ps])
                        k.tt("vector", x1t[:, nb * 512:(nb + 1) * 512], ps[:], xt[:, nb * 512:(nb + 1) * 512], ALU.add,
                             reads=[ps, xt], writes=[x1t])
                    k.dma("sync", P.x1[rows, :], x1t[:], reads=[x1t])
                    rmsnorm_T(P, x1t, junk, ss, rstd, xn, P.pTt, gb, hT_[:, :, tt * 128:(tt + 1) * 128], hT_)
                    yield
            hT_next = hTs.get()
            for _ in norm_gen(0, hT_next):
                pass
            for g in range(NG):
                cols = slice(g * 512, (g + 1) * 512)
                hT = hT_next
                gen = None
                if g + 1 < NG:
                    hT_next = hTs.get()
                    gen = norm_gen(g + 1, hT_next)
                if g == 1:
                    load_w2(P, wdb, P.w_down[l], 22, 2)
                for c in range(22):
                    if gen is not None and c in (4, 9, 14, 19):
                        next(gen, None)
                    cs = slice(c * 128, (c + 1) * 128)
                    pg, pu = k.ps(), k.ps()
                    for kk in range(8):
                        k.mm(pg[:], wgb[:, kk, cs], hT[:, kk, :], start=(kk == 0), stop=(kk == 7), reads=[wg_c[c], hT], writes=[pg])
                    for kk in range(8):
                        k.mm(pu[:], wub[:, kk, cs], hT[:, kk, :], start=(kk == 0), stop=(kk == 7), reads=[wu_c[c], hT], writes=[pu])
                    s_ = sg.get()
                    k.act(s_[:], pg[:], AF.Silu, reads=[pg], writes=[s_])
                    a_ = ab.get()
                    k.tt("vector", a_[:], s_[:], pu[:], ALU.mult, reads=[s_, pu], writes=[a_])
                    k.dma("gpsimd", P.aT[cs, cols], a_[:], reads=[a_])
        with k.scope() as sc:
            gfb = sc.tile([128, D], F32)
            k.dma("sync", gfb[:], P.gfinb[:, :], writes=[gfb])
            ats = sc.pool(2, [128, 22, 256], BF16)
            xts = sc.pool(2, [128, D], F32)
            x2s = sc.pool(2, [128, D], F32)
            junk = sc.tile([128, D], BF16)
            ss = sc.pool(2, [128, 1], F32)
            aTv = P.aT.rearrange("(k p) t -> p k t", p=128)
            for hg in range(2 * NG):
                at = ats.get()
                k.dma("sync", at[:], aTv[:, :, hg * 256:(hg + 1) * 256], writes=[at])
                for tt in range(2):
                    rows = slice(hg * 256 + tt * 128, hg * 256 + (tt + 1) * 128)
                    xt = xts.get()
                    k.dma("sync", xt[:], P.x1[rows, :], writes=[xt])
                    x2 = x2s.get()
                    for nb in range(2):
                        ps = k.ps()
                        for c in range(22):
                            k.mm(ps[:], at[:, c, tt * 128:(tt + 1) * 128], wdb[:, c, nb * 512:(nb + 1) * 512],
                                 start=(c == 0), stop=(c == 21), reads=[at, wdb], writes=[ps])
                        k.tt("vector", x2[:, nb * 512:(nb + 1) * 512], ps[:], xt[:, nb * 512:(nb + 1) * 512], ALU.add,
                             reads=[ps, xt], writes=[x2])
                    if not last:
                        k.dma("gpsimd", P.xr[rows, :], x2[:], reads=[x2])
                    else:
                        s1 = ss.get()
                        k.act(junk[:], x2[:], AF.Square, accum_out=s1[:], reads=[x2], writes=[junk, s1])
                        k.act(s1[:], s1[:], AF.Sqrt, bias=EPS, scale=1.0 / D, reads=[s1], writes=[s1])
                        k.recip(s1[:], s1[:], reads=[s1], writes=[s1])
                        k.stt("vector", x2[:], x2[:], s1[:, 0:1], gfb[:], ALU.mult, ALU.mult, reads=[x2, s1, gfb], writes=[x2])
                        k.dma("gpsimd", P.out[rows, :], x2[:], reads=[x2])


def make_in_maps(inputs):
    f = lambda a: np.ascontiguousarray(np.asarray(a))
    inp = {kk: f(v) for kk, v in inputs.items()}
    shared = {
        "consts": make_consts(),
        "vecs": np.stack([make_vecs(inp, l) for l in range(L)]),
        "lnwb": np.stack([np.broadcast_to(np.concatenate([inp["a_ln_w"][l], inp["a_ln_b"][l]])[None, :], (128, 512)).copy()
                          for l in range(L)]),
        "gfinb": np.broadcast_to(inp["final_norm_g"][None, :], (128, D)).copy(),
        "w_in": np.stack([ktile(inp["w_in"][l]) for l in range(L)]),
        "w_uq": np.stack([ktile(inp["c_w_uq"][l]) for l in range(L)]),
        "w_ukv": np.stack([ktile(np.concatenate(
            [inp["c_w_ukv"][l].reshape(128, 4, 256)[:, :, :128].reshape(128, 512),
             inp["c_w_ukv"][l].reshape(128, 4, 256)[:, :, 128:].reshape(128, 512)], axis=1)) for l in range(L)]),
        "w_out": np.stack([ktile(inp["w_out"][l]) for l in range(L)]),
        "w_gate": np.stack([ktile(inp["ffn_w_gate"][l]) for l in range(L)]),
        "w_up": np.stack([ktile(inp["ffn_w_up"][l]) for l in range(L)]),
        "w_down": np.stack([ktile(inp["ffn_w_down"][l]) for l in range(L)]),
        "a_dec": inp["a_decay_up"], "a_icl": inp["a_iclr_up"], "a_gate": inp["a_gate_up"],
    }
    maps = []
    for b in range(8):
        m = dict(shared)
        m["x"] = inp["x"][b]
        m["pos"] = inp["positions"][b].reshape(1, S_LEN).astype(np.int32)
        maps.append(m)
    return maps


def kernel(**inputs):
    nc = build()
    maps = make_in_maps(inputs)
    res = run_bass_kernel_spmd(nc, maps, core_ids=list(range(8)))
    return np.stack([np.asarray(r["out"]) for r in res.results]).astype(np.float32)
```
